# Optimizing a Trainium2 kernel written in Bass

```python
import jax, jax.numpy as jnp
from jax import lax
import numpy as np

D_MODEL = 2048
BATCH = 4
SEQ = 4096
DEPTH = 2

N_HEADS_MLA = 8
QK_NOPE_DIM = 128
QK_ROPE_DIM = 64
V_HEAD_DIM = 128
Q_LORA_RANK = 512
KV_LORA_RANK = 256
D_MLA = N_HEADS_MLA * V_HEAD_DIM
ROPE_THETA = 10000.0
Q_BLOCK = 128
POOL_WINDOWS = (2, 4, 8, 16)
N_POOL_GROUPS = 4
POOL_GROUP_DIM = 128
D_POOL = N_POOL_GROUPS * POOL_GROUP_DIM
N_CONV_HEADS = 4
CONV_HEAD_DIM = 128
D_CONV = N_CONV_HEADS * CONV_HEAD_DIM
CONV_WIDTH = 3
D_MIX = D_MLA + D_POOL + D_CONV
SPLIT_SIZES = (Q_LORA_RANK, KV_LORA_RANK, QK_ROPE_DIM, D_MLA, D_POOL, D_POOL, D_CONV, D_CONV, D_CONV, D_CONV)
D_IN_PROJ = sum(SPLIT_SIZES)
LN_EPS = 1e-5
RMS_EPS = 1e-6
DEEPNORM_ALPHA = (2 * DEPTH) ** 0.25
DEEPNORM_BETA = (8 * DEPTH) ** -0.25

kernel_name = "hybrid_mla_pool_shortconv_deepnorm"


def layernorm(x, g, b):
    xf = x.astype(jnp.float32)
    mu = jnp.mean(xf, axis=-1, keepdims=True)
    var = jnp.mean(jnp.square(xf - mu), axis=-1, keepdims=True)
    y = (xf - mu) * lax.rsqrt(var + LN_EPS) * g.astype(jnp.float32) + b.astype(jnp.float32)
    return y.astype(x.dtype)


def rmsnorm(x, g):
    xf = x.astype(jnp.float32)
    y = xf * lax.rsqrt(jnp.mean(jnp.square(xf), axis=-1, keepdims=True) + RMS_EPS) * g.astype(jnp.float32)
    return y.astype(x.dtype)


def apply_rope(x, cos, sin):
    half = QK_ROPE_DIM // 2
    xf = x.astype(jnp.float32)
    x1, x2 = xf[..., :half], xf[..., half:]
    return jnp.concatenate([x1 * cos - x2 * sin, x2 * cos + x1 * sin], axis=-1).astype(x.dtype)


def mla_mixer(q_lat, kv_lat, k_rope, positions, q_norm_g, kv_norm_g, w_uq, w_ukv):
    Bn, S, _ = q_lat.shape
    H = N_HEADS_MLA
    q = (rmsnorm(q_lat, q_norm_g) @ w_uq).reshape(Bn, S, H, QK_NOPE_DIM + QK_ROPE_DIM)
    q_nope, q_rope = q[..., :QK_NOPE_DIM], q[..., QK_NOPE_DIM:]
    kv = (rmsnorm(kv_lat, kv_norm_g) @ w_ukv).reshape(Bn, S, H, QK_NOPE_DIM + V_HEAD_DIM)
    k_nope, v = kv[..., :QK_NOPE_DIM], kv[..., QK_NOPE_DIM:]
    half = QK_ROPE_DIM // 2
    inv_freq = ROPE_THETA ** (-jnp.arange(half, dtype=jnp.float32) / half)
    ang = positions.astype(jnp.float32)[..., None] * inv_freq
    cos, sin = jnp.cos(ang), jnp.sin(ang)
    q_rope = apply_rope(q_rope, cos[:, :, None, :], sin[:, :, None, :])
    k_rope = apply_rope(k_rope, cos, sin)
    scale = (QK_NOPE_DIM + QK_ROPE_DIM) ** -0.5
    nb = S // Q_BLOCK
    qn_blocks = q_nope.reshape(Bn, nb, Q_BLOCK, H, QK_NOPE_DIM).transpose(1, 0, 2, 3, 4)
    qr_blocks = q_rope.reshape(Bn, nb, Q_BLOCK, H, QK_ROPE_DIM).transpose(1, 0, 2, 3, 4)
    key_idx = jnp.arange(S)

    def attend(args):
        qn, qr, blk = args
        s = (jnp.einsum('bqhd,bkhd->bhqk', qn, k_nope).astype(jnp.float32)
             + jnp.einsum('bqhr,bkr->bhqk', qr, k_rope).astype(jnp.float32)) * scale
        q_idx = blk * Q_BLOCK + jnp.arange(Q_BLOCK)
        causal = key_idx[None, :] <= q_idx[:, None]
        s = jnp.where(causal[None, None], s, -jnp.inf)
        p = jax.nn.softmax(s, axis=-1).astype(v.dtype)
        return jnp.einsum('bhqk,bkhd->bqhd', p, v)

    out = lax.map(attend, (qn_blocks, qr_blocks, jnp.arange(nb)))
    return out.transpose(1, 0, 2, 3, 4).reshape(Bn, S, D_MLA)


def pool_mixer(h, w_pool, pool_scale):
    Bn, S, _ = h.shape
    hg = h.reshape(Bn, S, N_POOL_GROUPS, POOL_GROUP_DIM).astype(jnp.float32)
    cs = jnp.cumsum(hg, axis=1)
    t1 = jnp.arange(1, S + 1, dtype=jnp.float32)
    means = []
    for g, w in enumerate(POOL_WINDOWS):
        c = cs[:, :, g]
        lag = jnp.pad(c, ((0, 0), (w, 0), (0, 0)))[:, :S]
        means.append((c - lag) / jnp.minimum(t1, float(w))[None, :, None])
    pooled = (jnp.stack(means, axis=2) - hg).astype(h.dtype)
    y = jnp.einsum('bsgc,gcd->bsgd', pooled, w_pool).reshape(Bn, S, D_POOL)
    return y * pool_scale


def conv_mixer(h, b_gate, c_gate, conv_w):
    u = c_gate * h
    y = lax.conv_general_dilated(u, conv_w[:, None, :], window_strides=(1,),
                                 padding=((CONV_WIDTH - 1, 0),),
                                 dimension_numbers=('NWC', 'WIO', 'NWC'),
                                 feature_group_count=D_CONV)
    return b_gate * y


def hybrid_layer(x, positions, w_in, q_norm_g, kv_norm_g, w_uq, w_ukv, w_pool, pool_scale,
                 conv_w, w_out, b_out, ln_g, ln_b):
    split_idx = np.cumsum(SPLIT_SIZES)[:-1].tolist()
    proj = x @ w_in
    (q_lat, kv_lat, k_rope, g_mla, p_in, g_pool, c_h, c_b, c_c, g_conv) = jnp.split(proj, split_idx, axis=-1)
    y_mla = mla_mixer(q_lat, kv_lat, k_rope, positions, q_norm_g, kv_norm_g, w_uq, w_ukv) * jax.nn.silu(g_mla)
    y_pool = pool_mixer(p_in, w_pool, pool_scale) * jax.nn.silu(g_pool)
    y_conv = conv_mixer(c_h, c_b, c_c, conv_w) * jax.nn.silu(g_conv)
    mix = jnp.concatenate([y_mla, y_pool, y_conv], axis=-1)
    out = mix @ w_out + b_out
    return layernorm(DEEPNORM_ALPHA * x + out, ln_g, ln_b)


def setup_inputs(seed: int = 0) -> dict:
    key = jax.random.key(seed)
    ks = jax.random.split(key, 16)
    f32 = jnp.float32
    nrm = lambda k, shape, s: jax.random.normal(k, shape, f32) * s
    x = jax.random.normal(ks[0], (BATCH, SEQ, D_MODEL), f32)
    positions = jnp.broadcast_to(jnp.arange(SEQ, dtype=jnp.int32), (BATCH, SEQ))
    return {
        "x": x,
        "positions": positions,
        "emb_ln_g": 1.0 + nrm(ks[1], (D_MODEL,), 0.01),
        "emb_ln_b": nrm(ks[2], (D_MODEL,), 0.01),
        "w_in": nrm(ks[3], (DEPTH, D_MODEL, D_IN_PROJ), D_MODEL ** -0.5),
        "q_norm_g": 1.0 + nrm(ks[4], (DEPTH, Q_LORA_RANK), 0.01),
        "kv_norm_g": 1.0 + nrm(ks[5], (DEPTH, KV_LORA_RANK), 0.01),
        "w_uq": nrm(ks[6], (DEPTH, Q_LORA_RANK, N_HEADS_MLA * (QK_NOPE_DIM + QK_ROPE_DIM)), Q_LORA_RANK ** -0.5),
        "w_ukv": nrm(ks[7], (DEPTH, KV_LORA_RANK, N_HEADS_MLA * (QK_NOPE_DIM + V_HEAD_DIM)), KV_LORA_RANK ** -0.5),
        "w_pool": nrm(ks[8], (DEPTH, N_POOL_GROUPS, POOL_GROUP_DIM, POOL_GROUP_DIM), POOL_GROUP_DIM ** -0.5),
        "pool_scale": 1.0 + nrm(ks[9], (DEPTH, D_POOL), 0.1),
        "conv_w": nrm(ks[10], (DEPTH, CONV_WIDTH, D_CONV), CONV_WIDTH ** -0.5),
        "w_out": nrm(ks[11], (DEPTH, D_MIX, D_MODEL), DEEPNORM_BETA * D_MIX ** -0.5),
        "b_out": nrm(ks[12], (DEPTH, D_MODEL), 0.01),
        "ln_g": 1.0 + nrm(ks[13], (DEPTH, D_MODEL), 0.01),
        "ln_b": nrm(ks[14], (DEPTH, D_MODEL), 0.01),
    }


def reference(x, positions, emb_ln_g, emb_ln_b, w_in, q_norm_g, kv_norm_g, w_uq, w_ukv, w_pool,
              pool_scale, conv_w, w_out, b_out, ln_g, ln_b):
    h = layernorm(x, emb_ln_g, emb_ln_b)
    for l in range(DEPTH):
        h = hybrid_layer(h, positions, w_in[l], q_norm_g[l], kv_norm_g[l], w_uq[l], w_ukv[l], w_pool[l],
                         pool_scale[l], conv_w[l], w_out[l], b_out[l], ln_g[l], ln_b[l])
    return h
```

```python
import math
import contextlib
import numpy as np
import ml_dtypes
import concourse.bass as bass
import concourse.mybir as mybir
from concourse.bass_utils import run_bass_kernel_spmd

F32 = mybir.dt.float32
BF16 = mybir.dt.bfloat16
I32 = mybir.dt.int32
AF = mybir.ActivationFunctionType
ALU = mybir.AluOpType

S = 4096
SO = 2048
HL = 4
PAIRS = [[0, 1], [2, 3], [4, 5], [6, 7]]
D = 2048
DEPTH = 2
NH = 8
LN_EPS = 1e-5
RMS_EPS = 1e-6
ALPHA = (2 * DEPTH) ** 0.25
SCALE = 192 ** -0.5
NEG = -30000.0
POOL_W = (2, 4, 8, 16)
NG = 10

ENGS = ("pe", "act", "dve", "pool", "sp")


class Buf:
    __slots__ = ("name", "w", "r")

    def __init__(self, name=""):
        self.name = name
        self.w = None
        self.r = []


class Op:
    __slots__ = ("eng", "fn", "waits", "flag", "dma_key", "dma_val", "seq")

    def __init__(self, eng, fn):
        self.eng = eng
        self.fn = fn
        self.waits = []
        self.flag = False
        self.dma_key = None
        self.dma_val = 0
        self.seq = -1


class Prog:
    def __init__(self, nc):
        self.nc = nc
        self.ops = {e: [] for e in ENGS}
        self.seen = {e: {} for e in ENGS}
        self.seen_dma = {e: {} for e in ENGS}
        self.dma_counts = {}
        self.cc_keys = set()

    def _add(self, eng, fn, reads, writes, dma_key=None):
        op = Op(eng, fn)
        op.seq = len(self.ops[eng])
        deps = []
        for b in reads:
            if b.w is not None:
                deps.append(b.w)
        for b in writes:
            if b.w is not None:
                deps.append(b.w)
            deps.extend(b.r)
        best = {}
        dma_deps = {}
        for d in deps:
            if d.dma_key is not None:
                if dma_deps.get(d.dma_key, 0) < d.dma_val:
                    dma_deps[d.dma_key] = d.dma_val
            else:
                if d.eng == eng and eng == "pe":
                    continue
                if best.get(d.eng, -1) < d.seq:
                    best[d.eng] = d.seq
        for f, s in best.items():
            if self.seen[eng].get(f, -1) >= s:
                continue
            self.seen[eng][f] = s
            dop = self.ops[f][s]
            dop.flag = True
            op.waits.append(("eng", f, dop))
        for k, v in dma_deps.items():
            if self.seen_dma[eng].get(k, 0) >= v:
                continue
            self.seen_dma[eng][k] = v
            op.waits.append(("dma", k, v))
        if dma_key is not None:
            op.dma_key = dma_key
            self.dma_counts[dma_key] = self.dma_counts.get(dma_key, 0) + 16
            op.dma_val = self.dma_counts[dma_key]
        for b in reads:
            b.r.append(op)
        for b in writes:
            b.w = op
            b.r = []
        self.ops[eng].append(op)
        return op

    def op(self, eng, fn, reads=(), writes=()):
        return self._add(eng, fn, reads, writes, None)

    def dma(self, eng, out, in_, reads=(), writes=(), key=None):
        def fn(e):
            return e.dma_start(out=out, in_=in_)
        return self._add(eng, fn, reads, writes, key)

    def collective(self, src, dst, reads, writes, key):
        def fn(e):
            return e.collective_compute("AllGather", ALU.bypass, replica_groups=PAIRS, ins=[src.opt()], outs=[dst.opt()])
        o = self._add("pool", fn, reads, writes, key)
        self.dma_counts[key] -= 15
        o.dma_val = self.dma_counts[key]
        self.cc_keys.add(key)
        return o

    def barrier(self, junk):
        marks = []
        if not hasattr(self, "jb"):
            self.jb = [Buf(), Buf(), Buf()]
        jb = self.jb
        b = Buf()
        self.op("act", lambda e: e.activation(out=junk[:, 0:1], in_=junk[:, 4:5], func=AF.Copy), writes=[b, jb[0]])
        marks.append(b)
        b = Buf()
        self.op("dve", lambda e: e.memset(junk[:, 1:2], 0.0), writes=[b, jb[1]])
        marks.append(b)
        b = Buf()
        self.op("pool", lambda e: e.memset(junk[:, 2:3], 0.0), writes=[b, jb[2]])
        marks.append(b)
        fence = Buf()
        o = self.op("sp", lambda e: e.nop(), reads=marks, writes=[fence])
        for k, v in self.dma_counts.items():
            if self.seen_dma["sp"].get(k, 0) < v:
                self.seen_dma["sp"][k] = v
                o.waits.append(("dma", k, v))
        self.op("act", lambda e: e.activation(out=junk[:, 0:1], in_=junk[:, 4:5], func=AF.Copy), reads=[fence], writes=[jb[0]])
        self.op("dve", lambda e: e.memset(junk[:, 1:2], 0.0), reads=[fence], writes=[jb[1]])
        self.op("pool", lambda e: e.memset(junk[:, 2:3], 0.0), reads=[fence], writes=[jb[2]])
        self.op("pe", lambda e: e.nop(), reads=[fence])
        for e in ENGS:
            for k, v in self.dma_counts.items():
                if self.seen_dma[e].get(k, 0) < v:
                    self.seen_dma[e][k] = v

    def finish(self):
        nc = self.nc
        with contextlib.ExitStack() as st:
            esem = {e: st.enter_context(nc.semaphore("s_" + e)) for e in ENGS}
            dsem = {k: st.enter_context(nc.semaphore("d_%s" % (k,))) for k in self.dma_counts}
            block = st.enter_context(nc.Block())
            for e in ENGS:
                c = 0
                for o in self.ops[e]:
                    if o.flag:
                        c += 1
                        o.dma_val = c

            def emit(e, eng):
                for o in self.ops[e]:
                    for kind, k, v in o.waits:
                        if kind == "eng":
                            eng.wait_ge(esem[k], v.dma_val)
                        else:
                            eng.wait_ge(dsem[k], v)
                    inst = o.fn(eng)
                    if o.dma_key is not None and o.dma_key in self.cc_keys:
                        inst.then_inc(dsem[o.dma_key])
                    elif o.dma_key is not None:
                        inst.then_inc(dsem[o.dma_key], 16)
                    elif o.flag:
                        inst.then_inc(esem[e], 1)
                if e == "sp":
                    for k, v in self.dma_counts.items():
                        eng.wait_ge(dsem[k], v)

            @block.tensor
            def _(eng):
                emit("pe", eng)

            @block.scalar
            def _(eng):
                emit("act", eng)

            @block.vector
            def _(eng):
                emit("dve", eng)

            @block.gpsimd
            def _(eng):
                emit("pool", eng)

            @block.sync
            def _(eng):
                emit("sp", eng)


def build(debug=False, n_layers=DEPTH, stop_phase=None):
    nc = bass.Bass("TRN2", target_bir_lowering=False)
    P = Prog(nc)

    def din(name, shape, dt):
        return nc.dram_tensor(name, shape, dt, kind="ExternalInput").ap()

    x_d = din("x", [SO, D], F32)
    pos_d = din("pos", [1, S], I32)
    poso_d = din("pos_own", [1, SO], I32)
    coef_d = din("coef", [128, 2], F32)
    rc_d = din("ropec", [64, 2], F32)
    lnp_d = din("lnp", [2 + 3 * DEPTH, D], F32)
    win_d = din("w_in_g", [DEPTH * NG * 128, 16 * 512], F32)
    wq_d = din("wq", [DEPTH * 128, 4 * 1024], F32)
    wk_d = din("wk", [DEPTH * 128, 2 * 512], F32)
    wv_d = din("wv", [DEPTH * 128, 2 * 512], F32)
    wo_d = din("wo", [DEPTH * 128, 16 * 2048], F32)
    wp_d = din("wp", [DEPTH * 128, 4 * 128], F32)
    sm_d = din("small", [DEPTH * 128, 32], F32)
    idv_d = din("invdiv", [128, 64], F32)
    ident_d = din("ident", [128, 128], BF16)
    mask_d = din("masks", [128, 4 * 512], BF16)
    out_d = nc.dram_tensor("out", [SO, D], F32, kind="ExternalOutput").ap()

    skind = "ExternalOutput" if debug else "Internal"
    resid_d = nc.dram_tensor("resid", [SO, D], F32, kind=skind).ap()
    hT_d = nc.dram_tensor("hT", [D, SO], BF16, kind=skind).ap()
    mix_d = nc.dram_tensor("mixT", [D, SO], BF16, kind=skind).ap()
    cosA_d = nc.dram_tensor("cosA", [64, S], BF16).ap()
    sinA_d = nc.dram_tensor("sinA", [64, S], BF16).ap()
    cosO_d = nc.dram_tensor("cosO", [64, SO], BF16).ap()
    sinO_d = nc.dram_tensor("sinO", [64, SO], BF16).ap()
    e1a_src = nc.dram_tensor("e1a_src", [512, SO], BF16).ap()
    e1a_g = nc.dram_tensor("e1a_g", [1024, SO], BF16).ap()
    e1b_src = nc.dram_tensor("e1b_src", [320, SO], BF16).ap()
    e1b_g = nc.dram_tensor("e1b_g", [640, SO], BF16).ap()
    e2_src = [nc.dram_tensor("e2_src%d" % i, [128, S], BF16).ap() for i in range(4)]
    e2_g = [nc.dram_tensor("e2_g%d" % i, [256, S], BF16).ap() for i in range(4)]
    tl_src = nc.dram_tensor("tl_src", [D, 16], BF16).ap()
    tl_g = nc.dram_tensor("tl_g", [2 * D, 16], BF16).ap()
    if debug:
        dbg_e1a = nc.dram_tensor("dbg_e1a", [512, SO], BF16, kind="ExternalOutput").ap()
        dbg_e1b = nc.dram_tensor("dbg_e1b", [320, SO], BF16, kind="ExternalOutput").ap()
        dbg_e2 = [nc.dram_tensor("dbg_e2_%d" % i, [128, S], BF16, kind="ExternalOutput").ap() for i in range(4)]

    b_resid = [Buf("resid%d" % i) for i in range(16)]
    b_hT = [Buf("hT%d" % i) for i in range(4)]
    b_qn = [Buf() for i in range(4)]
    b_kvn = [Buf() for i in range(4)]
    b_kr = [Buf() for i in range(4)]
    b_mix = [[Buf() for t in range(4)] for r in range(16)]
    b_out = [Buf() for i in range(16)]
    b_e1g, b_tlsrc, b_tlg = Buf(), Buf(), Buf()
    b_e2g = [Buf() for i in range(4)]
    b_e2s = [[Buf() for t in range(8)] for r in range(4)]
    b_tabd = Buf()

    hT_v = hT_d.rearrange("(kc p) t -> p kc t", p=128)
    mix_v = mix_d.rearrange("(kc p) t -> p kc t", p=128)
    qn_v = e1a_src.rearrange("(kc p) t -> p kc t", p=128)
    kvn_v = e1b_src[0:256, :].rearrange("(kc p) t -> p kc t", p=128)
    kr_d = e1b_src[256:320, :]
    tls_v = tl_src.rearrange("(kc p) t -> p kc t", p=128)
    tlg_v = tl_g.rearrange("(kc p) t -> p kc t", p=128)

    with contextlib.ExitStack() as gst:
        ARENA_WORDS = 51200
        arena = gst.enter_context(nc.sbuf_tensor("arena", [128, ARENA_WORDS], F32))
        a_top = [0]
        a_mark = [0]

        def sb(name, shape, dt, st=None):
            n = 1
            for d_ in shape[1:]:
                n *= d_
            esz = 4 if dt in (F32, I32) else 2
            words = (n * esz + 3) // 4
            words = (words + 7) // 8 * 8
            off = a_top[0]
            assert off + words <= ARENA_WORDS, ("SBUF arena overflow", name, off, words)
            a_top[0] = off + words
            v = arena[0:shape[0], off:off + words]
            if dt != F32:
                v = v.bitcast(dt)
            v = v[:, 0:n]
            if len(shape) == 3:
                v = v.rearrange("p (a b) -> p a b", a=shape[1])
            return v

        def areset():
            a_top[0] = a_mark[0]

        ps_all = gst.enter_context(nc.psum_tensor("ps", [128, 8 * 512], F32))
        ps_bufs = [Buf("ps%d" % i) for i in range(8)]
        ps_ctr = [0]

        def psum(banks=None, ctr=None):
            if banks is None:
                i = ps_ctr[0] % 8
                ps_ctr[0] += 1
            else:
                i = banks[ctr[0] % len(banks)]
                ctr[0] += 1
            return ps_all[:, i * 512:(i + 1) * 512], ps_bufs[i]

        junk = sb("junk", [128, 8], F32)
        ident = sb("ident", [128, 128], BF16)
        ones = sb("ones", [128, 128], BF16)
        masks = sb("masks", [128, 4, 512], BF16)
        rc = sb("rc", [64, 2], F32)
        coef = sb("coef", [128, 2], F32)
        invdiv = sb("invdiv", [128, 4, 16], F32)
        b_const = Buf("const")
        b_tab = Buf("tab")

        P.op("dve", lambda e: e.memset(junk[:], 0.0), writes=[b_const])
        P.op("dve", lambda e: e.memset(ones[:], 1.0), writes=[b_const])
        P.dma("sp", ident[:], ident_d, writes=[b_const], key="c_ident")
        P.dma("sp", masks[:].rearrange("p a b -> p (a b)"), mask_d, writes=[b_const], key="c_mask")
        P.dma("sp", rc[:], rc_d, writes=[b_const], key="c_rc")
        P.dma("sp", coef[:], coef_d, writes=[b_const], key="c_coef")
        P.dma("sp", invdiv[:].rearrange("p a b -> p (a b)"), idv_d, writes=[b_const], key="c_idv")

        LN_EPS_AP = sb("lneps", [128, 4], F32)
        a_mark[0] = a_top[0]

        def rope_tables(pos_ap, N, cos_dst, sin_dst, tag):
            areset()
            posi = sb("posi", [64, N], I32)
            ang = sb("ang", [64, N], F32)
            ta = sb("ta", [64, N], F32)
            tb = sb("tb", [64, N], F32)
            ob = sb("ob", [64, N], BF16)
            b_posi, b_ang, b_ta, b_tb, b_ob = Buf(), Buf(), Buf(), Buf(), Buf()
            P.dma("sp", posi[:], pos_ap.partition_broadcast(64), writes=[b_posi], key="t_pos")
            P.op("dve", lambda e: e.tensor_copy(out=ang[:], in_=posi[:]), reads=[b_posi], writes=[b_ang])
            P.op("dve", lambda e: e.tensor_scalar(out=ang[:], in0=ang[:], scalar1=rc[:, 0:1], scalar2=None, op0=ALU.mult),
                 reads=[b_ang, b_const], writes=[b_ang])
            TWO_PI = 2.0 * math.pi
            for which, phase, dst in (("sin", 0.0, sin_dst), ("cos", math.pi / 2, cos_dst)):
                P.op("dve", lambda e, phase=phase: e.tensor_scalar(out=ta[:], in0=ang[:], scalar1=phase, scalar2=1.0 / TWO_PI,
                                                                    op0=ALU.add, op1=ALU.mult), reads=[b_ang], writes=[b_ta])
                P.op("dve", lambda e: e.tensor_copy(out=posi[:], in_=ta[:]), reads=[b_ta], writes=[b_posi])
                P.op("dve", lambda e: e.tensor_copy(out=ta[:], in_=posi[:]), reads=[b_posi], writes=[b_ta])
                P.op("dve", lambda e: e.scalar_tensor_tensor(out=tb[:], in0=ta[:], scalar=-TWO_PI, in1=ang[:], op0=ALU.mult, op1=ALU.add),
                     reads=[b_ta, b_ang], writes=[b_tb])
                P.op("dve", lambda e, phase=phase: e.tensor_scalar(out=tb[:], in0=tb[:], scalar1=phase, scalar2=None, op0=ALU.add),
                     reads=[b_tb], writes=[b_tb])
                P.op("dve", lambda e: e.tensor_scalar(out=ta[:], in0=tb[:], scalar1=math.pi, scalar2=TWO_PI, op0=ALU.is_gt, op1=ALU.mult),
                     reads=[b_tb], writes=[b_ta])
                P.op("dve", lambda e: e.tensor_tensor(out=tb[:], in0=tb[:], in1=ta[:], op=ALU.subtract), reads=[b_tb, b_ta], writes=[b_tb])
                P.op("dve", lambda e: e.tensor_scalar(out=tb[:], in0=tb[:], scalar1=math.pi, scalar2=-math.pi, op0=ALU.min, op1=ALU.max),
                     reads=[b_tb], writes=[b_tb])
                P.op("act", lambda e: e.activation(out=ta[:], in_=tb[:], func=AF.Sin), reads=[b_tb], writes=[b_ta])
                if which == "sin":
                    P.op("dve", lambda e: e.tensor_scalar(out=ob[:], in0=ta[:], scalar1=rc[:, 1:2], scalar2=None, op0=ALU.mult),
                         reads=[b_ta, b_const], writes=[b_ob])
                else:
                    P.op("dve", lambda e: e.tensor_copy(out=ob[:], in_=ta[:]), reads=[b_ta], writes=[b_ob])
                P.dma("sp", dst, ob[:], reads=[b_ob], writes=[b_tabd], key="t_ob")
            P.barrier(junk)

        rope_tables(pos_d, S, cosA_d, sinA_d, "a")
        rope_tables(poso_d, SO, cosO_d, sinO_d, "o")

        def ln_phase(st, layer_idx, src_kind, write_hT, final):
            areset()
            NB = 4 if src_kind == "x" else 3
            gb = sb("ln_g", [128, D], F32, st)
            bb = sb("ln_b", [128, D], F32, st)
            b_p = Buf()
            if src_kind == "x":
                grow, brow = 0, 1
            else:
                grow, brow = 2 + 3 * layer_idx, 3 + 3 * layer_idx
            P.dma("sp", gb[:], lnp_d[grow:grow + 1, :].partition_broadcast(128), writes=[b_p], key="ln_g")
            P.dma("sp", bb[:], lnp_d[brow:brow + 1, :].partition_broadcast(128), writes=[b_p], key="ln_b")
            ys = [sb("ln_y%d" % i, [128, D], F32, st) for i in range(NB)]
            b_ys = [Buf() for i in range(NB)]
            hbs = [sb("ln_hb%d" % i, [128, D], BF16, st) for i in range(2)]
            b_hbs = [Buf() for i in range(2)]
            stats = [sb("ln_st%d" % i, [128, 4, 6], F32, st) for i in range(NB)]
            mvs = [sb("ln_mv%d" % i, [128, 4], F32, st) for i in range(NB)]
            b_sts = [Buf() for i in range(NB)]
            stg = [sb("ln_stg%d" % i, [128, 16, 512], BF16, st) for i in range(1)] if write_hT else []
            b_stg = [Buf() for i in range(1)]
            if src_kind == "proj":
                bo = sb("ln_bo", [1, D], BF16, st)
                P.dma("pool", bo[:], lnp_d[4 + 3 * layer_idx:5 + 3 * layer_idx, :], writes=[b_p], key="ln_bo")
                wo = sb("wo", [128, 16, 2048], BF16, st)
                b_wop = [Buf() for i in range(4)]
                for i4 in range(4):
                    P.dma("pool", wo[:, i4 * 4:(i4 + 1) * 4, :].rearrange("p a b -> p (a b)"),
                          wo_d[layer_idx * 128:(layer_idx + 1) * 128, i4 * 8192:(i4 + 1) * 8192],
                          reads=([b_wop[i4 - 1]] if i4 else [b_p]), writes=[b_wop[i4]], key="wo%d" % i4)
                mts = [sb("mt%d" % i, [128, 16, 256], BF16, st) for i in range(2)]
                b_mts = [Buf() for i in range(2)]
                NR = 2
                rts = [sb("rt%d" % i, [128, D], F32, st) for i in range(NR)]
                b_rts = [Buf() for i in range(NR)]
                ea = [sb("ea%d" % i, [128, 8, 256], BF16, st) for i in range(2)]
                eb = [sb("eb%d" % i, [128, 8, 256], BF16, st) for i in range(2)]
                b_ea = [Buf() for i in range(2)]
                b_eb = [Buf() for i in range(2)]
                et = sb("et", [128, 8, 256], BF16, st)
                b_et = Buf()

            def s_load(tk):
                y, b_y = ys[tk % NB], b_ys[tk % NB]
                tsl = slice(tk * 128, (tk + 1) * 128)
                if src_kind == "x":
                    P.dma("sp", y[:], x_d[tsl, :], writes=[b_y], key="ln_y%d" % (tk % NB))
                else:
                    tt = tk // 4
                    t2 = tk // 2
                    mt, b_mt = mts[t2 % 2], b_mts[t2 % 2]
                    if tk % 2 == 0:
                        i2 = t2 % 2
                        P.dma("sp", mt[:], mix_v[:, :, t2 * 256:(t2 + 1) * 256], reads=[b_mix[r][tt] for r in range(16)],
                              writes=[b_mt], key="mt%d" % i2)
                        for rho in range(2):
                            for hl in range(4):
                                kcg = rho * 4 + hl
                                P.dma("sp", ea[i2][:, kcg, :], e2_g[hl][rho * 128:(rho + 1) * 128, t2 * 256:(t2 + 1) * 256],
                                      reads=[b_e2g[hl]], writes=[b_ea[i2]], key="ea%d_%d" % (i2, kcg))
                                P.dma("sp", eb[i2][:, kcg, :], e2_g[hl][rho * 128:(rho + 1) * 128, SO + t2 * 256:SO + (t2 + 1) * 256],
                                      reads=[b_e2g[hl]], writes=[b_eb[i2]], key="eb%d_%d" % (i2, kcg))
                        P.op("pool", lambda e, i2=i2: e.tensor_scalar(out=et[:], in0=ea[i2][:], scalar1=coef[:, 0:1], scalar2=None, op0=ALU.mult),
                             reads=[b_ea[i2], b_const], writes=[b_et])
                        P.op("dve", lambda e, i2=i2: e.scalar_tensor_tensor(out=et[:], in0=eb[i2][:], scalar=coef[:, 1:2], in1=et[:], op0=ALU.mult, op1=ALU.add),
                             reads=[b_eb[i2], b_et, b_const], writes=[b_et])
                        P.op("dve", lambda e, mt=mt: e.tensor_tensor(out=mt[:, 0:8, :], in0=mt[:, 0:8, :], in1=et[:], op=ALU.mult),
                             reads=[b_et, b_mt], writes=[b_mt])
                    rt, b_rt = rts[tk % NR], b_rts[tk % NR]
                    P.dma("sp", rt[:], resid_d[tsl, :], reads=[b_resid[tk]], writes=[b_rt], key="rt%d" % (tk % NR))

            def s1(tk):
                y, b_y = ys[tk % NB], b_ys[tk % NB]
                stt, mv, b_st = stats[tk % NB], mvs[tk % NB], b_sts[tk % NB]
                if src_kind != "x":
                    t2 = tk // 2
                    mt, b_mt = mts[t2 % 2], b_mts[t2 % 2]
                    rt, b_rt = rts[tk % NR], b_rts[tk % NR]
                    for cg in range(4):
                        pt, b_pt = psum()
                        P.op("pe", lambda e, pt=pt, cg=cg: e.matmul(pt, ones[0:1, :], bo[0:1, cg * 512:(cg + 1) * 512], start=True, stop=False),
                             reads=[b_p, b_const], writes=[b_pt])
                        for kc in range(16):
                            P.op("pe", lambda e, pt=pt, kc=kc, cg=cg: e.matmul(
                                pt, mt[:, kc, (tk % 2) * 128:(tk % 2 + 1) * 128], wo[:, kc, cg * 512:(cg + 1) * 512],
                                start=False, stop=(kc == 15)), reads=[b_mt, b_wop[kc // 4]], writes=[b_pt])
                        P.op("dve", lambda e, pt=pt, cg=cg: e.scalar_tensor_tensor(
                            out=y[:, cg * 512:(cg + 1) * 512], in0=rt[:, cg * 512:(cg + 1) * 512], scalar=ALPHA, in1=pt,
                            op0=ALU.mult, op1=ALU.add), reads=[b_pt, b_rt], writes=[b_y])
                for c in range(4):
                    P.op("dve", lambda e, c=c: e.bn_stats(out=stt[:, c, :], in_=y[:, c * 512:(c + 1) * 512]),
                         reads=[b_y], writes=[b_st])
                P.op("dve", lambda e: e.bn_aggr(out=mv[:, 0:2], in_=stt[:].rearrange("p a b -> p (a b)")),
                     reads=[b_st], writes=[b_st])
                P.op("act", lambda e: e.activation(out=mv[:, 2:3], in_=mv[:, 1:2], func=AF.Sqrt, bias=LN_EPS_AP[:, 0:1], scale=1.0),
                     reads=[b_st, b_const], writes=[b_st])
                P.op("dve", lambda e: e.reciprocal(out=mv[:, 2:3], in_=mv[:, 2:3]), reads=[b_st], writes=[b_st])
                P.op("dve", lambda e: e.scalar_tensor_tensor(out=mv[:, 3:4], in0=mv[:, 0:1], scalar=-1.0, in1=mv[:, 2:3],
                                                              op0=ALU.mult, op1=ALU.mult), reads=[b_st], writes=[b_st])

            def s2a(tk):
                y, b_y = ys[tk % NB], b_ys[tk % NB]
                mv, b_st = mvs[tk % NB], b_sts[tk % NB]
                P.op("act", lambda e: e.activation(out=y[:], in_=y[:], func=AF.Identity, bias=mv[:, 3:4], scale=mv[:, 2:3]),
                     reads=[b_y, b_st], writes=[b_y])

            def s2b(tk):
                y, b_y = ys[tk % NB], b_ys[tk % NB]
                P.op("dve", lambda e: e.tensor_tensor(out=y[:], in0=y[:], in1=gb[:], op=ALU.mult), reads=[b_y, b_p], writes=[b_y])
                P.op("pool", lambda e: e.tensor_tensor(out=y[:], in0=y[:], in1=bb[:], op=ALU.add), reads=[b_y, b_p], writes=[b_y])

            def s3(tk):
                y, b_y = ys[tk % NB], b_ys[tk % NB]
                hb, b_hb = hbs[tk % 2], b_hbs[tk % 2]
                tsl = slice(tk * 128, (tk + 1) * 128)
                if final:
                    P.dma("sp", out_d[tsl, :], y[:], reads=[b_y], writes=[b_out[tk]], key="ln_o%d" % (tk % NB))
                else:
                    P.dma("sp", resid_d[tsl, :], y[:], reads=[b_y], writes=[b_resid[tk]], key="ln_o%d" % (tk % NB))
                if write_hT:
                    P.op("act", lambda e: e.copy(out=hb[:], in_=y[:]), reads=[b_y], writes=[b_hb])
                    sg_, b_sg_ = stg[0], b_stg[0]
                    for q4 in range(4):
                        pt, b_pt = psum()
                        ptb = pt.bitcast(BF16)
                        for j in range(4):
                            kc = q4 * 4 + j
                            P.op("pe", lambda e, ptb=ptb, kc=kc, j=j: e.transpose(
                                ptb[:, j * 128:(j + 1) * 128], hb[:, kc * 128:(kc + 1) * 128], ident[:]),
                                reads=[b_hb, b_const], writes=[b_pt])
                        if q4 % 2 == 0:
                            P.op("dve", lambda e, ptb=ptb, q4=q4: e.tensor_copy(
                                out=sg_[:, q4 * 4:(q4 + 1) * 4, (tk % 4) * 128:(tk % 4 + 1) * 128],
                                in_=ptb[:, 0:512].rearrange("p (a b) -> p a b", a=4)), reads=[b_pt], writes=[b_sg_])
                        else:
                            P.op("act", lambda e, ptb=ptb, q4=q4: e.copy(
                                out=sg_[:, q4 * 4:(q4 + 1) * 4, (tk % 4) * 128:(tk % 4 + 1) * 128],
                                in_=ptb[:, 0:512].rearrange("p (a b) -> p a b", a=4)), reads=[b_pt], writes=[b_sg_])
                    if tk % 4 == 3:
                        tt = tk // 4
                        P.dma("sp", hT_v[:, :, tt * 512:(tt + 1) * 512], sg_[:], reads=[b_sg_], writes=[b_hT[tt]], key="ln_stg")
                        if tk == NTK - 1:
                            P.dma("sp", tls_v, sg_[:, :, 496:512], reads=[b_sg_], writes=[b_tlsrc], key="ln_tl")
                            P.collective(tl_src, tl_g, reads=[b_tlsrc], writes=[b_tlg], key="cc_e3")

            NTK = SO // 128
            s_load(0)
            for i in range(NTK + 2):
                if i + 1 < NTK:
                    s_load(i + 1)
                if 0 <= i - 1 < NTK:
                    s2a(i - 1)
                if i < NTK:
                    s1(i)
                if 0 <= i - 1 < NTK:
                    s2b(i - 1)
                if 0 <= i - 2 < NTK:
                    s3(i - 2)

        P.op("dve", lambda e: e.memset(LN_EPS_AP[:, 0:1], LN_EPS), writes=[b_const])
        P.op("dve", lambda e: e.memset(LN_EPS_AP[:, 1:2], RMS_EPS), writes=[b_const])

        with contextlib.ExitStack() as st:
            ln_phase(st, 0, "x", True, False)
            P.barrier(junk)

        def phase_A(L):
            with contextlib.ExitStack() as st:
                areset()
                hTs = sb("hTs", [128, 16, 2048], BF16, st)
                b_hTs = Buf()
                wr = [sb("wr%d" % i, [128, 16, 512], BF16, st) for i in range(2)]
                b_wr = [Buf() for i in range(2)]
                sm = sb("sm", [128, 32], F32, st)
                wp = sb("wp", [128, 4, 128], BF16, st)
                b_sm = Buf()
                P.dma("sp", sm[:], sm_d[L * 128:(L + 1) * 128, :], writes=[b_sm], key="sm")
                P.dma("pool", wp[:].rearrange("p a b -> p (a b)"), wp_d[L * 128:(L + 1) * 128, :], writes=[b_sm], key="wp")
                hp = sb("hp", [128, 4, 16], F32, st)
                hc = sb("hc", [128, 4, 2], F32, st)
                b_hp = [Buf() for i in range(4)]
                b_hc = [Buf() for i in range(4)]
                P.op("pool", lambda e: e.memset(hp[:], 0.0), writes=b_hp)
                P.op("pool", lambda e: e.memset(hc[:], 0.0), writes=b_hc)
                stgs = [sb("stg%d" % i, [128, 4, 512], BF16, st) for i in range(2)]
                b_stgs = [Buf() for i in range(2)]
                stg_ctr = [0]
                NTMP = 6
                tmps = [sb("tmp%d" % i, [128, 528], F32, st) for i in range(NTMP)]
                b_tmps = [Buf() for i in range(NTMP)]
                tmp_ctr = [0]
                sqs = [sb("sq%d" % i, [128, 512], BF16, st) for i in range(2)]
                b_sqs = [Buf() for i in range(2)]
                sq_ctr = [0]
                pls = [sb("pl%d" % i, [128, 512], BF16, st) for i in range(4)]
                b_pls = [Buf() for i in range(4)]
                sgps = [sb("sgp%d" % i, [128, 512], F32, st) for i in range(4)]
                b_sgps = [Buf() for i in range(4)]
                pl_ctr = [0]
                pending = []
                cosT = sb("cosO", [64, SO], BF16, st)
                sinT = sb("sinO", [64, SO], BF16, st)
                b_tab = Buf()
                P.dma("sp", cosT[:], cosO_d, reads=[b_tabd], writes=[b_tab], key="cosO")
                P.dma("sp", sinT[:], sinO_d, reads=[b_tabd], writes=[b_tab], key="sinO")
                hTh = sb("hTh", [128, 16, 16], BF16, st)
                b_hTh = Buf()
                P.dma("sp", hTh[:], tlg_v[:, 0:16, :], reads=[b_tlg], writes=[b_hTh], key="hTh")

                def proj_halo(w, c0):
                    pt, b_pt = psum()
                    for kc in range(16):
                        P.op("pe", lambda e, pt=pt, w=w, kc=kc: e.matmul(
                            pt[:, 0:16], w[0][:, kc, c0:c0 + 128], hTh[:, kc, :],
                            start=(kc == 0), stop=(kc == 15)), reads=[w[1], b_hTh], writes=[b_pt])
                    return pt, b_pt

                def tmp():
                    i = tmp_ctr[0] % NTMP
                    tmp_ctr[0] += 1
                    return tmps[i], b_tmps[i]

                def proj(w, c0, ncols, tt):
                    pt, b_pt = psum()
                    for kc in range(16):
                        P.op("pe", lambda e, pt=pt, w=w, kc=kc: e.matmul(
                            pt[0:ncols, :], w[0][:, kc, c0:c0 + ncols], hTs[:, kc, tt * 512:(tt + 1) * 512],
                            start=(kc == 0), stop=(kc == 15)), reads=[w[1], b_hTs], writes=[b_pt])
                    return pt, b_pt

                def rmsnorm_group(w, c0, nch, tt, dim, dst_v, dst_bufs, gtt, key):
                    pts = [proj(w, c0 + c * 128, 128, tt) for c in range(nch)]
                    ss, b_ss = psum()
                    for c, (pt, b_pt) in enumerate(pts):
                        i = sq_ctr[0] % 2
                        sq_ctr[0] += 1
                        sq, b_sq = sqs[i], b_sqs[i]
                        P.op("act", lambda e, pt=pt, sq=sq: e.activation(out=sq[:], in_=pt, func=AF.Square), reads=[b_pt], writes=[b_sq])
                        P.op("pe", lambda e, ss=ss, sq=sq, c=c: e.matmul(ss, ones[:], sq[:], start=(c == 0), stop=(c == nch - 1)),
                             reads=[b_sq, b_const], writes=[b_ss])
                    rs, b_rs = tmp()
                    P.op("act", lambda e, rs=rs, ss=ss: e.activation(out=rs[:, 0:512], in_=ss, func=AF.Sqrt, bias=LN_EPS_AP[:, 1:2],
                                                                      scale=1.0 / dim), reads=[b_ss, b_const], writes=[b_rs])
                    P.op("dve", lambda e, rs=rs: e.reciprocal(out=rs[:, 0:512], in_=rs[:, 0:512]), reads=[b_rs], writes=[b_rs])
                    i = stg_ctr[0] % 2
                    stg_ctr[0] += 1
                    sg_, b_sg_ = stgs[i], b_stgs[i]
                    for c, (pt, b_pt) in enumerate(pts):
                        P.op("dve", lambda e, pt=pt, rs=rs, sg_=sg_, c=c: e.tensor_tensor(out=sg_[:, c, :], in0=pt, in1=rs[:, 0:512], op=ALU.mult),
                             reads=[b_pt, b_rs], writes=[b_sg_])
                    P.dma("sp", dst_v[:, :, gtt * 512:(gtt + 1) * 512], sg_[:, 0:nch, :], reads=[b_sg_], writes=[dst_bufs[gtt]], key="stg%d" % i)

                for hf in range(1):
                    P.dma("sp", hTs[:], hT_v[:, :, hf * 2048:(hf + 1) * 2048], reads=b_hT[hf * 4:(hf + 1) * 4], writes=[b_hTs], key="hTs")
                    for g in range(NG):
                        wi = (hf * NG + g) % 2
                        w = (wr[wi], b_wr[wi])
                        row0 = (L * NG + g) * 128
                        P.dma("pool", wr[wi][:].rearrange("p a b -> p (a b)"), win_d[row0:row0 + 128, :], writes=[b_wr[wi]], key="wr%d" % wi)
                        for tt in range(4):
                            gtt = hf * 4 + tt
                            tok = slice(gtt * 512, (gtt + 1) * 512)
                            if g == 0:
                                rmsnorm_group(w, 0, 2, tt, 256.0, kvn_v, b_kvn, gtt, "kvn")
                                pa, b_pa = proj(w, 256, 64, tt)
                                pb, b_pb = proj(w, 320, 64, tt)
                                t1, b_t1 = tmp()
                                t2, b_t2 = tmp()
                                P.op("dve", lambda e, pa=pa, t1=t1, tok=tok: e.tensor_tensor(out=t1[0:64, 0:512], in0=pa[0:64, :], in1=cosT[:, tok], op=ALU.mult),
                                     reads=[b_pa, b_tab], writes=[b_t1])
                                P.op("dve", lambda e, pb=pb, t2=t2, tok=tok: e.tensor_tensor(out=t2[0:64, 0:512], in0=pb[0:64, :], in1=sinT[:, tok], op=ALU.mult),
                                     reads=[b_pb, b_tab], writes=[b_t2])
                                i = stg_ctr[0] % 2
                                stg_ctr[0] += 1
                                sg_, b_sg_ = stgs[i], b_stgs[i]
                                P.op("pool", lambda e, t1=t1, t2=t2, sg_=sg_: e.tensor_tensor(out=sg_[0:64, 0, :], in0=t1[0:64, 0:512], in1=t2[0:64, 0:512], op=ALU.add),
                                     reads=[b_t1, b_t2], writes=[b_sg_])
                                P.dma("sp", kr_d[:, tok], sg_[0:64, 0, :], reads=[b_sg_], writes=[b_kr[gtt]], key="stg%d" % i)
                            elif g == 1:
                                rmsnorm_group(w, 0, 4, tt, 512.0, qn_v, b_qn, gtt, "qn")
                            elif g in (2, 3):
                                i = stg_ctr[0] % 2
                                stg_ctr[0] += 1
                                sg_, b_sg_ = stgs[i], b_stgs[i]
                                for c in range(4):
                                    pt, b_pt = proj(w, c * 128, 128, tt)
                                    P.op("act", lambda e, pt=pt, sg_=sg_, c=c: e.activation(out=sg_[:, c, :], in_=pt, func=AF.Silu),
                                         reads=[b_pt], writes=[b_sg_])
                                r0 = (g - 2) * 4
                                P.dma("sp", mix_v[:, r0:r0 + 4, tok], sg_[:], reads=[b_sg_], writes=[b_mix[r0 + c][gtt] for c in range(4)], key="stg%d" % i)
                            elif g in (4, 5):
                                tails = []
                                for s_ in range(2):
                                    pg = (g - 4) * 2 + s_
                                    wlen = POOL_W[pg]
                                    px, b_px = proj(w, s_ * 256, 128, tt)
                                    pgt, b_pgt = proj(w, s_ * 256 + 128, 128, tt)
                                    xb, b_xb = tmp()
                                    sa, b_sa = tmp()
                                    sb_, b_sb = tmp()
                                    P.op("act", lambda e, px=px, xb=xb: e.copy(out=xb[:, 16:528], in_=px), reads=[b_px], writes=[b_xb])
                                    if tt == 0:
                                        ph, b_ph = proj_halo(w, s_ * 256)
                                        P.op("dve", lambda e, xb=xb, ph=ph: e.tensor_scalar(out=xb[:, 0:16], in0=ph[:, 0:16], scalar1=coef[:, 1:2], scalar2=None, op0=ALU.mult),
                                             reads=[b_ph, b_xb, b_const], writes=[b_xb])
                                    else:
                                        P.op("pool", lambda e, xb=xb, pg=pg: e.tensor_copy(out=xb[:, 0:16], in_=hp[:, pg, :]), reads=[b_hp[pg], b_xb], writes=[b_xb])
                                    P.op("pool", lambda e, xb=xb, pg=pg: e.tensor_copy(out=hp[:, pg, :], in_=xb[:, 512:528]), reads=[b_xb], writes=[b_hp[pg]])
                                    P.op("pool", lambda e, xb=xb, sa=sa: e.tensor_tensor(out=sa[:, 1:528], in0=xb[:, 1:528], in1=xb[:, 0:527], op=ALU.add),
                                         reads=[b_xb], writes=[b_sa])
                                    fin, b_fin = sa, b_sa
                                    if wlen >= 4:
                                        P.op("pool", lambda e, sa=sa, sb_=sb_: e.tensor_tensor(out=sb_[:, 3:528], in0=sa[:, 3:528], in1=sa[:, 1:526], op=ALU.add),
                                             reads=[b_sa], writes=[b_sb])
                                        fin, b_fin = sb_, b_sb
                                    if wlen >= 8:
                                        P.op("pool", lambda e, sa=sa, sb_=sb_: e.tensor_tensor(out=sa[:, 7:528], in0=sb_[:, 7:528], in1=sb_[:, 3:524], op=ALU.add),
                                             reads=[b_sb, b_sa], writes=[b_sa])
                                        fin, b_fin = sa, b_sa
                                    if wlen >= 16:
                                        P.op("pool", lambda e, sa=sa, sb_=sb_: e.tensor_tensor(out=sb_[:, 15:528], in0=sa[:, 15:528], in1=sa[:, 7:520], op=ALU.add),
                                             reads=[b_sa, b_sb], writes=[b_sb])
                                        fin, b_fin = sb_, b_sb
                                    ip = pl_ctr[0] % 4
                                    pl_ctr[0] += 1
                                    pl, b_pl = pls[ip], b_pls[ip]
                                    sgp, b_sgp = sgps[ip], b_sgps[ip]
                                    P.op("dve", lambda e, fin=fin, xb=xb, pl=pl, wlen=wlen: e.scalar_tensor_tensor(
                                        out=pl[:], in0=fin[:, 16:528], scalar=1.0 / wlen, in1=xb[:, 16:528], op0=ALU.mult, op1=ALU.subtract),
                                        reads=[b_fin, b_xb], writes=[b_pl])
                                    if gtt == 0:
                                        t16, b_t16 = tmp()
                                        P.op("dve", lambda e, fin=fin, t16=t16, pg=pg: e.tensor_tensor(out=t16[:, 0:16], in0=fin[:, 16:32], in1=invdiv[:, pg, :], op=ALU.mult),
                                             reads=[b_fin, b_const], writes=[b_t16])
                                        P.op("dve", lambda e, t16=t16, xb=xb, pl=pl: e.tensor_tensor(out=pl[:, 0:16], in0=t16[:, 0:16], in1=xb[:, 16:32], op=ALU.subtract),
                                             reads=[b_t16, b_xb, b_pl], writes=[b_pl])
                                    P.op("act", lambda e, pgt=pgt, sgp=sgp: e.activation(out=sgp[:], in_=pgt, func=AF.Silu), reads=[b_pgt], writes=[b_sgp])
                                    tails.append((s_, pg, pl, b_pl, sgp, b_sgp))

                                def pool_tail(tails=tails, g=g, gtt=gtt, tok=tok):
                                    i = stg_ctr[0] % 2
                                    stg_ctr[0] += 1
                                    sg_, b_sg_ = stgs[i], b_stgs[i]
                                    for (s_, pg, pl, b_pl, sgp, b_sgp) in tails:
                                        py, b_py = psum()
                                        P.op("pe", lambda e, py=py, pl=pl, pg=pg: e.matmul(py, wp[:, pg, :], pl[:], start=True, stop=True),
                                             reads=[b_pl, b_sm], writes=[b_py])
                                        P.op("dve", lambda e, py=py, sgp=sgp, pg=pg, s_=s_: e.scalar_tensor_tensor(
                                            out=sg_[:, s_, :], in0=py, scalar=sm[:, 6 + pg:7 + pg], in1=sgp[:], op0=ALU.mult, op1=ALU.mult),
                                            reads=[b_py, b_sgp, b_sm], writes=[b_sg_])
                                    r0 = 8 + (g - 4) * 2
                                    P.dma("sp", mix_v[:, r0:r0 + 2, tok], sg_[:, 0:2, :], reads=[b_sg_], writes=[b_mix[r0 + c][gtt] for c in range(2)], key="stg%d" % i)

                                for fn_ in pending:
                                    fn_()
                                pending.clear()
                                pending.append(pool_tail)
                                if tt == 3:
                                    for fn_ in pending:
                                        fn_()
                                    pending.clear()
                            else:
                                j = g - 6
                                i = stg_ctr[0] % 2
                                stg_ctr[0] += 1
                                sg_, b_sg_ = stgs[i], b_stgs[i]
                                pch, b_pch = proj(w, 0, 128, tt)
                                pcc, b_pcc = proj(w, 128, 128, tt)
                                pcb, b_pcb = proj(w, 256, 128, tt)
                                pgc, b_pgc = proj(w, 384, 128, tt)
                                chs, b_chs = tmp()
                                ub, b_ub = tmp()
                                t1, b_t1 = tmp()
                                t2, b_t2 = tmp()
                                P.op("act", lambda e, pch=pch, chs=chs: e.copy(out=chs[:, 0:512], in_=pch), reads=[b_pch], writes=[b_chs])
                                P.op("dve", lambda e, pcc=pcc, chs=chs, ub=ub: e.tensor_tensor(out=ub[:, 2:514], in0=pcc, in1=chs[:, 0:512], op=ALU.mult),
                                     reads=[b_pcc, b_chs], writes=[b_ub])
                                if tt == 0:
                                    ph1, b_ph1 = proj_halo(w, 0)
                                    ph2, b_ph2 = proj_halo(w, 128)
                                    hh, b_hh = tmp()
                                    P.op("act", lambda e, ph1=ph1, hh=hh: e.copy(out=hh[:, 0:16], in_=ph1[:, 0:16]), reads=[b_ph1], writes=[b_hh])
                                    P.op("dve", lambda e, ph2=ph2, hh=hh: e.tensor_tensor(out=hh[:, 16:32], in0=ph2[:, 0:16], in1=hh[:, 0:16], op=ALU.mult),
                                         reads=[b_ph2, b_hh], writes=[b_hh])
                                    P.op("dve", lambda e, ub=ub, hh=hh: e.tensor_scalar(out=ub[:, 0:2], in0=hh[:, 30:32], scalar1=coef[:, 1:2], scalar2=None, op0=ALU.mult),
                                         reads=[b_hh, b_ub, b_const], writes=[b_ub])
                                else:
                                    P.op("pool", lambda e, ub=ub, j=j: e.tensor_copy(out=ub[:, 0:2], in_=hc[:, j, :]), reads=[b_hc[j], b_ub], writes=[b_ub])
                                P.op("pool", lambda e, ub=ub, j=j: e.tensor_copy(out=hc[:, j, :], in_=ub[:, 512:514]), reads=[b_ub], writes=[b_hc[j]])
                                cw0 = 10 + j * 3
                                P.op("pool", lambda e, ub=ub, t1=t1, cw0=cw0: e.tensor_scalar(out=t1[:, 0:512], in0=ub[:, 0:512], scalar1=sm[:, cw0:cw0 + 1], scalar2=None, op0=ALU.mult),
                                     reads=[b_ub, b_sm], writes=[b_t1])
                                P.op("dve", lambda e, ub=ub, t1=t1, t2=t2, cw0=cw0: e.scalar_tensor_tensor(
                                    out=t2[:, 0:512], in0=ub[:, 1:513], scalar=sm[:, cw0 + 1:cw0 + 2], in1=t1[:, 0:512], op0=ALU.mult, op1=ALU.add),
                                    reads=[b_ub, b_t1, b_sm], writes=[b_t2])
                                P.op("dve", lambda e, ub=ub, t1=t1, t2=t2, cw0=cw0: e.scalar_tensor_tensor(
                                    out=t1[:, 0:512], in0=ub[:, 2:514], scalar=sm[:, cw0 + 2:cw0 + 3], in1=t2[:, 0:512], op0=ALU.mult, op1=ALU.add),
                                    reads=[b_ub, b_t2, b_t1, b_sm], writes=[b_t1])
                                P.op("dve", lambda e, pcb=pcb, t1=t1, t2=t2: e.tensor_tensor(out=t2[:, 0:512], in0=pcb, in1=t1[:, 0:512], op=ALU.mult),
                                     reads=[b_pcb, b_t1, b_t2], writes=[b_t2])
                                P.op("act", lambda e, pgc=pgc, chs=chs: e.activation(out=chs[:, 0:512], in_=pgc, func=AF.Silu), reads=[b_pgc, b_chs], writes=[b_chs])
                                P.op("pool", lambda e, t2=t2, chs=chs, sg_=sg_: e.tensor_tensor(out=sg_[:, 0, :], in0=t2[:, 0:512], in1=chs[:, 0:512], op=ALU.mult),
                                     reads=[b_t2, b_chs], writes=[b_sg_])
                                r0 = 12 + j
                                P.dma("sp", mix_v[:, r0, tok], sg_[:, 0, :], reads=[b_sg_], writes=[b_mix[r0][gtt]], key="stg%d" % i)
                if debug:
                    P.dma("sp", dbg_e1a, e1a_src, reads=b_qn, writes=[Buf()], key="dbg_e1a")
                    P.dma("sp", dbg_e1b, e1b_src, reads=b_kvn + b_kr, writes=[Buf()], key="dbg_e1b")
                P.collective(e1a_src, e1a_g, reads=b_qn, writes=[b_e1g], key="cc_e1a")
                P.collective(e1b_src, e1b_g, reads=b_kvn + b_kr + [b_e1g], writes=[b_e1g], key="cc_e1b")
                P.barrier(junk)

        def phase_B(L):
            with contextlib.ExitStack() as st:
                areset()
                SB_ = [4, 5, 6, 7]
                sctr = [0]
                octr = [0]
                lctr = [0]
                qns = sb("qns", [128, 4, S], BF16, st)
                kvns = sb("kvns", [128, 2, S], BF16, st)
                krs = sb("krs", [64, S], BF16, st)
                b_lat = Buf()
                for rho in range(2):
                    csl = slice(rho * SO, (rho + 1) * SO)
                    P.dma("sp", qns[:, :, csl], e1a_g[rho * 512:(rho + 1) * 512, :].rearrange("(kc p) t -> p kc t", p=128), reads=[b_e1g], writes=[b_lat], key="qns%d" % rho)
                    P.dma("sp", kvns[:, :, csl], e1b_g[rho * 320:rho * 320 + 256, :].rearrange("(kc p) t -> p kc t", p=128), reads=[b_e1g], writes=[b_lat], key="kvns%d" % rho)
                    P.dma("sp", krs[:, csl], e1b_g[rho * 320 + 256:rho * 320 + 320, :], reads=[b_e1g], writes=[b_lat], key="krs%d" % rho)
                cosT = sb("cosA", [64, S], BF16, st)
                sinT = sb("sinA", [64, S], BF16, st)
                b_tab = Buf()
                P.dma("sp", cosT[:], cosA_d, reads=[b_tabd], writes=[b_tab], key="cosA")
                P.dma("sp", sinT[:], sinA_d, reads=[b_tabd], writes=[b_tab], key="sinA")
                wq = sb("wq", [128, 4, 1024], BF16, st)
                wk = sb("wk", [128, 2, 512], BF16, st)
                wv = sb("wv", [128, 2, 512], BF16, st)
                sm = sb("smB", [128, 32], F32, st)
                b_w = Buf()
                P.dma("sp", sm[:], sm_d[L * 128:(L + 1) * 128, :], writes=[b_w], key="smB")
                b_w1, b_w2 = Buf(), Buf()
                P.dma("pool", wq[:].rearrange("p a b -> p (a b)"), wq_d[L * 128:(L + 1) * 128, :], writes=[b_w1], key="wq")
                P.dma("pool", wk[:].rearrange("p a b -> p (a b)"), wk_d[L * 128:(L + 1) * 128, :], reads=[b_w1], writes=[b_w2], key="wk")
                P.dma("pool", wv[:].rearrange("p a b -> p (a b)"), wv_d[L * 128:(L + 1) * 128, :], reads=[b_w1, b_w2], writes=[b_w], key="wv")
                for kc in range(4):
                    P.op("dve", lambda e, kc=kc: e.tensor_scalar(out=wq[:, kc, :], in0=wq[:, kc, :], scalar1=sm[:, kc:kc + 1], scalar2=None, op0=ALU.mult),
                         reads=[b_w], writes=[b_w])
                for kc in range(2):
                    P.op("dve", lambda e, kc=kc: e.tensor_scalar(out=wk[:, kc, :], in0=wk[:, kc, :], scalar1=sm[:, 4 + kc:5 + kc], scalar2=None, op0=ALU.mult),
                         reads=[b_w], writes=[b_w])
                    P.op("dve", lambda e, kc=kc: e.tensor_scalar(out=wv[:, kc, :], in0=wv[:, kc, :], scalar1=sm[:, 4 + kc:5 + kc], scalar2=None, op0=ALU.mult),
                         reads=[b_w], writes=[b_w])
                Vq = sb("Vq", [128, 32, 512], BF16, st)
                b_V = [Buf() for i in range(32)]
                kTh = sb("kTh", [128, S], BF16, st)
                qTh = sb("qTh", [128, S], BF16, st)
                qrh = sb("qrh", [64, S], BF16, st)
                b_kT = [Buf() for i in range(8)]
                b_qT = [Buf() for i in range(8)]
                b_qr = [Buf() for i in range(8)]
                pTs = [sb("pT%d" % i, [128, 512], BF16, st) for i in range(4)]
                b_pTs = [Buf() for i in range(4)]
                pT_ctr = [0]
                rls = [sb("rl%d" % i, [128, 512], F32, st) for i in range(2)]
                b_rls = [Buf() for i in range(2)]
                outs = [sb("ob%d" % i, [128, 512], BF16, st) for i in range(2)]
                b_outs = [Buf() for i in range(2)]
                rt1 = [sb("rta%d" % i, [64, 512], F32, st) for i in range(2)]
                rt2 = [sb("rtb%d" % i, [64, 512], F32, st) for i in range(2)]
                b_rt1 = [Buf() for i in range(2)]
                b_rt2 = [Buf() for i in range(2)]
                ev = [0]

                def evac(out_ap, in_ap, reads, writes):
                    ev[0] += 1
                    if ev[0] % 2 == 0:
                        P.op("dve", lambda e: e.tensor_copy(out=out_ap, in_=in_ap), reads=reads, writes=writes)
                    else:
                        P.op("act", lambda e: e.copy(out=out_ap, in_=in_ap), reads=reads, writes=writes)

                uc = [0]
                for h in range(HL):
                    if h % 4 == 0:
                        hq = h // 4
                        for tk in range(32):
                            pt, b_pt = psum(SB_, sctr)
                            for kc in range(2):
                                P.op("pe", lambda e, pt=pt, kc=kc, tk=tk, hq=hq: e.matmul(
                                    pt, kvns[:, kc, tk * 128:(tk + 1) * 128], wv[:, kc, hq * 512:(hq + 1) * 512], start=(kc == 0), stop=(kc == 1)),
                                    reads=[b_lat, b_w], writes=[b_pt])
                            evac(Vq[:, tk, :], pt, [b_pt], [b_V[tk]])
                    for tt in range(8):
                        tok = slice(tt * 512, (tt + 1) * 512)
                        pt, b_pt = psum(SB_, sctr)
                        for kc in range(2):
                            P.op("pe", lambda e, pt=pt, kc=kc, tok=tok, h=h: e.matmul(
                                pt, wk[:, kc, h * 128:(h + 1) * 128], kvns[:, kc, tok], start=(kc == 0), stop=(kc == 1)),
                                reads=[b_lat, b_w], writes=[b_pt])
                        evac(kTh[:, tok], pt, [b_pt], [b_kT[tt]])
                        pt, b_pt = psum(SB_, sctr)
                        for kc in range(4):
                            P.op("pe", lambda e, pt=pt, kc=kc, tok=tok, h=h: e.matmul(
                                pt, wq[:, kc, h * 256:h * 256 + 128], qns[:, kc, tok], start=(kc == 0), stop=(kc == 3)),
                                reads=[b_lat, b_w], writes=[b_pt])
                        evac(qTh[:, tok], pt, [b_pt], [b_qT[tt]])
                        pa, b_pa = psum(SB_, sctr)
                        for kc in range(4):
                            P.op("pe", lambda e, pa=pa, kc=kc, tok=tok, h=h: e.matmul(
                                pa[0:64, :], wq[:, kc, h * 256 + 128:h * 256 + 192], qns[:, kc, tok], start=(kc == 0), stop=(kc == 3)),
                                reads=[b_lat, b_w], writes=[b_pa])
                        pb, b_pb = psum(SB_, sctr)
                        for kc in range(4):
                            P.op("pe", lambda e, pb=pb, kc=kc, tok=tok, h=h: e.matmul(
                                pb[0:64, :], wq[:, kc, h * 256 + 192:h * 256 + 256], qns[:, kc, tok], start=(kc == 0), stop=(kc == 3)),
                                reads=[b_lat, b_w], writes=[b_pb])
                        i = tt % 2
                        P.op("dve", lambda e, pa=pa, i=i, tok=tok: e.tensor_tensor(out=rt1[i][:], in0=pa[0:64, :], in1=cosT[:, tok], op=ALU.mult),
                             reads=[b_pa, b_tab], writes=[b_rt1[i]])
                        P.op("dve", lambda e, pb=pb, i=i, tok=tok: e.tensor_tensor(out=rt2[i][:], in0=pb[0:64, :], in1=sinT[:, tok], op=ALU.mult),
                             reads=[b_pb, b_tab], writes=[b_rt2[i]])
                        P.op("pool", lambda e, i=i, tok=tok: e.tensor_tensor(out=qrh[:, tok], in0=rt1[i][:], in1=rt2[i][:], op=ALU.add),
                             reads=[b_rt1[i], b_rt2[i]], writes=[b_qr[tt]])
                    units = [(qb, kb) for qb in range(8) for kb in range(4 * qb + 4)]
                    LOOK = 2
                    acc = {}
                    sc = {}

                    def emit_scores(u):
                        qb, kb = units[u]
                        qtok = slice(qb * 512, (qb + 1) * 512)
                        ktok = slice(kb * 128, (kb + 1) * 128)
                        ps_, b_ps_ = psum(SB_, sctr)
                        diag = kb >= 4 * qb
                        P.op("pe", lambda e: e.matmul(ps_, kTh[:, ktok], qTh[:, qtok], start=True, stop=False),
                             reads=[b_kT[kb // 4], b_qT[qb]], writes=[b_ps_])
                        P.op("pe", lambda e: e.matmul(ps_, krs[:, ktok], qrh[:, qtok], start=False, stop=(not diag)),
                             reads=[b_lat, b_qr[qb]], writes=[b_ps_])
                        if diag:
                            jm = kb - 4 * qb
                            P.op("pe", lambda e: e.matmul(ps_, ident[:], masks[:, jm, :], start=False, stop=True),
                                 reads=[b_const], writes=[b_ps_])
                        sc[u] = (ps_, b_ps_)

                    def emit_rest(u, h=h):
                        qb, kb = units[u]
                        nkb = 4 * qb + 4
                        qtok = slice(qb * 512, (qb + 1) * 512)
                        if kb == 0:
                            acc[qb] = (psum([0, 1], octr), psum([2, 3], lctr))
                        (po, b_po), (pl_, b_pl_) = acc[qb]
                        ps_, b_ps_ = sc.pop(u)
                        ip = pT_ctr[0] % 4
                        pT_ctr[0] += 1
                        pT, b_pT = pTs[ip], b_pTs[ip]
                        P.op("act", lambda e: e.activation(out=pT[:], in_=ps_, func=AF.Exp, scale=SCALE), reads=[b_ps_], writes=[b_pT])
                        P.op("pe", lambda e: e.matmul(po, Vq[:, kb, (h % 4) * 128:(h % 4 + 1) * 128], pT[:], start=(kb == 0), stop=(kb == nkb - 1)),
                             reads=[b_V[kb], b_pT], writes=[b_po])
                        P.op("pe", lambda e: e.matmul(pl_, ones[:], pT[:], start=(kb == 0), stop=(kb == nkb - 1)),
                             reads=[b_const, b_pT], writes=[b_pl_])
                        if kb == nkb - 1:
                            i = uc[0] % 2
                            uc[0] += 1
                            P.op("dve", lambda e: e.reciprocal(out=rls[i][:], in_=pl_), reads=[b_pl_], writes=[b_rls[i]])
                            P.op("dve", lambda e: e.tensor_tensor(out=outs[i][:], in0=po, in1=rls[i][:], op=ALU.mult),
                                 reads=[b_po, b_rls[i]], writes=[b_outs[i]])
                            P.dma("sp", e2_src[h][:, qtok], outs[i][:], reads=[b_outs[i]], writes=[b_e2s[h][qb]], key="ob%d" % i)
                            if qb == 7:
                                if debug:
                                    P.dma("sp", dbg_e2[h], e2_src[h], reads=b_e2s[h], writes=[Buf()], key="dbg_e2")
                                P.collective(e2_src[h], e2_g[h], reads=b_e2s[h], writes=[b_e2g[h]], key="cc_e2_%d" % h)

                    for u in range(min(LOOK, len(units))):
                        emit_scores(u)
                    for u in range(len(units)):
                        if u + LOOK < len(units):
                            emit_scores(u + LOOK)
                        emit_rest(u)
                P.barrier(junk)

        for L in range(n_layers):
            if stop_phase == "ln0":
                break
            phase_A(L)
            if stop_phase == "A":
                break
            phase_B(L)
            if stop_phase == "B":
                break
            with contextlib.ExitStack() as st:
                last = (L == DEPTH - 1)
                ln_phase(st, L, "proj", not last, last)
                P.barrier(junk)
        P.finish()
    return nc


def _tile_k(w, ncols_pad=None):
    K, C = w.shape
    return np.ascontiguousarray(w.reshape(K // 128, 128, C).transpose(1, 0, 2))


def prep_inputs(x, positions, emb_ln_g, emb_ln_b, w_in, q_norm_g, kv_norm_g, w_uq, w_ukv, w_pool,
                pool_scale, conv_w, w_out, b_out, ln_g, ln_b):
    f32 = np.float32
    w_in = np.asarray(w_in, f32)
    offs = np.cumsum([0, 512, 256, 64, 1024, 512, 512, 512, 512, 512, 512])
    o_q, o_kv, o_kr, o_gm, o_pi, o_gp, o_ch, o_cb, o_cc, o_gc = offs[:10]
    groups = []
    zero128 = None
    for L in range(DEPTH):
        W = w_in[L]
        kr = W[:, o_kr:o_kr + 64]
        ksw = np.concatenate([kr[:, 32:64], kr[:, 0:32]], axis=1)
        g0 = np.concatenate([W[:, o_kv:o_kv + 256], kr, ksw, np.zeros((D, 128), f32)], axis=1)
        gl = [g0, W[:, o_q:o_q + 512], W[:, o_gm:o_gm + 512], W[:, o_gm + 512:o_gm + 1024]]
        for a in range(2):
            gl.append(np.concatenate([W[:, o_pi + (2 * a) * 128:o_pi + (2 * a + 1) * 128], W[:, o_gp + (2 * a) * 128:o_gp + (2 * a + 1) * 128],
                                      W[:, o_pi + (2 * a + 1) * 128:o_pi + (2 * a + 2) * 128], W[:, o_gp + (2 * a + 1) * 128:o_gp + (2 * a + 2) * 128]], axis=1))
        for j in range(4):
            sl = slice(j * 128, (j + 1) * 128)
            gl.append(np.concatenate([W[:, o_ch:o_ch + 512][:, sl], W[:, o_cc:o_cc + 512][:, sl], W[:, o_cb:o_cb + 512][:, sl], W[:, o_gc:o_gc + 512][:, sl]], axis=1))
        for gmat in gl:
            groups.append(_tile_k(gmat).reshape(128, 16 * 512))
    w_in_g = np.ascontiguousarray(np.concatenate(groups, axis=0))

    wq_l, wk_l, wv_l = [[], []], [[], []], [[], []]
    wo_l, wp_l, sm_l = [], [], []
    for L in range(DEPTH):
        wq = np.asarray(w_uq[L], f32).reshape(512, NH, 192)
        rope = wq[:, :, 128:192]
        sw = np.concatenate([rope[:, :, 32:64], rope[:, :, 0:32]], axis=2)
        wq2 = np.concatenate([wq, sw], axis=2)
        wkv = np.asarray(w_ukv[L], f32).reshape(256, NH, 256)
        for r in range(2):
            hs = slice(4 * r, 4 * r + 4)
            wq_l[r].append(_tile_k(np.ascontiguousarray(wq2[:, hs]).reshape(512, 1024)).reshape(128, 4 * 1024))
            wk_l[r].append(_tile_k(np.ascontiguousarray(wkv[:, hs, 0:128]).reshape(256, 512)).reshape(128, 2 * 512))
            wv_l[r].append(_tile_k(np.ascontiguousarray(wkv[:, hs, 128:256]).reshape(256, 512)).reshape(128, 2 * 512))
        wo_l.append(_tile_k(np.asarray(w_out[L], f32)).reshape(128, 16 * 2048))
        wp_l.append(np.ascontiguousarray(np.asarray(w_pool[L], f32).transpose(1, 0, 2)).reshape(128, 4 * 128))
        sm = np.zeros((128, 32), f32)
        sm[:, 0:4] = np.asarray(q_norm_g[L], f32).reshape(4, 128).T
        sm[:, 4:6] = np.asarray(kv_norm_g[L], f32).reshape(2, 128).T
        sm[:, 6:10] = np.asarray(pool_scale[L], f32).reshape(4, 128).T
        cw = np.asarray(conv_w[L], f32).reshape(3, 4, 128)
        sm[:, 10:22] = cw.transpose(2, 1, 0).reshape(128, 12)
        sm_l.append(sm)
    lnp = np.stack([np.asarray(emb_ln_g, f32), np.asarray(emb_ln_b, f32)] +
                   sum([[np.asarray(ln_g[L], f32), np.asarray(ln_b[L], f32), np.asarray(b_out[L], f32)] for L in range(DEPTH)], []), axis=0)
    half = 32
    inv_freq = (10000.0 ** (-np.arange(half, dtype=np.float32) / half)).astype(f32)
    ropec = np.zeros((64, 2), f32)
    ropec[:, 0] = np.concatenate([inv_freq, inv_freq])
    ropec[:, 1] = np.concatenate([-np.ones(32, f32), np.ones(32, f32)])
    invdiv = np.zeros((2, 128, 4, 16), f32)
    for g, w in enumerate(POOL_W):
        invdiv[0, :, g, :] = 1.0 / np.minimum(np.arange(1, 17, dtype=f32), float(w))
        invdiv[1, :, g, :] = 1.0 / float(w)
    ident = np.eye(128, dtype=f32).astype(ml_dtypes.bfloat16)
    kk = np.arange(128)[:, None]
    qq = np.arange(512)[None, :]
    masks = np.stack([np.where(j * 128 + kk <= qq, 0.0, NEG) for j in range(4)], axis=1).astype(f32)
    masks = masks.reshape(128, 4 * 512).astype(ml_dtypes.bfloat16)
    shared = {
        "ropec": ropec, "lnp": np.ascontiguousarray(lnp), "w_in_g": w_in_g,
        "wo": np.ascontiguousarray(np.concatenate(wo_l, 0)),
        "wp": np.ascontiguousarray(np.concatenate(wp_l, 0)), "small": np.ascontiguousarray(np.concatenate(sm_l, 0)),
        "ident": ident, "masks": masks,
    }
    per_rank = []
    for r in range(2):
        coef = np.zeros((128, 2), f32)
        coef[:, r] = 1.0
        per_rank.append({
            "wq": np.ascontiguousarray(np.concatenate(wq_l[r], 0)), "wk": np.ascontiguousarray(np.concatenate(wk_l[r], 0)),
            "wv": np.ascontiguousarray(np.concatenate(wv_l[r], 0)), "invdiv": np.ascontiguousarray(invdiv[r].reshape(128, 64)),
            "coef": coef,
        })
    x = np.asarray(x, f32)
    positions = np.asarray(positions, np.int32)
    in_maps = []
    for c in range(8):
        b, r = c // 2, c % 2
        m = dict(shared)
        m.update(per_rank[r])
        m["x"] = np.ascontiguousarray(x[b, r * SO:(r + 1) * SO])
        m["pos"] = np.ascontiguousarray(positions[b][None, :])
        m["pos_own"] = np.ascontiguousarray(positions[b, r * SO:(r + 1) * SO][None, :])
        in_maps.append(m)
    return in_maps


def kernel(**inputs):
    in_maps = prep_inputs(**inputs)
    nc = build()
    res = run_bass_kernel_spmd(nc, in_maps, core_ids=list(range(8)))
    out = np.empty((4, S, D), np.float32)
    for c in range(8):
        b, r = c // 2, c % 2
        out[b, r * SO:(r + 1) * SO] = np.asarray(res.results[c]["out"], dtype=np.float32)
    return out
```

```python
import math
import contextlib
import numpy as np
import ml_dtypes
import concourse.bass as bass
import concourse.mybir as mybir
from concourse.bass_utils import run_bass_kernel_spmd

F32 = mybir.dt.float32
BF16 = mybir.dt.bfloat16
I32 = mybir.dt.int32
AF = mybir.ActivationFunctionType
ALU = mybir.AluOpType

S = 4096
SO = 2048
HL = 4
PAIRS = [[0, 1], [2, 3], [4, 5], [6, 7]]
D = 2048
DEPTH = 2
NH = 8
LN_EPS = 1e-5
RMS_EPS = 1e-6
ALPHA = (2 * DEPTH) ** 0.25
SCALE = 192 ** -0.5
NEG = -30000.0
POOL_W = (2, 4, 8, 16)
NG = 10

ENGS = ("pe", "act", "dve", "pool", "sp")


class Buf:
    __slots__ = ("name", "w", "r")

    def __init__(self, name=""):
        self.name = name
        self.w = None
        self.r = []


class Op:
    __slots__ = ("eng", "fn", "waits", "flag", "dma_key", "dma_val", "seq")

    def __init__(self, eng, fn):
        self.eng = eng
        self.fn = fn
        self.waits = []
        self.flag = False
        self.dma_key = None
        self.dma_val = 0
        self.seq = -1


class Prog:
    def __init__(self, nc):
        self.nc = nc
        self.ops = {e: [] for e in ENGS}
        self.seen = {e: {} for e in ENGS}
        self.seen_dma = {e: {} for e in ENGS}
        self.dma_counts = {}
        self.cc_keys = set()

    def _add(self, eng, fn, reads, writes, dma_key=None):
        op = Op(eng, fn)
        op.seq = len(self.ops[eng])
        deps = []
        for b in reads:
            if b.w is not None:
                deps.append(b.w)
        for b in writes:
            if b.w is not None:
                deps.append(b.w)
            deps.extend(b.r)
        best = {}
        dma_deps = {}
        for d in deps:
            if d.dma_key is not None:
                if dma_deps.get(d.dma_key, 0) < d.dma_val:
                    dma_deps[d.dma_key] = d.dma_val
            else:
                if d.eng == eng and eng == "pe":
                    continue
                if best.get(d.eng, -1) < d.seq:
                    best[d.eng] = d.seq
        for f, s in best.items():
            if self.seen[eng].get(f, -1) >= s:
                continue
            self.seen[eng][f] = s
            dop = self.ops[f][s]
            dop.flag = True
            op.waits.append(("eng", f, dop))
        for k, v in dma_deps.items():
            if self.seen_dma[eng].get(k, 0) >= v:
                continue
            self.seen_dma[eng][k] = v
            op.waits.append(("dma", k, v))
        if dma_key is not None:
            op.dma_key = dma_key
            self.dma_counts[dma_key] = self.dma_counts.get(dma_key, 0) + 16
            op.dma_val = self.dma_counts[dma_key]
        for b in reads:
            b.r.append(op)
        for b in writes:
            b.w = op
            b.r = []
        self.ops[eng].append(op)
        return op

    def op(self, eng, fn, reads=(), writes=()):
        return self._add(eng, fn, reads, writes, None)

    def dma(self, eng, out, in_, reads=(), writes=(), key=None):
        def fn(e):
            return e.dma_start(out=out, in_=in_)
        return self._add(eng, fn, reads, writes, key)

    def collective(self, src, dst, reads, writes, key):
        def fn(e):
            return e.collective_compute("AllGather", ALU.bypass, replica_groups=PAIRS, ins=[src.opt()], outs=[dst.opt()])
        o = self._add("pool", fn, reads, writes, key)
        self.dma_counts[key] -= 15
        o.dma_val = self.dma_counts[key]
        self.cc_keys.add(key)
        return o

    def barrier(self, junk):
        marks = []
        if not hasattr(self, "jb"):
            self.jb = [Buf(), Buf(), Buf()]
        jb = self.jb
        b = Buf()
        self.op("act", lambda e: e.activation(out=junk[:, 0:1], in_=junk[:, 4:5], func=AF.Copy), writes=[b, jb[0]])
        marks.append(b)
        b = Buf()
        self.op("dve", lambda e: e.memset(junk[:, 1:2], 0.0), writes=[b, jb[1]])
        marks.append(b)
        b = Buf()
        self.op("pool", lambda e: e.memset(junk[:, 2:3], 0.0), writes=[b, jb[2]])
        marks.append(b)
        fence = Buf()
        o = self.op("sp", lambda e: e.nop(), reads=marks, writes=[fence])
        for k, v in self.dma_counts.items():
            if self.seen_dma["sp"].get(k, 0) < v:
                self.seen_dma["sp"][k] = v
                o.waits.append(("dma", k, v))
        self.op("act", lambda e: e.activation(out=junk[:, 0:1], in_=junk[:, 4:5], func=AF.Copy), reads=[fence], writes=[jb[0]])
        self.op("dve", lambda e: e.memset(junk[:, 1:2], 0.0), reads=[fence], writes=[jb[1]])
        self.op("pool", lambda e: e.memset(junk[:, 2:3], 0.0), reads=[fence], writes=[jb[2]])
        self.op("pe", lambda e: e.nop(), reads=[fence])
        for e in ENGS:
            for k, v in self.dma_counts.items():
                if self.seen_dma[e].get(k, 0) < v:
                    self.seen_dma[e][k] = v

    def finish(self):
        nc = self.nc
        with contextlib.ExitStack() as st:
            esem = {e: st.enter_context(nc.semaphore("s_" + e)) for e in ENGS}
            dsem = {k: st.enter_context(nc.semaphore("d_%s" % (k,))) for k in self.dma_counts}
            block = st.enter_context(nc.Block())
            for e in ENGS:
                c = 0
                for o in self.ops[e]:
                    if o.flag:
                        c += 1
                        o.dma_val = c

            def emit(e, eng):
                for o in self.ops[e]:
                    for kind, k, v in o.waits:
                        if kind == "eng":
                            eng.wait_ge(esem[k], v.dma_val)
                        else:
                            eng.wait_ge(dsem[k], v)
                    inst = o.fn(eng)
                    if o.dma_key is not None and o.dma_key in self.cc_keys:
                        inst.then_inc(dsem[o.dma_key])
                    elif o.dma_key is not None:
                        inst.then_inc(dsem[o.dma_key], 16)
                    elif o.flag:
                        inst.then_inc(esem[e], 1)
                if e == "sp":
                    for k, v in self.dma_counts.items():
                        eng.wait_ge(dsem[k], v)

            @block.tensor
            def _(eng):
                emit("pe", eng)

            @block.scalar
            def _(eng):
                emit("act", eng)

            @block.vector
            def _(eng):
                emit("dve", eng)

            @block.gpsimd
            def _(eng):
                emit("pool", eng)

            @block.sync
            def _(eng):
                emit("sp", eng)


def build(debug=False, n_layers=DEPTH, stop_phase=None):
    nc = bass.Bass("TRN2", target_bir_lowering=False)
    P = Prog(nc)

    def din(name, shape, dt):
        return nc.dram_tensor(name, shape, dt, kind="ExternalInput").ap()

    x_d = din("x", [SO, D], F32)
    pos_d = din("pos", [1, S], I32)
    poso_d = din("pos_own", [1, SO], I32)
    coef_d = din("coef", [128, 2], F32)
    rc_d = din("ropec", [64, 2], F32)
    lnp_d = din("lnp", [2 + 3 * DEPTH, D], F32)
    win_d = din("w_in_g", [DEPTH * NG * 128, 16 * 512], F32)
    wq_d = din("wq", [DEPTH * 128, 4 * 1024], F32)
    wk_d = din("wk", [DEPTH * 128, 2 * 512], F32)
    wv_d = din("wv", [DEPTH * 128, 2 * 512], F32)
    wo_d = din("wo", [DEPTH * 128, 16 * 2048], F32)
    wp_d = din("wp", [DEPTH * 128, 4 * 128], F32)
    sm_d = din("small", [DEPTH * 128, 32], F32)
    idv_d = din("invdiv", [128, 64], F32)
    ident_d = din("ident", [128, 128], BF16)
    mask_d = din("masks", [128, 4 * 512], BF16)
    out_d = nc.dram_tensor("out", [SO, D], F32, kind="ExternalOutput").ap()

    skind = "ExternalOutput" if debug else "Internal"
    resid_d = nc.dram_tensor("resid", [SO, D], F32, kind=skind).ap()
    hT_d = nc.dram_tensor("hT", [D, SO], BF16, kind=skind).ap()
    mix_d = nc.dram_tensor("mixT", [D, SO], BF16, kind=skind).ap()
    cosA_d = nc.dram_tensor("cosA", [64, S], BF16).ap()
    sinA_d = nc.dram_tensor("sinA", [64, S], BF16).ap()
    cosO_d = nc.dram_tensor("cosO", [64, SO], BF16).ap()
    sinO_d = nc.dram_tensor("sinO", [64, SO], BF16).ap()
    e1a_src = nc.dram_tensor("e1a_src", [512, SO], BF16).ap()
    e1a_g = nc.dram_tensor("e1a_g", [1024, SO], BF16).ap()
    e1b_src = nc.dram_tensor("e1b_src", [320, SO], BF16).ap()
    e1b_g = nc.dram_tensor("e1b_g", [640, SO], BF16).ap()
    e2_src = [nc.dram_tensor("e2_src%d" % i, [128, S], BF16).ap() for i in range(4)]
    e2_g = [nc.dram_tensor("e2_g%d" % i, [256, S], BF16).ap() for i in range(4)]
    tl_src = nc.dram_tensor("tl_src", [D, 16], BF16).ap()
    tl_g = nc.dram_tensor("tl_g", [2 * D, 16], BF16).ap()
    if debug:
        dbg_e1a = nc.dram_tensor("dbg_e1a", [512, SO], BF16, kind="ExternalOutput").ap()
        dbg_e1b = nc.dram_tensor("dbg_e1b", [320, SO], BF16, kind="ExternalOutput").ap()
        dbg_e2 = [nc.dram_tensor("dbg_e2_%d" % i, [128, S], BF16, kind="ExternalOutput").ap() for i in range(4)]

    b_resid = [Buf("resid%d" % i) for i in range(16)]
    b_hT = [Buf("hT%d" % i) for i in range(4)]
    b_qn = [Buf() for i in range(4)]
    b_kvn = [Buf() for i in range(4)]
    b_kr = [Buf() for i in range(4)]
    b_mix = [[Buf() for t in range(4)] for r in range(16)]
    b_out = [Buf() for i in range(16)]
    b_e1g, b_tlsrc, b_tlg = Buf(), Buf(), Buf()
    b_e2g = [Buf() for i in range(4)]
    b_e2s = [[Buf() for t in range(8)] for r in range(4)]
    b_tabd = Buf()

    hT_v = hT_d.rearrange("(kc p) t -> p kc t", p=128)
    mix_v = mix_d.rearrange("(kc p) t -> p kc t", p=128)
    qn_v = e1a_src.rearrange("(kc p) t -> p kc t", p=128)
    kvn_v = e1b_src[0:256, :].rearrange("(kc p) t -> p kc t", p=128)
    kr_d = e1b_src[256:320, :]
    tls_v = tl_src.rearrange("(kc p) t -> p kc t", p=128)
    tlg_v = tl_g.rearrange("(kc p) t -> p kc t", p=128)

    with contextlib.ExitStack() as gst:
        ARENA_WORDS = 51200
        arena = gst.enter_context(nc.sbuf_tensor("arena", [128, ARENA_WORDS], F32))
        a_top = [0]
        a_mark = [0]

        def sb(name, shape, dt, st=None):
            n = 1
            for d_ in shape[1:]:
                n *= d_
            esz = 4 if dt in (F32, I32) else 2
            words = (n * esz + 3) // 4
            words = (words + 7) // 8 * 8
            off = a_top[0]
            assert off + words <= ARENA_WORDS, ("SBUF arena overflow", name, off, words)
            a_top[0] = off + words
            v = arena[0:shape[0], off:off + words]
            if dt != F32:
                v = v.bitcast(dt)
            v = v[:, 0:n]
            if len(shape) == 3:
                v = v.rearrange("p (a b) -> p a b", a=shape[1])
            return v

        def areset():
            a_top[0] = a_mark[0]

        ps_all = gst.enter_context(nc.psum_tensor("ps", [128, 8 * 512], F32))
        ps_bufs = [Buf("ps%d" % i) for i in range(8)]
        ps_ctr = [0]

        def psum(banks=None, ctr=None):
            if banks is None:
                i = ps_ctr[0] % 8
                ps_ctr[0] += 1
            else:
                i = banks[ctr[0] % len(banks)]
                ctr[0] += 1
            return ps_all[:, i * 512:(i + 1) * 512], ps_bufs[i]

        junk = sb("junk", [128, 8], F32)
        ident = sb("ident", [128, 128], BF16)
        ones = sb("ones", [128, 128], BF16)
        masks = sb("masks", [128, 4, 512], BF16)
        rc = sb("rc", [64, 2], F32)
        coef = sb("coef", [128, 2], F32)
        invdiv = sb("invdiv", [128, 4, 16], F32)
        b_const = Buf("const")
        b_tab = Buf("tab")

        P.op("dve", lambda e: e.memset(junk[:], 0.0), writes=[b_const])
        P.op("dve", lambda e: e.memset(ones[:], 1.0), writes=[b_const])
        P.dma("sp", ident[:], ident_d, writes=[b_const], key="c_ident")
        P.dma("sp", masks[:].rearrange("p a b -> p (a b)"), mask_d, writes=[b_const], key="c_mask")
        P.dma("sp", rc[:], rc_d, writes=[b_const], key="c_rc")
        P.dma("sp", coef[:], coef_d, writes=[b_const], key="c_coef")
        P.dma("sp", invdiv[:].rearrange("p a b -> p (a b)"), idv_d, writes=[b_const], key="c_idv")

        LN_EPS_AP = sb("lneps", [128, 4], F32)
        a_mark[0] = a_top[0]

        def rope_tables(pos_ap, N, cos_dst, sin_dst, tag):
            areset()
            posi = sb("posi", [64, N], I32)
            ang = sb("ang", [64, N], F32)
            ta = sb("ta", [64, N], F32)
            tb = sb("tb", [64, N], F32)
            ob = sb("ob", [64, N], BF16)
            b_posi, b_ang, b_ta, b_tb, b_ob = Buf(), Buf(), Buf(), Buf(), Buf()
            P.dma("sp", posi[:], pos_ap.partition_broadcast(64), writes=[b_posi], key="t_pos")
            P.op("dve", lambda e: e.tensor_copy(out=ang[:], in_=posi[:]), reads=[b_posi], writes=[b_ang])
            P.op("dve", lambda e: e.tensor_scalar(out=ang[:], in0=ang[:], scalar1=rc[:, 0:1], scalar2=None, op0=ALU.mult),
                 reads=[b_ang, b_const], writes=[b_ang])
            TWO_PI = 2.0 * math.pi
            for which, phase, dst in (("sin", 0.0, sin_dst), ("cos", math.pi / 2, cos_dst)):
                P.op("dve", lambda e, phase=phase: e.tensor_scalar(out=ta[:], in0=ang[:], scalar1=phase, scalar2=1.0 / TWO_PI,
                                                                    op0=ALU.add, op1=ALU.mult), reads=[b_ang], writes=[b_ta])
                P.op("dve", lambda e: e.tensor_copy(out=posi[:], in_=ta[:]), reads=[b_ta], writes=[b_posi])
                P.op("dve", lambda e: e.tensor_copy(out=ta[:], in_=posi[:]), reads=[b_posi], writes=[b_ta])
                P.op("dve", lambda e: e.scalar_tensor_tensor(out=tb[:], in0=ta[:], scalar=-TWO_PI, in1=ang[:], op0=ALU.mult, op1=ALU.add),
                     reads=[b_ta, b_ang], writes=[b_tb])
                P.op("dve", lambda e, phase=phase: e.tensor_scalar(out=tb[:], in0=tb[:], scalar1=phase, scalar2=None, op0=ALU.add),
                     reads=[b_tb], writes=[b_tb])
                P.op("dve", lambda e: e.tensor_scalar(out=ta[:], in0=tb[:], scalar1=math.pi, scalar2=TWO_PI, op0=ALU.is_gt, op1=ALU.mult),
                     reads=[b_tb], writes=[b_ta])
                P.op("dve", lambda e: e.tensor_tensor(out=tb[:], in0=tb[:], in1=ta[:], op=ALU.subtract), reads=[b_tb, b_ta], writes=[b_tb])
                P.op("dve", lambda e: e.tensor_scalar(out=tb[:], in0=tb[:], scalar1=math.pi, scalar2=-math.pi, op0=ALU.min, op1=ALU.max),
                     reads=[b_tb], writes=[b_tb])
                P.op("act", lambda e: e.activation(out=ta[:], in_=tb[:], func=AF.Sin), reads=[b_tb], writes=[b_ta])
                if which == "sin":
                    P.op("dve", lambda e: e.tensor_scalar(out=ob[:], in0=ta[:], scalar1=rc[:, 1:2], scalar2=None, op0=ALU.mult),
                         reads=[b_ta, b_const], writes=[b_ob])
                else:
                    P.op("dve", lambda e: e.tensor_copy(out=ob[:], in_=ta[:]), reads=[b_ta], writes=[b_ob])
                P.dma("sp", dst, ob[:], reads=[b_ob], writes=[b_tabd], key="t_ob")
            P.barrier(junk)

        rope_tables(pos_d, S, cosA_d, sinA_d, "a")
        rope_tables(poso_d, SO, cosO_d, sinO_d, "o")

        def ln_phase(st, layer_idx, src_kind, write_hT, final):
            areset()
            NB = 4 if src_kind == "x" else 3
            gb = sb("ln_g", [128, D], F32, st)
            bb = sb("ln_b", [128, D], F32, st)
            b_p = Buf()
            if src_kind == "x":
                grow, brow = 0, 1
            else:
                grow, brow = 2 + 3 * layer_idx, 3 + 3 * layer_idx
            P.dma("sp", gb[:], lnp_d[grow:grow + 1, :].partition_broadcast(128), writes=[b_p], key="ln_g")
            P.dma("sp", bb[:], lnp_d[brow:brow + 1, :].partition_broadcast(128), writes=[b_p], key="ln_b")
            ys = [sb("ln_y%d" % i, [128, D], F32, st) for i in range(NB)]
            b_ys = [Buf() for i in range(NB)]
            hbs = [sb("ln_hb%d" % i, [128, D], BF16, st) for i in range(2)]
            b_hbs = [Buf() for i in range(2)]
            stats = [sb("ln_st%d" % i, [128, 4, 6], F32, st) for i in range(NB)]
            mvs = [sb("ln_mv%d" % i, [128, 4], F32, st) for i in range(NB)]
            b_sts = [Buf() for i in range(NB)]
            stg = [sb("ln_stg%d" % i, [128, 16, 512], BF16, st) for i in range(1)] if write_hT else []
            b_stg = [Buf() for i in range(1)]
            if src_kind == "proj":
                bo = sb("ln_bo", [1, D], BF16, st)
                P.dma("pool", bo[:], lnp_d[4 + 3 * layer_idx:5 + 3 * layer_idx, :], writes=[b_p], key="ln_bo")
                wo = sb("wo", [128, 16, 2048], BF16, st)
                b_wop = [Buf() for i in range(4)]
                for i4 in range(4):
                    P.dma("pool", wo[:, i4 * 4:(i4 + 1) * 4, :].rearrange("p a b -> p (a b)"),
                          wo_d[layer_idx * 128:(layer_idx + 1) * 128, i4 * 8192:(i4 + 1) * 8192],
                          reads=([b_wop[i4 - 1]] if i4 else [b_p]), writes=[b_wop[i4]], key="wo%d" % i4)
                mts = [sb("mt%d" % i, [128, 16, 256], BF16, st) for i in range(2)]
                b_mts = [Buf() for i in range(2)]
                NR = 2
                rts = [sb("rt%d" % i, [128, D], F32, st) for i in range(NR)]
                b_rts = [Buf() for i in range(NR)]
                ea = [sb("ea%d" % i, [128, 8, 256], BF16, st) for i in range(2)]
                eb = [sb("eb%d" % i, [128, 8, 256], BF16, st) for i in range(2)]
                b_ea = [Buf() for i in range(2)]
                b_eb = [Buf() for i in range(2)]
                et = sb("et", [128, 8, 256], BF16, st)
                b_et = Buf()

            def s_pair(tk):
                tt = tk // 4
                t2 = tk // 2
                i2 = t2 % 2
                mt, b_mt = mts[i2], b_mts[i2]
                P.dma("sp", mt[:], mix_v[:, :, t2 * 256:(t2 + 1) * 256], reads=[b_mix[r][tt] for r in range(16)],
                      writes=[b_mt], key="mt%d" % i2)
                for rho in range(2):
                    for hl in range(4):
                        kcg = rho * 4 + hl
                        P.dma("sp", ea[i2][:, kcg, :], e2_g[hl][rho * 128:(rho + 1) * 128, t2 * 256:(t2 + 1) * 256],
                              reads=[b_e2g[hl]], writes=[b_ea[i2]], key="ea%d_%d" % (i2, kcg))
                        P.dma("sp", eb[i2][:, kcg, :], e2_g[hl][rho * 128:(rho + 1) * 128, SO + t2 * 256:SO + (t2 + 1) * 256],
                              reads=[b_e2g[hl]], writes=[b_eb[i2]], key="eb%d_%d" % (i2, kcg))

            def s_blend(tk):
                i2 = (tk // 2) % 2
                mt, b_mt = mts[i2], b_mts[i2]
                P.op("pool", lambda e: e.tensor_scalar(out=et[:], in0=ea[i2][:], scalar1=coef[:, 0:1], scalar2=None, op0=ALU.mult),
                     reads=[b_ea[i2], b_const], writes=[b_et])
                P.op("dve", lambda e: e.scalar_tensor_tensor(out=et[:], in0=eb[i2][:], scalar=coef[:, 1:2], in1=et[:], op0=ALU.mult, op1=ALU.add),
                     reads=[b_eb[i2], b_et, b_const], writes=[b_et])
                P.op("dve", lambda e: e.tensor_tensor(out=mt[:, 0:8, :], in0=mt[:, 0:8, :], in1=et[:], op=ALU.mult),
                     reads=[b_et, b_mt], writes=[b_mt])

            def s_load(tk):
                y, b_y = ys[tk % NB], b_ys[tk % NB]
                tsl = slice(tk * 128, (tk + 1) * 128)
                if src_kind == "x":
                    P.dma("sp", y[:], x_d[tsl, :], writes=[b_y], key="ln_y%d" % (tk % NB))
                else:
                    rt, b_rt = rts[tk % NR], b_rts[tk % NR]
                    P.dma("sp", rt[:], resid_d[tsl, :], reads=[b_resid[tk]], writes=[b_rt], key="rt%d" % (tk % NR))

            def s1(tk):
                y, b_y = ys[tk % NB], b_ys[tk % NB]
                stt, mv, b_st = stats[tk % NB], mvs[tk % NB], b_sts[tk % NB]
                if src_kind != "x":
                    t2 = tk // 2
                    mt, b_mt = mts[t2 % 2], b_mts[t2 % 2]
                    rt, b_rt = rts[tk % NR], b_rts[tk % NR]
                    pts = [psum() for cg in range(4)]
                    for cg in range(4):
                        pt, b_pt = pts[cg]
                        P.op("pe", lambda e, pt=pt, cg=cg: e.matmul(pt, ones[0:1, :], bo[0:1, cg * 512:(cg + 1) * 512], start=True, stop=False),
                             reads=[b_p, b_const], writes=[b_pt])
                    for kc in range(16):
                        for cg in range(4):
                            pt, b_pt = pts[cg]
                            P.op("pe", lambda e, pt=pt, kc=kc, cg=cg: e.matmul(
                                pt, mt[:, kc, (tk % 2) * 128:(tk % 2 + 1) * 128], wo[:, kc, cg * 512:(cg + 1) * 512],
                                start=False, stop=(kc == 15)), reads=[b_mt, b_wop[kc // 4]], writes=[b_pt])
                    for cg in range(4):
                        pt, b_pt = pts[cg]
                        P.op("dve", lambda e, pt=pt, cg=cg: e.scalar_tensor_tensor(
                            out=y[:, cg * 512:(cg + 1) * 512], in0=rt[:, cg * 512:(cg + 1) * 512], scalar=ALPHA, in1=pt,
                            op0=ALU.mult, op1=ALU.add), reads=[b_pt, b_rt], writes=[b_y])
                for c in range(4):
                    P.op("dve", lambda e, c=c: e.bn_stats(out=stt[:, c, :], in_=y[:, c * 512:(c + 1) * 512]),
                         reads=[b_y], writes=[b_st])
                P.op("dve", lambda e: e.bn_aggr(out=mv[:, 0:2], in_=stt[:].rearrange("p a b -> p (a b)")),
                     reads=[b_st], writes=[b_st])
                P.op("act", lambda e: e.activation(out=mv[:, 2:3], in_=mv[:, 1:2], func=AF.Sqrt, bias=LN_EPS_AP[:, 0:1], scale=1.0),
                     reads=[b_st, b_const], writes=[b_st])
                P.op("dve", lambda e: e.reciprocal(out=mv[:, 2:3], in_=mv[:, 2:3]), reads=[b_st], writes=[b_st])
                P.op("dve", lambda e: e.scalar_tensor_tensor(out=mv[:, 3:4], in0=mv[:, 0:1], scalar=-1.0, in1=mv[:, 2:3],
                                                              op0=ALU.mult, op1=ALU.mult), reads=[b_st], writes=[b_st])

            def s2a(tk):
                y, b_y = ys[tk % NB], b_ys[tk % NB]
                mv, b_st = mvs[tk % NB], b_sts[tk % NB]
                P.op("act", lambda e: e.activation(out=y[:], in_=y[:], func=AF.Identity, bias=mv[:, 3:4], scale=mv[:, 2:3]),
                     reads=[b_y, b_st], writes=[b_y])

            def s2b(tk):
                y, b_y = ys[tk % NB], b_ys[tk % NB]
                P.op("dve", lambda e: e.tensor_tensor(out=y[:], in0=y[:], in1=gb[:], op=ALU.mult), reads=[b_y, b_p], writes=[b_y])
                P.op("pool", lambda e: e.tensor_tensor(out=y[:], in0=y[:], in1=bb[:], op=ALU.add), reads=[b_y, b_p], writes=[b_y])

            def s3(tk):
                y, b_y = ys[tk % NB], b_ys[tk % NB]
                hb, b_hb = hbs[tk % 2], b_hbs[tk % 2]
                tsl = slice(tk * 128, (tk + 1) * 128)
                if final:
                    P.dma("sp", out_d[tsl, :], y[:], reads=[b_y], writes=[b_out[tk]], key="ln_o%d" % (tk % NB))
                else:
                    P.dma("sp", resid_d[tsl, :], y[:], reads=[b_y], writes=[b_resid[tk]], key="ln_o%d" % (tk % NB))
                if write_hT:
                    P.op("act", lambda e: e.copy(out=hb[:], in_=y[:]), reads=[b_y], writes=[b_hb])
                    sg_, b_sg_ = stg[0], b_stg[0]
                    for q4 in range(4):
                        pt, b_pt = psum()
                        ptb = pt.bitcast(BF16)
                        for j in range(4):
                            kc = q4 * 4 + j
                            P.op("pe", lambda e, ptb=ptb, kc=kc, j=j: e.transpose(
                                ptb[:, j * 128:(j + 1) * 128], hb[:, kc * 128:(kc + 1) * 128], ident[:]),
                                reads=[b_hb, b_const], writes=[b_pt])
                        if q4 % 2 == 0:
                            P.op("dve", lambda e, ptb=ptb, q4=q4: e.tensor_copy(
                                out=sg_[:, q4 * 4:(q4 + 1) * 4, (tk % 4) * 128:(tk % 4 + 1) * 128],
                                in_=ptb[:, 0:512].rearrange("p (a b) -> p a b", a=4)), reads=[b_pt], writes=[b_sg_])
                        else:
                            P.op("act", lambda e, ptb=ptb, q4=q4: e.copy(
                                out=sg_[:, q4 * 4:(q4 + 1) * 4, (tk % 4) * 128:(tk % 4 + 1) * 128],
                                in_=ptb[:, 0:512].rearrange("p (a b) -> p a b", a=4)), reads=[b_pt], writes=[b_sg_])
                    if tk % 4 == 3:
                        tt = tk // 4
                        P.dma("sp", hT_v[:, :, tt * 512:(tt + 1) * 512], sg_[:], reads=[b_sg_], writes=[b_hT[tt]], key="ln_stg")
                        if tk == NTK - 1:
                            P.dma("sp", tls_v, sg_[:, :, 496:512], reads=[b_sg_], writes=[b_tlsrc], key="ln_tl")
                            P.collective(tl_src, tl_g, reads=[b_tlsrc], writes=[b_tlg], key="cc_e3")

            NTK = SO // 128
            is_proj = (src_kind != "x")
            if is_proj:
                s_pair(0)
                s_blend(0)
            s_load(0)
            for i in range(NTK + 2):
                if is_proj and i % 2 == 0 and i + 2 < NTK:
                    s_pair(i + 2)
                if is_proj and i % 2 == 1 and i + 1 < NTK:
                    s_blend(i + 1)
                if i + 1 < NTK:
                    s_load(i + 1)
                if 0 <= i - 1 < NTK:
                    s2a(i - 1)
                if i < NTK:
                    s1(i)
                if 0 <= i - 1 < NTK:
                    s2b(i - 1)
                if 0 <= i - 2 < NTK:
                    s3(i - 2)

        P.op("dve", lambda e: e.memset(LN_EPS_AP[:, 0:1], LN_EPS), writes=[b_const])
        P.op("dve", lambda e: e.memset(LN_EPS_AP[:, 1:2], RMS_EPS), writes=[b_const])

        with contextlib.ExitStack() as st:
            ln_phase(st, 0, "x", True, False)
            P.barrier(junk)

        def phase_A(L):
            with contextlib.ExitStack() as st:
                areset()
                hTs = sb("hTs", [128, 16, 2048], BF16, st)
                b_hTs = Buf()
                wr = [sb("wr%d" % i, [128, 16, 512], BF16, st) for i in range(2)]
                b_wr = [Buf() for i in range(2)]
                sm = sb("sm", [128, 32], F32, st)
                wp = sb("wp", [128, 4, 128], BF16, st)
                b_sm = Buf()
                P.dma("sp", sm[:], sm_d[L * 128:(L + 1) * 128, :], writes=[b_sm], key="sm")
                P.dma("pool", wp[:].rearrange("p a b -> p (a b)"), wp_d[L * 128:(L + 1) * 128, :], writes=[b_sm], key="wp")
                hp = sb("hp", [128, 4, 16], F32, st)
                hc = sb("hc", [128, 4, 2], F32, st)
                b_hp = [Buf() for i in range(4)]
                b_hc = [Buf() for i in range(4)]
                P.op("pool", lambda e: e.memset(hp[:], 0.0), writes=b_hp)
                P.op("pool", lambda e: e.memset(hc[:], 0.0), writes=b_hc)
                stgs = [sb("stg%d" % i, [128, 4, 512], BF16, st) for i in range(2)]
                b_stgs = [Buf() for i in range(2)]
                stg_ctr = [0]
                NTMP = 6
                tmps = [sb("tmp%d" % i, [128, 528], F32, st) for i in range(NTMP)]
                b_tmps = [Buf() for i in range(NTMP)]
                tmp_ctr = [0]
                sqs = [sb("sq%d" % i, [128, 512], BF16, st) for i in range(2)]
                b_sqs = [Buf() for i in range(2)]
                sq_ctr = [0]
                pls = [sb("pl%d" % i, [128, 512], BF16, st) for i in range(4)]
                b_pls = [Buf() for i in range(4)]
                sgps = [sb("sgp%d" % i, [128, 512], F32, st) for i in range(4)]
                b_sgps = [Buf() for i in range(4)]
                pl_ctr = [0]
                pending = []
                cosT = sb("cosO", [64, SO], BF16, st)
                sinT = sb("sinO", [64, SO], BF16, st)
                b_tab = Buf()
                P.dma("sp", cosT[:], cosO_d, reads=[b_tabd], writes=[b_tab], key="cosO")
                P.dma("sp", sinT[:], sinO_d, reads=[b_tabd], writes=[b_tab], key="sinO")
                hTh = sb("hTh", [128, 16, 16], BF16, st)
                b_hTh = Buf()
                P.dma("sp", hTh[:], tlg_v[:, 0:16, :], reads=[b_tlg], writes=[b_hTh], key="hTh")

                def proj_halo(w, c0):
                    pt, b_pt = psum()
                    for kc in range(16):
                        P.op("pe", lambda e, pt=pt, w=w, kc=kc: e.matmul(
                            pt[:, 0:16], w[0][:, kc, c0:c0 + 128], hTh[:, kc, :],
                            start=(kc == 0), stop=(kc == 15)), reads=[w[1], b_hTh], writes=[b_pt])
                    return pt, b_pt

                def tmp():
                    i = tmp_ctr[0] % NTMP
                    tmp_ctr[0] += 1
                    return tmps[i], b_tmps[i]

                def proj(w, c0, ncols, tt):
                    pt, b_pt = psum()
                    for kc in range(16):
                        P.op("pe", lambda e, pt=pt, w=w, kc=kc: e.matmul(
                            pt[0:ncols, :], w[0][:, kc, c0:c0 + ncols], hTs[:, kc, tt * 512:(tt + 1) * 512],
                            start=(kc == 0), stop=(kc == 15)), reads=[w[1], b_hTs], writes=[b_pt])
                    return pt, b_pt

                def rmsnorm_group(w, c0, nch, tt, dim, dst_v, dst_bufs, gtt, key):
                    pts = [proj(w, c0 + c * 128, 128, tt) for c in range(nch)]
                    ss, b_ss = psum()
                    for c, (pt, b_pt) in enumerate(pts):
                        i = sq_ctr[0] % 2
                        sq_ctr[0] += 1
                        sq, b_sq = sqs[i], b_sqs[i]
                        P.op("act", lambda e, pt=pt, sq=sq: e.activation(out=sq[:], in_=pt, func=AF.Square), reads=[b_pt], writes=[b_sq])
                        P.op("pe", lambda e, ss=ss, sq=sq, c=c: e.matmul(ss, ones[:], sq[:], start=(c == 0), stop=(c == nch - 1)),
                             reads=[b_sq, b_const], writes=[b_ss])
                    rs, b_rs = tmp()
                    P.op("act", lambda e, rs=rs, ss=ss: e.activation(out=rs[:, 0:512], in_=ss, func=AF.Sqrt, bias=LN_EPS_AP[:, 1:2],
                                                                      scale=1.0 / dim), reads=[b_ss, b_const], writes=[b_rs])
                    P.op("dve", lambda e, rs=rs: e.reciprocal(out=rs[:, 0:512], in_=rs[:, 0:512]), reads=[b_rs], writes=[b_rs])
                    i = stg_ctr[0] % 2
                    stg_ctr[0] += 1
                    sg_, b_sg_ = stgs[i], b_stgs[i]
                    for c, (pt, b_pt) in enumerate(pts):
                        P.op("dve", lambda e, pt=pt, rs=rs, sg_=sg_, c=c: e.tensor_tensor(out=sg_[:, c, :], in0=pt, in1=rs[:, 0:512], op=ALU.mult),
                             reads=[b_pt, b_rs], writes=[b_sg_])
                    P.dma("sp", dst_v[:, :, gtt * 512:(gtt + 1) * 512], sg_[:, 0:nch, :], reads=[b_sg_], writes=[dst_bufs[gtt]], key="stg%d" % i)

                for hf in range(1):
                    P.dma("sp", hTs[:], hT_v[:, :, hf * 2048:(hf + 1) * 2048], reads=b_hT[hf * 4:(hf + 1) * 4], writes=[b_hTs], key="hTs")
                    for g in range(NG):
                        wi = (hf * NG + g) % 2
                        w = (wr[wi], b_wr[wi])
                        row0 = (L * NG + g) * 128
                        P.dma("pool", wr[wi][:].rearrange("p a b -> p (a b)"), win_d[row0:row0 + 128, :], writes=[b_wr[wi]], key="wr%d" % wi)
                        if g == 2:
                            if debug:
                                P.dma("sp", dbg_e1a, e1a_src, reads=b_qn, writes=[Buf()], key="dbg_e1a")
                                P.dma("sp", dbg_e1b, e1b_src, reads=b_kvn + b_kr, writes=[Buf()], key="dbg_e1b")
                            P.collective(e1b_src, e1b_g, reads=b_kvn + b_kr, writes=[b_e1g], key="cc_e1b")
                            P.collective(e1a_src, e1a_g, reads=b_qn + [b_e1g], writes=[b_e1g], key="cc_e1a")
                        for tt in range(4):
                            gtt = hf * 4 + tt
                            tok = slice(gtt * 512, (gtt + 1) * 512)
                            if g == 0:
                                rmsnorm_group(w, 0, 2, tt, 256.0, kvn_v, b_kvn, gtt, "kvn")
                                pa, b_pa = proj(w, 256, 64, tt)
                                pb, b_pb = proj(w, 320, 64, tt)
                                t1, b_t1 = tmp()
                                t2, b_t2 = tmp()
                                P.op("dve", lambda e, pa=pa, t1=t1, tok=tok: e.tensor_tensor(out=t1[0:64, 0:512], in0=pa[0:64, :], in1=cosT[:, tok], op=ALU.mult),
                                     reads=[b_pa, b_tab], writes=[b_t1])
                                P.op("dve", lambda e, pb=pb, t2=t2, tok=tok: e.tensor_tensor(out=t2[0:64, 0:512], in0=pb[0:64, :], in1=sinT[:, tok], op=ALU.mult),
                                     reads=[b_pb, b_tab], writes=[b_t2])
                                i = stg_ctr[0] % 2
                                stg_ctr[0] += 1
                                sg_, b_sg_ = stgs[i], b_stgs[i]
                                P.op("pool", lambda e, t1=t1, t2=t2, sg_=sg_: e.tensor_tensor(out=sg_[0:64, 0, :], in0=t1[0:64, 0:512], in1=t2[0:64, 0:512], op=ALU.add),
                                     reads=[b_t1, b_t2], writes=[b_sg_])
                                P.dma("sp", kr_d[:, tok], sg_[0:64, 0, :], reads=[b_sg_], writes=[b_kr[gtt]], key="stg%d" % i)
                            elif g == 1:
                                rmsnorm_group(w, 0, 4, tt, 512.0, qn_v, b_qn, gtt, "qn")
                            elif g in (2, 3):
                                i = stg_ctr[0] % 2
                                stg_ctr[0] += 1
                                sg_, b_sg_ = stgs[i], b_stgs[i]
                                for c in range(4):
                                    pt, b_pt = proj(w, c * 128, 128, tt)
                                    P.op("act", lambda e, pt=pt, sg_=sg_, c=c: e.activation(out=sg_[:, c, :], in_=pt, func=AF.Silu),
                                         reads=[b_pt], writes=[b_sg_])
                                r0 = (g - 2) * 4
                                P.dma("sp", mix_v[:, r0:r0 + 4, tok], sg_[:], reads=[b_sg_], writes=[b_mix[r0 + c][gtt] for c in range(4)], key="stg%d" % i)
                            elif g in (4, 5):
                                tails = []
                                for s_ in range(2):
                                    pg = (g - 4) * 2 + s_
                                    wlen = POOL_W[pg]
                                    px, b_px = proj(w, s_ * 256, 128, tt)
                                    pgt, b_pgt = proj(w, s_ * 256 + 128, 128, tt)
                                    xb, b_xb = tmp()
                                    sa, b_sa = tmp()
                                    sb_, b_sb = tmp()
                                    P.op("act", lambda e, px=px, xb=xb: e.copy(out=xb[:, 16:528], in_=px), reads=[b_px], writes=[b_xb])
                                    if tt == 0:
                                        ph, b_ph = proj_halo(w, s_ * 256)
                                        P.op("dve", lambda e, xb=xb, ph=ph: e.tensor_scalar(out=xb[:, 0:16], in0=ph[:, 0:16], scalar1=coef[:, 1:2], scalar2=None, op0=ALU.mult),
                                             reads=[b_ph, b_xb, b_const], writes=[b_xb])
                                    else:
                                        P.op("pool", lambda e, xb=xb, pg=pg: e.tensor_copy(out=xb[:, 0:16], in_=hp[:, pg, :]), reads=[b_hp[pg], b_xb], writes=[b_xb])
                                    P.op("pool", lambda e, xb=xb, pg=pg: e.tensor_copy(out=hp[:, pg, :], in_=xb[:, 512:528]), reads=[b_xb], writes=[b_hp[pg]])
                                    P.op("pool", lambda e, xb=xb, sa=sa: e.tensor_tensor(out=sa[:, 1:528], in0=xb[:, 1:528], in1=xb[:, 0:527], op=ALU.add),
                                         reads=[b_xb], writes=[b_sa])
                                    fin, b_fin = sa, b_sa
                                    if wlen >= 4:
                                        P.op("pool", lambda e, sa=sa, sb_=sb_: e.tensor_tensor(out=sb_[:, 3:528], in0=sa[:, 3:528], in1=sa[:, 1:526], op=ALU.add),
                                             reads=[b_sa], writes=[b_sb])
                                        fin, b_fin = sb_, b_sb
                                    if wlen >= 8:
                                        P.op("pool", lambda e, sa=sa, sb_=sb_: e.tensor_tensor(out=sa[:, 7:528], in0=sb_[:, 7:528], in1=sb_[:, 3:524], op=ALU.add),
                                             reads=[b_sb, b_sa], writes=[b_sa])
                                        fin, b_fin = sa, b_sa
                                    if wlen >= 16:
                                        P.op("pool", lambda e, sa=sa, sb_=sb_: e.tensor_tensor(out=sb_[:, 15:528], in0=sa[:, 15:528], in1=sa[:, 7:520], op=ALU.add),
                                             reads=[b_sa, b_sb], writes=[b_sb])
                                        fin, b_fin = sb_, b_sb
                                    ip = pl_ctr[0] % 4
                                    pl_ctr[0] += 1
                                    pl, b_pl = pls[ip], b_pls[ip]
                                    sgp, b_sgp = sgps[ip], b_sgps[ip]
                                    P.op("dve", lambda e, fin=fin, xb=xb, pl=pl, wlen=wlen: e.scalar_tensor_tensor(
                                        out=pl[:], in0=fin[:, 16:528], scalar=1.0 / wlen, in1=xb[:, 16:528], op0=ALU.mult, op1=ALU.subtract),
                                        reads=[b_fin, b_xb], writes=[b_pl])
                                    if gtt == 0:
                                        t16, b_t16 = tmp()
                                        P.op("dve", lambda e, fin=fin, t16=t16, pg=pg: e.tensor_tensor(out=t16[:, 0:16], in0=fin[:, 16:32], in1=invdiv[:, pg, :], op=ALU.mult),
                                             reads=[b_fin, b_const], writes=[b_t16])
                                        P.op("dve", lambda e, t16=t16, xb=xb, pl=pl: e.tensor_tensor(out=pl[:, 0:16], in0=t16[:, 0:16], in1=xb[:, 16:32], op=ALU.subtract),
                                             reads=[b_t16, b_xb, b_pl], writes=[b_pl])
                                    P.op("act", lambda e, pgt=pgt, sgp=sgp: e.activation(out=sgp[:], in_=pgt, func=AF.Silu), reads=[b_pgt], writes=[b_sgp])
                                    tails.append((s_, pg, pl, b_pl, sgp, b_sgp))

                                def pool_tail(tails=tails, g=g, gtt=gtt, tok=tok):
                                    i = stg_ctr[0] % 2
                                    stg_ctr[0] += 1
                                    sg_, b_sg_ = stgs[i], b_stgs[i]
                                    for (s_, pg, pl, b_pl, sgp, b_sgp) in tails:
                                        py, b_py = psum()
                                        P.op("pe", lambda e, py=py, pl=pl, pg=pg: e.matmul(py, wp[:, pg, :], pl[:], start=True, stop=True),
                                             reads=[b_pl, b_sm], writes=[b_py])
                                        P.op("dve", lambda e, py=py, sgp=sgp, pg=pg, s_=s_: e.scalar_tensor_tensor(
                                            out=sg_[:, s_, :], in0=py, scalar=sm[:, 6 + pg:7 + pg], in1=sgp[:], op0=ALU.mult, op1=ALU.mult),
                                            reads=[b_py, b_sgp, b_sm], writes=[b_sg_])
                                    r0 = 8 + (g - 4) * 2
                                    P.dma("sp", mix_v[:, r0:r0 + 2, tok], sg_[:, 0:2, :], reads=[b_sg_], writes=[b_mix[r0 + c][gtt] for c in range(2)], key="stg%d" % i)

                                for fn_ in pending:
                                    fn_()
                                pending.clear()
                                pending.append(pool_tail)
                                if tt == 3:
                                    for fn_ in pending:
                                        fn_()
                                    pending.clear()
                            else:
                                j = g - 6
                                i = stg_ctr[0] % 2
                                stg_ctr[0] += 1
                                sg_, b_sg_ = stgs[i], b_stgs[i]
                                pch, b_pch = proj(w, 0, 128, tt)
                                pcc, b_pcc = proj(w, 128, 128, tt)
                                pcb, b_pcb = proj(w, 256, 128, tt)
                                pgc, b_pgc = proj(w, 384, 128, tt)
                                chs, b_chs = tmp()
                                ub, b_ub = tmp()
                                t1, b_t1 = tmp()
                                t2, b_t2 = tmp()
                                P.op("act", lambda e, pch=pch, chs=chs: e.copy(out=chs[:, 0:512], in_=pch), reads=[b_pch], writes=[b_chs])
                                P.op("dve", lambda e, pcc=pcc, chs=chs, ub=ub: e.tensor_tensor(out=ub[:, 2:514], in0=pcc, in1=chs[:, 0:512], op=ALU.mult),
                                     reads=[b_pcc, b_chs], writes=[b_ub])
                                if tt == 0:
                                    ph1, b_ph1 = proj_halo(w, 0)
                                    ph2, b_ph2 = proj_halo(w, 128)
                                    hh, b_hh = tmp()
                                    P.op("act", lambda e, ph1=ph1, hh=hh: e.copy(out=hh[:, 0:16], in_=ph1[:, 0:16]), reads=[b_ph1], writes=[b_hh])
                                    P.op("dve", lambda e, ph2=ph2, hh=hh: e.tensor_tensor(out=hh[:, 16:32], in0=ph2[:, 0:16], in1=hh[:, 0:16], op=ALU.mult),
                                         reads=[b_ph2, b_hh], writes=[b_hh])
                                    P.op("dve", lambda e, ub=ub, hh=hh: e.tensor_scalar(out=ub[:, 0:2], in0=hh[:, 30:32], scalar1=coef[:, 1:2], scalar2=None, op0=ALU.mult),
                                         reads=[b_hh, b_ub, b_const], writes=[b_ub])
                                else:
                                    P.op("pool", lambda e, ub=ub, j=j: e.tensor_copy(out=ub[:, 0:2], in_=hc[:, j, :]), reads=[b_hc[j], b_ub], writes=[b_ub])
                                P.op("pool", lambda e, ub=ub, j=j: e.tensor_copy(out=hc[:, j, :], in_=ub[:, 512:514]), reads=[b_ub], writes=[b_hc[j]])
                                cw0 = 10 + j * 3
                                P.op("pool", lambda e, ub=ub, t1=t1, cw0=cw0: e.tensor_scalar(out=t1[:, 0:512], in0=ub[:, 0:512], scalar1=sm[:, cw0:cw0 + 1], scalar2=None, op0=ALU.mult),
                                     reads=[b_ub, b_sm], writes=[b_t1])
                                P.op("dve", lambda e, ub=ub, t1=t1, t2=t2, cw0=cw0: e.scalar_tensor_tensor(
                                    out=t2[:, 0:512], in0=ub[:, 1:513], scalar=sm[:, cw0 + 1:cw0 + 2], in1=t1[:, 0:512], op0=ALU.mult, op1=ALU.add),
                                    reads=[b_ub, b_t1, b_sm], writes=[b_t2])
                                P.op("dve", lambda e, ub=ub, t1=t1, t2=t2, cw0=cw0: e.scalar_tensor_tensor(
                                    out=t1[:, 0:512], in0=ub[:, 2:514], scalar=sm[:, cw0 + 2:cw0 + 3], in1=t2[:, 0:512], op0=ALU.mult, op1=ALU.add),
                                    reads=[b_ub, b_t2, b_t1, b_sm], writes=[b_t1])
                                P.op("dve", lambda e, pcb=pcb, t1=t1, t2=t2: e.tensor_tensor(out=t2[:, 0:512], in0=pcb, in1=t1[:, 0:512], op=ALU.mult),
                                     reads=[b_pcb, b_t1, b_t2], writes=[b_t2])
                                P.op("act", lambda e, pgc=pgc, chs=chs: e.activation(out=chs[:, 0:512], in_=pgc, func=AF.Silu), reads=[b_pgc, b_chs], writes=[b_chs])
                                P.op("pool", lambda e, t2=t2, chs=chs, sg_=sg_: e.tensor_tensor(out=sg_[:, 0, :], in0=t2[:, 0:512], in1=chs[:, 0:512], op=ALU.mult),
                                     reads=[b_t2, b_chs], writes=[b_sg_])
                                r0 = 12 + j
                                P.dma("sp", mix_v[:, r0, tok], sg_[:, 0, :], reads=[b_sg_], writes=[b_mix[r0][gtt]], key="stg%d" % i)
                P.barrier(junk)

        def phase_B(L):
            with contextlib.ExitStack() as st:
                areset()
                SB_ = [4, 5, 6, 7]
                sctr = [0]
                octr = [0]
                lctr = [0]
                qns = sb("qns", [128, 4, S], BF16, st)
                kvns = sb("kvns", [128, 2, S], BF16, st)
                krs = sb("krs", [64, S], BF16, st)
                b_lat = Buf()
                for rho in range(2):
                    csl = slice(rho * SO, (rho + 1) * SO)
                    P.dma("sp", qns[:, :, csl], e1a_g[rho * 512:(rho + 1) * 512, :].rearrange("(kc p) t -> p kc t", p=128), reads=[b_e1g], writes=[b_lat], key="qns%d" % rho)
                    P.dma("sp", kvns[:, :, csl], e1b_g[rho * 320:rho * 320 + 256, :].rearrange("(kc p) t -> p kc t", p=128), reads=[b_e1g], writes=[b_lat], key="kvns%d" % rho)
                    P.dma("sp", krs[:, csl], e1b_g[rho * 320 + 256:rho * 320 + 320, :], reads=[b_e1g], writes=[b_lat], key="krs%d" % rho)
                cosT = sb("cosA", [64, S], BF16, st)
                sinT = sb("sinA", [64, S], BF16, st)
                b_tab = Buf()
                P.dma("sp", cosT[:], cosA_d, reads=[b_tabd], writes=[b_tab], key="cosA")
                P.dma("sp", sinT[:], sinA_d, reads=[b_tabd], writes=[b_tab], key="sinA")
                wq = sb("wq", [128, 4, 1024], BF16, st)
                wk = sb("wk", [128, 2, 512], BF16, st)
                wv = sb("wv", [128, 2, 512], BF16, st)
                sm = sb("smB", [128, 32], F32, st)
                b_w = Buf()
                P.dma("sp", sm[:], sm_d[L * 128:(L + 1) * 128, :], writes=[b_w], key="smB")
                b_w1, b_w2 = Buf(), Buf()
                P.dma("pool", wq[:].rearrange("p a b -> p (a b)"), wq_d[L * 128:(L + 1) * 128, :], writes=[b_w1], key="wq")
                P.dma("pool", wk[:].rearrange("p a b -> p (a b)"), wk_d[L * 128:(L + 1) * 128, :], reads=[b_w1], writes=[b_w2], key="wk")
                P.dma("pool", wv[:].rearrange("p a b -> p (a b)"), wv_d[L * 128:(L + 1) * 128, :], reads=[b_w1, b_w2], writes=[b_w], key="wv")
                for kc in range(4):
                    P.op("dve", lambda e, kc=kc: e.tensor_scalar(out=wq[:, kc, :], in0=wq[:, kc, :], scalar1=sm[:, kc:kc + 1], scalar2=None, op0=ALU.mult),
                         reads=[b_w], writes=[b_w])
                for kc in range(2):
                    P.op("dve", lambda e, kc=kc: e.tensor_scalar(out=wk[:, kc, :], in0=wk[:, kc, :], scalar1=sm[:, 4 + kc:5 + kc], scalar2=None, op0=ALU.mult),
                         reads=[b_w], writes=[b_w])
                    P.op("dve", lambda e, kc=kc: e.tensor_scalar(out=wv[:, kc, :], in0=wv[:, kc, :], scalar1=sm[:, 4 + kc:5 + kc], scalar2=None, op0=ALU.mult),
                         reads=[b_w], writes=[b_w])
                Vq = sb("Vq", [128, 32, 512], BF16, st)
                b_V = [Buf() for i in range(32)]
                kTh = sb("kTh", [128, S], BF16, st)
                qTh = sb("qTh", [128, S], BF16, st)
                qrh = sb("qrh", [64, S], BF16, st)
                b_kT = [Buf() for i in range(8)]
                b_qT = [Buf() for i in range(8)]
                b_qr = [Buf() for i in range(8)]
                pTs = [sb("pT%d" % i, [128, 512], BF16, st) for i in range(4)]
                b_pTs = [Buf() for i in range(4)]
                pT_ctr = [0]
                rls = [sb("rl%d" % i, [128, 512], F32, st) for i in range(2)]
                b_rls = [Buf() for i in range(2)]
                outs = [sb("ob%d" % i, [128, 512], BF16, st) for i in range(2)]
                b_outs = [Buf() for i in range(2)]
                rt1 = [sb("rta%d" % i, [64, 512], F32, st) for i in range(2)]
                rt2 = [sb("rtb%d" % i, [64, 512], F32, st) for i in range(2)]
                b_rt1 = [Buf() for i in range(2)]
                b_rt2 = [Buf() for i in range(2)]
                ev = [0]

                def evac(out_ap, in_ap, reads, writes):
                    ev[0] += 1
                    if ev[0] % 2 == 0:
                        P.op("dve", lambda e: e.tensor_copy(out=out_ap, in_=in_ap), reads=reads, writes=writes)
                    else:
                        P.op("act", lambda e: e.copy(out=out_ap, in_=in_ap), reads=reads, writes=writes)

                uc = [0]
                for h in range(HL):
                    if h % 4 == 0:
                        hq = h // 4
                        for tk in range(32):
                            pt, b_pt = psum(SB_, sctr)
                            for kc in range(2):
                                P.op("pe", lambda e, pt=pt, kc=kc, tk=tk, hq=hq: e.matmul(
                                    pt, kvns[:, kc, tk * 128:(tk + 1) * 128], wv[:, kc, hq * 512:(hq + 1) * 512], start=(kc == 0), stop=(kc == 1)),
                                    reads=[b_lat, b_w], writes=[b_pt])
                            evac(Vq[:, tk, :], pt, [b_pt], [b_V[tk]])
                    for tt in range(8):
                        tok = slice(tt * 512, (tt + 1) * 512)
                        pt, b_pt = psum(SB_, sctr)
                        for kc in range(2):
                            P.op("pe", lambda e, pt=pt, kc=kc, tok=tok, h=h: e.matmul(
                                pt, wk[:, kc, h * 128:(h + 1) * 128], kvns[:, kc, tok], start=(kc == 0), stop=(kc == 1)),
                                reads=[b_lat, b_w], writes=[b_pt])
                        evac(kTh[:, tok], pt, [b_pt], [b_kT[tt]])
                        pt, b_pt = psum(SB_, sctr)
                        for kc in range(4):
                            P.op("pe", lambda e, pt=pt, kc=kc, tok=tok, h=h: e.matmul(
                                pt, wq[:, kc, h * 256:h * 256 + 128], qns[:, kc, tok], start=(kc == 0), stop=(kc == 3)),
                                reads=[b_lat, b_w], writes=[b_pt])
                        evac(qTh[:, tok], pt, [b_pt], [b_qT[tt]])
                        pa, b_pa = psum(SB_, sctr)
                        for kc in range(4):
                            P.op("pe", lambda e, pa=pa, kc=kc, tok=tok, h=h: e.matmul(
                                pa[0:64, :], wq[:, kc, h * 256 + 128:h * 256 + 192], qns[:, kc, tok], start=(kc == 0), stop=(kc == 3)),
                                reads=[b_lat, b_w], writes=[b_pa])
                        pb, b_pb = psum(SB_, sctr)
                        for kc in range(4):
                            P.op("pe", lambda e, pb=pb, kc=kc, tok=tok, h=h: e.matmul(
                                pb[0:64, :], wq[:, kc, h * 256 + 192:h * 256 + 256], qns[:, kc, tok], start=(kc == 0), stop=(kc == 3)),
                                reads=[b_lat, b_w], writes=[b_pb])
                        i = tt % 2
                        P.op("dve", lambda e, pa=pa, i=i, tok=tok: e.tensor_tensor(out=rt1[i][:], in0=pa[0:64, :], in1=cosT[:, tok], op=ALU.mult),
                             reads=[b_pa, b_tab], writes=[b_rt1[i]])
                        P.op("dve", lambda e, pb=pb, i=i, tok=tok: e.tensor_tensor(out=rt2[i][:], in0=pb[0:64, :], in1=sinT[:, tok], op=ALU.mult),
                             reads=[b_pb, b_tab], writes=[b_rt2[i]])
                        P.op("pool", lambda e, i=i, tok=tok: e.tensor_tensor(out=qrh[:, tok], in0=rt1[i][:], in1=rt2[i][:], op=ALU.add),
                             reads=[b_rt1[i], b_rt2[i]], writes=[b_qr[tt]])
                    units = [(qb, kb) for qb in range(8) for kb in range(4 * qb + 4)]
                    LOOK = 2
                    acc = {}
                    sc = {}

                    def emit_scores(u):
                        qb, kb = units[u]
                        qtok = slice(qb * 512, (qb + 1) * 512)
                        ktok = slice(kb * 128, (kb + 1) * 128)
                        ps_, b_ps_ = psum(SB_, sctr)
                        diag = kb >= 4 * qb
                        P.op("pe", lambda e: e.matmul(ps_, kTh[:, ktok], qTh[:, qtok], start=True, stop=False),
                             reads=[b_kT[kb // 4], b_qT[qb]], writes=[b_ps_])
                        P.op("pe", lambda e: e.matmul(ps_, krs[:, ktok], qrh[:, qtok], start=False, stop=(not diag)),
                             reads=[b_lat, b_qr[qb]], writes=[b_ps_])
                        if diag:
                            jm = kb - 4 * qb
                            P.op("pe", lambda e: e.matmul(ps_, ident[:], masks[:, jm, :], start=False, stop=True),
                                 reads=[b_const], writes=[b_ps_])
                        sc[u] = (ps_, b_ps_)

                    def emit_rest(u, h=h):
                        qb, kb = units[u]
                        nkb = 4 * qb + 4
                        qtok = slice(qb * 512, (qb + 1) * 512)
                        if kb == 0:
                            acc[qb] = (psum([0, 1], octr), psum([2, 3], lctr))
                        (po, b_po), (pl_, b_pl_) = acc[qb]
                        ps_, b_ps_ = sc.pop(u)
                        ip = pT_ctr[0] % 4
                        pT_ctr[0] += 1
                        pT, b_pT = pTs[ip], b_pTs[ip]
                        P.op("act", lambda e: e.activation(out=pT[:], in_=ps_, func=AF.Exp, scale=SCALE), reads=[b_ps_], writes=[b_pT])
                        P.op("pe", lambda e: e.matmul(po, Vq[:, kb, (h % 4) * 128:(h % 4 + 1) * 128], pT[:], start=(kb == 0), stop=(kb == nkb - 1)),
                             reads=[b_V[kb], b_pT], writes=[b_po])
                        P.op("pe", lambda e: e.matmul(pl_, ones[:], pT[:], start=(kb == 0), stop=(kb == nkb - 1)),
                             reads=[b_const, b_pT], writes=[b_pl_])
                        if kb == nkb - 1:
                            i = uc[0] % 2
                            uc[0] += 1
                            P.op("dve", lambda e: e.reciprocal(out=rls[i][:], in_=pl_), reads=[b_pl_], writes=[b_rls[i]])
                            P.op("dve", lambda e: e.tensor_tensor(out=outs[i][:], in0=po, in1=rls[i][:], op=ALU.mult),
                                 reads=[b_po, b_rls[i]], writes=[b_outs[i]])
                            P.dma("sp", e2_src[h][:, qtok], outs[i][:], reads=[b_outs[i]], writes=[b_e2s[h][qb]], key="ob%d" % i)
                            if qb == 7:
                                if debug:
                                    P.dma("sp", dbg_e2[h], e2_src[h], reads=b_e2s[h], writes=[Buf()], key="dbg_e2")
                                P.collective(e2_src[h], e2_g[h], reads=b_e2s[h], writes=[b_e2g[h]], key="cc_e2_%d" % h)

                    for u in range(min(LOOK, len(units))):
                        emit_scores(u)
                    for u in range(len(units)):
                        if u + LOOK < len(units):
                            emit_scores(u + LOOK)
                        emit_rest(u)
                P.barrier(junk)

        for L in range(n_layers):
            if stop_phase == "ln0":
                break
            phase_A(L)
            if stop_phase == "A":
                break
            phase_B(L)
            if stop_phase == "B":
                break
            with contextlib.ExitStack() as st:
                last = (L == DEPTH - 1)
                ln_phase(st, L, "proj", not last, last)
                P.barrier(junk)
        P.finish()
    return nc


def _tile_k(w, ncols_pad=None):
    K, C = w.shape
    return np.ascontiguousarray(w.reshape(K // 128, 128, C).transpose(1, 0, 2))


def prep_inputs(x, positions, emb_ln_g, emb_ln_b, w_in, q_norm_g, kv_norm_g, w_uq, w_ukv, w_pool,
                pool_scale, conv_w, w_out, b_out, ln_g, ln_b):
    f32 = np.float32
    w_in = np.asarray(w_in, f32)
    offs = np.cumsum([0, 512, 256, 64, 1024, 512, 512, 512, 512, 512, 512])
    o_q, o_kv, o_kr, o_gm, o_pi, o_gp, o_ch, o_cb, o_cc, o_gc = offs[:10]
    groups = []
    zero128 = None
    for L in range(DEPTH):
        W = w_in[L]
        kr = W[:, o_kr:o_kr + 64]
        ksw = np.concatenate([kr[:, 32:64], kr[:, 0:32]], axis=1)
        g0 = np.concatenate([W[:, o_kv:o_kv + 256], kr, ksw, np.zeros((D, 128), f32)], axis=1)
        gl = [g0, W[:, o_q:o_q + 512], W[:, o_gm:o_gm + 512], W[:, o_gm + 512:o_gm + 1024]]
        for a in range(2):
            gl.append(np.concatenate([W[:, o_pi + (2 * a) * 128:o_pi + (2 * a + 1) * 128], W[:, o_gp + (2 * a) * 128:o_gp + (2 * a + 1) * 128],
                                      W[:, o_pi + (2 * a + 1) * 128:o_pi + (2 * a + 2) * 128], W[:, o_gp + (2 * a + 1) * 128:o_gp + (2 * a + 2) * 128]], axis=1))
        for j in range(4):
            sl = slice(j * 128, (j + 1) * 128)
            gl.append(np.concatenate([W[:, o_ch:o_ch + 512][:, sl], W[:, o_cc:o_cc + 512][:, sl], W[:, o_cb:o_cb + 512][:, sl], W[:, o_gc:o_gc + 512][:, sl]], axis=1))
        for gmat in gl:
            groups.append(_tile_k(gmat).reshape(128, 16 * 512))
    w_in_g = np.ascontiguousarray(np.concatenate(groups, axis=0))

    wq_l, wk_l, wv_l = [[], []], [[], []], [[], []]
    wo_l, wp_l, sm_l = [], [], []
    for L in range(DEPTH):
        wq = np.asarray(w_uq[L], f32).reshape(512, NH, 192)
        rope = wq[:, :, 128:192]
        sw = np.concatenate([rope[:, :, 32:64], rope[:, :, 0:32]], axis=2)
        wq2 = np.concatenate([wq, sw], axis=2)
        wkv = np.asarray(w_ukv[L], f32).reshape(256, NH, 256)
        for r in range(2):
            hs = slice(4 * r, 4 * r + 4)
            wq_l[r].append(_tile_k(np.ascontiguousarray(wq2[:, hs]).reshape(512, 1024)).reshape(128, 4 * 1024))
            wk_l[r].append(_tile_k(np.ascontiguousarray(wkv[:, hs, 0:128]).reshape(256, 512)).reshape(128, 2 * 512))
            wv_l[r].append(_tile_k(np.ascontiguousarray(wkv[:, hs, 128:256]).reshape(256, 512)).reshape(128, 2 * 512))
        wo_l.append(_tile_k(np.asarray(w_out[L], f32)).reshape(128, 16 * 2048))
        wp_l.append(np.ascontiguousarray(np.asarray(w_pool[L], f32).transpose(1, 0, 2)).reshape(128, 4 * 128))
        sm = np.zeros((128, 32), f32)
        sm[:, 0:4] = np.asarray(q_norm_g[L], f32).reshape(4, 128).T
        sm[:, 4:6] = np.asarray(kv_norm_g[L], f32).reshape(2, 128).T
        sm[:, 6:10] = np.asarray(pool_scale[L], f32).reshape(4, 128).T
        cw = np.asarray(conv_w[L], f32).reshape(3, 4, 128)
        sm[:, 10:22] = cw.transpose(2, 1, 0).reshape(128, 12)
        sm_l.append(sm)
    lnp = np.stack([np.asarray(emb_ln_g, f32), np.asarray(emb_ln_b, f32)] +
                   sum([[np.asarray(ln_g[L], f32), np.asarray(ln_b[L], f32), np.asarray(b_out[L], f32)] for L in range(DEPTH)], []), axis=0)
    half = 32
    inv_freq = (10000.0 ** (-np.arange(half, dtype=np.float32) / half)).astype(f32)
    ropec = np.zeros((64, 2), f32)
    ropec[:, 0] = np.concatenate([inv_freq, inv_freq])
    ropec[:, 1] = np.concatenate([-np.ones(32, f32), np.ones(32, f32)])
    invdiv = np.zeros((2, 128, 4, 16), f32)
    for g, w in enumerate(POOL_W):
        invdiv[0, :, g, :] = 1.0 / np.minimum(np.arange(1, 17, dtype=f32), float(w))
        invdiv[1, :, g, :] = 1.0 / float(w)
    ident = np.eye(128, dtype=f32).astype(ml_dtypes.bfloat16)
    kk = np.arange(128)[:, None]
    qq = np.arange(512)[None, :]
    masks = np.stack([np.where(j * 128 + kk <= qq, 0.0, NEG) for j in range(4)], axis=1).astype(f32)
    masks = masks.reshape(128, 4 * 512).astype(ml_dtypes.bfloat16)
    shared = {
        "ropec": ropec, "lnp": np.ascontiguousarray(lnp), "w_in_g": w_in_g,
        "wo": np.ascontiguousarray(np.concatenate(wo_l, 0)),
        "wp": np.ascontiguousarray(np.concatenate(wp_l, 0)), "small": np.ascontiguousarray(np.concatenate(sm_l, 0)),
        "ident": ident, "masks": masks,
    }
    per_rank = []
    for r in range(2):
        coef = np.zeros((128, 2), f32)
        coef[:, r] = 1.0
        per_rank.append({
            "wq": np.ascontiguousarray(np.concatenate(wq_l[r], 0)), "wk": np.ascontiguousarray(np.concatenate(wk_l[r], 0)),
            "wv": np.ascontiguousarray(np.concatenate(wv_l[r], 0)), "invdiv": np.ascontiguousarray(invdiv[r].reshape(128, 64)),
            "coef": coef,
        })
    x = np.asarray(x, f32)
    positions = np.asarray(positions, np.int32)
    in_maps = []
    for c in range(8):
        b, r = c // 2, c % 2
        m = dict(shared)
        m.update(per_rank[r])
        m["x"] = np.ascontiguousarray(x[b, r * SO:(r + 1) * SO])
        m["pos"] = np.ascontiguousarray(positions[b][None, :])
        m["pos_own"] = np.ascontiguousarray(positions[b, r * SO:(r + 1) * SO][None, :])
        in_maps.append(m)
    return in_maps


def kernel(**inputs):
    in_maps = prep_inputs(**inputs)
    nc = build()
    res = run_bass_kernel_spmd(nc, in_maps, core_ids=list(range(8)))
    out = np.empty((4, S, D), np.float32)
    for c in range(8):
        b, r = c // 2, c % 2
        out[b, r * SO:(r + 1) * SO] = np.asarray(res.results[c]["out"], dtype=np.float32)
    return out
```

```python
import math
import contextlib
import numpy as np
import ml_dtypes
import concourse.bass as bass
import concourse.mybir as mybir
from concourse.bass_utils import run_bass_kernel_spmd

F32 = mybir.dt.float32
BF16 = mybir.dt.bfloat16
I32 = mybir.dt.int32
AF = mybir.ActivationFunctionType
ALU = mybir.AluOpType

S = 4096
SO = 2048
HL = 4
PAIRS = [[0, 1], [2, 3], [4, 5], [6, 7]]
D = 2048
DEPTH = 2
NH = 8
LN_EPS = 1e-5
RMS_EPS = 1e-6
ALPHA = (2 * DEPTH) ** 0.25
SCALE = 192 ** -0.5
NEG = -30000.0
POOL_W = (2, 4, 8, 16)
NG = 10

ENGS = ("pe", "act", "dve", "pool", "sp")


class Buf:
    __slots__ = ("name", "w", "r")

    def __init__(self, name=""):
        self.name = name
        self.w = None
        self.r = []


class Op:
    __slots__ = ("eng", "fn", "waits", "flag", "dma_key", "dma_val", "seq")

    def __init__(self, eng, fn):
        self.eng = eng
        self.fn = fn
        self.waits = []
        self.flag = False
        self.dma_key = None
        self.dma_val = 0
        self.seq = -1


class Prog:
    def __init__(self, nc):
        self.nc = nc
        self.ops = {e: [] for e in ENGS}
        self.seen = {e: {} for e in ENGS}
        self.seen_dma = {e: {} for e in ENGS}
        self.dma_counts = {}
        self.cc_keys = set()
        self.jb = [Buf(), Buf(), Buf()]

    def _add(self, eng, fn, reads, writes, dma_key=None):
        op = Op(eng, fn)
        op.seq = len(self.ops[eng])
        deps = []
        for b in reads:
            if b.w is not None:
                deps.append(b.w)
        for b in writes:
            if b.w is not None:
                deps.append(b.w)
            deps.extend(b.r)
        best = {}
        dma_deps = {}
        for d in deps:
            if d.dma_key is not None:
                if dma_deps.get(d.dma_key, 0) < d.dma_val:
                    dma_deps[d.dma_key] = d.dma_val
            else:
                if d.eng == eng and eng == "pe":
                    continue
                if best.get(d.eng, -1) < d.seq:
                    best[d.eng] = d.seq
        for f, s in best.items():
            if self.seen[eng].get(f, -1) >= s:
                continue
            self.seen[eng][f] = s
            dop = self.ops[f][s]
            dop.flag = True
            op.waits.append(("eng", f, dop))
        for k, v in dma_deps.items():
            if self.seen_dma[eng].get(k, 0) >= v:
                continue
            self.seen_dma[eng][k] = v
            op.waits.append(("dma", k, v))
        if dma_key is not None:
            op.dma_key = dma_key
            self.dma_counts[dma_key] = self.dma_counts.get(dma_key, 0) + 16
            op.dma_val = self.dma_counts[dma_key]
        for b in reads:
            b.r.append(op)
        for b in writes:
            b.w = op
            b.r = []
        self.ops[eng].append(op)
        return op

    def op(self, eng, fn, reads=(), writes=()):
        return self._add(eng, fn, reads, writes, None)

    def dma(self, eng, out, in_, reads=(), writes=(), key=None):
        def fn(e):
            return e.dma_start(out=out, in_=in_)
        return self._add(eng, fn, reads, writes, key)

    def collective(self, src, dst, reads, writes, key):
        def fn(e):
            return e.collective_compute("AllGather", ALU.bypass, replica_groups=PAIRS, ins=[src.opt()], outs=[dst.opt()])
        o = self._add("pool", fn, reads, writes, key)
        self.dma_counts[key] -= 15
        o.dma_val = self.dma_counts[key]
        self.cc_keys.add(key)
        return o

    def barrier(self, junk):
        marks = []
        jb = self.jb
        b = Buf()
        self.op("act", lambda e: e.activation(out=junk[:, 0:1], in_=junk[:, 4:5], func=AF.Copy), writes=[b, jb[0]])
        marks.append(b)
        b = Buf()
        self.op("dve", lambda e: e.memset(junk[:, 1:2], 0.0), writes=[b, jb[1]])
        marks.append(b)
        b = Buf()
        self.op("pool", lambda e: e.memset(junk[:, 2:3], 0.0), writes=[b, jb[2]])
        marks.append(b)
        fence = Buf()
        o = self.op("sp", lambda e: e.nop(), reads=marks, writes=[fence])
        for k, v in self.dma_counts.items():
            if self.seen_dma["sp"].get(k, 0) < v:
                self.seen_dma["sp"][k] = v
                o.waits.append(("dma", k, v))
        self.op("act", lambda e: e.activation(out=junk[:, 0:1], in_=junk[:, 4:5], func=AF.Copy), reads=[fence], writes=[jb[0]])
        self.op("dve", lambda e: e.memset(junk[:, 1:2], 0.0), reads=[fence], writes=[jb[1]])
        self.op("pool", lambda e: e.memset(junk[:, 2:3], 0.0), reads=[fence], writes=[jb[2]])
        self.op("pe", lambda e: e.nop(), reads=[fence])
        for e in ENGS:
            for k, v in self.dma_counts.items():
                if self.seen_dma[e].get(k, 0) < v:
                    self.seen_dma[e][k] = v

    def finish(self):
        nc = self.nc
        with contextlib.ExitStack() as st:
            esem = {e: st.enter_context(nc.semaphore("s_" + e)) for e in ENGS}
            dsem = {k: st.enter_context(nc.semaphore("d_%s" % (k,))) for k in self.dma_counts}
            block = st.enter_context(nc.Block())
            for e in ENGS:
                c = 0
                for o in self.ops[e]:
                    if o.flag:
                        c += 1
                        o.dma_val = c

            def emit(e, eng):
                for o in self.ops[e]:
                    for kind, k, v in o.waits:
                        if kind == "eng":
                            eng.wait_ge(esem[k], v.dma_val)
                        else:
                            eng.wait_ge(dsem[k], v)
                    inst = o.fn(eng)
                    if o.dma_key is not None and o.dma_key in self.cc_keys:
                        inst.then_inc(dsem[o.dma_key])
                    elif o.dma_key is not None:
                        inst.then_inc(dsem[o.dma_key], 16)
                    elif o.flag:
                        inst.then_inc(esem[e], 1)
                if e == "sp":
                    for k, v in self.dma_counts.items():
                        eng.wait_ge(dsem[k], v)

            @block.tensor
            def _(eng):
                emit("pe", eng)

            @block.scalar
            def _(eng):
                emit("act", eng)

            @block.vector
            def _(eng):
                emit("dve", eng)

            @block.gpsimd
            def _(eng):
                emit("pool", eng)

            @block.sync
            def _(eng):
                emit("sp", eng)


def build(debug=False, n_layers=DEPTH, stop_phase=None):
    nc = bass.Bass("TRN2", target_bir_lowering=False)
    P = Prog(nc)

    def din(name, shape, dt):
        return nc.dram_tensor(name, shape, dt, kind="ExternalInput").ap()

    x_d = din("x", [SO, D], F32)
    pos_d = din("pos", [1, S], I32)
    poso_d = din("pos_own", [1, SO], I32)
    coef_d = din("coef", [128, 2], F32)
    rc_d = din("ropec", [64, 2], F32)
    lnp_d = din("lnp", [2 + 3 * DEPTH, D], F32)
    win_d = din("w_in_g", [DEPTH * NG * 128, 16 * 512], F32)
    wq_d = din("wq", [DEPTH * 128, 4 * 1024], F32)
    wk_d = din("wk", [DEPTH * 128, 2 * 512], F32)
    wv_d = din("wv", [DEPTH * 128, 2 * 512], F32)
    wo_d = din("wo", [DEPTH * 128, 16 * 2048], F32)
    wp_d = din("wp", [DEPTH * 128, 4 * 128], F32)
    sm_d = din("small", [DEPTH * 128, 32], F32)
    idv_d = din("invdiv", [128, 64], F32)
    ident_d = din("ident", [128, 128], BF16)
    mask_d = din("masks", [128, 4 * 512], BF16)
    out_d = nc.dram_tensor("out", [SO, D], F32, kind="ExternalOutput").ap()

    skind = "ExternalOutput" if debug else "Internal"
    resid_d = nc.dram_tensor("resid", [SO, D], F32, kind=skind).ap()
    hT_d = nc.dram_tensor("hT", [D, SO], BF16, kind=skind).ap()
    mix_d = nc.dram_tensor("mixT", [D, SO], BF16, kind=skind).ap()
    cosA_d = nc.dram_tensor("cosA", [64, S], BF16).ap()
    sinA_d = nc.dram_tensor("sinA", [64, S], BF16).ap()
    cosO_d = nc.dram_tensor("cosO", [64, SO], BF16).ap()
    sinO_d = nc.dram_tensor("sinO", [64, SO], BF16).ap()
    e1a_src = nc.dram_tensor("e1a_src", [512, SO], BF16).ap()
    e1a_g = nc.dram_tensor("e1a_g", [1024, SO], BF16).ap()
    e1b_src = nc.dram_tensor("e1b_src", [320, SO], BF16).ap()
    e1b_g = nc.dram_tensor("e1b_g", [640, SO], BF16).ap()
    e2_src = [nc.dram_tensor("e2_src%d" % i, [128, S], BF16).ap() for i in range(4)]
    e2_g = [nc.dram_tensor("e2_g%d" % i, [256, S], BF16).ap() for i in range(4)]
    tl_src = nc.dram_tensor("tl_src", [D, 16], BF16).ap()
    tl_g = nc.dram_tensor("tl_g", [2 * D, 16], BF16).ap()
    if debug:
        dbg_e1a = nc.dram_tensor("dbg_e1a", [512, SO], BF16, kind="ExternalOutput").ap()
        dbg_e1b = nc.dram_tensor("dbg_e1b", [320, SO], BF16, kind="ExternalOutput").ap()
        dbg_e2 = [nc.dram_tensor("dbg_e2_%d" % i, [128, S], BF16, kind="ExternalOutput").ap() for i in range(4)]

    b_resid = [Buf("resid%d" % i) for i in range(16)]
    b_hT = [Buf("hT%d" % i) for i in range(4)]
    b_qn = [Buf() for i in range(4)]
    b_kvn = [Buf() for i in range(4)]
    b_kr = [Buf() for i in range(4)]
    b_mix = [[Buf() for t in range(4)] for r in range(16)]
    b_out = [Buf() for i in range(16)]
    b_e1g, b_tlsrc, b_tlg = Buf(), Buf(), Buf()
    b_e2g = [Buf() for i in range(4)]
    b_e2s = [[Buf() for t in range(8)] for r in range(4)]
    b_tabd = Buf()

    hT_v = hT_d.rearrange("(kc p) t -> p kc t", p=128)
    mix_v = mix_d.rearrange("(kc p) t -> p kc t", p=128)
    qn_v = e1a_src.rearrange("(kc p) t -> p kc t", p=128)
    kvn_v = e1b_src[0:256, :].rearrange("(kc p) t -> p kc t", p=128)
    kr_d = e1b_src[256:320, :]
    tls_v = tl_src.rearrange("(kc p) t -> p kc t", p=128)
    tlg_v = tl_g.rearrange("(kc p) t -> p kc t", p=128)

    with contextlib.ExitStack() as gst:
        ARENA_WORDS = 51200
        arena = gst.enter_context(nc.sbuf_tensor("arena", [128, ARENA_WORDS], F32))
        a_top = [0]
        a_mark = [0]

        def sb(name, shape, dt, st=None):
            n = 1
            for d_ in shape[1:]:
                n *= d_
            esz = 4 if dt in (F32, I32) else 2
            words = (n * esz + 3) // 4
            words = (words + 7) // 8 * 8
            off = a_top[0]
            assert off + words <= ARENA_WORDS, ("SBUF arena overflow", name, off, words)
            a_top[0] = off + words
            v = arena[0:shape[0], off:off + words]
            if dt != F32:
                v = v.bitcast(dt)
            v = v[:, 0:n]
            if len(shape) == 3:
                v = v.rearrange("p (a b) -> p a b", a=shape[1])
            return v

        def areset():
            a_top[0] = a_mark[0]

        ps_all = gst.enter_context(nc.psum_tensor("ps", [128, 8 * 512], F32))
        ps_bufs = [Buf("ps%d" % i) for i in range(8)]
        ps_ctr = [0]

        def psum(banks=None, ctr=None):
            if banks is None:
                i = ps_ctr[0] % 8
                ps_ctr[0] += 1
            else:
                i = banks[ctr[0] % len(banks)]
                ctr[0] += 1
            return ps_all[:, i * 512:(i + 1) * 512], ps_bufs[i]

        junk = sb("junk", [128, 8], F32)
        ident = sb("ident", [128, 128], BF16)
        ones = sb("ones", [128, 128], BF16)
        masks = sb("masks", [128, 4, 512], BF16)
        rc = sb("rc", [64, 2], F32)
        coef = sb("coef", [128, 2], F32)
        invdiv = sb("invdiv", [128, 4, 16], F32)
        b_const = Buf("const")
        b_tab = Buf("tab")

        P.op("dve", lambda e: e.memset(junk[:], 0.0), writes=[b_const] + P.jb)
        P.op("dve", lambda e: e.memset(ones[:], 1.0), writes=[b_const])
        P.dma("sp", ident[:], ident_d, writes=[b_const], key="c_ident")
        P.dma("sp", masks[:].rearrange("p a b -> p (a b)"), mask_d, writes=[b_const], key="c_mask")
        P.dma("sp", rc[:], rc_d, writes=[b_const], key="c_rc")
        P.dma("sp", coef[:], coef_d, writes=[b_const], key="c_coef")
        P.dma("sp", invdiv[:].rearrange("p a b -> p (a b)"), idv_d, writes=[b_const], key="c_idv")

        LN_EPS_AP = sb("lneps", [128, 4], F32)
        a_mark[0] = a_top[0]

        def rope_tables(pos_ap, N, cos_dst, sin_dst, tag):
            areset()
            posi = sb("posi", [64, N], I32)
            ang = sb("ang", [64, N], F32)
            ta = sb("ta", [64, N], F32)
            tb = sb("tb", [64, N], F32)
            ob = sb("ob", [64, N], BF16)
            b_posi, b_ang, b_ta, b_tb, b_ob = Buf(), Buf(), Buf(), Buf(), Buf()
            P.dma("sp", posi[:], pos_ap.partition_broadcast(64), writes=[b_posi], key="t_pos")
            P.op("dve", lambda e: e.tensor_copy(out=ang[:], in_=posi[:]), reads=[b_posi], writes=[b_ang])
            P.op("dve", lambda e: e.tensor_scalar(out=ang[:], in0=ang[:], scalar1=rc[:, 0:1], scalar2=None, op0=ALU.mult),
                 reads=[b_ang, b_const], writes=[b_ang])
            TWO_PI = 2.0 * math.pi
            for which, phase, dst in (("sin", 0.0, sin_dst), ("cos", math.pi / 2, cos_dst)):
                P.op("dve", lambda e, phase=phase: e.tensor_scalar(out=ta[:], in0=ang[:], scalar1=phase, scalar2=1.0 / TWO_PI,
                                                                    op0=ALU.add, op1=ALU.mult), reads=[b_ang], writes=[b_ta])
                P.op("dve", lambda e: e.tensor_copy(out=posi[:], in_=ta[:]), reads=[b_ta], writes=[b_posi])
                P.op("dve", lambda e: e.tensor_copy(out=ta[:], in_=posi[:]), reads=[b_posi], writes=[b_ta])
                P.op("dve", lambda e: e.scalar_tensor_tensor(out=tb[:], in0=ta[:], scalar=-TWO_PI, in1=ang[:], op0=ALU.mult, op1=ALU.add),
                     reads=[b_ta, b_ang], writes=[b_tb])
                P.op("dve", lambda e, phase=phase: e.tensor_scalar(out=tb[:], in0=tb[:], scalar1=phase, scalar2=None, op0=ALU.add),
                     reads=[b_tb], writes=[b_tb])
                P.op("dve", lambda e: e.tensor_scalar(out=ta[:], in0=tb[:], scalar1=math.pi, scalar2=TWO_PI, op0=ALU.is_gt, op1=ALU.mult),
                     reads=[b_tb], writes=[b_ta])
                P.op("dve", lambda e: e.tensor_tensor(out=tb[:], in0=tb[:], in1=ta[:], op=ALU.subtract), reads=[b_tb, b_ta], writes=[b_tb])
                P.op("dve", lambda e: e.tensor_scalar(out=tb[:], in0=tb[:], scalar1=math.pi, scalar2=-math.pi, op0=ALU.min, op1=ALU.max),
                     reads=[b_tb], writes=[b_tb])
                P.op("act", lambda e: e.activation(out=ta[:], in_=tb[:], func=AF.Sin), reads=[b_tb], writes=[b_ta])
                if which == "sin":
                    P.op("dve", lambda e: e.tensor_scalar(out=ob[:], in0=ta[:], scalar1=rc[:, 1:2], scalar2=None, op0=ALU.mult),
                         reads=[b_ta, b_const], writes=[b_ob])
                else:
                    P.op("dve", lambda e: e.tensor_copy(out=ob[:], in_=ta[:]), reads=[b_ta], writes=[b_ob])
                P.dma("sp", dst, ob[:], reads=[b_ob], writes=[b_tabd], key="t_ob")
            P.barrier(junk)

        rope_tables(pos_d, S, cosA_d, sinA_d, "a")
        rope_tables(poso_d, SO, cosO_d, sinO_d, "o")

        def ln_phase(st, layer_idx, src_kind, write_hT, final):
            areset()
            NB = 4 if src_kind == "x" else 3
            gb = sb("ln_g", [128, D], F32, st)
            bb = sb("ln_b", [128, D], F32, st)
            b_p = Buf()
            if src_kind == "x":
                grow, brow = 0, 1
            else:
                grow, brow = 2 + 3 * layer_idx, 3 + 3 * layer_idx
            P.dma("sp", gb[:], lnp_d[grow:grow + 1, :].partition_broadcast(128), writes=[b_p], key="ln_g")
            P.dma("sp", bb[:], lnp_d[brow:brow + 1, :].partition_broadcast(128), writes=[b_p], key="ln_b")
            ys = [sb("ln_y%d" % i, [128, D], F32, st) for i in range(NB)]
            b_ys = [Buf() for i in range(NB)]
            hbs = [sb("ln_hb%d" % i, [128, D], BF16, st) for i in range(2)]
            b_hbs = [Buf() for i in range(2)]
            stats = [sb("ln_st%d" % i, [128, 4, 6], F32, st) for i in range(NB)]
            mvs = [sb("ln_mv%d" % i, [128, 4], F32, st) for i in range(NB)]
            b_sts = [Buf() for i in range(NB)]
            stg = [sb("ln_stg%d" % i, [128, 16, 512], BF16, st) for i in range(1)] if write_hT else []
            b_stg = [Buf() for i in range(1)]
            if src_kind == "proj":
                bo = sb("ln_bo", [1, D], BF16, st)
                P.dma("pool", bo[:], lnp_d[4 + 3 * layer_idx:5 + 3 * layer_idx, :], writes=[b_p], key="ln_bo")
                wo = sb("wo", [128, 16, 2048], BF16, st)
                b_wop = [Buf() for i in range(4)]
                for i4 in range(4):
                    P.dma("pool", wo[:, i4 * 4:(i4 + 1) * 4, :].rearrange("p a b -> p (a b)"),
                          wo_d[layer_idx * 128:(layer_idx + 1) * 128, i4 * 8192:(i4 + 1) * 8192],
                          reads=([b_wop[i4 - 1]] if i4 else [b_p]), writes=[b_wop[i4]], key="wo%d" % i4)
                mts = [sb("mt%d" % i, [128, 16, 256], BF16, st) for i in range(2)]
                b_mts = [Buf() for i in range(2)]
                NR = 2
                rts = [sb("rt%d" % i, [128, D], F32, st) for i in range(NR)]
                b_rts = [Buf() for i in range(NR)]
                ea = [sb("ea%d" % i, [128, 8, 256], BF16, st) for i in range(2)]
                eb = [sb("eb%d" % i, [128, 8, 256], BF16, st) for i in range(2)]
                b_ea = [Buf() for i in range(2)]
                b_eb = [Buf() for i in range(2)]
                et = sb("et", [128, 8, 256], BF16, st)
                b_et = Buf()

            def s_pair(tk):
                tt = tk // 4
                t2 = tk // 2
                i2 = t2 % 2
                mt, b_mt = mts[i2], b_mts[i2]
                P.dma("sp", mt[:], mix_v[:, :, t2 * 256:(t2 + 1) * 256], reads=[b_mix[r][tt] for r in range(16)],
                      writes=[b_mt], key="mt%d" % i2)
                for rho in range(2):
                    for hl in range(4):
                        kcg = rho * 4 + hl
                        P.dma("sp", ea[i2][:, kcg, :], e2_g[hl][rho * 128:(rho + 1) * 128, t2 * 256:(t2 + 1) * 256],
                              reads=[b_e2g[hl]], writes=[b_ea[i2]], key="ea%d_%d" % (i2, kcg))
                        P.dma("sp", eb[i2][:, kcg, :], e2_g[hl][rho * 128:(rho + 1) * 128, SO + t2 * 256:SO + (t2 + 1) * 256],
                              reads=[b_e2g[hl]], writes=[b_eb[i2]], key="eb%d_%d" % (i2, kcg))

            def s_blend(tk):
                i2 = (tk // 2) % 2
                mt, b_mt = mts[i2], b_mts[i2]
                P.op("dve", lambda e: e.tensor_scalar(out=et[:], in0=ea[i2][:], scalar1=coef[:, 0:1], scalar2=None, op0=ALU.mult),
                     reads=[b_ea[i2], b_const], writes=[b_et])
                P.op("dve", lambda e: e.scalar_tensor_tensor(out=et[:], in0=eb[i2][:], scalar=coef[:, 1:2], in1=et[:], op0=ALU.mult, op1=ALU.add),
                     reads=[b_eb[i2], b_et, b_const], writes=[b_et])
                P.op("pool", lambda e: e.tensor_tensor(out=mt[:, 0:8, :], in0=mt[:, 0:8, :], in1=et[:], op=ALU.mult),
                     reads=[b_et, b_mt], writes=[b_mt])

            def s_load(tk):
                y, b_y = ys[tk % NB], b_ys[tk % NB]
                tsl = slice(tk * 128, (tk + 1) * 128)
                if src_kind == "x":
                    P.dma("sp", y[:], x_d[tsl, :], writes=[b_y], key="ln_y%d" % (tk % NB))
                else:
                    rt, b_rt = rts[tk % NR], b_rts[tk % NR]
                    P.dma("sp", rt[:], resid_d[tsl, :], reads=[b_resid[tk]], writes=[b_rt], key="rt%d" % (tk % NR))

            def s1(tk):
                y, b_y = ys[tk % NB], b_ys[tk % NB]
                stt, mv, b_st = stats[tk % NB], mvs[tk % NB], b_sts[tk % NB]
                if src_kind != "x":
                    t2 = tk // 2
                    mt, b_mt = mts[t2 % 2], b_mts[t2 % 2]
                    rt, b_rt = rts[tk % NR], b_rts[tk % NR]
                    pts = [psum() for cg in range(4)]
                    for cg in range(4):
                        pt, b_pt = pts[cg]
                        P.op("pe", lambda e, pt=pt, cg=cg: e.matmul(pt, ones[0:1, :], bo[0:1, cg * 512:(cg + 1) * 512], start=True, stop=False),
                             reads=[b_p, b_const], writes=[b_pt])
                    for kc in range(16):
                        for cg in range(4):
                            pt, b_pt = pts[cg]
                            P.op("pe", lambda e, pt=pt, kc=kc, cg=cg: e.matmul(
                                pt, mt[:, kc, (tk % 2) * 128:(tk % 2 + 1) * 128], wo[:, kc, cg * 512:(cg + 1) * 512],
                                start=False, stop=(kc == 15)), reads=[b_mt, b_wop[kc // 4]], writes=[b_pt])
                    for cg in range(4):
                        pt, b_pt = pts[cg]
                        P.op("dve", lambda e, pt=pt, cg=cg: e.scalar_tensor_tensor(
                            out=y[:, cg * 512:(cg + 1) * 512], in0=rt[:, cg * 512:(cg + 1) * 512], scalar=ALPHA, in1=pt,
                            op0=ALU.mult, op1=ALU.add), reads=[b_pt, b_rt], writes=[b_y])
                for c in range(4):
                    P.op("dve", lambda e, c=c: e.bn_stats(out=stt[:, c, :], in_=y[:, c * 512:(c + 1) * 512]),
                         reads=[b_y], writes=[b_st])
                P.op("dve", lambda e: e.bn_aggr(out=mv[:, 0:2], in_=stt[:].rearrange("p a b -> p (a b)")),
                     reads=[b_st], writes=[b_st])
                P.op("act", lambda e: e.activation(out=mv[:, 2:3], in_=mv[:, 1:2], func=AF.Sqrt, bias=LN_EPS_AP[:, 0:1], scale=1.0),
                     reads=[b_st, b_const], writes=[b_st])
                P.op("dve", lambda e: e.reciprocal(out=mv[:, 2:3], in_=mv[:, 2:3]), reads=[b_st], writes=[b_st])
                P.op("dve", lambda e: e.scalar_tensor_tensor(out=mv[:, 3:4], in0=mv[:, 0:1], scalar=-1.0, in1=mv[:, 2:3],
                                                              op0=ALU.mult, op1=ALU.mult), reads=[b_st], writes=[b_st])

            def s2a(tk):
                y, b_y = ys[tk % NB], b_ys[tk % NB]
                mv, b_st = mvs[tk % NB], b_sts[tk % NB]
                P.op("act", lambda e: e.activation(out=y[:], in_=y[:], func=AF.Identity, bias=mv[:, 3:4], scale=mv[:, 2:3]),
                     reads=[b_y, b_st], writes=[b_y])

            def s2b(tk):
                y, b_y = ys[tk % NB], b_ys[tk % NB]
                P.op("dve", lambda e: e.tensor_tensor(out=y[:], in0=y[:], in1=gb[:], op=ALU.mult), reads=[b_y, b_p], writes=[b_y])
                P.op("pool", lambda e: e.tensor_tensor(out=y[:], in0=y[:], in1=bb[:], op=ALU.add), reads=[b_y, b_p], writes=[b_y])

            def s3(tk):
                y, b_y = ys[tk % NB], b_ys[tk % NB]
                hb, b_hb = hbs[tk % 2], b_hbs[tk % 2]
                tsl = slice(tk * 128, (tk + 1) * 128)
                if final:
                    P.dma("sp", out_d[tsl, :], y[:], reads=[b_y], writes=[b_out[tk]], key="ln_o%d" % (tk % NB))
                else:
                    P.dma("sp", resid_d[tsl, :], y[:], reads=[b_y], writes=[b_resid[tk]], key="ln_o%d" % (tk % NB))
                if write_hT:
                    P.op("act", lambda e: e.copy(out=hb[:], in_=y[:]), reads=[b_y], writes=[b_hb])
                    sg_, b_sg_ = stg[0], b_stg[0]
                    for q4 in range(4):
                        pt, b_pt = psum()
                        ptb = pt.bitcast(BF16)
                        for j in range(4):
                            kc = q4 * 4 + j
                            P.op("pe", lambda e, ptb=ptb, kc=kc, j=j: e.transpose(
                                ptb[:, j * 128:(j + 1) * 128], hb[:, kc * 128:(kc + 1) * 128], ident[:]),
                                reads=[b_hb, b_const], writes=[b_pt])
                        if q4 % 2 == 0:
                            P.op("dve", lambda e, ptb=ptb, q4=q4: e.tensor_copy(
                                out=sg_[:, q4 * 4:(q4 + 1) * 4, (tk % 4) * 128:(tk % 4 + 1) * 128],
                                in_=ptb[:, 0:512].rearrange("p (a b) -> p a b", a=4)), reads=[b_pt], writes=[b_sg_])
                        else:
                            P.op("act", lambda e, ptb=ptb, q4=q4: e.copy(
                                out=sg_[:, q4 * 4:(q4 + 1) * 4, (tk % 4) * 128:(tk % 4 + 1) * 128],
                                in_=ptb[:, 0:512].rearrange("p (a b) -> p a b", a=4)), reads=[b_pt], writes=[b_sg_])
                    if tk % 4 == 3:
                        tt = tk // 4
                        P.dma("sp", hT_v[:, :, tt * 512:(tt + 1) * 512], sg_[:], reads=[b_sg_], writes=[b_hT[tt]], key="ln_stg")
                        if tk == NTK - 1:
                            P.dma("sp", tls_v, sg_[:, :, 496:512], reads=[b_sg_], writes=[b_tlsrc], key="ln_tl")
                            P.collective(tl_src, tl_g, reads=[b_tlsrc], writes=[b_tlg], key="cc_e3")

            NTK = SO // 128
            is_proj = (src_kind != "x")
            if is_proj:
                s_pair(0)
                s_blend(0)
            s_load(0)
            for i in range(NTK + 2):
                if is_proj and i % 2 == 0 and i + 2 < NTK:
                    s_pair(i + 2)
                if is_proj and i % 2 == 1 and i + 1 < NTK:
                    s_blend(i + 1)
                if i + 1 < NTK:
                    s_load(i + 1)
                if 0 <= i - 1 < NTK:
                    s2a(i - 1)
                if i < NTK:
                    s1(i)
                if 0 <= i - 1 < NTK:
                    s2b(i - 1)
                if 0 <= i - 2 < NTK:
                    s3(i - 2)

        P.op("dve", lambda e: e.memset(LN_EPS_AP[:, 0:1], LN_EPS), writes=[b_const])
        P.op("dve", lambda e: e.memset(LN_EPS_AP[:, 1:2], RMS_EPS), writes=[b_const])

        with contextlib.ExitStack() as st:
            ln_phase(st, 0, "x", True, False)
            P.barrier(junk)

        def phase_A(L):
            with contextlib.ExitStack() as st:
                areset()
                hTs = sb("hTs", [128, 16, 2048], BF16, st)
                b_hTs = Buf()
                wr = [sb("wr%d" % i, [128, 16, 512], BF16, st) for i in range(2)]
                b_wr = [Buf() for i in range(2)]
                sm = sb("sm", [128, 32], F32, st)
                wp = sb("wp", [128, 4, 128], BF16, st)
                b_sm = Buf()
                P.dma("sp", sm[:], sm_d[L * 128:(L + 1) * 128, :], writes=[b_sm], key="sm")
                P.dma("pool", wp[:].rearrange("p a b -> p (a b)"), wp_d[L * 128:(L + 1) * 128, :], writes=[b_sm], key="wp")
                hp = sb("hp", [128, 4, 16], F32, st)
                hc = sb("hc", [128, 4, 2], F32, st)
                b_hp = [Buf() for i in range(4)]
                b_hc = [Buf() for i in range(4)]
                P.op("pool", lambda e: e.memset(hp[:], 0.0), writes=b_hp)
                P.op("pool", lambda e: e.memset(hc[:], 0.0), writes=b_hc)
                stgs = [sb("stg%d" % i, [128, 4, 512], BF16, st) for i in range(2)]
                b_stgs = [Buf() for i in range(2)]
                stg_ctr = [0]
                NTMP = 6
                tmps = [sb("tmp%d" % i, [128, 528], F32, st) for i in range(NTMP)]
                b_tmps = [Buf() for i in range(NTMP)]
                tmp_ctr = [0]
                sqs = [sb("sq%d" % i, [128, 512], BF16, st) for i in range(2)]
                b_sqs = [Buf() for i in range(2)]
                sq_ctr = [0]
                pls = [sb("pl%d" % i, [128, 512], BF16, st) for i in range(4)]
                b_pls = [Buf() for i in range(4)]
                sgps = [sb("sgp%d" % i, [128, 512], F32, st) for i in range(4)]
                b_sgps = [Buf() for i in range(4)]
                pl_ctr = [0]
                pending = []
                cosT = sb("cosO", [64, SO], BF16, st)
                sinT = sb("sinO", [64, SO], BF16, st)
                b_tab = Buf()
                P.dma("sp", cosT[:], cosO_d, reads=[b_tabd], writes=[b_tab], key="cosO")
                P.dma("sp", sinT[:], sinO_d, reads=[b_tabd], writes=[b_tab], key="sinO")
                hTh = sb("hTh", [128, 16, 16], BF16, st)
                b_hTh = Buf()
                P.dma("sp", hTh[:], tlg_v[:, 0:16, :], reads=[b_tlg], writes=[b_hTh], key="hTh")

                def proj_halo(w, c0):
                    pt, b_pt = psum()
                    for kc in range(16):
                        P.op("pe", lambda e, pt=pt, w=w, kc=kc: e.matmul(
                            pt[:, 0:16], w[0][:, kc, c0:c0 + 128], hTh[:, kc, :],
                            start=(kc == 0), stop=(kc == 15)), reads=[w[1], b_hTh], writes=[b_pt])
                    return pt, b_pt

                def tmp():
                    i = tmp_ctr[0] % NTMP
                    tmp_ctr[0] += 1
                    return tmps[i], b_tmps[i]

                def proj(w, c0, ncols, tt):
                    pt, b_pt = psum()
                    for kc in range(16):
                        P.op("pe", lambda e, pt=pt, w=w, kc=kc: e.matmul(
                            pt[0:ncols, :], w[0][:, kc, c0:c0 + ncols], hTs[:, kc, tt * 512:(tt + 1) * 512],
                            start=(kc == 0), stop=(kc == 15)), reads=[w[1], b_hTs], writes=[b_pt])
                    return pt, b_pt

                def rmsnorm_group(w, c0, nch, tt, dim, dst_v, dst_bufs, gtt, key):
                    pts = [proj(w, c0 + c * 128, 128, tt) for c in range(nch)]
                    ss, b_ss = psum()
                    for c, (pt, b_pt) in enumerate(pts):
                        i = sq_ctr[0] % 2
                        sq_ctr[0] += 1
                        sq, b_sq = sqs[i], b_sqs[i]
                        P.op("act", lambda e, pt=pt, sq=sq: e.activation(out=sq[:], in_=pt, func=AF.Square), reads=[b_pt], writes=[b_sq])
                        P.op("pe", lambda e, ss=ss, sq=sq, c=c: e.matmul(ss, ones[:], sq[:], start=(c == 0), stop=(c == nch - 1)),
                             reads=[b_sq, b_const], writes=[b_ss])
                    rs, b_rs = tmp()
                    P.op("act", lambda e, rs=rs, ss=ss: e.activation(out=rs[:, 0:512], in_=ss, func=AF.Sqrt, bias=LN_EPS_AP[:, 1:2],
                                                                      scale=1.0 / dim), reads=[b_ss, b_const], writes=[b_rs])
                    P.op("dve", lambda e, rs=rs: e.reciprocal(out=rs[:, 0:512], in_=rs[:, 0:512]), reads=[b_rs], writes=[b_rs])
                    i = stg_ctr[0] % 2
                    stg_ctr[0] += 1
                    sg_, b_sg_ = stgs[i], b_stgs[i]
                    for c, (pt, b_pt) in enumerate(pts):
                        P.op("dve", lambda e, pt=pt, rs=rs, sg_=sg_, c=c: e.tensor_tensor(out=sg_[:, c, :], in0=pt, in1=rs[:, 0:512], op=ALU.mult),
                             reads=[b_pt, b_rs], writes=[b_sg_])
                    P.dma("sp", dst_v[:, :, gtt * 512:(gtt + 1) * 512], sg_[:, 0:nch, :], reads=[b_sg_], writes=[dst_bufs[gtt]], key="stg%d" % i)

                for hf in range(1):
                    P.dma("sp", hTs[:], hT_v[:, :, hf * 2048:(hf + 1) * 2048], reads=b_hT[hf * 4:(hf + 1) * 4], writes=[b_hTs], key="hTs")
                    for g in range(NG):
                        wi = (hf * NG + g) % 2
                        w = (wr[wi], b_wr[wi])
                        row0 = (L * NG + g) * 128
                        P.dma("pool", wr[wi][:].rearrange("p a b -> p (a b)"), win_d[row0:row0 + 128, :], writes=[b_wr[wi]], key="wr%d" % wi)
                        if g == 2:
                            if debug:
                                P.dma("sp", dbg_e1a, e1a_src, reads=b_qn, writes=[Buf()], key="dbg_e1a")
                                P.dma("sp", dbg_e1b, e1b_src, reads=b_kvn + b_kr, writes=[Buf()], key="dbg_e1b")
                            P.collective(e1b_src, e1b_g, reads=b_kvn + b_kr, writes=[b_e1g], key="cc_e1b")
                            P.collective(e1a_src, e1a_g, reads=b_qn + [b_e1g], writes=[b_e1g], key="cc_e1a")
                        for tt in range(4):
                            gtt = hf * 4 + tt
                            tok = slice(gtt * 512, (gtt + 1) * 512)
                            if g == 0:
                                rmsnorm_group(w, 0, 2, tt, 256.0, kvn_v, b_kvn, gtt, "kvn")
                                pa, b_pa = proj(w, 256, 64, tt)
                                pb, b_pb = proj(w, 320, 64, tt)
                                t1, b_t1 = tmp()
                                t2, b_t2 = tmp()
                                P.op("dve", lambda e, pa=pa, t1=t1, tok=tok: e.tensor_tensor(out=t1[0:64, 0:512], in0=pa[0:64, :], in1=cosT[:, tok], op=ALU.mult),
                                     reads=[b_pa, b_tab], writes=[b_t1])
                                P.op("dve", lambda e, pb=pb, t2=t2, tok=tok: e.tensor_tensor(out=t2[0:64, 0:512], in0=pb[0:64, :], in1=sinT[:, tok], op=ALU.mult),
                                     reads=[b_pb, b_tab], writes=[b_t2])
                                i = stg_ctr[0] % 2
                                stg_ctr[0] += 1
                                sg_, b_sg_ = stgs[i], b_stgs[i]
                                P.op("pool", lambda e, t1=t1, t2=t2, sg_=sg_: e.tensor_tensor(out=sg_[0:64, 0, :], in0=t1[0:64, 0:512], in1=t2[0:64, 0:512], op=ALU.add),
                                     reads=[b_t1, b_t2], writes=[b_sg_])
                                P.dma("sp", kr_d[:, tok], sg_[0:64, 0, :], reads=[b_sg_], writes=[b_kr[gtt]], key="stg%d" % i)
                            elif g == 1:
                                rmsnorm_group(w, 0, 4, tt, 512.0, qn_v, b_qn, gtt, "qn")
                            elif g in (2, 3):
                                i = stg_ctr[0] % 2
                                stg_ctr[0] += 1
                                sg_, b_sg_ = stgs[i], b_stgs[i]
                                for c in range(4):
                                    pt, b_pt = proj(w, c * 128, 128, tt)
                                    P.op("act", lambda e, pt=pt, sg_=sg_, c=c: e.activation(out=sg_[:, c, :], in_=pt, func=AF.Silu),
                                         reads=[b_pt], writes=[b_sg_])
                                r0 = (g - 2) * 4
                                P.dma("sp", mix_v[:, r0:r0 + 4, tok], sg_[:], reads=[b_sg_], writes=[b_mix[r0 + c][gtt] for c in range(4)], key="stg%d" % i)
                            elif g in (4, 5):
                                tails = []
                                for s_ in range(2):
                                    pg = (g - 4) * 2 + s_
                                    wlen = POOL_W[pg]
                                    px, b_px = proj(w, s_ * 256, 128, tt)
                                    pgt, b_pgt = proj(w, s_ * 256 + 128, 128, tt)
                                    xb, b_xb = tmp()
                                    sa, b_sa = tmp()
                                    sb_, b_sb = tmp()
                                    P.op("act", lambda e, px=px, xb=xb: e.copy(out=xb[:, 16:528], in_=px), reads=[b_px], writes=[b_xb])
                                    if tt == 0:
                                        ph, b_ph = proj_halo(w, s_ * 256)
                                        P.op("dve", lambda e, xb=xb, ph=ph: e.tensor_scalar(out=xb[:, 0:16], in0=ph[:, 0:16], scalar1=coef[:, 1:2], scalar2=None, op0=ALU.mult),
                                             reads=[b_ph, b_xb, b_const], writes=[b_xb])
                                    else:
                                        P.op("pool", lambda e, xb=xb, pg=pg: e.tensor_copy(out=xb[:, 0:16], in_=hp[:, pg, :]), reads=[b_hp[pg], b_xb], writes=[b_xb])
                                    P.op("pool", lambda e, xb=xb, pg=pg: e.tensor_copy(out=hp[:, pg, :], in_=xb[:, 512:528]), reads=[b_xb], writes=[b_hp[pg]])
                                    P.op("pool", lambda e, xb=xb, sa=sa: e.tensor_tensor(out=sa[:, 1:528], in0=xb[:, 1:528], in1=xb[:, 0:527], op=ALU.add),
                                         reads=[b_xb], writes=[b_sa])
                                    fin, b_fin = sa, b_sa
                                    if wlen >= 4:
                                        P.op("pool", lambda e, sa=sa, sb_=sb_: e.tensor_tensor(out=sb_[:, 3:528], in0=sa[:, 3:528], in1=sa[:, 1:526], op=ALU.add),
                                             reads=[b_sa], writes=[b_sb])
                                        fin, b_fin = sb_, b_sb
                                    if wlen >= 8:
                                        P.op("pool", lambda e, sa=sa, sb_=sb_: e.tensor_tensor(out=sa[:, 7:528], in0=sb_[:, 7:528], in1=sb_[:, 3:524], op=ALU.add),
                                             reads=[b_sb, b_sa], writes=[b_sa])
                                        fin, b_fin = sa, b_sa
                                    if wlen >= 16:
                                        P.op("pool", lambda e, sa=sa, sb_=sb_: e.tensor_tensor(out=sb_[:, 15:528], in0=sa[:, 15:528], in1=sa[:, 7:520], op=ALU.add),
                                             reads=[b_sa, b_sb], writes=[b_sb])
                                        fin, b_fin = sb_, b_sb
                                    ip = pl_ctr[0] % 4
                                    pl_ctr[0] += 1
                                    pl, b_pl = pls[ip], b_pls[ip]
                                    sgp, b_sgp = sgps[ip], b_sgps[ip]
                                    P.op("dve", lambda e, fin=fin, xb=xb, pl=pl, wlen=wlen: e.scalar_tensor_tensor(
                                        out=pl[:], in0=fin[:, 16:528], scalar=1.0 / wlen, in1=xb[:, 16:528], op0=ALU.mult, op1=ALU.subtract),
                                        reads=[b_fin, b_xb], writes=[b_pl])
                                    if gtt == 0:
                                        t16, b_t16 = tmp()
                                        P.op("dve", lambda e, fin=fin, t16=t16, pg=pg: e.tensor_tensor(out=t16[:, 0:16], in0=fin[:, 16:32], in1=invdiv[:, pg, :], op=ALU.mult),
                                             reads=[b_fin, b_const], writes=[b_t16])
                                        P.op("dve", lambda e, t16=t16, xb=xb, pl=pl: e.tensor_tensor(out=pl[:, 0:16], in0=t16[:, 0:16], in1=xb[:, 16:32], op=ALU.subtract),
                                             reads=[b_t16, b_xb, b_pl], writes=[b_pl])
                                    P.op("act", lambda e, pgt=pgt, sgp=sgp: e.activation(out=sgp[:], in_=pgt, func=AF.Silu), reads=[b_pgt], writes=[b_sgp])
                                    tails.append((s_, pg, pl, b_pl, sgp, b_sgp))

                                def pool_tail(tails=tails, g=g, gtt=gtt, tok=tok):
                                    i = stg_ctr[0] % 2
                                    stg_ctr[0] += 1
                                    sg_, b_sg_ = stgs[i], b_stgs[i]
                                    for (s_, pg, pl, b_pl, sgp, b_sgp) in tails:
                                        py, b_py = psum()
                                        P.op("pe", lambda e, py=py, pl=pl, pg=pg: e.matmul(py, wp[:, pg, :], pl[:], start=True, stop=True),
                                             reads=[b_pl, b_sm], writes=[b_py])
                                        P.op("dve", lambda e, py=py, sgp=sgp, pg=pg, s_=s_: e.scalar_tensor_tensor(
                                            out=sg_[:, s_, :], in0=py, scalar=sm[:, 6 + pg:7 + pg], in1=sgp[:], op0=ALU.mult, op1=ALU.mult),
                                            reads=[b_py, b_sgp, b_sm], writes=[b_sg_])
                                    r0 = 8 + (g - 4) * 2
                                    P.dma("sp", mix_v[:, r0:r0 + 2, tok], sg_[:, 0:2, :], reads=[b_sg_], writes=[b_mix[r0 + c][gtt] for c in range(2)], key="stg%d" % i)

                                for fn_ in pending:
                                    fn_()
                                pending.clear()
                                pending.append(pool_tail)
                                if tt == 3:
                                    for fn_ in pending:
                                        fn_()
                                    pending.clear()
                            else:
                                j = g - 6
                                i = stg_ctr[0] % 2
                                stg_ctr[0] += 1
                                sg_, b_sg_ = stgs[i], b_stgs[i]
                                pch, b_pch = proj(w, 0, 128, tt)
                                pcc, b_pcc = proj(w, 128, 128, tt)
                                pcb, b_pcb = proj(w, 256, 128, tt)
                                pgc, b_pgc = proj(w, 384, 128, tt)
                                chs, b_chs = tmp()
                                ub, b_ub = tmp()
                                t1, b_t1 = tmp()
                                t2, b_t2 = tmp()
                                P.op("act", lambda e, pch=pch, chs=chs: e.copy(out=chs[:, 0:512], in_=pch), reads=[b_pch], writes=[b_chs])
                                P.op("dve", lambda e, pcc=pcc, chs=chs, ub=ub: e.tensor_tensor(out=ub[:, 2:514], in0=pcc, in1=chs[:, 0:512], op=ALU.mult),
                                     reads=[b_pcc, b_chs], writes=[b_ub])
                                if tt == 0:
                                    ph1, b_ph1 = proj_halo(w, 0)
                                    ph2, b_ph2 = proj_halo(w, 128)
                                    hh, b_hh = tmp()
                                    P.op("act", lambda e, ph1=ph1, hh=hh: e.copy(out=hh[:, 0:16], in_=ph1[:, 0:16]), reads=[b_ph1], writes=[b_hh])
                                    P.op("dve", lambda e, ph2=ph2, hh=hh: e.tensor_tensor(out=hh[:, 16:32], in0=ph2[:, 0:16], in1=hh[:, 0:16], op=ALU.mult),
                                         reads=[b_ph2, b_hh], writes=[b_hh])
                                    P.op("dve", lambda e, ub=ub, hh=hh: e.tensor_scalar(out=ub[:, 0:2], in0=hh[:, 30:32], scalar1=coef[:, 1:2], scalar2=None, op0=ALU.mult),
                                         reads=[b_hh, b_ub, b_const], writes=[b_ub])
                                else:
                                    P.op("pool", lambda e, ub=ub, j=j: e.tensor_copy(out=ub[:, 0:2], in_=hc[:, j, :]), reads=[b_hc[j], b_ub], writes=[b_ub])
                                P.op("pool", lambda e, ub=ub, j=j: e.tensor_copy(out=hc[:, j, :], in_=ub[:, 512:514]), reads=[b_ub], writes=[b_hc[j]])
                                cw0 = 10 + j * 3
                                P.op("dve", lambda e, ub=ub, t1=t1, cw0=cw0: e.tensor_scalar(out=t1[:, 0:512], in0=ub[:, 0:512], scalar1=sm[:, cw0:cw0 + 1], scalar2=None, op0=ALU.mult),
                                     reads=[b_ub, b_sm], writes=[b_t1])
                                P.op("dve", lambda e, ub=ub, t1=t1, t2=t2, cw0=cw0: e.scalar_tensor_tensor(
                                    out=t2[:, 0:512], in0=ub[:, 1:513], scalar=sm[:, cw0 + 1:cw0 + 2], in1=t1[:, 0:512], op0=ALU.mult, op1=ALU.add),
                                    reads=[b_ub, b_t1, b_sm], writes=[b_t2])
                                P.op("dve", lambda e, ub=ub, t1=t1, t2=t2, cw0=cw0: e.scalar_tensor_tensor(
                                    out=t1[:, 0:512], in0=ub[:, 2:514], scalar=sm[:, cw0 + 2:cw0 + 3], in1=t2[:, 0:512], op0=ALU.mult, op1=ALU.add),
                                    reads=[b_ub, b_t2, b_t1, b_sm], writes=[b_t1])
                                P.op("dve", lambda e, pcb=pcb, t1=t1, t2=t2: e.tensor_tensor(out=t2[:, 0:512], in0=pcb, in1=t1[:, 0:512], op=ALU.mult),
                                     reads=[b_pcb, b_t1, b_t2], writes=[b_t2])
                                P.op("act", lambda e, pgc=pgc, chs=chs: e.activation(out=chs[:, 0:512], in_=pgc, func=AF.Silu), reads=[b_pgc, b_chs], writes=[b_chs])
                                P.op("pool", lambda e, t2=t2, chs=chs, sg_=sg_: e.tensor_tensor(out=sg_[:, 0, :], in0=t2[:, 0:512], in1=chs[:, 0:512], op=ALU.mult),
                                     reads=[b_t2, b_chs], writes=[b_sg_])
                                r0 = 12 + j
                                P.dma("sp", mix_v[:, r0, tok], sg_[:, 0, :], reads=[b_sg_], writes=[b_mix[r0][gtt]], key="stg%d" % i)
                P.barrier(junk)

        def phase_B(L):
            with contextlib.ExitStack() as st:
                areset()
                SB_ = [4, 5, 6, 7]
                sctr = [0]
                octr = [0]
                lctr = [0]
                qns = sb("qns", [128, 4, S], BF16, st)
                kvns = sb("kvns", [128, 2, S], BF16, st)
                krs = sb("krs", [64, S], BF16, st)
                b_lat = Buf()
                for rho in range(2):
                    csl = slice(rho * SO, (rho + 1) * SO)
                    P.dma("sp", qns[:, :, csl], e1a_g[rho * 512:(rho + 1) * 512, :].rearrange("(kc p) t -> p kc t", p=128), reads=[b_e1g], writes=[b_lat], key="qns%d" % rho)
                    P.dma("sp", kvns[:, :, csl], e1b_g[rho * 320:rho * 320 + 256, :].rearrange("(kc p) t -> p kc t", p=128), reads=[b_e1g], writes=[b_lat], key="kvns%d" % rho)
                    P.dma("sp", krs[:, csl], e1b_g[rho * 320 + 256:rho * 320 + 320, :], reads=[b_e1g], writes=[b_lat], key="krs%d" % rho)
                cosT = sb("cosA", [64, S], BF16, st)
                sinT = sb("sinA", [64, S], BF16, st)
                b_tab = Buf()
                P.dma("sp", cosT[:], cosA_d, reads=[b_tabd], writes=[b_tab], key="cosA")
                P.dma("sp", sinT[:], sinA_d, reads=[b_tabd], writes=[b_tab], key="sinA")
                wq = sb("wq", [128, 4, 1024], BF16, st)
                wk = sb("wk", [128, 2, 512], BF16, st)
                wv = sb("wv", [128, 2, 512], BF16, st)
                sm = sb("smB", [128, 32], F32, st)
                b_w = Buf()
                P.dma("sp", sm[:], sm_d[L * 128:(L + 1) * 128, :], writes=[b_w], key="smB")
                b_w1, b_w2 = Buf(), Buf()
                P.dma("pool", wq[:].rearrange("p a b -> p (a b)"), wq_d[L * 128:(L + 1) * 128, :], writes=[b_w1], key="wq")
                P.dma("pool", wk[:].rearrange("p a b -> p (a b)"), wk_d[L * 128:(L + 1) * 128, :], reads=[b_w1], writes=[b_w2], key="wk")
                P.dma("pool", wv[:].rearrange("p a b -> p (a b)"), wv_d[L * 128:(L + 1) * 128, :], reads=[b_w1, b_w2], writes=[b_w], key="wv")
                for kc in range(4):
                    P.op("dve", lambda e, kc=kc: e.tensor_scalar(out=wq[:, kc, :], in0=wq[:, kc, :], scalar1=sm[:, kc:kc + 1], scalar2=None, op0=ALU.mult),
                         reads=[b_w], writes=[b_w])
                for kc in range(2):
                    P.op("dve", lambda e, kc=kc: e.tensor_scalar(out=wk[:, kc, :], in0=wk[:, kc, :], scalar1=sm[:, 4 + kc:5 + kc], scalar2=None, op0=ALU.mult),
                         reads=[b_w], writes=[b_w])
                    P.op("dve", lambda e, kc=kc: e.tensor_scalar(out=wv[:, kc, :], in0=wv[:, kc, :], scalar1=sm[:, 4 + kc:5 + kc], scalar2=None, op0=ALU.mult),
                         reads=[b_w], writes=[b_w])
                Vq = sb("Vq", [128, 32, 512], BF16, st)
                b_V = [Buf() for i in range(32)]
                kTh = sb("kTh", [128, S], BF16, st)
                qTh = sb("qTh", [128, S], BF16, st)
                qrh = sb("qrh", [64, S], BF16, st)
                b_kT = [Buf() for i in range(8)]
                b_qT = [Buf() for i in range(8)]
                b_qr = [Buf() for i in range(8)]
                pTs = [sb("pT%d" % i, [128, 512], BF16, st) for i in range(4)]
                b_pTs = [Buf() for i in range(4)]
                pT_ctr = [0]
                rls = [sb("rl%d" % i, [128, 512], F32, st) for i in range(2)]
                b_rls = [Buf() for i in range(2)]
                outs = [sb("ob%d" % i, [128, 512], BF16, st) for i in range(2)]
                b_outs = [Buf() for i in range(2)]
                rt1 = [sb("rta%d" % i, [64, 512], F32, st) for i in range(2)]
                rt2 = [sb("rtb%d" % i, [64, 512], F32, st) for i in range(2)]
                b_rt1 = [Buf() for i in range(2)]
                b_rt2 = [Buf() for i in range(2)]
                ev = [0]

                def evac(out_ap, in_ap, reads, writes):
                    ev[0] += 1
                    if ev[0] % 2 == 0:
                        P.op("dve", lambda e: e.tensor_copy(out=out_ap, in_=in_ap), reads=reads, writes=writes)
                    else:
                        P.op("act", lambda e: e.copy(out=out_ap, in_=in_ap), reads=reads, writes=writes)

                uc = [0]
                for h in range(HL):
                    if h % 4 == 0:
                        hq = h // 4
                        for tk in range(32):
                            pt, b_pt = psum(SB_, sctr)
                            for kc in range(2):
                                P.op("pe", lambda e, pt=pt, kc=kc, tk=tk, hq=hq: e.matmul(
                                    pt, kvns[:, kc, tk * 128:(tk + 1) * 128], wv[:, kc, hq * 512:(hq + 1) * 512], start=(kc == 0), stop=(kc == 1)),
                                    reads=[b_lat, b_w], writes=[b_pt])
                            evac(Vq[:, tk, :], pt, [b_pt], [b_V[tk]])
                    for tt in range(8):
                        tok = slice(tt * 512, (tt + 1) * 512)
                        pt, b_pt = psum(SB_, sctr)
                        for kc in range(2):
                            P.op("pe", lambda e, pt=pt, kc=kc, tok=tok, h=h: e.matmul(
                                pt, wk[:, kc, h * 128:(h + 1) * 128], kvns[:, kc, tok], start=(kc == 0), stop=(kc == 1)),
                                reads=[b_lat, b_w], writes=[b_pt])
                        evac(kTh[:, tok], pt, [b_pt], [b_kT[tt]])
                        pt, b_pt = psum(SB_, sctr)
                        for kc in range(4):
                            P.op("pe", lambda e, pt=pt, kc=kc, tok=tok, h=h: e.matmul(
                                pt, wq[:, kc, h * 256:h * 256 + 128], qns[:, kc, tok], start=(kc == 0), stop=(kc == 3)),
                                reads=[b_lat, b_w], writes=[b_pt])
                        evac(qTh[:, tok], pt, [b_pt], [b_qT[tt]])
                        pa, b_pa = psum(SB_, sctr)
                        for kc in range(4):
                            P.op("pe", lambda e, pa=pa, kc=kc, tok=tok, h=h: e.matmul(
                                pa[0:64, :], wq[:, kc, h * 256 + 128:h * 256 + 192], qns[:, kc, tok], start=(kc == 0), stop=(kc == 3)),
                                reads=[b_lat, b_w], writes=[b_pa])
                        pb, b_pb = psum(SB_, sctr)
                        for kc in range(4):
                            P.op("pe", lambda e, pb=pb, kc=kc, tok=tok, h=h: e.matmul(
                                pb[0:64, :], wq[:, kc, h * 256 + 192:h * 256 + 256], qns[:, kc, tok], start=(kc == 0), stop=(kc == 3)),
                                reads=[b_lat, b_w], writes=[b_pb])
                        i = tt % 2
                        P.op("dve", lambda e, pa=pa, i=i, tok=tok: e.tensor_tensor(out=rt1[i][:], in0=pa[0:64, :], in1=cosT[:, tok], op=ALU.mult),
                             reads=[b_pa, b_tab], writes=[b_rt1[i]])
                        P.op("dve", lambda e, pb=pb, i=i, tok=tok: e.tensor_tensor(out=rt2[i][:], in0=pb[0:64, :], in1=sinT[:, tok], op=ALU.mult),
                             reads=[b_pb, b_tab], writes=[b_rt2[i]])
                        P.op("pool", lambda e, i=i, tok=tok: e.tensor_tensor(out=qrh[:, tok], in0=rt1[i][:], in1=rt2[i][:], op=ALU.add),
                             reads=[b_rt1[i], b_rt2[i]], writes=[b_qr[tt]])
                    units = [(qb, kb) for qb in range(8) for kb in range(4 * qb + 4)]
                    LOOK = 2
                    acc = {}
                    sc = {}

                    def emit_scores(u):
                        qb, kb = units[u]
                        qtok = slice(qb * 512, (qb + 1) * 512)
                        ktok = slice(kb * 128, (kb + 1) * 128)
                        ps_, b_ps_ = psum(SB_, sctr)
                        diag = kb >= 4 * qb
                        P.op("pe", lambda e: e.matmul(ps_, kTh[:, ktok], qTh[:, qtok], start=True, stop=False),
                             reads=[b_kT[kb // 4], b_qT[qb]], writes=[b_ps_])
                        P.op("pe", lambda e: e.matmul(ps_, krs[:, ktok], qrh[:, qtok], start=False, stop=(not diag)),
                             reads=[b_lat, b_qr[qb]], writes=[b_ps_])
                        if diag:
                            jm = kb - 4 * qb
                            P.op("pe", lambda e: e.matmul(ps_, ident[:], masks[:, jm, :], start=False, stop=True),
                                 reads=[b_const], writes=[b_ps_])
                        sc[u] = (ps_, b_ps_)

                    def emit_rest(u, h=h):
                        qb, kb = units[u]
                        nkb = 4 * qb + 4
                        qtok = slice(qb * 512, (qb + 1) * 512)
                        if kb == 0:
                            acc[qb] = (psum([0, 1], octr), psum([2, 3], lctr))
                        (po, b_po), (pl_, b_pl_) = acc[qb]
                        ps_, b_ps_ = sc.pop(u)
                        ip = pT_ctr[0] % 4
                        pT_ctr[0] += 1
                        pT, b_pT = pTs[ip], b_pTs[ip]
                        P.op("act", lambda e: e.activation(out=pT[:], in_=ps_, func=AF.Exp, scale=SCALE), reads=[b_ps_], writes=[b_pT])
                        P.op("pe", lambda e: e.matmul(po, Vq[:, kb, (h % 4) * 128:(h % 4 + 1) * 128], pT[:], start=(kb == 0), stop=(kb == nkb - 1)),
                             reads=[b_V[kb], b_pT], writes=[b_po])
                        P.op("pe", lambda e: e.matmul(pl_, ones[:], pT[:], start=(kb == 0), stop=(kb == nkb - 1)),
                             reads=[b_const, b_pT], writes=[b_pl_])
                        if kb == nkb - 1:
                            i = uc[0] % 2
                            uc[0] += 1
                            P.op("dve", lambda e: e.reciprocal(out=rls[i][:], in_=pl_), reads=[b_pl_], writes=[b_rls[i]])
                            P.op("dve", lambda e: e.tensor_tensor(out=outs[i][:], in0=po, in1=rls[i][:], op=ALU.mult),
                                 reads=[b_po, b_rls[i]], writes=[b_outs[i]])
                            P.dma("sp", e2_src[h][:, qtok], outs[i][:], reads=[b_outs[i]], writes=[b_e2s[h][qb]], key="ob%d" % i)
                            if qb == 7:
                                if debug:
                                    P.dma("sp", dbg_e2[h], e2_src[h], reads=b_e2s[h], writes=[Buf()], key="dbg_e2")
                                P.collective(e2_src[h], e2_g[h], reads=b_e2s[h], writes=[b_e2g[h]], key="cc_e2_%d" % h)

                    for u in range(min(LOOK, len(units))):
                        emit_scores(u)
                    for u in range(len(units)):
                        if u + LOOK < len(units):
                            emit_scores(u + LOOK)
                        emit_rest(u)
                P.barrier(junk)

        for L in range(n_layers):
            if stop_phase == "ln0":
                break
            phase_A(L)
            if stop_phase == "A":
                break
            phase_B(L)
            if stop_phase == "B":
                break
            with contextlib.ExitStack() as st:
                last = (L == DEPTH - 1)
                ln_phase(st, L, "proj", not last, last)
                P.barrier(junk)
        P.finish()
    return nc


def _tile_k(w, ncols_pad=None):
    K, C = w.shape
    return np.ascontiguousarray(w.reshape(K // 128, 128, C).transpose(1, 0, 2))


def prep_inputs(x, positions, emb_ln_g, emb_ln_b, w_in, q_norm_g, kv_norm_g, w_uq, w_ukv, w_pool,
                pool_scale, conv_w, w_out, b_out, ln_g, ln_b):
    f32 = np.float32
    w_in = np.asarray(w_in, f32)
    offs = np.cumsum([0, 512, 256, 64, 1024, 512, 512, 512, 512, 512, 512])
    o_q, o_kv, o_kr, o_gm, o_pi, o_gp, o_ch, o_cb, o_cc, o_gc = offs[:10]
    groups = []
    zero128 = None
    for L in range(DEPTH):
        W = w_in[L]
        kr = W[:, o_kr:o_kr + 64]
        ksw = np.concatenate([kr[:, 32:64], kr[:, 0:32]], axis=1)
        g0 = np.concatenate([W[:, o_kv:o_kv + 256], kr, ksw, np.zeros((D, 128), f32)], axis=1)
        gl = [g0, W[:, o_q:o_q + 512], W[:, o_gm:o_gm + 512], W[:, o_gm + 512:o_gm + 1024]]
        for a in range(2):
            gl.append(np.concatenate([W[:, o_pi + (2 * a) * 128:o_pi + (2 * a + 1) * 128], W[:, o_gp + (2 * a) * 128:o_gp + (2 * a + 1) * 128],
                                      W[:, o_pi + (2 * a + 1) * 128:o_pi + (2 * a + 2) * 128], W[:, o_gp + (2 * a + 1) * 128:o_gp + (2 * a + 2) * 128]], axis=1))
        for j in range(4):
            sl = slice(j * 128, (j + 1) * 128)
            gl.append(np.concatenate([W[:, o_ch:o_ch + 512][:, sl], W[:, o_cc:o_cc + 512][:, sl], W[:, o_cb:o_cb + 512][:, sl], W[:, o_gc:o_gc + 512][:, sl]], axis=1))
        for gmat in gl:
            groups.append(_tile_k(gmat).reshape(128, 16 * 512))
    w_in_g = np.ascontiguousarray(np.concatenate(groups, axis=0))

    wq_l, wk_l, wv_l = [[], []], [[], []], [[], []]
    wo_l, wp_l, sm_l = [], [], []
    for L in range(DEPTH):
        wq = np.asarray(w_uq[L], f32).reshape(512, NH, 192)
        rope = wq[:, :, 128:192]
        sw = np.concatenate([rope[:, :, 32:64], rope[:, :, 0:32]], axis=2)
        wq2 = np.concatenate([wq, sw], axis=2)
        wkv = np.asarray(w_ukv[L], f32).reshape(256, NH, 256)
        for r in range(2):
            hs = slice(4 * r, 4 * r + 4)
            wq_l[r].append(_tile_k(np.ascontiguousarray(wq2[:, hs]).reshape(512, 1024)).reshape(128, 4 * 1024))
            wk_l[r].append(_tile_k(np.ascontiguousarray(wkv[:, hs, 0:128]).reshape(256, 512)).reshape(128, 2 * 512))
            wv_l[r].append(_tile_k(np.ascontiguousarray(wkv[:, hs, 128:256]).reshape(256, 512)).reshape(128, 2 * 512))
        wo_l.append(_tile_k(np.asarray(w_out[L], f32)).reshape(128, 16 * 2048))
        wp_l.append(np.ascontiguousarray(np.asarray(w_pool[L], f32).transpose(1, 0, 2)).reshape(128, 4 * 128))
        sm = np.zeros((128, 32), f32)
        sm[:, 0:4] = np.asarray(q_norm_g[L], f32).reshape(4, 128).T
        sm[:, 4:6] = np.asarray(kv_norm_g[L], f32).reshape(2, 128).T
        sm[:, 6:10] = np.asarray(pool_scale[L], f32).reshape(4, 128).T
        cw = np.asarray(conv_w[L], f32).reshape(3, 4, 128)
        sm[:, 10:22] = cw.transpose(2, 1, 0).reshape(128, 12)
        sm_l.append(sm)
    lnp = np.stack([np.asarray(emb_ln_g, f32), np.asarray(emb_ln_b, f32)] +
                   sum([[np.asarray(ln_g[L], f32), np.asarray(ln_b[L], f32), np.asarray(b_out[L], f32)] for L in range(DEPTH)], []), axis=0)
    half = 32
    inv_freq = (10000.0 ** (-np.arange(half, dtype=np.float32) / half)).astype(f32)
    ropec = np.zeros((64, 2), f32)
    ropec[:, 0] = np.concatenate([inv_freq, inv_freq])
    ropec[:, 1] = np.concatenate([-np.ones(32, f32), np.ones(32, f32)])
    invdiv = np.zeros((2, 128, 4, 16), f32)
    for g, w in enumerate(POOL_W):
        invdiv[0, :, g, :] = 1.0 / np.minimum(np.arange(1, 17, dtype=f32), float(w))
        invdiv[1, :, g, :] = 1.0 / float(w)
    ident = np.eye(128, dtype=f32).astype(ml_dtypes.bfloat16)
    kk = np.arange(128)[:, None]
    qq = np.arange(512)[None, :]
    masks = np.stack([np.where(j * 128 + kk <= qq, 0.0, NEG) for j in range(4)], axis=1).astype(f32)
    masks = masks.reshape(128, 4 * 512).astype(ml_dtypes.bfloat16)
    shared = {
        "ropec": ropec, "lnp": np.ascontiguousarray(lnp), "w_in_g": w_in_g,
        "wo": np.ascontiguousarray(np.concatenate(wo_l, 0)),
        "wp": np.ascontiguousarray(np.concatenate(wp_l, 0)), "small": np.ascontiguousarray(np.concatenate(sm_l, 0)),
        "ident": ident, "masks": masks,
    }
    per_rank = []
    for r in range(2):
        coef = np.zeros((128, 2), f32)
        coef[:, r] = 1.0
        per_rank.append({
            "wq": np.ascontiguousarray(np.concatenate(wq_l[r], 0)), "wk": np.ascontiguousarray(np.concatenate(wk_l[r], 0)),
            "wv": np.ascontiguousarray(np.concatenate(wv_l[r], 0)), "invdiv": np.ascontiguousarray(invdiv[r].reshape(128, 64)),
            "coef": coef,
        })
    x = np.asarray(x, f32)
    positions = np.asarray(positions, np.int32)
    in_maps = []
    for c in range(8):
        b, r = c // 2, c % 2
        m = dict(shared)
        m.update(per_rank[r])
        m["x"] = np.ascontiguousarray(x[b, r * SO:(r + 1) * SO])
        m["pos"] = np.ascontiguousarray(positions[b][None, :])
        m["pos_own"] = np.ascontiguousarray(positions[b, r * SO:(r + 1) * SO][None, :])
        in_maps.append(m)
    return in_maps


def kernel(**inputs):
    in_maps = prep_inputs(**inputs)
    nc = build()
    res = run_bass_kernel_spmd(nc, in_maps, core_ids=list(range(8)))
    out = np.empty((4, S, D), np.float32)
    for c in range(8):
        b, r = c // 2, c % 2
        out[b, r * SO:(r + 1) * SO] = np.asarray(res.results[c]["out"], dtype=np.float32)
    return out
```

```python
import math
import contextlib
import numpy as np
import ml_dtypes
import concourse.bass as bass
import concourse.mybir as mybir
from concourse.bass_utils import run_bass_kernel_spmd

F32 = mybir.dt.float32
BF16 = mybir.dt.bfloat16
I32 = mybir.dt.int32
AF = mybir.ActivationFunctionType
ALU = mybir.AluOpType

S = 4096
SO = 2048
HL = 4
PAIRS = [[0, 1], [2, 3], [4, 5], [6, 7]]
D = 2048
DEPTH = 2
NH = 8
LN_EPS = 1e-5
RMS_EPS = 1e-6
ALPHA = (2 * DEPTH) ** 0.25
SCALE = 192 ** -0.5
NEG = -30000.0
POOL_W = (2, 4, 8, 16)
NG = 10

ENGS = ("pe", "act", "dve", "pool", "sp")


class Buf:
    __slots__ = ("name", "w", "r")

    def __init__(self, name=""):
        self.name = name
        self.w = None
        self.r = []


class Op:
    __slots__ = ("eng", "fn", "waits", "flag", "dma_key", "dma_val", "seq")

    def __init__(self, eng, fn):
        self.eng = eng
        self.fn = fn
        self.waits = []
        self.flag = False
        self.dma_key = None
        self.dma_val = 0
        self.seq = -1


class Prog:
    def __init__(self, nc):
        self.nc = nc
        self.ops = {e: [] for e in ENGS}
        self.seen = {e: {} for e in ENGS}
        self.seen_dma = {e: {} for e in ENGS}
        self.dma_counts = {}
        self.cc_keys = set()
        self.jb = [Buf(), Buf(), Buf()]

    def _add(self, eng, fn, reads, writes, dma_key=None):
        op = Op(eng, fn)
        op.seq = len(self.ops[eng])
        deps = []
        for b in reads:
            if b.w is not None:
                deps.append(b.w)
        for b in writes:
            if b.w is not None:
                deps.append(b.w)
            deps.extend(b.r)
        best = {}
        dma_deps = {}
        for d in deps:
            if d.dma_key is not None:
                if dma_deps.get(d.dma_key, 0) < d.dma_val:
                    dma_deps[d.dma_key] = d.dma_val
            else:
                if d.eng == eng and eng == "pe":
                    continue
                if best.get(d.eng, -1) < d.seq:
                    best[d.eng] = d.seq
        for f, s in best.items():
            if self.seen[eng].get(f, -1) >= s:
                continue
            self.seen[eng][f] = s
            dop = self.ops[f][s]
            dop.flag = True
            op.waits.append(("eng", f, dop))
        for k, v in dma_deps.items():
            if self.seen_dma[eng].get(k, 0) >= v:
                continue
            self.seen_dma[eng][k] = v
            op.waits.append(("dma", k, v))
        if dma_key is not None:
            op.dma_key = dma_key
            self.dma_counts[dma_key] = self.dma_counts.get(dma_key, 0) + 16
            op.dma_val = self.dma_counts[dma_key]
        for b in reads:
            b.r.append(op)
        for b in writes:
            b.w = op
            b.r = []
        self.ops[eng].append(op)
        return op

    def op(self, eng, fn, reads=(), writes=()):
        return self._add(eng, fn, reads, writes, None)

    def dma(self, eng, out, in_, reads=(), writes=(), key=None):
        def fn(e):
            return e.dma_start(out=out, in_=in_)
        return self._add(eng, fn, reads, writes, key)

    def collective(self, src, dst, reads, writes, key):
        def fn(e):
            return e.collective_compute("AllGather", ALU.bypass, replica_groups=PAIRS, ins=[src.opt()], outs=[dst.opt()])
        o = self._add("pool", fn, reads, writes, key)
        self.dma_counts[key] -= 15
        o.dma_val = self.dma_counts[key]
        self.cc_keys.add(key)
        return o

    def barrier(self, junk):
        marks = []
        jb = self.jb
        b = Buf()
        self.op("act", lambda e: e.activation(out=junk[:, 0:1], in_=junk[:, 4:5], func=AF.Copy), writes=[b, jb[0]])
        marks.append(b)
        b = Buf()
        self.op("dve", lambda e: e.memset(junk[:, 1:2], 0.0), writes=[b, jb[1]])
        marks.append(b)
        b = Buf()
        self.op("pool", lambda e: e.memset(junk[:, 2:3], 0.0), writes=[b, jb[2]])
        marks.append(b)
        fence = Buf()
        o = self.op("sp", lambda e: e.nop(), reads=marks, writes=[fence])
        for k, v in self.dma_counts.items():
            if self.seen_dma["sp"].get(k, 0) < v:
                self.seen_dma["sp"][k] = v
                o.waits.append(("dma", k, v))
        self.op("act", lambda e: e.activation(out=junk[:, 0:1], in_=junk[:, 4:5], func=AF.Copy), reads=[fence], writes=[jb[0]])
        self.op("dve", lambda e: e.memset(junk[:, 1:2], 0.0), reads=[fence], writes=[jb[1]])
        self.op("pool", lambda e: e.memset(junk[:, 2:3], 0.0), reads=[fence], writes=[jb[2]])
        self.op("pe", lambda e: e.nop(), reads=[fence])
        for e in ENGS:
            for k, v in self.dma_counts.items():
                if self.seen_dma[e].get(k, 0) < v:
                    self.seen_dma[e][k] = v

    def finish(self):
        nc = self.nc
        with contextlib.ExitStack() as st:
            esem = {e: st.enter_context(nc.semaphore("s_" + e)) for e in ENGS}
            dsem = {k: st.enter_context(nc.semaphore("d_%s" % (k,))) for k in self.dma_counts}
            block = st.enter_context(nc.Block())
            for e in ENGS:
                c = 0
                for o in self.ops[e]:
                    if o.flag:
                        c += 1
                        o.dma_val = c

            def emit(e, eng):
                for o in self.ops[e]:
                    for kind, k, v in o.waits:
                        if kind == "eng":
                            eng.wait_ge(esem[k], v.dma_val)
                        else:
                            eng.wait_ge(dsem[k], v)
                    inst = o.fn(eng)
                    if o.dma_key is not None and o.dma_key in self.cc_keys:
                        inst.then_inc(dsem[o.dma_key])
                    elif o.dma_key is not None:
                        inst.then_inc(dsem[o.dma_key], 16)
                    elif o.flag:
                        inst.then_inc(esem[e], 1)
                if e == "sp":
                    for k, v in self.dma_counts.items():
                        eng.wait_ge(dsem[k], v)

            @block.tensor
            def _(eng):
                emit("pe", eng)

            @block.scalar
            def _(eng):
                emit("act", eng)

            @block.vector
            def _(eng):
                emit("dve", eng)

            @block.gpsimd
            def _(eng):
                emit("pool", eng)

            @block.sync
            def _(eng):
                emit("sp", eng)


def build(debug=False, n_layers=DEPTH, stop_phase=None):
    nc = bass.Bass("TRN2", target_bir_lowering=False)
    P = Prog(nc)

    def din(name, shape, dt):
        return nc.dram_tensor(name, shape, dt, kind="ExternalInput").ap()

    x_d = din("x", [SO, D], F32)
    pos_d = din("pos", [1, S], I32)
    poso_d = din("pos_own", [1, SO], I32)
    coef_d = din("coef", [128, 2], F32)
    rc_d = din("ropec", [128, 2], F32)
    lnp_d = din("lnp", [2 + 3 * DEPTH, D], F32)
    win_d = din("w_in_g", [DEPTH * NG * 128, 16 * 512], F32)
    wq_d = din("wq", [DEPTH * 128, 4 * 1024], F32)
    wk_d = din("wk", [DEPTH * 128, 2 * 512], F32)
    wv_d = din("wv", [DEPTH * 128, 2 * 512], F32)
    wo_d = din("wo", [DEPTH * 128, 16 * 2048], F32)
    wp_d = din("wp", [DEPTH * 128, 4 * 128], F32)
    sm_d = din("small", [DEPTH * 128, 32], F32)
    idv_d = din("invdiv", [128, 64], F32)
    ident_d = din("ident", [128, 128], BF16)
    mask_d = din("masks", [128, 4 * 512], BF16)
    out_d = nc.dram_tensor("out", [SO, D], F32, kind="ExternalOutput").ap()

    skind = "ExternalOutput" if debug else "Internal"
    resid_d = nc.dram_tensor("resid", [SO, D], F32, kind=skind).ap()
    hT_d = nc.dram_tensor("hT", [D, SO], BF16, kind=skind).ap()
    mix_d = nc.dram_tensor("mixT", [D, SO], BF16, kind=skind).ap()
    cosA_d = nc.dram_tensor("cosA", [64, S], BF16).ap()
    sinA_d = nc.dram_tensor("sinA", [64, S], BF16).ap()
    cosO_d = nc.dram_tensor("cosO", [64, SO], BF16).ap()
    sinO_d = nc.dram_tensor("sinO", [64, SO], BF16).ap()
    e1a_src = nc.dram_tensor("e1a_src", [512, SO], BF16).ap()
    e1a_g = nc.dram_tensor("e1a_g", [1024, SO], BF16).ap()
    e1b_src = nc.dram_tensor("e1b_src", [320, SO], BF16).ap()
    e1b_g = nc.dram_tensor("e1b_g", [640, SO], BF16).ap()
    e2_src = [nc.dram_tensor("e2_src%d" % i, [128, S], BF16).ap() for i in range(4)]
    e2_g = [nc.dram_tensor("e2_g%d" % i, [256, S], BF16).ap() for i in range(4)]
    tl_src = nc.dram_tensor("tl_src", [D, 16], BF16).ap()
    tl_g = nc.dram_tensor("tl_g", [2 * D, 16], BF16).ap()
    if debug:
        dbg_e1a = nc.dram_tensor("dbg_e1a", [512, SO], BF16, kind="ExternalOutput").ap()
        dbg_e1b = nc.dram_tensor("dbg_e1b", [320, SO], BF16, kind="ExternalOutput").ap()
        dbg_e2 = [nc.dram_tensor("dbg_e2_%d" % i, [128, S], BF16, kind="ExternalOutput").ap() for i in range(4)]

    b_resid = [Buf("resid%d" % i) for i in range(16)]
    b_hT = [Buf("hT%d" % i) for i in range(4)]
    b_qn = [Buf() for i in range(4)]
    b_kvn = [Buf() for i in range(4)]
    b_kr = [Buf() for i in range(4)]
    b_mix = [[Buf() for t in range(4)] for r in range(16)]
    b_out = [Buf() for i in range(16)]
    b_e1g, b_tlsrc, b_tlg = Buf(), Buf(), Buf()
    b_e2g = [Buf() for i in range(4)]
    b_e2s = [[Buf() for t in range(8)] for r in range(4)]
    b_tabd = Buf()

    hT_v = hT_d.rearrange("(kc p) t -> p kc t", p=128)
    mix_v = mix_d.rearrange("(kc p) t -> p kc t", p=128)
    qn_v = e1a_src.rearrange("(kc p) t -> p kc t", p=128)
    kvn_v = e1b_src[0:256, :].rearrange("(kc p) t -> p kc t", p=128)
    kr_d = e1b_src[256:320, :]
    tls_v = tl_src.rearrange("(kc p) t -> p kc t", p=128)
    tlg_v = tl_g.rearrange("(kc p) t -> p kc t", p=128)

    with contextlib.ExitStack() as gst:
        ARENA_WORDS = 51200
        arena = gst.enter_context(nc.sbuf_tensor("arena", [128, ARENA_WORDS], F32))
        a_top = [0]
        a_mark = [0]

        def sb(name, shape, dt, st=None):
            n = 1
            for d_ in shape[1:]:
                n *= d_
            esz = 4 if dt in (F32, I32) else 2
            words = (n * esz + 3) // 4
            words = (words + 7) // 8 * 8
            off = a_top[0]
            assert off + words <= ARENA_WORDS, ("SBUF arena overflow", name, off, words)
            a_top[0] = off + words
            v = arena[0:shape[0], off:off + words]
            if dt != F32:
                v = v.bitcast(dt)
            v = v[:, 0:n]
            if len(shape) == 3:
                v = v.rearrange("p (a b) -> p a b", a=shape[1])
            return v

        def areset():
            a_top[0] = a_mark[0]

        ps_all = gst.enter_context(nc.psum_tensor("ps", [128, 8 * 512], F32))
        ps_bufs = [Buf("ps%d" % i) for i in range(8)]
        ps_ctr = [0]

        def psum(banks=None, ctr=None):
            if banks is None:
                i = ps_ctr[0] % 8
                ps_ctr[0] += 1
            else:
                i = banks[ctr[0] % len(banks)]
                ctr[0] += 1
            return ps_all[:, i * 512:(i + 1) * 512], ps_bufs[i]

        junk = sb("junk", [128, 8], F32)
        ident = sb("ident", [128, 128], BF16)
        ones = sb("ones", [128, 128], BF16)
        masks = sb("masks", [128, 4, 512], BF16)
        rc = sb("rc", [128, 2], F32)
        coef = sb("coef", [128, 2], F32)
        invdiv = sb("invdiv", [128, 4, 16], F32)
        b_const = Buf("const")
        b_tab = Buf("tab")

        P.op("dve", lambda e: e.memset(junk[:], 0.0), writes=[b_const] + P.jb)
        P.op("dve", lambda e: e.memset(ones[:], 1.0), writes=[b_const])
        P.dma("sp", ident[:], ident_d, writes=[b_const], key="c_ident")
        P.dma("sp", masks[:].rearrange("p a b -> p (a b)"), mask_d, writes=[b_const], key="c_mask")
        P.dma("sp", rc[:], rc_d, writes=[b_const], key="c_rc")
        P.dma("sp", coef[:], coef_d, writes=[b_const], key="c_coef")
        P.dma("sp", invdiv[:].rearrange("p a b -> p (a b)"), idv_d, writes=[b_const], key="c_idv")

        LN_EPS_AP = sb("lneps", [128, 4], F32)
        a_mark[0] = a_top[0]

        def rope_tables(pos_ap, N, cos_dst, sin_dst, tag):
            H = N // 2
            posi = sb("posi" + tag, [128, H], I32)
            ang = sb("ang" + tag, [128, H], F32)
            ta = sb("ta" + tag, [128, H], F32)
            tb = sb("tb" + tag, [128, H], F32)
            obs = [sb("ob%d" % i + tag, [128, H], BF16) for i in range(2)]
            b_posi, b_ang, b_ta, b_tb = Buf(), Buf(), Buf(), Buf()
            b_obs = [Buf(), Buf()]
            P.dma("sp", posi[0:64, :], pos_ap[:, 0:H].partition_broadcast(64), writes=[b_posi], key="t_pos0" + tag)
            P.dma("sp", posi[64:128, :], pos_ap[:, H:N].partition_broadcast(64), writes=[b_posi], key="t_pos1" + tag)
            P.op("dve", lambda e: e.tensor_copy(out=ang[:], in_=posi[:]), reads=[b_posi], writes=[b_ang])
            P.op("dve", lambda e: e.tensor_scalar(out=ang[:], in0=ang[:], scalar1=rc[:, 0:1], scalar2=None, op0=ALU.mult),
                 reads=[b_ang, b_const], writes=[b_ang])
            TWO_PI = 2.0 * math.pi
            for wi_, (which, phase, dst) in enumerate((("sin", 0.0, sin_dst), ("cos", math.pi / 2, cos_dst))):
                ob, b_ob = obs[wi_], b_obs[wi_]
                P.op("dve", lambda e, phase=phase: e.tensor_scalar(out=ta[:], in0=ang[:], scalar1=phase, scalar2=1.0 / TWO_PI,
                                                                    op0=ALU.add, op1=ALU.mult), reads=[b_ang], writes=[b_ta])
                P.op("dve", lambda e: e.tensor_copy(out=posi[:], in_=ta[:]), reads=[b_ta], writes=[b_posi])
                P.op("dve", lambda e: e.tensor_copy(out=ta[:], in_=posi[:]), reads=[b_posi], writes=[b_ta])
                P.op("dve", lambda e: e.scalar_tensor_tensor(out=tb[:], in0=ta[:], scalar=-TWO_PI, in1=ang[:], op0=ALU.mult, op1=ALU.add),
                     reads=[b_ta, b_ang], writes=[b_tb])
                P.op("dve", lambda e, phase=phase: e.tensor_scalar(out=tb[:], in0=tb[:], scalar1=phase, scalar2=None, op0=ALU.add),
                     reads=[b_tb], writes=[b_tb])
                P.op("dve", lambda e: e.tensor_scalar(out=ta[:], in0=tb[:], scalar1=math.pi, scalar2=TWO_PI, op0=ALU.is_gt, op1=ALU.mult),
                     reads=[b_tb], writes=[b_ta])
                P.op("dve", lambda e: e.tensor_tensor(out=tb[:], in0=tb[:], in1=ta[:], op=ALU.subtract), reads=[b_tb, b_ta], writes=[b_tb])
                P.op("dve", lambda e: e.tensor_scalar(out=tb[:], in0=tb[:], scalar1=math.pi, scalar2=-math.pi, op0=ALU.min, op1=ALU.max),
                     reads=[b_tb], writes=[b_tb])
                P.op("act", lambda e: e.activation(out=ta[:], in_=tb[:], func=AF.Sin), reads=[b_tb], writes=[b_ta])
                if which == "sin":
                    P.op("dve", lambda e, ob=ob: e.tensor_scalar(out=ob[:], in0=ta[:], scalar1=rc[:, 1:2], scalar2=None, op0=ALU.mult),
                         reads=[b_ta, b_const], writes=[b_ob])
                else:
                    P.op("dve", lambda e, ob=ob: e.tensor_copy(out=ob[:], in_=ta[:]), reads=[b_ta], writes=[b_ob])
                P.dma("sp", dst[:, 0:H], ob[0:64, :], reads=[b_ob], writes=[b_tabd], key="t_ob0%d" % wi_ + tag)
                P.dma("sp", dst[:, H:N], ob[64:128, :], reads=[b_ob], writes=[b_tabd], key="t_ob1%d" % wi_ + tag)

        areset()
        rope_tables(pos_d, S, cosA_d, sinA_d, "a")
        rope_tables(poso_d, SO, cosO_d, sinO_d, "o")

        def ln_phase(st, layer_idx, src_kind, write_hT, final):
            if src_kind != "x":
                areset()
            NB = 4 if src_kind == "x" else 3
            gb = sb("ln_g", [128, D], F32, st)
            bb = sb("ln_b", [128, D], F32, st)
            b_p = Buf()
            if src_kind == "x":
                grow, brow = 0, 1
            else:
                grow, brow = 2 + 3 * layer_idx, 3 + 3 * layer_idx
            b_pg, b_pb = Buf(), Buf()
            P.dma("sp", gb[:], lnp_d[grow:grow + 1, :].partition_broadcast(128), writes=[b_pg], key="ln_g")
            P.dma("sp", bb[:], lnp_d[brow:brow + 1, :].partition_broadcast(128), writes=[b_pb], key="ln_b")
            ys = [sb("ln_y%d" % i, [128, D], F32, st) for i in range(NB)]
            b_ys = [Buf() for i in range(NB)]
            hbs = [sb("ln_hb%d" % i, [128, D], BF16, st) for i in range(2)]
            b_hbs = [Buf() for i in range(2)]
            stats = [sb("ln_st%d" % i, [128, 4, 6], F32, st) for i in range(NB)]
            mvs = [sb("ln_mv%d" % i, [128, 4], F32, st) for i in range(NB)]
            b_sts = [Buf() for i in range(NB)]
            stg = [sb("ln_stg%d" % i, [128, 16, 512], BF16, st) for i in range(1)] if write_hT else []
            b_stg = [Buf() for i in range(1)]
            if src_kind == "proj":
                bo = sb("ln_bo", [1, D], BF16, st)
                P.dma("pool", bo[:], lnp_d[4 + 3 * layer_idx:5 + 3 * layer_idx, :], writes=[b_p], key="ln_bo")
                wo = sb("wo", [128, 16, 2048], BF16, st)
                b_wop = [Buf() for i in range(4)]
                for i4 in range(4):
                    P.dma("pool", wo[:, i4 * 4:(i4 + 1) * 4, :].rearrange("p a b -> p (a b)"),
                          wo_d[layer_idx * 128:(layer_idx + 1) * 128, i4 * 8192:(i4 + 1) * 8192],
                          reads=([b_wop[i4 - 1]] if i4 else [b_p]), writes=[b_wop[i4]], key="wo%d" % i4)
                mts = [sb("mt%d" % i, [128, 16, 256], BF16, st) for i in range(2)]
                b_mts = [Buf() for i in range(2)]
                NR = 2
                rts = [sb("rt%d" % i, [128, D], F32, st) for i in range(NR)]
                b_rts = [Buf() for i in range(NR)]
                ea = [sb("ea%d" % i, [128, 8, 256], BF16, st) for i in range(2)]
                eb = [sb("eb%d" % i, [128, 8, 256], BF16, st) for i in range(2)]
                b_eac = [Buf(), Buf()]
                b_ebc = [Buf(), Buf()]
                b_ea = [[Buf() for k in range(8)] for i in range(2)]
                b_eb = [[Buf() for k in range(8)] for i in range(2)]
                et = sb("et", [128, 8, 256], BF16, st)
                b_et = Buf()

            def s_pair(tk):
                tt = tk // 4
                t2 = tk // 2
                i2 = t2 % 2
                mt, b_mt = mts[i2], b_mts[i2]
                P.dma("sp", mt[:], mix_v[:, :, t2 * 256:(t2 + 1) * 256], reads=[b_mix[r][tt] for r in range(16)],
                      writes=[b_mt], key="mt%d" % i2)
                for rho in range(2):
                    for hl in range(4):
                        kcg = rho * 4 + hl
                        P.dma("sp", ea[i2][:, kcg, :], e2_g[hl][rho * 128:(rho + 1) * 128, t2 * 256:(t2 + 1) * 256],
                              reads=[b_e2g[hl]], writes=[b_ea[i2][kcg], b_eac[i2]], key="ea%d" % i2)
                        P.dma("sp", eb[i2][:, kcg, :], e2_g[hl][rho * 128:(rho + 1) * 128, SO + t2 * 256:SO + (t2 + 1) * 256],
                              reads=[b_e2g[hl]], writes=[b_eb[i2][kcg], b_ebc[i2]], key="eb%d" % i2)

            def s_blend(tk):
                i2 = (tk // 2) % 2
                mt, b_mt = mts[i2], b_mts[i2]
                P.op("dve", lambda e: e.tensor_scalar(out=et[:], in0=ea[i2][:], scalar1=coef[:, 0:1], scalar2=None, op0=ALU.mult),
                     reads=b_ea[i2] + [b_const], writes=[b_et])
                P.op("dve", lambda e: e.scalar_tensor_tensor(out=et[:], in0=eb[i2][:], scalar=coef[:, 1:2], in1=et[:], op0=ALU.mult, op1=ALU.add),
                     reads=b_eb[i2] + [b_et, b_const], writes=[b_et])
                P.op("pool", lambda e: e.tensor_tensor(out=mt[:, 0:8, :], in0=mt[:, 0:8, :], in1=et[:], op=ALU.mult),
                     reads=[b_et, b_mt], writes=[b_mt])

            def s_load(tk):
                y, b_y = ys[tk % NB], b_ys[tk % NB]
                tsl = slice(tk * 128, (tk + 1) * 128)
                if src_kind == "x":
                    P.dma("sp", y[:], x_d[tsl, :], writes=[b_y], key="ln_y%d" % (tk % NB))
                else:
                    rt, b_rt = rts[tk % NR], b_rts[tk % NR]
                    P.dma("sp", rt[:], resid_d[tsl, :], reads=[b_resid[tk]], writes=[b_rt], key="rt%d" % (tk % NR))

            def s1(tk):
                y, b_y = ys[tk % NB], b_ys[tk % NB]
                stt, mv, b_st = stats[tk % NB], mvs[tk % NB], b_sts[tk % NB]
                if src_kind != "x":
                    t2 = tk // 2
                    mt, b_mt = mts[t2 % 2], b_mts[t2 % 2]
                    rt, b_rt = rts[tk % NR], b_rts[tk % NR]
                    pts = [psum() for cg in range(4)]
                    for cg in range(4):
                        pt, b_pt = pts[cg]
                        P.op("pe", lambda e, pt=pt, cg=cg: e.matmul(pt, ones[0:1, :], bo[0:1, cg * 512:(cg + 1) * 512], start=True, stop=False),
                             reads=[b_p, b_const], writes=[b_pt])
                    for kc in range(16):
                        for cg in range(4):
                            pt, b_pt = pts[cg]
                            P.op("pe", lambda e, pt=pt, kc=kc, cg=cg: e.matmul(
                                pt, mt[:, kc, (tk % 2) * 128:(tk % 2 + 1) * 128], wo[:, kc, cg * 512:(cg + 1) * 512],
                                start=False, stop=(kc == 15)), reads=[b_mt, b_wop[kc // 4]], writes=[b_pt])
                    for cg in range(4):
                        pt, b_pt = pts[cg]
                        P.op("dve", lambda e, pt=pt, cg=cg: e.scalar_tensor_tensor(
                            out=y[:, cg * 512:(cg + 1) * 512], in0=rt[:, cg * 512:(cg + 1) * 512], scalar=ALPHA, in1=pt,
                            op0=ALU.mult, op1=ALU.add), reads=[b_pt, b_rt], writes=[b_y])
                for c in range(4):
                    P.op("dve", lambda e, c=c: e.bn_stats(out=stt[:, c, :], in_=y[:, c * 512:(c + 1) * 512]),
                         reads=[b_y], writes=[b_st])
                P.op("dve", lambda e: e.bn_aggr(out=mv[:, 0:2], in_=stt[:].rearrange("p a b -> p (a b)")),
                     reads=[b_st], writes=[b_st])
                P.op("act", lambda e: e.activation(out=mv[:, 2:3], in_=mv[:, 1:2], func=AF.Sqrt, bias=LN_EPS_AP[:, 0:1], scale=1.0),
                     reads=[b_st, b_const], writes=[b_st])
                P.op("dve", lambda e: e.reciprocal(out=mv[:, 2:3], in_=mv[:, 2:3]), reads=[b_st], writes=[b_st])
                P.op("dve", lambda e: e.scalar_tensor_tensor(out=mv[:, 3:4], in0=mv[:, 0:1], scalar=-1.0, in1=mv[:, 2:3],
                                                              op0=ALU.mult, op1=ALU.mult), reads=[b_st], writes=[b_st])

            def s2a(tk):
                y, b_y = ys[tk % NB], b_ys[tk % NB]
                mv, b_st = mvs[tk % NB], b_sts[tk % NB]
                P.op("act", lambda e: e.activation(out=y[:], in_=y[:], func=AF.Identity, bias=mv[:, 3:4], scale=mv[:, 2:3]),
                     reads=[b_y, b_st], writes=[b_y])

            def s2b(tk):
                y, b_y = ys[tk % NB], b_ys[tk % NB]
                P.op("dve", lambda e: e.tensor_tensor(out=y[:], in0=y[:], in1=gb[:], op=ALU.mult), reads=[b_y, b_pg], writes=[b_y])
                P.op("pool", lambda e: e.tensor_tensor(out=y[:], in0=y[:], in1=bb[:], op=ALU.add), reads=[b_y, b_pb], writes=[b_y])

            def s3(tk):
                y, b_y = ys[tk % NB], b_ys[tk % NB]
                hb, b_hb = hbs[tk % 2], b_hbs[tk % 2]
                tsl = slice(tk * 128, (tk + 1) * 128)
                if final:
                    P.dma("sp", out_d[tsl, :], y[:], reads=[b_y], writes=[b_out[tk]], key="ln_o%d" % (tk % NB))
                else:
                    P.dma("sp", resid_d[tsl, :], y[:], reads=[b_y], writes=[b_resid[tk]], key="ln_o%d" % (tk % NB))
                if write_hT:
                    P.op("act", lambda e: e.copy(out=hb[:], in_=y[:]), reads=[b_y], writes=[b_hb])
                    sg_, b_sg_ = stg[0], b_stg[0]
                    for q4 in range(4):
                        pt, b_pt = psum()
                        ptb = pt.bitcast(BF16)
                        for j in range(4):
                            kc = q4 * 4 + j
                            P.op("pe", lambda e, ptb=ptb, kc=kc, j=j: e.transpose(
                                ptb[:, j * 128:(j + 1) * 128], hb[:, kc * 128:(kc + 1) * 128], ident[:]),
                                reads=[b_hb, b_const], writes=[b_pt])
                        if q4 % 2 == 0:
                            P.op("dve", lambda e, ptb=ptb, q4=q4: e.tensor_copy(
                                out=sg_[:, q4 * 4:(q4 + 1) * 4, (tk % 4) * 128:(tk % 4 + 1) * 128],
                                in_=ptb[:, 0:512].rearrange("p (a b) -> p a b", a=4)), reads=[b_pt], writes=[b_sg_])
                        else:
                            P.op("act", lambda e, ptb=ptb, q4=q4: e.copy(
                                out=sg_[:, q4 * 4:(q4 + 1) * 4, (tk % 4) * 128:(tk % 4 + 1) * 128],
                                in_=ptb[:, 0:512].rearrange("p (a b) -> p a b", a=4)), reads=[b_pt], writes=[b_sg_])
                    if tk % 4 == 3:
                        tt = tk // 4
                        P.dma("sp", hT_v[:, :, tt * 512:(tt + 1) * 512], sg_[:], reads=[b_sg_], writes=[b_hT[tt]], key="ln_stg")
                        if tk == NTK - 1:
                            P.dma("sp", tls_v, sg_[:, :, 496:512], reads=[b_sg_], writes=[b_tlsrc], key="ln_tl")
                            P.collective(tl_src, tl_g, reads=[b_tlsrc], writes=[b_tlg], key="cc_e3")

            NTK = SO // 128
            is_proj = (src_kind != "x")
            if is_proj:
                s_pair(0)
                s_blend(0)
            s_load(0)
            for i in range(NTK + 2):
                if is_proj and i % 2 == 0 and i + 2 < NTK:
                    s_pair(i + 2)
                if is_proj and i % 2 == 1 and i + 1 < NTK:
                    s_blend(i + 1)
                if i + 1 < NTK:
                    s_load(i + 1)
                if 0 <= i - 1 < NTK:
                    s2a(i - 1)
                if i < NTK:
                    s1(i)
                if 0 <= i - 1 < NTK:
                    s2b(i - 1)
                if 0 <= i - 2 < NTK:
                    s3(i - 2)

        P.op("dve", lambda e: e.memset(LN_EPS_AP[:, 0:1], LN_EPS), writes=[b_const])
        P.op("dve", lambda e: e.memset(LN_EPS_AP[:, 1:2], RMS_EPS), writes=[b_const])

        with contextlib.ExitStack() as st:
            ln_phase(st, 0, "x", True, False)
            P.barrier(junk)

        def phase_A(L):
            with contextlib.ExitStack() as st:
                areset()
                hTs = sb("hTs", [128, 16, 2048], BF16, st)
                b_hTs = Buf()
                wr = [sb("wr%d" % i, [128, 16, 512], BF16, st) for i in range(2)]
                b_wr = [Buf() for i in range(2)]
                sm = sb("sm", [128, 32], F32, st)
                wp = sb("wp", [128, 4, 128], BF16, st)
                b_sm = Buf()
                P.dma("sp", sm[:], sm_d[L * 128:(L + 1) * 128, :], writes=[b_sm], key="sm")
                P.dma("pool", wp[:].rearrange("p a b -> p (a b)"), wp_d[L * 128:(L + 1) * 128, :], writes=[b_sm], key="wp")
                hp = sb("hp", [128, 4, 16], F32, st)
                hc = sb("hc", [128, 4, 2], F32, st)
                b_hp = [Buf() for i in range(4)]
                b_hc = [Buf() for i in range(4)]
                P.op("pool", lambda e: e.memset(hp[:], 0.0), writes=b_hp)
                P.op("pool", lambda e: e.memset(hc[:], 0.0), writes=b_hc)
                stgs = [sb("stg%d" % i, [128, 4, 512], BF16, st) for i in range(2)]
                b_stgs = [Buf() for i in range(2)]
                stg_ctr = [0]
                NTMP = 6
                tmps = [sb("tmp%d" % i, [128, 528], F32, st) for i in range(NTMP)]
                b_tmps = [Buf() for i in range(NTMP)]
                tmp_ctr = [0]
                sqs = [sb("sq%d" % i, [128, 512], BF16, st) for i in range(2)]
                b_sqs = [Buf() for i in range(2)]
                sq_ctr = [0]
                pls = [sb("pl%d" % i, [128, 512], BF16, st) for i in range(4)]
                b_pls = [Buf() for i in range(4)]
                sgps = [sb("sgp%d" % i, [128, 512], F32, st) for i in range(4)]
                b_sgps = [Buf() for i in range(4)]
                pl_ctr = [0]
                pending = []
                cosT = sb("cosO", [64, SO], BF16, st)
                sinT = sb("sinO", [64, SO], BF16, st)
                b_tab = Buf()
                P.dma("sp", cosT[:], cosO_d, reads=[b_tabd], writes=[b_tab], key="cosO")
                P.dma("sp", sinT[:], sinO_d, reads=[b_tabd], writes=[b_tab], key="sinO")
                hTh = sb("hTh", [128, 16, 16], BF16, st)
                b_hTh = Buf()
                P.dma("sp", hTh[:], tlg_v[:, 0:16, :], reads=[b_tlg], writes=[b_hTh], key="hTh")

                def proj_halo(w, c0):
                    pt, b_pt = psum()
                    for kc in range(16):
                        P.op("pe", lambda e, pt=pt, w=w, kc=kc: e.matmul(
                            pt[:, 0:16], w[0][:, kc, c0:c0 + 128], hTh[:, kc, :],
                            start=(kc == 0), stop=(kc == 15)), reads=[w[1], b_hTh], writes=[b_pt])
                    return pt, b_pt

                def tmp():
                    i = tmp_ctr[0] % NTMP
                    tmp_ctr[0] += 1
                    return tmps[i], b_tmps[i]

                def proj(w, c0, ncols, tt):
                    pt, b_pt = psum()
                    for kc in range(16):
                        P.op("pe", lambda e, pt=pt, w=w, kc=kc: e.matmul(
                            pt[0:ncols, :], w[0][:, kc, c0:c0 + ncols], hTs[:, kc, tt * 512:(tt + 1) * 512],
                            start=(kc == 0), stop=(kc == 15)), reads=[w[1], b_hTs], writes=[b_pt])
                    return pt, b_pt

                def rmsnorm_group(w, c0, nch, tt, dim, dst_v, dst_bufs, gtt, key):
                    pts = [proj(w, c0 + c * 128, 128, tt) for c in range(nch)]
                    ss, b_ss = psum()
                    for c, (pt, b_pt) in enumerate(pts):
                        i = sq_ctr[0] % 2
                        sq_ctr[0] += 1
                        sq, b_sq = sqs[i], b_sqs[i]
                        P.op("act", lambda e, pt=pt, sq=sq: e.activation(out=sq[:], in_=pt, func=AF.Square), reads=[b_pt], writes=[b_sq])
                        P.op("pe", lambda e, ss=ss, sq=sq, c=c: e.matmul(ss, ones[:], sq[:], start=(c == 0), stop=(c == nch - 1)),
                             reads=[b_sq, b_const], writes=[b_ss])
                    rs, b_rs = tmp()
                    P.op("act", lambda e, rs=rs, ss=ss: e.activation(out=rs[:, 0:512], in_=ss, func=AF.Sqrt, bias=LN_EPS_AP[:, 1:2],
                                                                      scale=1.0 / dim), reads=[b_ss, b_const], writes=[b_rs])
                    P.op("dve", lambda e, rs=rs: e.reciprocal(out=rs[:, 0:512], in_=rs[:, 0:512]), reads=[b_rs], writes=[b_rs])
                    i = stg_ctr[0] % 2
                    stg_ctr[0] += 1
                    sg_, b_sg_ = stgs[i], b_stgs[i]
                    for c, (pt, b_pt) in enumerate(pts):
                        P.op("dve", lambda e, pt=pt, rs=rs, sg_=sg_, c=c: e.tensor_tensor(out=sg_[:, c, :], in0=pt, in1=rs[:, 0:512], op=ALU.mult),
                             reads=[b_pt, b_rs], writes=[b_sg_])
                    P.dma("sp", dst_v[:, :, gtt * 512:(gtt + 1) * 512], sg_[:, 0:nch, :], reads=[b_sg_], writes=[dst_bufs[gtt]], key="stg%d" % i)

                for hf in range(1):
                    P.dma("sp", hTs[:], hT_v[:, :, hf * 2048:(hf + 1) * 2048], reads=b_hT[hf * 4:(hf + 1) * 4], writes=[b_hTs], key="hTs")
                    for g in range(NG):
                        wi = (hf * NG + g) % 2
                        w = (wr[wi], b_wr[wi])
                        row0 = (L * NG + g) * 128
                        P.dma("pool", wr[wi][:].rearrange("p a b -> p (a b)"), win_d[row0:row0 + 128, :], writes=[b_wr[wi]], key="wr%d" % wi)
                        if g == 2:
                            if debug:
                                P.dma("sp", dbg_e1a, e1a_src, reads=b_qn, writes=[Buf()], key="dbg_e1a")
                                P.dma("sp", dbg_e1b, e1b_src, reads=b_kvn + b_kr, writes=[Buf()], key="dbg_e1b")
                            P.collective(e1b_src, e1b_g, reads=b_kvn + b_kr, writes=[b_e1g], key="cc_e1b")
                            P.collective(e1a_src, e1a_g, reads=b_qn + [b_e1g], writes=[b_e1g], key="cc_e1a")
                        for tt in range(4):
                            gtt = hf * 4 + tt
                            tok = slice(gtt * 512, (gtt + 1) * 512)
                            if g == 0:
                                rmsnorm_group(w, 0, 2, tt, 256.0, kvn_v, b_kvn, gtt, "kvn")
                                pa, b_pa = proj(w, 256, 64, tt)
                                pb, b_pb = proj(w, 320, 64, tt)
                                t1, b_t1 = tmp()
                                t2, b_t2 = tmp()
                                P.op("dve", lambda e, pa=pa, t1=t1, tok=tok: e.tensor_tensor(out=t1[0:64, 0:512], in0=pa[0:64, :], in1=cosT[:, tok], op=ALU.mult),
                                     reads=[b_pa, b_tab], writes=[b_t1])
                                P.op("dve", lambda e, pb=pb, t2=t2, tok=tok: e.tensor_tensor(out=t2[0:64, 0:512], in0=pb[0:64, :], in1=sinT[:, tok], op=ALU.mult),
                                     reads=[b_pb, b_tab], writes=[b_t2])
                                i = stg_ctr[0] % 2
                                stg_ctr[0] += 1
                                sg_, b_sg_ = stgs[i], b_stgs[i]
                                P.op("pool", lambda e, t1=t1, t2=t2, sg_=sg_: e.tensor_tensor(out=sg_[0:64, 0, :], in0=t1[0:64, 0:512], in1=t2[0:64, 0:512], op=ALU.add),
                                     reads=[b_t1, b_t2], writes=[b_sg_])
                                P.dma("sp", kr_d[:, tok], sg_[0:64, 0, :], reads=[b_sg_], writes=[b_kr[gtt]], key="stg%d" % i)
                            elif g == 1:
                                rmsnorm_group(w, 0, 4, tt, 512.0, qn_v, b_qn, gtt, "qn")
                            elif g in (2, 3):
                                i = stg_ctr[0] % 2
                                stg_ctr[0] += 1
                                sg_, b_sg_ = stgs[i], b_stgs[i]
                                for c in range(4):
                                    pt, b_pt = proj(w, c * 128, 128, tt)
                                    P.op("act", lambda e, pt=pt, sg_=sg_, c=c: e.activation(out=sg_[:, c, :], in_=pt, func=AF.Silu),
                                         reads=[b_pt], writes=[b_sg_])
                                r0 = (g - 2) * 4
                                P.dma("sp", mix_v[:, r0:r0 + 4, tok], sg_[:], reads=[b_sg_], writes=[b_mix[r0 + c][gtt] for c in range(4)], key="stg%d" % i)
                            elif g in (4, 5):
                                tails = []
                                for s_ in range(2):
                                    pg = (g - 4) * 2 + s_
                                    wlen = POOL_W[pg]
                                    px, b_px = proj(w, s_ * 256, 128, tt)
                                    pgt, b_pgt = proj(w, s_ * 256 + 128, 128, tt)
                                    xb, b_xb = tmp()
                                    sa, b_sa = tmp()
                                    sb_, b_sb = tmp()
                                    P.op("act", lambda e, px=px, xb=xb: e.copy(out=xb[:, 16:528], in_=px), reads=[b_px], writes=[b_xb])
                                    if tt == 0:
                                        ph, b_ph = proj_halo(w, s_ * 256)
                                        P.op("dve", lambda e, xb=xb, ph=ph: e.tensor_scalar(out=xb[:, 0:16], in0=ph[:, 0:16], scalar1=coef[:, 1:2], scalar2=None, op0=ALU.mult),
                                             reads=[b_ph, b_xb, b_const], writes=[b_xb])
                                    else:
                                        P.op("pool", lambda e, xb=xb, pg=pg: e.tensor_copy(out=xb[:, 0:16], in_=hp[:, pg, :]), reads=[b_hp[pg], b_xb], writes=[b_xb])
                                    P.op("pool", lambda e, xb=xb, pg=pg: e.tensor_copy(out=hp[:, pg, :], in_=xb[:, 512:528]), reads=[b_xb], writes=[b_hp[pg]])
                                    P.op("pool", lambda e, xb=xb, sa=sa: e.tensor_tensor(out=sa[:, 1:528], in0=xb[:, 1:528], in1=xb[:, 0:527], op=ALU.add),
                                         reads=[b_xb], writes=[b_sa])
                                    fin, b_fin = sa, b_sa
                                    if wlen >= 4:
                                        P.op("pool", lambda e, sa=sa, sb_=sb_: e.tensor_tensor(out=sb_[:, 3:528], in0=sa[:, 3:528], in1=sa[:, 1:526], op=ALU.add),
                                             reads=[b_sa], writes=[b_sb])
                                        fin, b_fin = sb_, b_sb
                                    if wlen >= 8:
                                        P.op("pool", lambda e, sa=sa, sb_=sb_: e.tensor_tensor(out=sa[:, 7:528], in0=sb_[:, 7:528], in1=sb_[:, 3:524], op=ALU.add),
                                             reads=[b_sb, b_sa], writes=[b_sa])
                                        fin, b_fin = sa, b_sa
                                    if wlen >= 16:
                                        P.op("pool", lambda e, sa=sa, sb_=sb_: e.tensor_tensor(out=sb_[:, 15:528], in0=sa[:, 15:528], in1=sa[:, 7:520], op=ALU.add),
                                             reads=[b_sa, b_sb], writes=[b_sb])
                                        fin, b_fin = sb_, b_sb
                                    ip = pl_ctr[0] % 4
                                    pl_ctr[0] += 1
                                    pl, b_pl = pls[ip], b_pls[ip]
                                    sgp, b_sgp = sgps[ip], b_sgps[ip]
                                    P.op("dve", lambda e, fin=fin, xb=xb, pl=pl, wlen=wlen: e.scalar_tensor_tensor(
                                        out=pl[:], in0=fin[:, 16:528], scalar=1.0 / wlen, in1=xb[:, 16:528], op0=ALU.mult, op1=ALU.subtract),
                                        reads=[b_fin, b_xb], writes=[b_pl])
                                    if gtt == 0:
                                        t16, b_t16 = tmp()
                                        P.op("dve", lambda e, fin=fin, t16=t16, pg=pg: e.tensor_tensor(out=t16[:, 0:16], in0=fin[:, 16:32], in1=invdiv[:, pg, :], op=ALU.mult),
                                             reads=[b_fin, b_const], writes=[b_t16])
                                        P.op("dve", lambda e, t16=t16, xb=xb, pl=pl: e.tensor_tensor(out=pl[:, 0:16], in0=t16[:, 0:16], in1=xb[:, 16:32], op=ALU.subtract),
                                             reads=[b_t16, b_xb, b_pl], writes=[b_pl])
                                    P.op("act", lambda e, pgt=pgt, sgp=sgp: e.activation(out=sgp[:], in_=pgt, func=AF.Silu), reads=[b_pgt], writes=[b_sgp])
                                    tails.append((s_, pg, pl, b_pl, sgp, b_sgp))

                                def pool_tail(tails=tails, g=g, gtt=gtt, tok=tok):
                                    i = stg_ctr[0] % 2
                                    stg_ctr[0] += 1
                                    sg_, b_sg_ = stgs[i], b_stgs[i]
                                    for (s_, pg, pl, b_pl, sgp, b_sgp) in tails:
                                        py, b_py = psum()
                                        P.op("pe", lambda e, py=py, pl=pl, pg=pg: e.matmul(py, wp[:, pg, :], pl[:], start=True, stop=True),
                                             reads=[b_pl, b_sm], writes=[b_py])
                                        P.op("dve", lambda e, py=py, sgp=sgp, pg=pg, s_=s_: e.scalar_tensor_tensor(
                                            out=sg_[:, s_, :], in0=py, scalar=sm[:, 6 + pg:7 + pg], in1=sgp[:], op0=ALU.mult, op1=ALU.mult),
                                            reads=[b_py, b_sgp, b_sm], writes=[b_sg_])
                                    r0 = 8 + (g - 4) * 2
                                    P.dma("sp", mix_v[:, r0:r0 + 2, tok], sg_[:, 0:2, :], reads=[b_sg_], writes=[b_mix[r0 + c][gtt] for c in range(2)], key="stg%d" % i)

                                for fn_ in pending:
                                    fn_()
                                pending.clear()
                                pending.append(pool_tail)
                                if tt == 3:
                                    for fn_ in pending:
                                        fn_()
                                    pending.clear()
                            else:
                                j = g - 6
                                i = stg_ctr[0] % 2
                                stg_ctr[0] += 1
                                sg_, b_sg_ = stgs[i], b_stgs[i]
                                pch, b_pch = proj(w, 0, 128, tt)
                                pcc, b_pcc = proj(w, 128, 128, tt)
                                pcb, b_pcb = proj(w, 256, 128, tt)
                                pgc, b_pgc = proj(w, 384, 128, tt)
                                chs, b_chs = tmp()
                                ub, b_ub = tmp()
                                t1, b_t1 = tmp()
                                t2, b_t2 = tmp()
                                P.op("act", lambda e, pch=pch, chs=chs: e.copy(out=chs[:, 0:512], in_=pch), reads=[b_pch], writes=[b_chs])
                                P.op("dve", lambda e, pcc=pcc, chs=chs, ub=ub: e.tensor_tensor(out=ub[:, 2:514], in0=pcc, in1=chs[:, 0:512], op=ALU.mult),
                                     reads=[b_pcc, b_chs], writes=[b_ub])
                                if tt == 0:
                                    ph1, b_ph1 = proj_halo(w, 0)
                                    ph2, b_ph2 = proj_halo(w, 128)
                                    hh, b_hh = tmp()
                                    P.op("act", lambda e, ph1=ph1, hh=hh: e.copy(out=hh[:, 0:16], in_=ph1[:, 0:16]), reads=[b_ph1], writes=[b_hh])
                                    P.op("dve", lambda e, ph2=ph2, hh=hh: e.tensor_tensor(out=hh[:, 16:32], in0=ph2[:, 0:16], in1=hh[:, 0:16], op=ALU.mult),
                                         reads=[b_ph2, b_hh], writes=[b_hh])
                                    P.op("dve", lambda e, ub=ub, hh=hh: e.tensor_scalar(out=ub[:, 0:2], in0=hh[:, 30:32], scalar1=coef[:, 1:2], scalar2=None, op0=ALU.mult),
                                         reads=[b_hh, b_ub, b_const], writes=[b_ub])
                                else:
                                    P.op("pool", lambda e, ub=ub, j=j: e.tensor_copy(out=ub[:, 0:2], in_=hc[:, j, :]), reads=[b_hc[j], b_ub], writes=[b_ub])
                                P.op("pool", lambda e, ub=ub, j=j: e.tensor_copy(out=hc[:, j, :], in_=ub[:, 512:514]), reads=[b_ub], writes=[b_hc[j]])
                                cw0 = 10 + j * 3
                                P.op("dve", lambda e, ub=ub, t1=t1, cw0=cw0: e.tensor_scalar(out=t1[:, 0:512], in0=ub[:, 0:512], scalar1=sm[:, cw0:cw0 + 1], scalar2=None, op0=ALU.mult),
                                     reads=[b_ub, b_sm], writes=[b_t1])
                                P.op("dve", lambda e, ub=ub, t1=t1, t2=t2, cw0=cw0: e.scalar_tensor_tensor(
                                    out=t2[:, 0:512], in0=ub[:, 1:513], scalar=sm[:, cw0 + 1:cw0 + 2], in1=t1[:, 0:512], op0=ALU.mult, op1=ALU.add),
                                    reads=[b_ub, b_t1, b_sm], writes=[b_t2])
                                P.op("dve", lambda e, ub=ub, t1=t1, t2=t2, cw0=cw0: e.scalar_tensor_tensor(
                                    out=t1[:, 0:512], in0=ub[:, 2:514], scalar=sm[:, cw0 + 2:cw0 + 3], in1=t2[:, 0:512], op0=ALU.mult, op1=ALU.add),
                                    reads=[b_ub, b_t2, b_t1, b_sm], writes=[b_t1])
                                P.op("dve", lambda e, pcb=pcb, t1=t1, t2=t2: e.tensor_tensor(out=t2[:, 0:512], in0=pcb, in1=t1[:, 0:512], op=ALU.mult),
                                     reads=[b_pcb, b_t1, b_t2], writes=[b_t2])
                                P.op("act", lambda e, pgc=pgc, chs=chs: e.activation(out=chs[:, 0:512], in_=pgc, func=AF.Silu), reads=[b_pgc, b_chs], writes=[b_chs])
                                P.op("pool", lambda e, t2=t2, chs=chs, sg_=sg_: e.tensor_tensor(out=sg_[:, 0, :], in0=t2[:, 0:512], in1=chs[:, 0:512], op=ALU.mult),
                                     reads=[b_t2, b_chs], writes=[b_sg_])
                                r0 = 12 + j
                                P.dma("sp", mix_v[:, r0, tok], sg_[:, 0, :], reads=[b_sg_], writes=[b_mix[r0][gtt]], key="stg%d" % i)
                P.barrier(junk)

        def phase_B(L):
            with contextlib.ExitStack() as st:
                areset()
                SB_ = [4, 5, 6, 7]
                sctr = [0]
                octr = [0]
                lctr = [0]
                qns = sb("qns", [128, 4, S], BF16, st)
                kvns = sb("kvns", [128, 2, S], BF16, st)
                krs = sb("krs", [64, S], BF16, st)
                b_kvl, b_krl, b_qnl = [Buf(), Buf()], [Buf(), Buf()], [Buf(), Buf()]
                b_lc = [Buf(), Buf()]
                for rho in range(2):
                    csl = slice(rho * SO, (rho + 1) * SO)
                    P.dma("sp", kvns[:, :, csl], e1b_g[rho * 320:rho * 320 + 256, :].rearrange("(kc p) t -> p kc t", p=128), reads=[b_e1g], writes=[b_kvl[rho], b_lc[rho]], key="kvns%d" % rho)
                    P.dma("sp", krs[:, csl], e1b_g[rho * 320 + 256:rho * 320 + 320, :], reads=[b_e1g], writes=[b_krl[rho], b_lc[rho]], key="krs%d" % rho)
                    P.dma("sp", qns[:, :, csl], e1a_g[rho * 512:(rho + 1) * 512, :].rearrange("(kc p) t -> p kc t", p=128), reads=[b_e1g], writes=[b_qnl[rho], b_lc[rho]], key="qns%d" % rho)
                cosT = sb("cosA", [64, S], BF16, st)
                sinT = sb("sinA", [64, S], BF16, st)
                b_tab = Buf()
                P.dma("sp", cosT[:], cosA_d, reads=[b_tabd], writes=[b_tab], key="cosA")
                P.dma("sp", sinT[:], sinA_d, reads=[b_tabd], writes=[b_tab], key="sinA")
                wq = sb("wq", [128, 4, 1024], BF16, st)
                wk = sb("wk", [128, 2, 512], BF16, st)
                wv = sb("wv", [128, 2, 512], BF16, st)
                sm = sb("smB", [128, 32], F32, st)
                b_w = Buf()
                P.dma("sp", sm[:], sm_d[L * 128:(L + 1) * 128, :], writes=[b_w], key="smB")
                b_w1, b_w2 = Buf(), Buf()
                P.dma("pool", wq[:].rearrange("p a b -> p (a b)"), wq_d[L * 128:(L + 1) * 128, :], writes=[b_w1], key="wq")
                P.dma("pool", wk[:].rearrange("p a b -> p (a b)"), wk_d[L * 128:(L + 1) * 128, :], reads=[b_w1], writes=[b_w2], key="wk")
                P.dma("pool", wv[:].rearrange("p a b -> p (a b)"), wv_d[L * 128:(L + 1) * 128, :], reads=[b_w1, b_w2], writes=[b_w], key="wv")
                for kc in range(4):
                    P.op("dve", lambda e, kc=kc: e.tensor_scalar(out=wq[:, kc, :], in0=wq[:, kc, :], scalar1=sm[:, kc:kc + 1], scalar2=None, op0=ALU.mult),
                         reads=[b_w], writes=[b_w])
                for kc in range(2):
                    P.op("dve", lambda e, kc=kc: e.tensor_scalar(out=wk[:, kc, :], in0=wk[:, kc, :], scalar1=sm[:, 4 + kc:5 + kc], scalar2=None, op0=ALU.mult),
                         reads=[b_w], writes=[b_w])
                    P.op("dve", lambda e, kc=kc: e.tensor_scalar(out=wv[:, kc, :], in0=wv[:, kc, :], scalar1=sm[:, 4 + kc:5 + kc], scalar2=None, op0=ALU.mult),
                         reads=[b_w], writes=[b_w])
                Vq = sb("Vq", [128, 32, 512], BF16, st)
                b_V = [Buf() for i in range(32)]
                kTh = sb("kTh", [128, S], BF16, st)
                qTh = sb("qTh", [128, S], BF16, st)
                qrh = sb("qrh", [64, S], BF16, st)
                b_kT = [Buf() for i in range(8)]
                b_qT = [Buf() for i in range(8)]
                b_qr = [Buf() for i in range(8)]
                pTs = [sb("pT%d" % i, [128, 512], BF16, st) for i in range(4)]
                b_pTs = [Buf() for i in range(4)]
                pT_ctr = [0]
                rls = [sb("rl%d" % i, [128, 512], F32, st) for i in range(2)]
                b_rls = [Buf() for i in range(2)]
                outs = [sb("ob%d" % i, [128, 512], BF16, st) for i in range(2)]
                b_outs = [Buf() for i in range(2)]
                rt1 = [sb("rta%d" % i, [64, 512], F32, st) for i in range(2)]
                rt2 = [sb("rtb%d" % i, [64, 512], F32, st) for i in range(2)]
                b_rt1 = [Buf() for i in range(2)]
                b_rt2 = [Buf() for i in range(2)]
                ev = [0]

                def evac(out_ap, in_ap, reads, writes):
                    ev[0] += 1
                    if ev[0] % 2 == 0:
                        P.op("dve", lambda e: e.tensor_copy(out=out_ap, in_=in_ap), reads=reads, writes=writes)
                    else:
                        P.op("act", lambda e: e.copy(out=out_ap, in_=in_ap), reads=reads, writes=writes)

                uc = [0]
                for h in range(HL):
                    if h % 4 == 0:
                        hq = h // 4
                        for tk in range(32):
                            pt, b_pt = psum(SB_, sctr)
                            for kc in range(2):
                                P.op("pe", lambda e, pt=pt, kc=kc, tk=tk, hq=hq: e.matmul(
                                    pt, kvns[:, kc, tk * 128:(tk + 1) * 128], wv[:, kc, hq * 512:(hq + 1) * 512], start=(kc == 0), stop=(kc == 1)),
                                    reads=[b_kvl[tk // 16], b_w], writes=[b_pt])
                            evac(Vq[:, tk, :], pt, [b_pt], [b_V[tk]])
                    for tt in range(8):
                        tok = slice(tt * 512, (tt + 1) * 512)
                        pt, b_pt = psum(SB_, sctr)
                        for kc in range(2):
                            P.op("pe", lambda e, pt=pt, kc=kc, tok=tok, h=h: e.matmul(
                                pt, wk[:, kc, h * 128:(h + 1) * 128], kvns[:, kc, tok], start=(kc == 0), stop=(kc == 1)),
                                reads=[b_kvl[tt // 4], b_w], writes=[b_pt])
                        evac(kTh[:, tok], pt, [b_pt], [b_kT[tt]])
                        pt, b_pt = psum(SB_, sctr)
                        for kc in range(4):
                            P.op("pe", lambda e, pt=pt, kc=kc, tok=tok, h=h: e.matmul(
                                pt, wq[:, kc, h * 256:h * 256 + 128], qns[:, kc, tok], start=(kc == 0), stop=(kc == 3)),
                                reads=[b_qnl[tt // 4], b_w], writes=[b_pt])
                        evac(qTh[:, tok], pt, [b_pt], [b_qT[tt]])
                        pa, b_pa = psum(SB_, sctr)
                        for kc in range(4):
                            P.op("pe", lambda e, pa=pa, kc=kc, tok=tok, h=h: e.matmul(
                                pa[0:64, :], wq[:, kc, h * 256 + 128:h * 256 + 192], qns[:, kc, tok], start=(kc == 0), stop=(kc == 3)),
                                reads=[b_qnl[tt // 4], b_w], writes=[b_pa])
                        pb, b_pb = psum(SB_, sctr)
                        for kc in range(4):
                            P.op("pe", lambda e, pb=pb, kc=kc, tok=tok, h=h: e.matmul(
                                pb[0:64, :], wq[:, kc, h * 256 + 192:h * 256 + 256], qns[:, kc, tok], start=(kc == 0), stop=(kc == 3)),
                                reads=[b_qnl[tt // 4], b_w], writes=[b_pb])
                        i = tt % 2
                        P.op("dve", lambda e, pa=pa, i=i, tok=tok: e.tensor_tensor(out=rt1[i][:], in0=pa[0:64, :], in1=cosT[:, tok], op=ALU.mult),
                             reads=[b_pa, b_tab], writes=[b_rt1[i]])
                        P.op("dve", lambda e, pb=pb, i=i, tok=tok: e.tensor_tensor(out=rt2[i][:], in0=pb[0:64, :], in1=sinT[:, tok], op=ALU.mult),
                             reads=[b_pb, b_tab], writes=[b_rt2[i]])
                        P.op("pool", lambda e, i=i, tok=tok: e.tensor_tensor(out=qrh[:, tok], in0=rt1[i][:], in1=rt2[i][:], op=ALU.add),
                             reads=[b_rt1[i], b_rt2[i]], writes=[b_qr[tt]])
                    units = [(qb, kb) for qb in range(8) for kb in range(4 * qb + 4)]
                    LOOK = 2
                    acc = {}
                    sc = {}

                    def emit_scores(u):
                        qb, kb = units[u]
                        qtok = slice(qb * 512, (qb + 1) * 512)
                        ktok = slice(kb * 128, (kb + 1) * 128)
                        ps_, b_ps_ = psum(SB_, sctr)
                        diag = kb >= 4 * qb
                        P.op("pe", lambda e: e.matmul(ps_, kTh[:, ktok], qTh[:, qtok], start=True, stop=False),
                             reads=[b_kT[kb // 4], b_qT[qb]], writes=[b_ps_])
                        P.op("pe", lambda e: e.matmul(ps_, krs[:, ktok], qrh[:, qtok], start=False, stop=(not diag)),
                             reads=[b_krl[kb // 16], b_qr[qb]], writes=[b_ps_])
                        if diag:
                            jm = kb - 4 * qb
                            P.op("pe", lambda e: e.matmul(ps_, ident[:], masks[:, jm, :], start=False, stop=True),
                                 reads=[b_const], writes=[b_ps_])
                        sc[u] = (ps_, b_ps_)

                    def emit_rest(u, h=h):
                        qb, kb = units[u]
                        nkb = 4 * qb + 4
                        qtok = slice(qb * 512, (qb + 1) * 512)
                        if kb == 0:
                            acc[qb] = (psum([0, 1], octr), psum([2, 3], lctr))
                        (po, b_po), (pl_, b_pl_) = acc[qb]
                        ps_, b_ps_ = sc.pop(u)
                        ip = pT_ctr[0] % 4
                        pT_ctr[0] += 1
                        pT, b_pT = pTs[ip], b_pTs[ip]
                        P.op("act", lambda e: e.activation(out=pT[:], in_=ps_, func=AF.Exp, scale=SCALE), reads=[b_ps_], writes=[b_pT])
                        P.op("pe", lambda e: e.matmul(po, Vq[:, kb, (h % 4) * 128:(h % 4 + 1) * 128], pT[:], start=(kb == 0), stop=(kb == nkb - 1)),
                             reads=[b_V[kb], b_pT], writes=[b_po])
                        P.op("pe", lambda e: e.matmul(pl_, ones[:], pT[:], start=(kb == 0), stop=(kb == nkb - 1)),
                             reads=[b_const, b_pT], writes=[b_pl_])
                        if kb == nkb - 1:
                            i = uc[0] % 2
                            uc[0] += 1
                            P.op("dve", lambda e: e.reciprocal(out=rls[i][:], in_=pl_), reads=[b_pl_], writes=[b_rls[i]])
                            P.op("dve", lambda e: e.tensor_tensor(out=outs[i][:], in0=po, in1=rls[i][:], op=ALU.mult),
                                 reads=[b_po, b_rls[i]], writes=[b_outs[i]])
                            P.dma("sp", e2_src[h][:, qtok], outs[i][:], reads=[b_outs[i]], writes=[b_e2s[h][qb]], key="ob%d" % i)
                            if qb == 7:
                                if debug:
                                    P.dma("sp", dbg_e2[h], e2_src[h], reads=b_e2s[h], writes=[Buf()], key="dbg_e2")
                                P.collective(e2_src[h], e2_g[h], reads=b_e2s[h], writes=[b_e2g[h]], key="cc_e2_%d" % h)

                    for u in range(min(LOOK, len(units))):
                        emit_scores(u)
                    for u in range(len(units)):
                        if u + LOOK < len(units):
                            emit_scores(u + LOOK)
                        emit_rest(u)
                P.barrier(junk)

        for L in range(n_layers):
            if stop_phase == "ln0":
                break
            phase_A(L)
            if stop_phase == "A":
                break
            phase_B(L)
            if stop_phase == "B":
                break
            with contextlib.ExitStack() as st:
                last = (L == DEPTH - 1)
                ln_phase(st, L, "proj", not last, last)
                P.barrier(junk)
        P.finish()
    return nc


def _tile_k(w, ncols_pad=None):
    K, C = w.shape
    return np.ascontiguousarray(w.reshape(K // 128, 128, C).transpose(1, 0, 2))


def prep_inputs(x, positions, emb_ln_g, emb_ln_b, w_in, q_norm_g, kv_norm_g, w_uq, w_ukv, w_pool,
                pool_scale, conv_w, w_out, b_out, ln_g, ln_b):
    f32 = np.float32
    w_in = np.asarray(w_in, f32)
    offs = np.cumsum([0, 512, 256, 64, 1024, 512, 512, 512, 512, 512, 512])
    o_q, o_kv, o_kr, o_gm, o_pi, o_gp, o_ch, o_cb, o_cc, o_gc = offs[:10]
    groups = []
    zero128 = None
    for L in range(DEPTH):
        W = w_in[L]
        kr = W[:, o_kr:o_kr + 64]
        ksw = np.concatenate([kr[:, 32:64], kr[:, 0:32]], axis=1)
        g0 = np.concatenate([W[:, o_kv:o_kv + 256], kr, ksw, np.zeros((D, 128), f32)], axis=1)
        gl = [g0, W[:, o_q:o_q + 512], W[:, o_gm:o_gm + 512], W[:, o_gm + 512:o_gm + 1024]]
        for a in range(2):
            gl.append(np.concatenate([W[:, o_pi + (2 * a) * 128:o_pi + (2 * a + 1) * 128], W[:, o_gp + (2 * a) * 128:o_gp + (2 * a + 1) * 128],
                                      W[:, o_pi + (2 * a + 1) * 128:o_pi + (2 * a + 2) * 128], W[:, o_gp + (2 * a + 1) * 128:o_gp + (2 * a + 2) * 128]], axis=1))
        for j in range(4):
            sl = slice(j * 128, (j + 1) * 128)
            gl.append(np.concatenate([W[:, o_ch:o_ch + 512][:, sl], W[:, o_cc:o_cc + 512][:, sl], W[:, o_cb:o_cb + 512][:, sl], W[:, o_gc:o_gc + 512][:, sl]], axis=1))
        for gmat in gl:
            groups.append(_tile_k(gmat).reshape(128, 16 * 512))
    w_in_g = np.ascontiguousarray(np.concatenate(groups, axis=0))

    wq_l, wk_l, wv_l = [[], []], [[], []], [[], []]
    wo_l, wp_l, sm_l = [], [], []
    for L in range(DEPTH):
        wq = np.asarray(w_uq[L], f32).reshape(512, NH, 192)
        rope = wq[:, :, 128:192]
        sw = np.concatenate([rope[:, :, 32:64], rope[:, :, 0:32]], axis=2)
        wq2 = np.concatenate([wq, sw], axis=2)
        wkv = np.asarray(w_ukv[L], f32).reshape(256, NH, 256)
        for r in range(2):
            hs = slice(4 * r, 4 * r + 4)
            wq_l[r].append(_tile_k(np.ascontiguousarray(wq2[:, hs]).reshape(512, 1024)).reshape(128, 4 * 1024))
            wk_l[r].append(_tile_k(np.ascontiguousarray(wkv[:, hs, 0:128]).reshape(256, 512)).reshape(128, 2 * 512))
            wv_l[r].append(_tile_k(np.ascontiguousarray(wkv[:, hs, 128:256]).reshape(256, 512)).reshape(128, 2 * 512))
        wo_l.append(_tile_k(np.asarray(w_out[L], f32)).reshape(128, 16 * 2048))
        wp_l.append(np.ascontiguousarray(np.asarray(w_pool[L], f32).transpose(1, 0, 2)).reshape(128, 4 * 128))
        sm = np.zeros((128, 32), f32)
        sm[:, 0:4] = np.asarray(q_norm_g[L], f32).reshape(4, 128).T
        sm[:, 4:6] = np.asarray(kv_norm_g[L], f32).reshape(2, 128).T
        sm[:, 6:10] = np.asarray(pool_scale[L], f32).reshape(4, 128).T
        cw = np.asarray(conv_w[L], f32).reshape(3, 4, 128)
        sm[:, 10:22] = cw.transpose(2, 1, 0).reshape(128, 12)
        sm_l.append(sm)
    lnp = np.stack([np.asarray(emb_ln_g, f32), np.asarray(emb_ln_b, f32)] +
                   sum([[np.asarray(ln_g[L], f32), np.asarray(ln_b[L], f32), np.asarray(b_out[L], f32)] for L in range(DEPTH)], []), axis=0)
    half = 32
    inv_freq = (10000.0 ** (-np.arange(half, dtype=np.float32) / half)).astype(f32)
    ropec = np.zeros((128, 2), f32)
    ropec[:, 0] = np.concatenate([inv_freq] * 4)
    ropec[:, 1] = np.concatenate([-np.ones(32, f32), np.ones(32, f32)] * 2)
    invdiv = np.zeros((2, 128, 4, 16), f32)
    for g, w in enumerate(POOL_W):
        invdiv[0, :, g, :] = 1.0 / np.minimum(np.arange(1, 17, dtype=f32), float(w))
        invdiv[1, :, g, :] = 1.0 / float(w)
    ident = np.eye(128, dtype=f32).astype(ml_dtypes.bfloat16)
    kk = np.arange(128)[:, None]
    qq = np.arange(512)[None, :]
    masks = np.stack([np.where(j * 128 + kk <= qq, 0.0, NEG) for j in range(4)], axis=1).astype(f32)
    masks = masks.reshape(128, 4 * 512).astype(ml_dtypes.bfloat16)
    shared = {
        "ropec": ropec, "lnp": np.ascontiguousarray(lnp), "w_in_g": w_in_g,
        "wo": np.ascontiguousarray(np.concatenate(wo_l, 0)),
        "wp": np.ascontiguousarray(np.concatenate(wp_l, 0)), "small": np.ascontiguousarray(np.concatenate(sm_l, 0)),
        "ident": ident, "masks": masks,
    }
    per_rank = []
    for r in range(2):
        coef = np.zeros((128, 2), f32)
        coef[:, r] = 1.0
        per_rank.append({
            "wq": np.ascontiguousarray(np.concatenate(wq_l[r], 0)), "wk": np.ascontiguousarray(np.concatenate(wk_l[r], 0)),
            "wv": np.ascontiguousarray(np.concatenate(wv_l[r], 0)), "invdiv": np.ascontiguousarray(invdiv[r].reshape(128, 64)),
            "coef": coef,
        })
    x = np.asarray(x, f32)
    positions = np.asarray(positions, np.int32)
    in_maps = []
    for c in range(8):
        b, r = c // 2, c % 2
        m = dict(shared)
        m.update(per_rank[r])
        m["x"] = np.ascontiguousarray(x[b, r * SO:(r + 1) * SO])
        m["pos"] = np.ascontiguousarray(positions[b][None, :])
        m["pos_own"] = np.ascontiguousarray(positions[b, r * SO:(r + 1) * SO][None, :])
        in_maps.append(m)
    return in_maps


def kernel(**inputs):
    in_maps = prep_inputs(**inputs)
    nc = build()
    res = run_bass_kernel_spmd(nc, in_maps, core_ids=list(range(8)))
    out = np.empty((4, S, D), np.float32)
    for c in range(8):
        b, r = c // 2, c % 2
        out[b, r * SO:(r + 1) * SO] = np.asarray(res.results[c]["out"], dtype=np.float32)
    return out
```

```python
import math
import contextlib
import numpy as np
import ml_dtypes
import concourse.bass as bass
import concourse.mybir as mybir
from concourse.bass_utils import run_bass_kernel_spmd

F32 = mybir.dt.float32
BF16 = mybir.dt.bfloat16
I32 = mybir.dt.int32
AF = mybir.ActivationFunctionType
ALU = mybir.AluOpType

S = 4096
SO = 2048
HL = 4
PAIRS = [[0, 1], [2, 3], [4, 5], [6, 7]]
D = 2048
DEPTH = 2
NH = 8
LN_EPS = 1e-5
RMS_EPS = 1e-6
ALPHA = (2 * DEPTH) ** 0.25
SCALE = 192 ** -0.5
NEG = -30000.0
POOL_W = (2, 4, 8, 16)
NG = 10

ENGS = ("pe", "act", "dve", "pool", "sp")


class Buf:
    __slots__ = ("name", "w", "r")

    def __init__(self, name=""):
        self.name = name
        self.w = None
        self.r = []


class Op:
    __slots__ = ("eng", "fn", "waits", "flag", "dma_key", "dma_val", "seq")

    def __init__(self, eng, fn):
        self.eng = eng
        self.fn = fn
        self.waits = []
        self.flag = False
        self.dma_key = None
        self.dma_val = 0
        self.seq = -1


class Prog:
    def __init__(self, nc):
        self.nc = nc
        self.ops = {e: [] for e in ENGS}
        self.seen = {e: {} for e in ENGS}
        self.seen_dma = {e: {} for e in ENGS}
        self.dma_counts = {}
        self.cc_keys = set()
        self.jb = [Buf(), Buf(), Buf()]

    def _add(self, eng, fn, reads, writes, dma_key=None):
        op = Op(eng, fn)
        op.seq = len(self.ops[eng])
        deps = []
        for b in reads:
            if b.w is not None:
                deps.append(b.w)
        for b in writes:
            if b.w is not None:
                deps.append(b.w)
            deps.extend(b.r)
        best = {}
        dma_deps = {}
        for d in deps:
            if d.dma_key is not None:
                if dma_deps.get(d.dma_key, 0) < d.dma_val:
                    dma_deps[d.dma_key] = d.dma_val
            else:
                if d.eng == eng and eng == "pe":
                    continue
                if best.get(d.eng, -1) < d.seq:
                    best[d.eng] = d.seq
        for f, s in best.items():
            if self.seen[eng].get(f, -1) >= s:
                continue
            self.seen[eng][f] = s
            dop = self.ops[f][s]
            dop.flag = True
            op.waits.append(("eng", f, dop))
        for k, v in dma_deps.items():
            if self.seen_dma[eng].get(k, 0) >= v:
                continue
            self.seen_dma[eng][k] = v
            op.waits.append(("dma", k, v))
        if dma_key is not None:
            op.dma_key = dma_key
            self.dma_counts[dma_key] = self.dma_counts.get(dma_key, 0) + 16
            op.dma_val = self.dma_counts[dma_key]
        for b in reads:
            b.r.append(op)
        for b in writes:
            b.w = op
            b.r = []
        self.ops[eng].append(op)
        return op

    def op(self, eng, fn, reads=(), writes=()):
        return self._add(eng, fn, reads, writes, None)

    def dma(self, eng, out, in_, reads=(), writes=(), key=None):
        def fn(e):
            return e.dma_start(out=out, in_=in_)
        return self._add(eng, fn, reads, writes, key)

    def collective(self, src, dst, reads, writes, key):
        def fn(e):
            return e.collective_compute("AllGather", ALU.bypass, replica_groups=PAIRS, ins=[src.opt()], outs=[dst.opt()])
        o = self._add("pool", fn, reads, writes, key)
        self.dma_counts[key] -= 15
        o.dma_val = self.dma_counts[key]
        self.cc_keys.add(key)
        return o

    def barrier(self, junk):
        marks = []
        jb = self.jb
        b = Buf()
        self.op("act", lambda e: e.activation(out=junk[:, 0:1], in_=junk[:, 4:5], func=AF.Copy), writes=[b, jb[0]])
        marks.append(b)
        b = Buf()
        self.op("dve", lambda e: e.memset(junk[:, 1:2], 0.0), writes=[b, jb[1]])
        marks.append(b)
        b = Buf()
        self.op("pool", lambda e: e.memset(junk[:, 2:3], 0.0), writes=[b, jb[2]])
        marks.append(b)
        fence = Buf()
        o = self.op("sp", lambda e: e.nop(), reads=marks, writes=[fence])
        for k, v in self.dma_counts.items():
            if self.seen_dma["sp"].get(k, 0) < v:
                self.seen_dma["sp"][k] = v
                o.waits.append(("dma", k, v))
        self.op("act", lambda e: e.activation(out=junk[:, 0:1], in_=junk[:, 4:5], func=AF.Copy), reads=[fence], writes=[jb[0]])
        self.op("dve", lambda e: e.memset(junk[:, 1:2], 0.0), reads=[fence], writes=[jb[1]])
        self.op("pool", lambda e: e.memset(junk[:, 2:3], 0.0), reads=[fence], writes=[jb[2]])
        self.op("pe", lambda e: e.nop(), reads=[fence])
        for e in ENGS:
            for k, v in self.dma_counts.items():
                if self.seen_dma[e].get(k, 0) < v:
                    self.seen_dma[e][k] = v

    def finish(self):
        nc = self.nc
        with contextlib.ExitStack() as st:
            esem = {e: st.enter_context(nc.semaphore("s_" + e)) for e in ENGS}
            dsem = {k: st.enter_context(nc.semaphore("d_%s" % (k,))) for k in self.dma_counts}
            block = st.enter_context(nc.Block())
            for e in ENGS:
                c = 0
                for o in self.ops[e]:
                    if o.flag:
                        c += 1
                        o.dma_val = c

            def emit(e, eng):
                for o in self.ops[e]:
                    for kind, k, v in o.waits:
                        if kind == "eng":
                            eng.wait_ge(esem[k], v.dma_val)
                        else:
                            eng.wait_ge(dsem[k], v)
                    inst = o.fn(eng)
                    if o.dma_key is not None and o.dma_key in self.cc_keys:
                        inst.then_inc(dsem[o.dma_key])
                    elif o.dma_key is not None:
                        inst.then_inc(dsem[o.dma_key], 16)
                    elif o.flag:
                        inst.then_inc(esem[e], 1)
                if e == "sp":
                    for k, v in self.dma_counts.items():
                        eng.wait_ge(dsem[k], v)

            @block.tensor
            def _(eng):
                emit("pe", eng)

            @block.scalar
            def _(eng):
                emit("act", eng)

            @block.vector
            def _(eng):
                emit("dve", eng)

            @block.gpsimd
            def _(eng):
                emit("pool", eng)

            @block.sync
            def _(eng):
                emit("sp", eng)


def build(debug=False, n_layers=DEPTH, stop_phase=None):
    nc = bass.Bass("TRN2", target_bir_lowering=False)
    P = Prog(nc)

    def din(name, shape, dt):
        return nc.dram_tensor(name, shape, dt, kind="ExternalInput").ap()

    x_d = din("x", [SO, D], F32)
    pos_d = din("pos", [1, S], I32)
    poso_d = din("pos_own", [1, SO], I32)
    coef_d = din("coef", [128, 2], F32)
    rc_d = din("ropec", [128, 2], F32)
    lnp_d = din("lnp", [2 + 3 * DEPTH, D], F32)
    win_d = din("w_in_g", [DEPTH * NG * 128, 16 * 512], F32)
    wq_d = din("wq", [DEPTH * 128, 4 * 1024], F32)
    wk_d = din("wk", [DEPTH * 128, 2 * 512], F32)
    wv_d = din("wv", [DEPTH * 128, 2 * 512], F32)
    wo_d = din("wo", [DEPTH * 128, 16 * 2048], F32)
    wp_d = din("wp", [DEPTH * 128, 4 * 128], F32)
    sm_d = din("small", [DEPTH * 128, 32], F32)
    idv_d = din("invdiv", [128, 64], F32)
    ident_d = din("ident", [128, 128], BF16)
    mask_d = din("masks", [128, 4 * 512], BF16)
    out_d = nc.dram_tensor("out", [SO, D], F32, kind="ExternalOutput").ap()

    skind = "ExternalOutput" if debug else "Internal"
    resid_d = nc.dram_tensor("resid", [SO, D], F32, kind=skind).ap()
    hT_d = nc.dram_tensor("hT", [D, SO], BF16, kind=skind).ap()
    mix_d = nc.dram_tensor("mixT", [D, SO], BF16, kind=skind).ap()
    cosA_d = nc.dram_tensor("cosA", [64, S], BF16).ap()
    sinA_d = nc.dram_tensor("sinA", [64, S], BF16).ap()
    cosO_d = nc.dram_tensor("cosO", [64, SO], BF16).ap()
    sinO_d = nc.dram_tensor("sinO", [64, SO], BF16).ap()
    e1a_src = nc.dram_tensor("e1a_src", [512, SO], BF16).ap()
    e1a_g = nc.dram_tensor("e1a_g", [1024, SO], BF16).ap()
    e1b_src = nc.dram_tensor("e1b_src", [320, SO], BF16).ap()
    e1b_g = nc.dram_tensor("e1b_g", [640, SO], BF16).ap()
    e2_src = [nc.dram_tensor("e2_src%d" % i, [128, S], BF16).ap() for i in range(4)]
    e2_gall = nc.dram_tensor("e2_gall", [4 * 256, S], BF16).ap()
    e2_g = [e2_gall[i * 256:(i + 1) * 256, :] for i in range(4)]
    e2g_v = e2_gall.rearrange("(h r p) t -> p h r t", h=4, r=2)
    tl_src = nc.dram_tensor("tl_src", [D, 16], BF16).ap()
    tl_g = nc.dram_tensor("tl_g", [2 * D, 16], BF16).ap()
    if debug:
        dbg_e1a = nc.dram_tensor("dbg_e1a", [512, SO], BF16, kind="ExternalOutput").ap()
        dbg_e1b = nc.dram_tensor("dbg_e1b", [320, SO], BF16, kind="ExternalOutput").ap()
        dbg_e2 = [nc.dram_tensor("dbg_e2_%d" % i, [128, S], BF16, kind="ExternalOutput").ap() for i in range(4)]

    b_resid = [Buf("resid%d" % i) for i in range(16)]
    b_hT = [Buf("hT%d" % i) for i in range(4)]
    b_qn = [Buf() for i in range(4)]
    b_kvn = [Buf() for i in range(4)]
    b_kr = [Buf() for i in range(4)]
    b_mix = [[Buf() for t in range(4)] for r in range(16)]
    b_out = [Buf() for i in range(16)]
    b_e1g, b_tlsrc, b_tlg = Buf(), Buf(), Buf()
    b_e2g = [Buf() for i in range(4)]
    b_e2s = [[Buf() for t in range(8)] for r in range(4)]
    b_tabd = Buf()

    hT_v = hT_d.rearrange("(kc p) t -> p kc t", p=128)
    mix_v = mix_d.rearrange("(kc p) t -> p kc t", p=128)
    qn_v = e1a_src.rearrange("(kc p) t -> p kc t", p=128)
    kvn_v = e1b_src[0:256, :].rearrange("(kc p) t -> p kc t", p=128)
    kr_d = e1b_src[256:320, :]
    tls_v = tl_src.rearrange("(kc p) t -> p kc t", p=128)
    tlg_v = tl_g.rearrange("(kc p) t -> p kc t", p=128)

    with contextlib.ExitStack() as gst:
        ARENA_WORDS = 51200
        arena = gst.enter_context(nc.sbuf_tensor("arena", [128, ARENA_WORDS], F32))
        a_top = [0]
        a_mark = [0]

        def sb(name, shape, dt, st=None):
            n = 1
            for d_ in shape[1:]:
                n *= d_
            esz = 4 if dt in (F32, I32) else 2
            words = (n * esz + 3) // 4
            words = (words + 7) // 8 * 8
            off = a_top[0]
            assert off + words <= ARENA_WORDS, ("SBUF arena overflow", name, off, words)
            a_top[0] = off + words
            v = arena[0:shape[0], off:off + words]
            if dt != F32:
                v = v.bitcast(dt)
            v = v[:, 0:n]
            if len(shape) == 3:
                v = v.rearrange("p (a b) -> p a b", a=shape[1])
            return v

        def areset():
            a_top[0] = a_mark[0]

        ps_all = gst.enter_context(nc.psum_tensor("ps", [128, 8 * 512], F32))
        ps_bufs = [Buf("ps%d" % i) for i in range(8)]
        ps_ctr = [0]

        def psum(banks=None, ctr=None):
            if banks is None:
                i = ps_ctr[0] % 8
                ps_ctr[0] += 1
            else:
                i = banks[ctr[0] % len(banks)]
                ctr[0] += 1
            return ps_all[:, i * 512:(i + 1) * 512], ps_bufs[i]

        junk = sb("junk", [128, 8], F32)
        ident = sb("ident", [128, 128], BF16)
        ones = sb("ones", [128, 128], BF16)
        masks = sb("masks", [128, 4, 512], BF16)
        rc = sb("rc", [128, 2], F32)
        coef = sb("coef", [128, 2], F32)
        invdiv = sb("invdiv", [128, 4, 16], F32)
        b_const = Buf("const")
        b_tab = Buf("tab")

        P.op("dve", lambda e: e.memset(junk[:], 0.0), writes=[b_const] + P.jb)
        P.op("dve", lambda e: e.memset(ones[:], 1.0), writes=[b_const])
        P.dma("sp", ident[:], ident_d, writes=[b_const], key="c_ident")
        P.dma("sp", masks[:].rearrange("p a b -> p (a b)"), mask_d, writes=[b_const], key="c_mask")
        P.dma("sp", rc[:], rc_d, writes=[b_const], key="c_rc")
        P.dma("sp", coef[:], coef_d, writes=[b_const], key="c_coef")
        P.dma("sp", invdiv[:].rearrange("p a b -> p (a b)"), idv_d, writes=[b_const], key="c_idv")

        LN_EPS_AP = sb("lneps", [128, 4], F32)
        a_mark[0] = a_top[0]

        def rope_tables(pos_ap, N, cos_dst, sin_dst, tag):
            H = N // 2
            posi = sb("posi" + tag, [128, H], I32)
            ang = sb("ang" + tag, [128, H], F32)
            ta = sb("ta" + tag, [128, H], F32)
            tb = sb("tb" + tag, [128, H], F32)
            obs = [sb("ob%d" % i + tag, [128, H], BF16) for i in range(2)]
            b_posi, b_ang, b_ta, b_tb = Buf(), Buf(), Buf(), Buf()
            b_obs = [Buf(), Buf()]
            P.dma("sp", posi[0:64, :], pos_ap[:, 0:H].partition_broadcast(64), writes=[b_posi], key="t_pos0" + tag)
            P.dma("sp", posi[64:128, :], pos_ap[:, H:N].partition_broadcast(64), writes=[b_posi], key="t_pos1" + tag)
            P.op("dve", lambda e: e.tensor_copy(out=ang[:], in_=posi[:]), reads=[b_posi], writes=[b_ang])
            P.op("dve", lambda e: e.tensor_scalar(out=ang[:], in0=ang[:], scalar1=rc[:, 0:1], scalar2=None, op0=ALU.mult),
                 reads=[b_ang, b_const], writes=[b_ang])
            TWO_PI = 2.0 * math.pi
            for wi_, (which, phase, dst) in enumerate((("sin", 0.0, sin_dst), ("cos", math.pi / 2, cos_dst))):
                ob, b_ob = obs[wi_], b_obs[wi_]
                P.op("dve", lambda e, phase=phase: e.tensor_scalar(out=ta[:], in0=ang[:], scalar1=phase, scalar2=1.0 / TWO_PI,
                                                                    op0=ALU.add, op1=ALU.mult), reads=[b_ang], writes=[b_ta])
                P.op("dve", lambda e: e.tensor_copy(out=posi[:], in_=ta[:]), reads=[b_ta], writes=[b_posi])
                P.op("dve", lambda e: e.tensor_copy(out=ta[:], in_=posi[:]), reads=[b_posi], writes=[b_ta])
                P.op("dve", lambda e: e.scalar_tensor_tensor(out=tb[:], in0=ta[:], scalar=-TWO_PI, in1=ang[:], op0=ALU.mult, op1=ALU.add),
                     reads=[b_ta, b_ang], writes=[b_tb])
                P.op("dve", lambda e, phase=phase: e.tensor_scalar(out=tb[:], in0=tb[:], scalar1=phase, scalar2=None, op0=ALU.add),
                     reads=[b_tb], writes=[b_tb])
                P.op("dve", lambda e: e.tensor_scalar(out=ta[:], in0=tb[:], scalar1=math.pi, scalar2=TWO_PI, op0=ALU.is_gt, op1=ALU.mult),
                     reads=[b_tb], writes=[b_ta])
                P.op("dve", lambda e: e.tensor_tensor(out=tb[:], in0=tb[:], in1=ta[:], op=ALU.subtract), reads=[b_tb, b_ta], writes=[b_tb])
                P.op("dve", lambda e: e.tensor_scalar(out=tb[:], in0=tb[:], scalar1=math.pi, scalar2=-math.pi, op0=ALU.min, op1=ALU.max),
                     reads=[b_tb], writes=[b_tb])
                P.op("act", lambda e: e.activation(out=ta[:], in_=tb[:], func=AF.Sin), reads=[b_tb], writes=[b_ta])
                if which == "sin":
                    P.op("dve", lambda e, ob=ob: e.tensor_scalar(out=ob[:], in0=ta[:], scalar1=rc[:, 1:2], scalar2=None, op0=ALU.mult),
                         reads=[b_ta, b_const], writes=[b_ob])
                else:
                    P.op("dve", lambda e, ob=ob: e.tensor_copy(out=ob[:], in_=ta[:]), reads=[b_ta], writes=[b_ob])
                P.dma("sp", dst[:, 0:H], ob[0:64, :], reads=[b_ob], writes=[b_tabd], key="t_ob0%d" % wi_ + tag)
                P.dma("sp", dst[:, H:N], ob[64:128, :], reads=[b_ob], writes=[b_tabd], key="t_ob1%d" % wi_ + tag)

        areset()
        rope_tables(pos_d, S, cosA_d, sinA_d, "a")
        rope_tables(poso_d, SO, cosO_d, sinO_d, "o")

        def ln_phase(st, layer_idx, src_kind, write_hT, final):
            if src_kind != "x":
                areset()
            NB = 4 if src_kind == "x" else 3
            gb = sb("ln_g", [128, D], F32, st)
            bb = sb("ln_b", [128, D], F32, st)
            b_p = Buf()
            if src_kind == "x":
                grow, brow = 0, 1
            else:
                grow, brow = 2 + 3 * layer_idx, 3 + 3 * layer_idx
            b_pg, b_pb = Buf(), Buf()
            P.dma("sp", gb[:], lnp_d[grow:grow + 1, :].partition_broadcast(128), writes=[b_pg], key="ln_g")
            P.dma("sp", bb[:], lnp_d[brow:brow + 1, :].partition_broadcast(128), writes=[b_pb], key="ln_b")
            ys = [sb("ln_y%d" % i, [128, D], F32, st) for i in range(NB)]
            b_ys = [Buf() for i in range(NB)]
            hbs = [sb("ln_hb%d" % i, [128, D], BF16, st) for i in range(2)]
            b_hbs = [Buf() for i in range(2)]
            stats = [sb("ln_st%d" % i, [128, 4, 6], F32, st) for i in range(NB)]
            mvs = [sb("ln_mv%d" % i, [128, 4], F32, st) for i in range(NB)]
            b_sts = [Buf() for i in range(NB)]
            stg = [sb("ln_stg%d" % i, [128, 16, 512], BF16, st) for i in range(1)] if write_hT else []
            b_stg = [Buf() for i in range(1)]
            if src_kind == "proj":
                bo = sb("ln_bo", [1, D], BF16, st)
                P.dma("pool", bo[:], lnp_d[4 + 3 * layer_idx:5 + 3 * layer_idx, :], writes=[b_p], key="ln_bo")
                wo = sb("wo", [128, 16, 2048], BF16, st)
                b_wop = [Buf() for i in range(4)]
                for i4 in range(4):
                    P.dma("pool", wo[:, i4 * 4:(i4 + 1) * 4, :].rearrange("p a b -> p (a b)"),
                          wo_d[layer_idx * 128:(layer_idx + 1) * 128, i4 * 8192:(i4 + 1) * 8192],
                          reads=([b_wop[i4 - 1]] if i4 else [b_p]), writes=[b_wop[i4]], key="wo%d" % i4)
                mts = [sb("mt%d" % i, [128, 16, 256], BF16, st) for i in range(2)]
                b_mts = [Buf() for i in range(2)]
                NR = 2
                rts = [sb("rt%d" % i, [128, D], F32, st) for i in range(NR)]
                b_rts = [Buf() for i in range(NR)]
                ea = [sb("ea%d" % i, [128, 8, 256], BF16, st) for i in range(2)]
                eb = [sb("eb%d" % i, [128, 8, 256], BF16, st) for i in range(2)]
                b_ea = [Buf() for i in range(2)]
                b_eb = [Buf() for i in range(2)]
                et = sb("et", [128, 8, 256], BF16, st)
                b_et = Buf()

            def s_pair(tk):
                tt = tk // 4
                t2 = tk // 2
                i2 = t2 % 2
                mt, b_mt = mts[i2], b_mts[i2]
                P.dma("sp", mt[:], mix_v[:, :, t2 * 256:(t2 + 1) * 256], reads=[b_mix[r][tt] for r in range(16)],
                      writes=[b_mt], key="mt%d" % i2)
                P.dma("sp", ea[i2][:].rearrange("p (h r) t -> p h r t", r=2), e2g_v[:, :, :, t2 * 256:(t2 + 1) * 256],
                      reads=b_e2g, writes=[b_ea[i2]], key="ea%d" % i2)
                P.dma("sp", eb[i2][:].rearrange("p (h r) t -> p h r t", r=2), e2g_v[:, :, :, SO + t2 * 256:SO + (t2 + 1) * 256],
                      reads=b_e2g, writes=[b_eb[i2]], key="eb%d" % i2)

            def s_blend(tk):
                i2 = (tk // 2) % 2
                mt, b_mt = mts[i2], b_mts[i2]
                P.op("dve", lambda e: e.tensor_scalar(out=et[:], in0=ea[i2][:], scalar1=coef[:, 0:1], scalar2=None, op0=ALU.mult),
                     reads=[b_ea[i2], b_const], writes=[b_et])
                P.op("dve", lambda e: e.scalar_tensor_tensor(out=et[:], in0=eb[i2][:], scalar=coef[:, 1:2], in1=et[:], op0=ALU.mult, op1=ALU.add),
                     reads=[b_eb[i2], b_et, b_const], writes=[b_et])
                mt8 = mt[:, 0:8, :].rearrange("p (r h) t -> p h r t", r=2)
                etv = et[:].rearrange("p (h r) t -> p h r t", r=2)
                P.op("pool", lambda e: e.tensor_tensor(out=mt8, in0=mt8, in1=etv, op=ALU.mult),
                     reads=[b_et, b_mt], writes=[b_mt])

            def s_load(tk):
                y, b_y = ys[tk % NB], b_ys[tk % NB]
                tsl = slice(tk * 128, (tk + 1) * 128)
                if src_kind == "x":
                    P.dma("sp", y[:], x_d[tsl, :], writes=[b_y], key="ln_y%d" % (tk % NB))
                else:
                    rt, b_rt = rts[tk % NR], b_rts[tk % NR]
                    P.dma("sp", rt[:], resid_d[tsl, :], reads=[b_resid[tk]], writes=[b_rt], key="rt%d" % (tk % NR))

            def s1(tk):
                y, b_y = ys[tk % NB], b_ys[tk % NB]
                stt, mv, b_st = stats[tk % NB], mvs[tk % NB], b_sts[tk % NB]
                if src_kind != "x":
                    t2 = tk // 2
                    mt, b_mt = mts[t2 % 2], b_mts[t2 % 2]
                    rt, b_rt = rts[tk % NR], b_rts[tk % NR]
                    pts = [psum() for cg in range(4)]
                    for cg in range(4):
                        pt, b_pt = pts[cg]
                        P.op("pe", lambda e, pt=pt, cg=cg: e.matmul(pt, ones[0:1, :], bo[0:1, cg * 512:(cg + 1) * 512], start=True, stop=False),
                             reads=[b_p, b_const], writes=[b_pt])
                    for kc in range(16):
                        for cg in range(4):
                            pt, b_pt = pts[cg]
                            P.op("pe", lambda e, pt=pt, kc=kc, cg=cg: e.matmul(
                                pt, mt[:, kc, (tk % 2) * 128:(tk % 2 + 1) * 128], wo[:, kc, cg * 512:(cg + 1) * 512],
                                start=False, stop=(kc == 15)), reads=[b_mt, b_wop[kc // 4]], writes=[b_pt])
                    for cg in range(4):
                        pt, b_pt = pts[cg]
                        P.op("dve", lambda e, pt=pt, cg=cg: e.scalar_tensor_tensor(
                            out=y[:, cg * 512:(cg + 1) * 512], in0=rt[:, cg * 512:(cg + 1) * 512], scalar=ALPHA, in1=pt,
                            op0=ALU.mult, op1=ALU.add), reads=[b_pt, b_rt], writes=[b_y])
                for c in range(4):
                    P.op("dve", lambda e, c=c: e.bn_stats(out=stt[:, c, :], in_=y[:, c * 512:(c + 1) * 512]),
                         reads=[b_y], writes=[b_st])
                P.op("dve", lambda e: e.bn_aggr(out=mv[:, 0:2], in_=stt[:].rearrange("p a b -> p (a b)")),
                     reads=[b_st], writes=[b_st])
                P.op("act", lambda e: e.activation(out=mv[:, 2:3], in_=mv[:, 1:2], func=AF.Sqrt, bias=LN_EPS_AP[:, 0:1], scale=1.0),
                     reads=[b_st, b_const], writes=[b_st])
                P.op("dve", lambda e: e.reciprocal(out=mv[:, 2:3], in_=mv[:, 2:3]), reads=[b_st], writes=[b_st])
                P.op("dve", lambda e: e.scalar_tensor_tensor(out=mv[:, 3:4], in0=mv[:, 0:1], scalar=-1.0, in1=mv[:, 2:3],
                                                              op0=ALU.mult, op1=ALU.mult), reads=[b_st], writes=[b_st])

            def s2a(tk):
                y, b_y = ys[tk % NB], b_ys[tk % NB]
                mv, b_st = mvs[tk % NB], b_sts[tk % NB]
                P.op("act", lambda e: e.activation(out=y[:], in_=y[:], func=AF.Identity, bias=mv[:, 3:4], scale=mv[:, 2:3]),
                     reads=[b_y, b_st], writes=[b_y])

            def s2b(tk):
                y, b_y = ys[tk % NB], b_ys[tk % NB]
                P.op("dve", lambda e: e.tensor_tensor(out=y[:], in0=y[:], in1=gb[:], op=ALU.mult), reads=[b_y, b_pg], writes=[b_y])
                P.op("pool", lambda e: e.tensor_tensor(out=y[:], in0=y[:], in1=bb[:], op=ALU.add), reads=[b_y, b_pb], writes=[b_y])

            def s3(tk):
                y, b_y = ys[tk % NB], b_ys[tk % NB]
                hb, b_hb = hbs[tk % 2], b_hbs[tk % 2]
                tsl = slice(tk * 128, (tk + 1) * 128)
                if final:
                    P.dma("sp", out_d[tsl, :], y[:], reads=[b_y], writes=[b_out[tk]], key="ln_o%d" % (tk % NB))
                else:
                    P.dma("sp", resid_d[tsl, :], y[:], reads=[b_y], writes=[b_resid[tk]], key="ln_o%d" % (tk % NB))
                if write_hT:
                    P.op("act", lambda e: e.copy(out=hb[:], in_=y[:]), reads=[b_y], writes=[b_hb])
                    sg_, b_sg_ = stg[0], b_stg[0]
                    for q4 in range(4):
                        pt, b_pt = psum()
                        ptb = pt.bitcast(BF16)
                        for j in range(4):
                            kc = q4 * 4 + j
                            P.op("pe", lambda e, ptb=ptb, kc=kc, j=j: e.transpose(
                                ptb[:, j * 128:(j + 1) * 128], hb[:, kc * 128:(kc + 1) * 128], ident[:]),
                                reads=[b_hb, b_const], writes=[b_pt])
                        if q4 % 2 == 0:
                            P.op("dve", lambda e, ptb=ptb, q4=q4: e.tensor_copy(
                                out=sg_[:, q4 * 4:(q4 + 1) * 4, (tk % 4) * 128:(tk % 4 + 1) * 128],
                                in_=ptb[:, 0:512].rearrange("p (a b) -> p a b", a=4)), reads=[b_pt], writes=[b_sg_])
                        else:
                            P.op("act", lambda e, ptb=ptb, q4=q4: e.copy(
                                out=sg_[:, q4 * 4:(q4 + 1) * 4, (tk % 4) * 128:(tk % 4 + 1) * 128],
                                in_=ptb[:, 0:512].rearrange("p (a b) -> p a b", a=4)), reads=[b_pt], writes=[b_sg_])
                    if tk % 4 == 3:
                        tt = tk // 4
                        P.dma("sp", hT_v[:, :, tt * 512:(tt + 1) * 512], sg_[:], reads=[b_sg_], writes=[b_hT[tt]], key="ln_stg")
                        if tk == NTK - 1:
                            P.dma("sp", tls_v, sg_[:, :, 496:512], reads=[b_sg_], writes=[b_tlsrc], key="ln_tl")
                            P.collective(tl_src, tl_g, reads=[b_tlsrc], writes=[b_tlg], key="cc_e3")

            NTK = SO // 128
            is_proj = (src_kind != "x")
            if is_proj:
                s_pair(0)
                s_blend(0)
            s_load(0)
            for i in range(NTK + 2):
                if is_proj and i % 2 == 0 and i + 2 < NTK:
                    s_pair(i + 2)
                if is_proj and i % 2 == 1 and i + 1 < NTK:
                    s_blend(i + 1)
                if i + 1 < NTK:
                    s_load(i + 1)
                if 0 <= i - 1 < NTK:
                    s2a(i - 1)
                if i < NTK:
                    s1(i)
                if 0 <= i - 1 < NTK:
                    s2b(i - 1)
                if 0 <= i - 2 < NTK:
                    s3(i - 2)

        P.op("dve", lambda e: e.memset(LN_EPS_AP[:, 0:1], LN_EPS), writes=[b_const])
        P.op("dve", lambda e: e.memset(LN_EPS_AP[:, 1:2], RMS_EPS), writes=[b_const])

        with contextlib.ExitStack() as st:
            ln_phase(st, 0, "x", True, False)
            P.barrier(junk)

        def phase_A(L):
            with contextlib.ExitStack() as st:
                areset()
                hTs = sb("hTs", [128, 16, 2048], BF16, st)
                b_hTs = Buf()
                wr = [sb("wr%d" % i, [128, 16, 512], BF16, st) for i in range(2)]
                b_wr = [Buf() for i in range(2)]
                sm = sb("sm", [128, 32], F32, st)
                wp = sb("wp", [128, 4, 128], BF16, st)
                b_sm = Buf()
                P.dma("sp", sm[:], sm_d[L * 128:(L + 1) * 128, :], writes=[b_sm], key="sm")
                P.dma("pool", wp[:].rearrange("p a b -> p (a b)"), wp_d[L * 128:(L + 1) * 128, :], writes=[b_sm], key="wp")
                hp = sb("hp", [128, 4, 16], F32, st)
                hc = sb("hc", [128, 4, 2], F32, st)
                b_hp = [Buf() for i in range(4)]
                b_hc = [Buf() for i in range(4)]
                P.op("pool", lambda e: e.memset(hp[:], 0.0), writes=b_hp)
                P.op("pool", lambda e: e.memset(hc[:], 0.0), writes=b_hc)
                stgs = [sb("stg%d" % i, [128, 4, 512], BF16, st) for i in range(2)]
                b_stgs = [Buf() for i in range(2)]
                stg_ctr = [0]
                NTMP = 6
                tmps = [sb("tmp%d" % i, [128, 528], F32, st) for i in range(NTMP)]
                b_tmps = [Buf() for i in range(NTMP)]
                tmp_ctr = [0]
                sqs = [sb("sq%d" % i, [128, 512], BF16, st) for i in range(2)]
                b_sqs = [Buf() for i in range(2)]
                sq_ctr = [0]
                pls = [sb("pl%d" % i, [128, 512], BF16, st) for i in range(4)]
                b_pls = [Buf() for i in range(4)]
                sgps = [sb("sgp%d" % i, [128, 512], F32, st) for i in range(4)]
                b_sgps = [Buf() for i in range(4)]
                pl_ctr = [0]
                pending = []
                cosT = sb("cosO", [64, SO], BF16, st)
                sinT = sb("sinO", [64, SO], BF16, st)
                b_tab = Buf()
                P.dma("sp", cosT[:], cosO_d, reads=[b_tabd], writes=[b_tab], key="cosO")
                P.dma("sp", sinT[:], sinO_d, reads=[b_tabd], writes=[b_tab], key="sinO")
                hTh = sb("hTh", [128, 16, 16], BF16, st)
                b_hTh = Buf()
                P.dma("sp", hTh[:], tlg_v[:, 0:16, :], reads=[b_tlg], writes=[b_hTh], key="hTh")

                def proj_halo(w, c0):
                    pt, b_pt = psum()
                    for kc in range(16):
                        P.op("pe", lambda e, pt=pt, w=w, kc=kc: e.matmul(
                            pt[:, 0:16], w[0][:, kc, c0:c0 + 128], hTh[:, kc, :],
                            start=(kc == 0), stop=(kc == 15)), reads=[w[1], b_hTh], writes=[b_pt])
                    return pt, b_pt

                def tmp():
                    i = tmp_ctr[0] % NTMP
                    tmp_ctr[0] += 1
                    return tmps[i], b_tmps[i]

                def proj(w, c0, ncols, tt):
                    pt, b_pt = psum()
                    for kc in range(16):
                        P.op("pe", lambda e, pt=pt, w=w, kc=kc: e.matmul(
                            pt[0:ncols, :], w[0][:, kc, c0:c0 + ncols], hTs[:, kc, tt * 512:(tt + 1) * 512],
                            start=(kc == 0), stop=(kc == 15)), reads=[w[1], b_hTs], writes=[b_pt])
                    return pt, b_pt

                def rmsnorm_group(w, c0, nch, tt, dim, dst_v, dst_bufs, gtt, key):
                    pts = [proj(w, c0 + c * 128, 128, tt) for c in range(nch)]
                    ss, b_ss = psum()
                    for c, (pt, b_pt) in enumerate(pts):
                        i = sq_ctr[0] % 2
                        sq_ctr[0] += 1
                        sq, b_sq = sqs[i], b_sqs[i]
                        P.op("act", lambda e, pt=pt, sq=sq: e.activation(out=sq[:], in_=pt, func=AF.Square), reads=[b_pt], writes=[b_sq])
                        P.op("pe", lambda e, ss=ss, sq=sq, c=c: e.matmul(ss, ones[:], sq[:], start=(c == 0), stop=(c == nch - 1)),
                             reads=[b_sq, b_const], writes=[b_ss])
                    rs, b_rs = tmp()
                    P.op("act", lambda e, rs=rs, ss=ss: e.activation(out=rs[:, 0:512], in_=ss, func=AF.Sqrt, bias=LN_EPS_AP[:, 1:2],
                                                                      scale=1.0 / dim), reads=[b_ss, b_const], writes=[b_rs])
                    P.op("dve", lambda e, rs=rs: e.reciprocal(out=rs[:, 0:512], in_=rs[:, 0:512]), reads=[b_rs], writes=[b_rs])
                    i = stg_ctr[0] % 2
                    stg_ctr[0] += 1
                    sg_, b_sg_ = stgs[i], b_stgs[i]
                    for c, (pt, b_pt) in enumerate(pts):
                        P.op("dve", lambda e, pt=pt, rs=rs, sg_=sg_, c=c: e.tensor_tensor(out=sg_[:, c, :], in0=pt, in1=rs[:, 0:512], op=ALU.mult),
                             reads=[b_pt, b_rs], writes=[b_sg_])
                    P.dma("sp", dst_v[:, :, gtt * 512:(gtt + 1) * 512], sg_[:, 0:nch, :], reads=[b_sg_], writes=[dst_bufs[gtt]], key="stg%d" % i)

                for hf in range(1):
                    P.dma("sp", hTs[:], hT_v[:, :, hf * 2048:(hf + 1) * 2048], reads=b_hT[hf * 4:(hf + 1) * 4], writes=[b_hTs], key="hTs")
                    for g in range(NG):
                        wi = (hf * NG + g) % 2
                        w = (wr[wi], b_wr[wi])
                        row0 = (L * NG + g) * 128
                        P.dma("pool", wr[wi][:].rearrange("p a b -> p (a b)"), win_d[row0:row0 + 128, :], writes=[b_wr[wi]], key="wr%d" % wi)
                        if g == 2:
                            if debug:
                                P.dma("sp", dbg_e1a, e1a_src, reads=b_qn, writes=[Buf()], key="dbg_e1a")
                                P.dma("sp", dbg_e1b, e1b_src, reads=b_kvn + b_kr, writes=[Buf()], key="dbg_e1b")
                            P.collective(e1b_src, e1b_g, reads=b_kvn + b_kr, writes=[b_e1g], key="cc_e1b")
                            P.collective(e1a_src, e1a_g, reads=b_qn + [b_e1g], writes=[b_e1g], key="cc_e1a")
                        for tt in range(4):
                            gtt = hf * 4 + tt
                            tok = slice(gtt * 512, (gtt + 1) * 512)
                            if g == 0:
                                rmsnorm_group(w, 0, 2, tt, 256.0, kvn_v, b_kvn, gtt, "kvn")
                                pa, b_pa = proj(w, 256, 64, tt)
                                pb, b_pb = proj(w, 320, 64, tt)
                                t1, b_t1 = tmp()
                                t2, b_t2 = tmp()
                                P.op("dve", lambda e, pa=pa, t1=t1, tok=tok: e.tensor_tensor(out=t1[0:64, 0:512], in0=pa[0:64, :], in1=cosT[:, tok], op=ALU.mult),
                                     reads=[b_pa, b_tab], writes=[b_t1])
                                P.op("dve", lambda e, pb=pb, t2=t2, tok=tok: e.tensor_tensor(out=t2[0:64, 0:512], in0=pb[0:64, :], in1=sinT[:, tok], op=ALU.mult),
                                     reads=[b_pb, b_tab], writes=[b_t2])
                                i = stg_ctr[0] % 2
                                stg_ctr[0] += 1
                                sg_, b_sg_ = stgs[i], b_stgs[i]
                                P.op("pool", lambda e, t1=t1, t2=t2, sg_=sg_: e.tensor_tensor(out=sg_[0:64, 0, :], in0=t1[0:64, 0:512], in1=t2[0:64, 0:512], op=ALU.add),
                                     reads=[b_t1, b_t2], writes=[b_sg_])
                                P.dma("sp", kr_d[:, tok], sg_[0:64, 0, :], reads=[b_sg_], writes=[b_kr[gtt]], key="stg%d" % i)
                            elif g == 1:
                                rmsnorm_group(w, 0, 4, tt, 512.0, qn_v, b_qn, gtt, "qn")
                            elif g in (2, 3):
                                i = stg_ctr[0] % 2
                                stg_ctr[0] += 1
                                sg_, b_sg_ = stgs[i], b_stgs[i]
                                for c in range(4):
                                    pt, b_pt = proj(w, c * 128, 128, tt)
                                    P.op("act", lambda e, pt=pt, sg_=sg_, c=c: e.activation(out=sg_[:, c, :], in_=pt, func=AF.Silu),
                                         reads=[b_pt], writes=[b_sg_])
                                r0 = (g - 2) * 4
                                P.dma("sp", mix_v[:, r0:r0 + 4, tok], sg_[:], reads=[b_sg_], writes=[b_mix[r0 + c][gtt] for c in range(4)], key="stg%d" % i)
                            elif g in (4, 5):
                                tails = []
                                for s_ in range(2):
                                    pg = (g - 4) * 2 + s_
                                    wlen = POOL_W[pg]
                                    px, b_px = proj(w, s_ * 256, 128, tt)
                                    pgt, b_pgt = proj(w, s_ * 256 + 128, 128, tt)
                                    xb, b_xb = tmp()
                                    sa, b_sa = tmp()
                                    sb_, b_sb = tmp()
                                    P.op("act", lambda e, px=px, xb=xb: e.copy(out=xb[:, 16:528], in_=px), reads=[b_px], writes=[b_xb])
                                    if tt == 0:
                                        ph, b_ph = proj_halo(w, s_ * 256)
                                        P.op("dve", lambda e, xb=xb, ph=ph: e.tensor_scalar(out=xb[:, 0:16], in0=ph[:, 0:16], scalar1=coef[:, 1:2], scalar2=None, op0=ALU.mult),
                                             reads=[b_ph, b_xb, b_const], writes=[b_xb])
                                    else:
                                        P.op("pool", lambda e, xb=xb, pg=pg: e.tensor_copy(out=xb[:, 0:16], in_=hp[:, pg, :]), reads=[b_hp[pg], b_xb], writes=[b_xb])
                                    P.op("pool", lambda e, xb=xb, pg=pg: e.tensor_copy(out=hp[:, pg, :], in_=xb[:, 512:528]), reads=[b_xb], writes=[b_hp[pg]])
                                    P.op("pool", lambda e, xb=xb, sa=sa: e.tensor_tensor(out=sa[:, 1:528], in0=xb[:, 1:528], in1=xb[:, 0:527], op=ALU.add),
                                         reads=[b_xb], writes=[b_sa])
                                    fin, b_fin = sa, b_sa
                                    if wlen >= 4:
                                        P.op("pool", lambda e, sa=sa, sb_=sb_: e.tensor_tensor(out=sb_[:, 3:528], in0=sa[:, 3:528], in1=sa[:, 1:526], op=ALU.add),
                                             reads=[b_sa], writes=[b_sb])
                                        fin, b_fin = sb_, b_sb
                                    if wlen >= 8:
                                        P.op("pool", lambda e, sa=sa, sb_=sb_: e.tensor_tensor(out=sa[:, 7:528], in0=sb_[:, 7:528], in1=sb_[:, 3:524], op=ALU.add),
                                             reads=[b_sb, b_sa], writes=[b_sa])
                                        fin, b_fin = sa, b_sa
                                    if wlen >= 16:
                                        P.op("pool", lambda e, sa=sa, sb_=sb_: e.tensor_tensor(out=sb_[:, 15:528], in0=sa[:, 15:528], in1=sa[:, 7:520], op=ALU.add),
                                             reads=[b_sa, b_sb], writes=[b_sb])
                                        fin, b_fin = sb_, b_sb
                                    ip = pl_ctr[0] % 4
                                    pl_ctr[0] += 1
                                    pl, b_pl = pls[ip], b_pls[ip]
                                    sgp, b_sgp = sgps[ip], b_sgps[ip]
                                    P.op("dve", lambda e, fin=fin, xb=xb, pl=pl, wlen=wlen: e.scalar_tensor_tensor(
                                        out=pl[:], in0=fin[:, 16:528], scalar=1.0 / wlen, in1=xb[:, 16:528], op0=ALU.mult, op1=ALU.subtract),
                                        reads=[b_fin, b_xb], writes=[b_pl])
                                    if gtt == 0:
                                        t16, b_t16 = tmp()
                                        P.op("dve", lambda e, fin=fin, t16=t16, pg=pg: e.tensor_tensor(out=t16[:, 0:16], in0=fin[:, 16:32], in1=invdiv[:, pg, :], op=ALU.mult),
                                             reads=[b_fin, b_const], writes=[b_t16])
                                        P.op("dve", lambda e, t16=t16, xb=xb, pl=pl: e.tensor_tensor(out=pl[:, 0:16], in0=t16[:, 0:16], in1=xb[:, 16:32], op=ALU.subtract),
                                             reads=[b_t16, b_xb, b_pl], writes=[b_pl])
                                    P.op("act", lambda e, pgt=pgt, sgp=sgp: e.activation(out=sgp[:], in_=pgt, func=AF.Silu), reads=[b_pgt], writes=[b_sgp])
                                    tails.append((s_, pg, pl, b_pl, sgp, b_sgp))

                                def pool_tail(tails=tails, g=g, gtt=gtt, tok=tok):
                                    i = stg_ctr[0] % 2
                                    stg_ctr[0] += 1
                                    sg_, b_sg_ = stgs[i], b_stgs[i]
                                    for (s_, pg, pl, b_pl, sgp, b_sgp) in tails:
                                        py, b_py = psum()
                                        P.op("pe", lambda e, py=py, pl=pl, pg=pg: e.matmul(py, wp[:, pg, :], pl[:], start=True, stop=True),
                                             reads=[b_pl, b_sm], writes=[b_py])
                                        P.op("dve", lambda e, py=py, sgp=sgp, pg=pg, s_=s_: e.scalar_tensor_tensor(
                                            out=sg_[:, s_, :], in0=py, scalar=sm[:, 6 + pg:7 + pg], in1=sgp[:], op0=ALU.mult, op1=ALU.mult),
                                            reads=[b_py, b_sgp, b_sm], writes=[b_sg_])
                                    r0 = 8 + (g - 4) * 2
                                    P.dma("sp", mix_v[:, r0:r0 + 2, tok], sg_[:, 0:2, :], reads=[b_sg_], writes=[b_mix[r0 + c][gtt] for c in range(2)], key="stg%d" % i)

                                for fn_ in pending:
                                    fn_()
                                pending.clear()
                                pending.append(pool_tail)
                                if tt == 3:
                                    for fn_ in pending:
                                        fn_()
                                    pending.clear()
                            else:
                                j = g - 6
                                i = stg_ctr[0] % 2
                                stg_ctr[0] += 1
                                sg_, b_sg_ = stgs[i], b_stgs[i]
                                pch, b_pch = proj(w, 0, 128, tt)
                                pcc, b_pcc = proj(w, 128, 128, tt)
                                pcb, b_pcb = proj(w, 256, 128, tt)
                                pgc, b_pgc = proj(w, 384, 128, tt)
                                chs, b_chs = tmp()
                                ub, b_ub = tmp()
                                t1, b_t1 = tmp()
                                t2, b_t2 = tmp()
                                P.op("act", lambda e, pch=pch, chs=chs: e.copy(out=chs[:, 0:512], in_=pch), reads=[b_pch], writes=[b_chs])
                                P.op("dve", lambda e, pcc=pcc, chs=chs, ub=ub: e.tensor_tensor(out=ub[:, 2:514], in0=pcc, in1=chs[:, 0:512], op=ALU.mult),
                                     reads=[b_pcc, b_chs], writes=[b_ub])
                                if tt == 0:
                                    ph1, b_ph1 = proj_halo(w, 0)
                                    ph2, b_ph2 = proj_halo(w, 128)
                                    hh, b_hh = tmp()
                                    P.op("act", lambda e, ph1=ph1, hh=hh: e.copy(out=hh[:, 0:16], in_=ph1[:, 0:16]), reads=[b_ph1], writes=[b_hh])
                                    P.op("dve", lambda e, ph2=ph2, hh=hh: e.tensor_tensor(out=hh[:, 16:32], in0=ph2[:, 0:16], in1=hh[:, 0:16], op=ALU.mult),
                                         reads=[b_ph2, b_hh], writes=[b_hh])
                                    P.op("dve", lambda e, ub=ub, hh=hh: e.tensor_scalar(out=ub[:, 0:2], in0=hh[:, 30:32], scalar1=coef[:, 1:2], scalar2=None, op0=ALU.mult),
                                         reads=[b_hh, b_ub, b_const], writes=[b_ub])
                                else:
                                    P.op("pool", lambda e, ub=ub, j=j: e.tensor_copy(out=ub[:, 0:2], in_=hc[:, j, :]), reads=[b_hc[j], b_ub], writes=[b_ub])
                                P.op("pool", lambda e, ub=ub, j=j: e.tensor_copy(out=hc[:, j, :], in_=ub[:, 512:514]), reads=[b_ub], writes=[b_hc[j]])
                                cw0 = 10 + j * 3
                                P.op("dve", lambda e, ub=ub, t1=t1, cw0=cw0: e.tensor_scalar(out=t1[:, 0:512], in0=ub[:, 0:512], scalar1=sm[:, cw0:cw0 + 1], scalar2=None, op0=ALU.mult),
                                     reads=[b_ub, b_sm], writes=[b_t1])
                                P.op("dve", lambda e, ub=ub, t1=t1, t2=t2, cw0=cw0: e.scalar_tensor_tensor(
                                    out=t2[:, 0:512], in0=ub[:, 1:513], scalar=sm[:, cw0 + 1:cw0 + 2], in1=t1[:, 0:512], op0=ALU.mult, op1=ALU.add),
                                    reads=[b_ub, b_t1, b_sm], writes=[b_t2])
                                P.op("dve", lambda e, ub=ub, t1=t1, t2=t2, cw0=cw0: e.scalar_tensor_tensor(
                                    out=t1[:, 0:512], in0=ub[:, 2:514], scalar=sm[:, cw0 + 2:cw0 + 3], in1=t2[:, 0:512], op0=ALU.mult, op1=ALU.add),
                                    reads=[b_ub, b_t2, b_t1, b_sm], writes=[b_t1])
                                P.op("dve", lambda e, pcb=pcb, t1=t1, t2=t2: e.tensor_tensor(out=t2[:, 0:512], in0=pcb, in1=t1[:, 0:512], op=ALU.mult),
                                     reads=[b_pcb, b_t1, b_t2], writes=[b_t2])
                                P.op("act", lambda e, pgc=pgc, chs=chs: e.activation(out=chs[:, 0:512], in_=pgc, func=AF.Silu), reads=[b_pgc, b_chs], writes=[b_chs])
                                P.op("pool", lambda e, t2=t2, chs=chs, sg_=sg_: e.tensor_tensor(out=sg_[:, 0, :], in0=t2[:, 0:512], in1=chs[:, 0:512], op=ALU.mult),
                                     reads=[b_t2, b_chs], writes=[b_sg_])
                                r0 = 12 + j
                                P.dma("sp", mix_v[:, r0, tok], sg_[:, 0, :], reads=[b_sg_], writes=[b_mix[r0][gtt]], key="stg%d" % i)
                P.barrier(junk)

        def phase_B(L):
            with contextlib.ExitStack() as st:
                areset()
                SB_ = [4, 5, 6, 7]
                sctr = [0]
                octr = [0]
                lctr = [0]
                qns = sb("qns", [128, 4, S], BF16, st)
                kvns = sb("kvns", [128, 2, S], BF16, st)
                krs = sb("krs", [64, S], BF16, st)
                b_kvl, b_krl, b_qnl = [Buf(), Buf()], [Buf(), Buf()], [Buf(), Buf()]
                b_lc = [Buf(), Buf()]
                for rho in range(2):
                    csl = slice(rho * SO, (rho + 1) * SO)
                    P.dma("sp", kvns[:, :, csl], e1b_g[rho * 320:rho * 320 + 256, :].rearrange("(kc p) t -> p kc t", p=128), reads=[b_e1g], writes=[b_kvl[rho], b_lc[rho]], key="kvns%d" % rho)
                    P.dma("sp", krs[:, csl], e1b_g[rho * 320 + 256:rho * 320 + 320, :], reads=[b_e1g], writes=[b_krl[rho], b_lc[rho]], key="krs%d" % rho)
                    P.dma("sp", qns[:, :, csl], e1a_g[rho * 512:(rho + 1) * 512, :].rearrange("(kc p) t -> p kc t", p=128), reads=[b_e1g], writes=[b_qnl[rho], b_lc[rho]], key="qns%d" % rho)
                cosT = sb("cosA", [64, S], BF16, st)
                sinT = sb("sinA", [64, S], BF16, st)
                b_tab = Buf()
                P.dma("sp", cosT[:], cosA_d, reads=[b_tabd], writes=[b_tab], key="cosA")
                P.dma("sp", sinT[:], sinA_d, reads=[b_tabd], writes=[b_tab], key="sinA")
                wq = sb("wq", [128, 4, 1024], BF16, st)
                wk = sb("wk", [128, 2, 512], BF16, st)
                wv = sb("wv", [128, 2, 512], BF16, st)
                sm = sb("smB", [128, 32], F32, st)
                b_w = Buf()
                P.dma("sp", sm[:], sm_d[L * 128:(L + 1) * 128, :], writes=[b_w], key="smB")
                b_w1, b_w2 = Buf(), Buf()
                P.dma("pool", wq[:].rearrange("p a b -> p (a b)"), wq_d[L * 128:(L + 1) * 128, :], writes=[b_w1], key="wq")
                P.dma("pool", wk[:].rearrange("p a b -> p (a b)"), wk_d[L * 128:(L + 1) * 128, :], reads=[b_w1], writes=[b_w2], key="wk")
                P.dma("pool", wv[:].rearrange("p a b -> p (a b)"), wv_d[L * 128:(L + 1) * 128, :], reads=[b_w1, b_w2], writes=[b_w], key="wv")
                for kc in range(4):
                    P.op("dve", lambda e, kc=kc: e.tensor_scalar(out=wq[:, kc, :], in0=wq[:, kc, :], scalar1=sm[:, kc:kc + 1], scalar2=None, op0=ALU.mult),
                         reads=[b_w], writes=[b_w])
                for kc in range(2):
                    P.op("dve", lambda e, kc=kc: e.tensor_scalar(out=wk[:, kc, :], in0=wk[:, kc, :], scalar1=sm[:, 4 + kc:5 + kc], scalar2=None, op0=ALU.mult),
                         reads=[b_w], writes=[b_w])
                    P.op("dve", lambda e, kc=kc: e.tensor_scalar(out=wv[:, kc, :], in0=wv[:, kc, :], scalar1=sm[:, 4 + kc:5 + kc], scalar2=None, op0=ALU.mult),
                         reads=[b_w], writes=[b_w])
                Vq = sb("Vq", [128, 32, 512], BF16, st)
                b_V = [Buf() for i in range(32)]
                kTh = sb("kTh", [128, S], BF16, st)
                qTh = sb("qTh", [128, S], BF16, st)
                qrh = sb("qrh", [64, S], BF16, st)
                b_kT = [Buf() for i in range(8)]
                b_qT = [Buf() for i in range(8)]
                b_qr = [Buf() for i in range(8)]
                pTs = [sb("pT%d" % i, [128, 512], BF16, st) for i in range(4)]
                b_pTs = [Buf() for i in range(4)]
                pT_ctr = [0]
                rls = [sb("rl%d" % i, [128, 512], F32, st) for i in range(2)]
                b_rls = [Buf() for i in range(2)]
                outs = [sb("ob%d" % i, [128, 512], BF16, st) for i in range(2)]
                b_outs = [Buf() for i in range(2)]
                rt1 = [sb("rta%d" % i, [64, 512], F32, st) for i in range(2)]
                rt2 = [sb("rtb%d" % i, [64, 512], F32, st) for i in range(2)]
                b_rt1 = [Buf() for i in range(2)]
                b_rt2 = [Buf() for i in range(2)]
                ev = [0]

                def evac(out_ap, in_ap, reads, writes):
                    ev[0] += 1
                    if ev[0] % 2 == 0:
                        P.op("dve", lambda e: e.tensor_copy(out=out_ap, in_=in_ap), reads=reads, writes=writes)
                    else:
                        P.op("act", lambda e: e.copy(out=out_ap, in_=in_ap), reads=reads, writes=writes)

                uc = [0]
                for h in range(HL):
                    if h % 4 == 0:
                        hq = h // 4
                        for tk in range(32):
                            pt, b_pt = psum(SB_, sctr)
                            for kc in range(2):
                                P.op("pe", lambda e, pt=pt, kc=kc, tk=tk, hq=hq: e.matmul(
                                    pt, kvns[:, kc, tk * 128:(tk + 1) * 128], wv[:, kc, hq * 512:(hq + 1) * 512], start=(kc == 0), stop=(kc == 1)),
                                    reads=[b_kvl[tk // 16], b_w], writes=[b_pt])
                            evac(Vq[:, tk, :], pt, [b_pt], [b_V[tk]])
                    for tt in range(8):
                        tok = slice(tt * 512, (tt + 1) * 512)
                        pt, b_pt = psum(SB_, sctr)
                        for kc in range(2):
                            P.op("pe", lambda e, pt=pt, kc=kc, tok=tok, h=h: e.matmul(
                                pt, wk[:, kc, h * 128:(h + 1) * 128], kvns[:, kc, tok], start=(kc == 0), stop=(kc == 1)),
                                reads=[b_kvl[tt // 4], b_w], writes=[b_pt])
                        evac(kTh[:, tok], pt, [b_pt], [b_kT[tt]])
                        pt, b_pt = psum(SB_, sctr)
                        for kc in range(4):
                            P.op("pe", lambda e, pt=pt, kc=kc, tok=tok, h=h: e.matmul(
                                pt, wq[:, kc, h * 256:h * 256 + 128], qns[:, kc, tok], start=(kc == 0), stop=(kc == 3)),
                                reads=[b_qnl[tt // 4], b_w], writes=[b_pt])
                        evac(qTh[:, tok], pt, [b_pt], [b_qT[tt]])
                        pa, b_pa = psum(SB_, sctr)
                        for kc in range(4):
                            P.op("pe", lambda e, pa=pa, kc=kc, tok=tok, h=h: e.matmul(
                                pa[0:64, :], wq[:, kc, h * 256 + 128:h * 256 + 192], qns[:, kc, tok], start=(kc == 0), stop=(kc == 3)),
                                reads=[b_qnl[tt // 4], b_w], writes=[b_pa])
                        pb, b_pb = psum(SB_, sctr)
                        for kc in range(4):
                            P.op("pe", lambda e, pb=pb, kc=kc, tok=tok, h=h: e.matmul(
                                pb[0:64, :], wq[:, kc, h * 256 + 192:h * 256 + 256], qns[:, kc, tok], start=(kc == 0), stop=(kc == 3)),
                                reads=[b_qnl[tt // 4], b_w], writes=[b_pb])
                        i = tt % 2
                        P.op("dve", lambda e, pa=pa, i=i, tok=tok: e.tensor_tensor(out=rt1[i][:], in0=pa[0:64, :], in1=cosT[:, tok], op=ALU.mult),
                             reads=[b_pa, b_tab], writes=[b_rt1[i]])
                        P.op("dve", lambda e, pb=pb, i=i, tok=tok: e.tensor_tensor(out=rt2[i][:], in0=pb[0:64, :], in1=sinT[:, tok], op=ALU.mult),
                             reads=[b_pb, b_tab], writes=[b_rt2[i]])
                        P.op("pool", lambda e, i=i, tok=tok: e.tensor_tensor(out=qrh[:, tok], in0=rt1[i][:], in1=rt2[i][:], op=ALU.add),
                             reads=[b_rt1[i], b_rt2[i]], writes=[b_qr[tt]])
                    units = [(qb, kb) for qb in range(8) for kb in range(4 * qb + 4)]
                    LOOK = 2
                    acc = {}
                    sc = {}

                    def emit_scores(u):
                        qb, kb = units[u]
                        qtok = slice(qb * 512, (qb + 1) * 512)
                        ktok = slice(kb * 128, (kb + 1) * 128)
                        ps_, b_ps_ = psum(SB_, sctr)
                        diag = kb >= 4 * qb
                        P.op("pe", lambda e: e.matmul(ps_, kTh[:, ktok], qTh[:, qtok], start=True, stop=False),
                             reads=[b_kT[kb // 4], b_qT[qb]], writes=[b_ps_])
                        P.op("pe", lambda e: e.matmul(ps_, krs[:, ktok], qrh[:, qtok], start=False, stop=(not diag)),
                             reads=[b_krl[kb // 16], b_qr[qb]], writes=[b_ps_])
                        if diag:
                            jm = kb - 4 * qb
                            P.op("pe", lambda e: e.matmul(ps_, ident[:], masks[:, jm, :], start=False, stop=True),
                                 reads=[b_const], writes=[b_ps_])
                        sc[u] = (ps_, b_ps_)

                    def emit_rest(u, h=h):
                        qb, kb = units[u]
                        nkb = 4 * qb + 4
                        qtok = slice(qb * 512, (qb + 1) * 512)
                        if kb == 0:
                            acc[qb] = (psum([0, 1], octr), psum([2, 3], lctr))
                        (po, b_po), (pl_, b_pl_) = acc[qb]
                        ps_, b_ps_ = sc.pop(u)
                        ip = pT_ctr[0] % 4
                        pT_ctr[0] += 1
                        pT, b_pT = pTs[ip], b_pTs[ip]
                        P.op("act", lambda e: e.activation(out=pT[:], in_=ps_, func=AF.Exp, scale=SCALE), reads=[b_ps_], writes=[b_pT])
                        P.op("pe", lambda e: e.matmul(po, Vq[:, kb, (h % 4) * 128:(h % 4 + 1) * 128], pT[:], start=(kb == 0), stop=(kb == nkb - 1)),
                             reads=[b_V[kb], b_pT], writes=[b_po])
                        P.op("pe", lambda e: e.matmul(pl_, ones[:], pT[:], start=(kb == 0), stop=(kb == nkb - 1)),
                             reads=[b_const, b_pT], writes=[b_pl_])
                        if kb == nkb - 1:
                            i = uc[0] % 2
                            uc[0] += 1
                            P.op("dve", lambda e: e.reciprocal(out=rls[i][:], in_=pl_), reads=[b_pl_], writes=[b_rls[i]])
                            P.op("dve", lambda e: e.tensor_tensor(out=outs[i][:], in0=po, in1=rls[i][:], op=ALU.mult),
                                 reads=[b_po, b_rls[i]], writes=[b_outs[i]])
                            P.dma("sp", e2_src[h][:, qtok], outs[i][:], reads=[b_outs[i]], writes=[b_e2s[h][qb]], key="ob%d" % i)
                            if qb == 7:
                                if debug:
                                    P.dma("sp", dbg_e2[h], e2_src[h], reads=b_e2s[h], writes=[Buf()], key="dbg_e2")
                                P.collective(e2_src[h], e2_g[h], reads=b_e2s[h], writes=[b_e2g[h]], key="cc_e2_%d" % h)

                    for u in range(min(LOOK, len(units))):
                        emit_scores(u)
                    for u in range(len(units)):
                        if u + LOOK < len(units):
                            emit_scores(u + LOOK)
                        emit_rest(u)
                P.barrier(junk)

        for L in range(n_layers):
            if stop_phase == "ln0":
                break
            phase_A(L)
            if stop_phase == "A":
                break
            phase_B(L)
            if stop_phase == "B":
                break
            with contextlib.ExitStack() as st:
                last = (L == DEPTH - 1)
                ln_phase(st, L, "proj", not last, last)
                P.barrier(junk)
        P.finish()
    return nc


def _tile_k(w, ncols_pad=None):
    K, C = w.shape
    return np.ascontiguousarray(w.reshape(K // 128, 128, C).transpose(1, 0, 2))


def prep_inputs(x, positions, emb_ln_g, emb_ln_b, w_in, q_norm_g, kv_norm_g, w_uq, w_ukv, w_pool,
                pool_scale, conv_w, w_out, b_out, ln_g, ln_b):
    f32 = np.float32
    w_in = np.asarray(w_in, f32)
    offs = np.cumsum([0, 512, 256, 64, 1024, 512, 512, 512, 512, 512, 512])
    o_q, o_kv, o_kr, o_gm, o_pi, o_gp, o_ch, o_cb, o_cc, o_gc = offs[:10]
    groups = []
    zero128 = None
    for L in range(DEPTH):
        W = w_in[L]
        kr = W[:, o_kr:o_kr + 64]
        ksw = np.concatenate([kr[:, 32:64], kr[:, 0:32]], axis=1)
        g0 = np.concatenate([W[:, o_kv:o_kv + 256], kr, ksw, np.zeros((D, 128), f32)], axis=1)
        gl = [g0, W[:, o_q:o_q + 512], W[:, o_gm:o_gm + 512], W[:, o_gm + 512:o_gm + 1024]]
        for a in range(2):
            gl.append(np.concatenate([W[:, o_pi + (2 * a) * 128:o_pi + (2 * a + 1) * 128], W[:, o_gp + (2 * a) * 128:o_gp + (2 * a + 1) * 128],
                                      W[:, o_pi + (2 * a + 1) * 128:o_pi + (2 * a + 2) * 128], W[:, o_gp + (2 * a + 1) * 128:o_gp + (2 * a + 2) * 128]], axis=1))
        for j in range(4):
            sl = slice(j * 128, (j + 1) * 128)
            gl.append(np.concatenate([W[:, o_ch:o_ch + 512][:, sl], W[:, o_cc:o_cc + 512][:, sl], W[:, o_cb:o_cb + 512][:, sl], W[:, o_gc:o_gc + 512][:, sl]], axis=1))
        for gmat in gl:
            groups.append(_tile_k(gmat).reshape(128, 16 * 512))
    w_in_g = np.ascontiguousarray(np.concatenate(groups, axis=0))

    wq_l, wk_l, wv_l = [[], []], [[], []], [[], []]
    wo_l, wp_l, sm_l = [], [], []
    for L in range(DEPTH):
        wq = np.asarray(w_uq[L], f32).reshape(512, NH, 192)
        rope = wq[:, :, 128:192]
        sw = np.concatenate([rope[:, :, 32:64], rope[:, :, 0:32]], axis=2)
        wq2 = np.concatenate([wq, sw], axis=2)
        wkv = np.asarray(w_ukv[L], f32).reshape(256, NH, 256)
        for r in range(2):
            hs = slice(4 * r, 4 * r + 4)
            wq_l[r].append(_tile_k(np.ascontiguousarray(wq2[:, hs]).reshape(512, 1024)).reshape(128, 4 * 1024))
            wk_l[r].append(_tile_k(np.ascontiguousarray(wkv[:, hs, 0:128]).reshape(256, 512)).reshape(128, 2 * 512))
            wv_l[r].append(_tile_k(np.ascontiguousarray(wkv[:, hs, 128:256]).reshape(256, 512)).reshape(128, 2 * 512))
        wo_l.append(_tile_k(np.asarray(w_out[L], f32)).reshape(128, 16 * 2048))
        wp_l.append(np.ascontiguousarray(np.asarray(w_pool[L], f32).transpose(1, 0, 2)).reshape(128, 4 * 128))
        sm = np.zeros((128, 32), f32)
        sm[:, 0:4] = np.asarray(q_norm_g[L], f32).reshape(4, 128).T
        sm[:, 4:6] = np.asarray(kv_norm_g[L], f32).reshape(2, 128).T
        sm[:, 6:10] = np.asarray(pool_scale[L], f32).reshape(4, 128).T
        cw = np.asarray(conv_w[L], f32).reshape(3, 4, 128)
        sm[:, 10:22] = cw.transpose(2, 1, 0).reshape(128, 12)
        sm_l.append(sm)
    lnp = np.stack([np.asarray(emb_ln_g, f32), np.asarray(emb_ln_b, f32)] +
                   sum([[np.asarray(ln_g[L], f32), np.asarray(ln_b[L], f32), np.asarray(b_out[L], f32)] for L in range(DEPTH)], []), axis=0)
    half = 32
    inv_freq = (10000.0 ** (-np.arange(half, dtype=np.float32) / half)).astype(f32)
    ropec = np.zeros((128, 2), f32)
    ropec[:, 0] = np.concatenate([inv_freq] * 4)
    ropec[:, 1] = np.concatenate([-np.ones(32, f32), np.ones(32, f32)] * 2)
    invdiv = np.zeros((2, 128, 4, 16), f32)
    for g, w in enumerate(POOL_W):
        invdiv[0, :, g, :] = 1.0 / np.minimum(np.arange(1, 17, dtype=f32), float(w))
        invdiv[1, :, g, :] = 1.0 / float(w)
    ident = np.eye(128, dtype=f32).astype(ml_dtypes.bfloat16)
    kk = np.arange(128)[:, None]
    qq = np.arange(512)[None, :]
    masks = np.stack([np.where(j * 128 + kk <= qq, 0.0, NEG) for j in range(4)], axis=1).astype(f32)
    masks = masks.reshape(128, 4 * 512).astype(ml_dtypes.bfloat16)
    shared = {
        "ropec": ropec, "lnp": np.ascontiguousarray(lnp), "w_in_g": w_in_g,
        "wo": np.ascontiguousarray(np.concatenate(wo_l, 0)),
        "wp": np.ascontiguousarray(np.concatenate(wp_l, 0)), "small": np.ascontiguousarray(np.concatenate(sm_l, 0)),
        "ident": ident, "masks": masks,
    }
    per_rank = []
    for r in range(2):
        coef = np.zeros((128, 2), f32)
        coef[:, r] = 1.0
        per_rank.append({
            "wq": np.ascontiguousarray(np.concatenate(wq_l[r], 0)), "wk": np.ascontiguousarray(np.concatenate(wk_l[r], 0)),
            "wv": np.ascontiguousarray(np.concatenate(wv_l[r], 0)), "invdiv": np.ascontiguousarray(invdiv[r].reshape(128, 64)),
            "coef": coef,
        })
    x = np.asarray(x, f32)
    positions = np.asarray(positions, np.int32)
    in_maps = []
    for c in range(8):
        b, r = c // 2, c % 2
        m = dict(shared)
        m.update(per_rank[r])
        m["x"] = np.ascontiguousarray(x[b, r * SO:(r + 1) * SO])
        m["pos"] = np.ascontiguousarray(positions[b][None, :])
        m["pos_own"] = np.ascontiguousarray(positions[b, r * SO:(r + 1) * SO][None, :])
        in_maps.append(m)
    return in_maps


def kernel(**inputs):
    in_maps = prep_inputs(**inputs)
    nc = build()
    res = run_bass_kernel_spmd(nc, in_maps, core_ids=list(range(8)))
    out = np.empty((4, S, D), np.float32)
    for c in range(8):
        b, r = c // 2, c % 2
        out[b, r * SO:(r + 1) * SO] = np.asarray(res.results[c]["out"], dtype=np.float32)
    return out
```

```python
import math
import contextlib
import numpy as np
import ml_dtypes
import concourse.bass as bass
import concourse.mybir as mybir
from concourse.bass_utils import run_bass_kernel_spmd

F32 = mybir.dt.float32
BF16 = mybir.dt.bfloat16
I32 = mybir.dt.int32
AF = mybir.ActivationFunctionType
ALU = mybir.AluOpType

S = 4096
SO = 2048
HL = 4
PAIRS = [[0, 1], [2, 3], [4, 5], [6, 7]]
D = 2048
DEPTH = 2
NH = 8
LN_EPS = 1e-5
RMS_EPS = 1e-6
ALPHA = (2 * DEPTH) ** 0.25
SCALE = 192 ** -0.5
NEG = -30000.0
POOL_W = (2, 4, 8, 16)
NG = 10

ENGS = ("pe", "act", "dve", "pool", "sp")


class Buf:
    __slots__ = ("name", "w", "r")

    def __init__(self, name=""):
        self.name = name
        self.w = None
        self.r = []


class Op:
    __slots__ = ("eng", "fn", "waits", "flag", "dma_key", "dma_val", "seq")

    def __init__(self, eng, fn):
        self.eng = eng
        self.fn = fn
        self.waits = []
        self.flag = False
        self.dma_key = None
        self.dma_val = 0
        self.seq = -1


class Prog:
    def __init__(self, nc):
        self.nc = nc
        self.ops = {e: [] for e in ENGS}
        self.seen = {e: {} for e in ENGS}
        self.seen_dma = {e: {} for e in ENGS}
        self.dma_counts = {}
        self.cc_keys = set()
        self.jb = [Buf(), Buf(), Buf()]

    def _add(self, eng, fn, reads, writes, dma_key=None):
        op = Op(eng, fn)
        op.seq = len(self.ops[eng])
        deps = []
        for b in reads:
            if b.w is not None:
                deps.append(b.w)
        for b in writes:
            if b.w is not None:
                deps.append(b.w)
            deps.extend(b.r)
        best = {}
        dma_deps = {}
        for d in deps:
            if d.dma_key is not None:
                if dma_deps.get(d.dma_key, 0) < d.dma_val:
                    dma_deps[d.dma_key] = d.dma_val
            else:
                if d.eng == eng and eng == "pe":
                    continue
                if best.get(d.eng, -1) < d.seq:
                    best[d.eng] = d.seq
        for f, s in best.items():
            if self.seen[eng].get(f, -1) >= s:
                continue
            self.seen[eng][f] = s
            dop = self.ops[f][s]
            dop.flag = True
            op.waits.append(("eng", f, dop))
        for k, v in dma_deps.items():
            if self.seen_dma[eng].get(k, 0) >= v:
                continue
            self.seen_dma[eng][k] = v
            op.waits.append(("dma", k, v))
        if dma_key is not None:
            op.dma_key = dma_key
            self.dma_counts[dma_key] = self.dma_counts.get(dma_key, 0) + 16
            op.dma_val = self.dma_counts[dma_key]
        for b in reads:
            b.r.append(op)
        for b in writes:
            b.w = op
            b.r = []
        self.ops[eng].append(op)
        return op

    def op(self, eng, fn, reads=(), writes=()):
        return self._add(eng, fn, reads, writes, None)

    def dma(self, eng, out, in_, reads=(), writes=(), key=None):
        def fn(e):
            return e.dma_start(out=out, in_=in_)
        return self._add(eng, fn, reads, writes, key)

    def collective(self, src, dst, reads, writes, key):
        def fn(e):
            return e.collective_compute("AllGather", ALU.bypass, replica_groups=PAIRS, ins=[src.opt()], outs=[dst.opt()])
        o = self._add("pool", fn, reads, writes, key)
        self.dma_counts[key] -= 15
        o.dma_val = self.dma_counts[key]
        self.cc_keys.add(key)
        return o

    def barrier(self, junk):
        marks = []
        jb = self.jb
        b = Buf()
        self.op("act", lambda e: e.activation(out=junk[:, 0:1], in_=junk[:, 4:5], func=AF.Copy), writes=[b, jb[0]])
        marks.append(b)
        b = Buf()
        self.op("dve", lambda e: e.memset(junk[:, 1:2], 0.0), writes=[b, jb[1]])
        marks.append(b)
        b = Buf()
        self.op("pool", lambda e: e.memset(junk[:, 2:3], 0.0), writes=[b, jb[2]])
        marks.append(b)
        fence = Buf()
        o = self.op("sp", lambda e: e.nop(), reads=marks, writes=[fence])
        for k, v in self.dma_counts.items():
            if self.seen_dma["sp"].get(k, 0) < v:
                self.seen_dma["sp"][k] = v
                o.waits.append(("dma", k, v))
        self.op("act", lambda e: e.activation(out=junk[:, 0:1], in_=junk[:, 4:5], func=AF.Copy), reads=[fence], writes=[jb[0]])
        self.op("dve", lambda e: e.memset(junk[:, 1:2], 0.0), reads=[fence], writes=[jb[1]])
        self.op("pool", lambda e: e.memset(junk[:, 2:3], 0.0), reads=[fence], writes=[jb[2]])
        self.op("pe", lambda e: e.nop(), reads=[fence])
        for e in ENGS:
            for k, v in self.dma_counts.items():
                if self.seen_dma[e].get(k, 0) < v:
                    self.seen_dma[e][k] = v

    def finish(self):
        nc = self.nc
        with contextlib.ExitStack() as st:
            esem = {e: st.enter_context(nc.semaphore("s_" + e)) for e in ENGS}
            dsem = {k: st.enter_context(nc.semaphore("d_%s" % (k,))) for k in self.dma_counts}
            block = st.enter_context(nc.Block())
            for e in ENGS:
                c = 0
                for o in self.ops[e]:
                    if o.flag:
                        c += 1
                        o.dma_val = c

            def emit(e, eng):
                for o in self.ops[e]:
                    for kind, k, v in o.waits:
                        if kind == "eng":
                            eng.wait_ge(esem[k], v.dma_val)
                        else:
                            eng.wait_ge(dsem[k], v)
                    inst = o.fn(eng)
                    if o.dma_key is not None and o.dma_key in self.cc_keys:
                        inst.then_inc(dsem[o.dma_key])
                    elif o.dma_key is not None:
                        inst.then_inc(dsem[o.dma_key], 16)
                    elif o.flag:
                        inst.then_inc(esem[e], 1)
                if e == "sp":
                    for k, v in self.dma_counts.items():
                        eng.wait_ge(dsem[k], v)

            @block.tensor
            def _(eng):
                emit("pe", eng)

            @block.scalar
            def _(eng):
                emit("act", eng)

            @block.vector
            def _(eng):
                emit("dve", eng)

            @block.gpsimd
            def _(eng):
                emit("pool", eng)

            @block.sync
            def _(eng):
                emit("sp", eng)


def build(debug=False, n_layers=DEPTH, stop_phase=None):
    nc = bass.Bass("TRN2", target_bir_lowering=False)
    P = Prog(nc)

    def din(name, shape, dt):
        return nc.dram_tensor(name, shape, dt, kind="ExternalInput").ap()

    x_d = din("x", [SO, D], F32)
    pos_d = din("pos", [1, S], I32)
    poso_d = din("pos_own", [1, SO], I32)
    coef_d = din("coef", [128, 2], F32)
    rc_d = din("ropec", [128, 2], F32)
    lnp_d = din("lnp", [2 + 3 * DEPTH, D], F32)
    win_d = din("w_in_g", [DEPTH * NG * 128, 16 * 512], F32)
    wq_d = din("wq", [DEPTH * 128, 4 * 1024], F32)
    wk_d = din("wk", [DEPTH * 128, 2 * 512], F32)
    wv_d = din("wv", [DEPTH * 128, 2 * 512], F32)
    wo_d = din("wo", [DEPTH * 128, 16 * 2048], F32)
    wp_d = din("wp", [DEPTH * 128, 4 * 128], F32)
    sm_d = din("small", [DEPTH * 128, 32], F32)
    idv_d = din("invdiv", [128, 64], F32)
    ident_d = din("ident", [128, 128], BF16)
    mask_d = din("masks", [128, 4 * 512], BF16)
    out_d = nc.dram_tensor("out", [SO, D], F32, kind="ExternalOutput").ap()

    skind = "ExternalOutput" if debug else "Internal"
    resid_d = nc.dram_tensor("resid", [SO, D], F32, kind=skind).ap()
    hT_d = nc.dram_tensor("hT", [D, SO], BF16, kind=skind).ap()
    mix_d = nc.dram_tensor("mixT", [D, SO], BF16, kind=skind).ap()
    cosA_d = nc.dram_tensor("cosA", [64, S], BF16).ap()
    sinA_d = nc.dram_tensor("sinA", [64, S], BF16).ap()
    cosO_d = nc.dram_tensor("cosO", [64, SO], BF16).ap()
    sinO_d = nc.dram_tensor("sinO", [64, SO], BF16).ap()
    e1a_src = nc.dram_tensor("e1a_src", [512, SO], BF16).ap()
    e1a_g = nc.dram_tensor("e1a_g", [1024, SO], BF16).ap()
    e1b_src = nc.dram_tensor("e1b_src", [320, SO], BF16).ap()
    e1b_g = nc.dram_tensor("e1b_g", [640, SO], BF16).ap()
    e2_src = [nc.dram_tensor("e2_src%d" % i, [128, S], BF16).ap() for i in range(4)]
    e2_gall = nc.dram_tensor("e2_gall", [4 * 256, S], BF16).ap()
    e2_g = [e2_gall[i * 256:(i + 1) * 256, :] for i in range(4)]
    e2g_v = e2_gall.rearrange("(h r p) t -> p h r t", h=4, r=2)
    tl_src = nc.dram_tensor("tl_src", [D, 16], BF16).ap()
    tl_g = nc.dram_tensor("tl_g", [2 * D, 16], BF16).ap()
    if debug:
        dbg_e1a = nc.dram_tensor("dbg_e1a", [512, SO], BF16, kind="ExternalOutput").ap()
        dbg_e1b = nc.dram_tensor("dbg_e1b", [320, SO], BF16, kind="ExternalOutput").ap()
        dbg_e2 = [nc.dram_tensor("dbg_e2_%d" % i, [128, S], BF16, kind="ExternalOutput").ap() for i in range(4)]

    b_resid = [Buf("resid%d" % i) for i in range(16)]
    b_hT = [Buf("hT%d" % i) for i in range(4)]
    b_qn = [Buf() for i in range(4)]
    b_kvn = [Buf() for i in range(4)]
    b_kr = [Buf() for i in range(4)]
    b_mix = [[Buf() for t in range(4)] for r in range(16)]
    b_out = [Buf() for i in range(16)]
    b_e1g, b_tlsrc, b_tlg = Buf(), Buf(), Buf()
    b_e2g = [Buf() for i in range(4)]
    b_e2s = [[Buf() for t in range(8)] for r in range(4)]
    b_tabd = Buf()

    hT_v = hT_d.rearrange("(kc p) t -> p kc t", p=128)
    mix_v = mix_d.rearrange("(kc p) t -> p kc t", p=128)
    qn_v = e1a_src.rearrange("(kc p) t -> p kc t", p=128)
    kvn_v = e1b_src[0:256, :].rearrange("(kc p) t -> p kc t", p=128)
    kr_d = e1b_src[256:320, :]
    tls_v = tl_src.rearrange("(kc p) t -> p kc t", p=128)
    tlg_v = tl_g.rearrange("(kc p) t -> p kc t", p=128)

    with contextlib.ExitStack() as gst:
        ARENA_WORDS = 52224
        a_hi = [ARENA_WORDS]
        arena = gst.enter_context(nc.sbuf_tensor("arena", [128, ARENA_WORDS], F32))
        a_top = [0]
        a_mark = [0]

        def sb(name, shape, dt, st=None):
            n = 1
            for d_ in shape[1:]:
                n *= d_
            esz = 4 if dt in (F32, I32) else 2
            words = (n * esz + 3) // 4
            words = (words + 7) // 8 * 8
            off = a_top[0]
            assert off + words <= a_hi[0], ("SBUF arena overflow", name, off, words)
            a_top[0] = off + words
            v = arena[0:shape[0], off:off + words]
            if dt != F32:
                v = v.bitcast(dt)
            v = v[:, 0:n]
            if len(shape) == 3:
                v = v.rearrange("p (a b) -> p a b", a=shape[1])
            return v

        def sb_top(name, shape, dt):
            n = 1
            for d_ in shape[1:]:
                n *= d_
            esz = 4 if dt in (F32, I32) else 2
            words = (n * esz + 3) // 4
            words = (words + 7) // 8 * 8
            off = a_hi[0] - words
            assert off >= a_top[0], ("SBUF arena overflow (top)", name)
            a_hi[0] = off
            v = arena[0:shape[0], off:off + words]
            if dt != F32:
                v = v.bitcast(dt)
            v = v[:, 0:n]
            if len(shape) == 3:
                v = v.rearrange("p (a b) -> p a b", a=shape[1])
            return v

        def areset():
            a_top[0] = a_mark[0]
            a_hi[0] = ARENA_WORDS

        ps_all = gst.enter_context(nc.psum_tensor("ps", [128, 8 * 512], F32))
        ps_bufs = [Buf("ps%d" % i) for i in range(8)]
        ps_ctr = [0]

        def psum(banks=None, ctr=None):
            if banks is None:
                i = ps_ctr[0] % 8
                ps_ctr[0] += 1
            else:
                i = banks[ctr[0] % len(banks)]
                ctr[0] += 1
            return ps_all[:, i * 512:(i + 1) * 512], ps_bufs[i]

        junk = sb("junk", [128, 8], F32)
        ident = sb("ident", [128, 128], BF16)
        ones = sb("ones", [128, 128], BF16)
        masks = sb("masks", [128, 4, 512], BF16)
        rc = sb("rc", [128, 2], F32)
        coef = sb("coef", [128, 2], F32)
        invdiv = sb("invdiv", [128, 4, 16], F32)
        b_const = Buf("const")
        b_tab = Buf("tab")

        P.op("dve", lambda e: e.memset(junk[:], 0.0), writes=[b_const] + P.jb)
        P.op("dve", lambda e: e.memset(ones[:], 1.0), writes=[b_const])
        P.dma("sp", ident[:], ident_d, writes=[b_const], key="c_ident")
        P.dma("sp", masks[:].rearrange("p a b -> p (a b)"), mask_d, writes=[b_const], key="c_mask")
        P.dma("sp", rc[:], rc_d, writes=[b_const], key="c_rc")
        P.dma("sp", coef[:], coef_d, writes=[b_const], key="c_coef")
        P.dma("sp", invdiv[:].rearrange("p a b -> p (a b)"), idv_d, writes=[b_const], key="c_idv")

        LN_EPS_AP = sb("lneps", [128, 4], F32)
        a_mark[0] = a_top[0]

        def rope_tables(pos_ap, N, cos_dst, sin_dst, tag):
            H = N // 2
            posi = sb("posi" + tag, [128, H], I32)
            ang = sb("ang" + tag, [128, H], F32)
            ta = sb("ta" + tag, [128, H], F32)
            tb = sb("tb" + tag, [128, H], F32)
            obs = [sb("ob%d" % i + tag, [128, H], BF16) for i in range(2)]
            b_posi, b_ang, b_ta, b_tb = Buf(), Buf(), Buf(), Buf()
            b_obs = [Buf(), Buf()]
            P.dma("sp", posi[0:64, :], pos_ap[:, 0:H].partition_broadcast(64), writes=[b_posi], key="t_pos0" + tag)
            P.dma("sp", posi[64:128, :], pos_ap[:, H:N].partition_broadcast(64), writes=[b_posi], key="t_pos1" + tag)
            P.op("dve", lambda e: e.tensor_copy(out=ang[:], in_=posi[:]), reads=[b_posi], writes=[b_ang])
            P.op("dve", lambda e: e.tensor_scalar(out=ang[:], in0=ang[:], scalar1=rc[:, 0:1], scalar2=None, op0=ALU.mult),
                 reads=[b_ang, b_const], writes=[b_ang])
            TWO_PI = 2.0 * math.pi
            for wi_, (which, phase, dst) in enumerate((("sin", 0.0, sin_dst), ("cos", math.pi / 2, cos_dst))):
                ob, b_ob = obs[wi_], b_obs[wi_]
                P.op("dve", lambda e, phase=phase: e.tensor_scalar(out=ta[:], in0=ang[:], scalar1=phase, scalar2=1.0 / TWO_PI,
                                                                    op0=ALU.add, op1=ALU.mult), reads=[b_ang], writes=[b_ta])
                P.op("dve", lambda e: e.tensor_copy(out=posi[:], in_=ta[:]), reads=[b_ta], writes=[b_posi])
                P.op("dve", lambda e: e.tensor_copy(out=ta[:], in_=posi[:]), reads=[b_posi], writes=[b_ta])
                P.op("dve", lambda e: e.scalar_tensor_tensor(out=tb[:], in0=ta[:], scalar=-TWO_PI, in1=ang[:], op0=ALU.mult, op1=ALU.add),
                     reads=[b_ta, b_ang], writes=[b_tb])
                P.op("dve", lambda e, phase=phase: e.tensor_scalar(out=tb[:], in0=tb[:], scalar1=phase, scalar2=None, op0=ALU.add),
                     reads=[b_tb], writes=[b_tb])
                P.op("dve", lambda e: e.tensor_scalar(out=ta[:], in0=tb[:], scalar1=math.pi, scalar2=TWO_PI, op0=ALU.is_gt, op1=ALU.mult),
                     reads=[b_tb], writes=[b_ta])
                P.op("dve", lambda e: e.tensor_tensor(out=tb[:], in0=tb[:], in1=ta[:], op=ALU.subtract), reads=[b_tb, b_ta], writes=[b_tb])
                P.op("dve", lambda e: e.tensor_scalar(out=tb[:], in0=tb[:], scalar1=math.pi, scalar2=-math.pi, op0=ALU.min, op1=ALU.max),
                     reads=[b_tb], writes=[b_tb])
                P.op("act", lambda e: e.activation(out=ta[:], in_=tb[:], func=AF.Sin), reads=[b_tb], writes=[b_ta])
                if which == "sin":
                    P.op("dve", lambda e, ob=ob: e.tensor_scalar(out=ob[:], in0=ta[:], scalar1=rc[:, 1:2], scalar2=None, op0=ALU.mult),
                         reads=[b_ta, b_const], writes=[b_ob])
                else:
                    P.op("dve", lambda e, ob=ob: e.tensor_copy(out=ob[:], in_=ta[:]), reads=[b_ta], writes=[b_ob])
                P.dma("sp", dst[:, 0:H], ob[0:64, :], reads=[b_ob], writes=[b_tabd], key="t_ob0%d" % wi_ + tag)
                P.dma("sp", dst[:, H:N], ob[64:128, :], reads=[b_ob], writes=[b_tabd], key="t_ob1%d" % wi_ + tag)

        areset()
        rope_tables(pos_d, S, cosA_d, sinA_d, "a")
        rope_tables(poso_d, SO, cosO_d, sinO_d, "o")

        def ln_phase(st, layer_idx, src_kind, write_hT, final, wo_hi_bufs=None):
            if src_kind != "x":
                areset()
            NB = 4 if src_kind == "x" else 3
            gb = sb("ln_g", [128, D], F32, st)
            bb = sb("ln_b", [128, D], F32, st)
            b_p = Buf()
            if src_kind == "x":
                grow, brow = 0, 1
            else:
                grow, brow = 2 + 3 * layer_idx, 3 + 3 * layer_idx
            b_pg, b_pb = Buf(), Buf()
            P.dma("sp", gb[:], lnp_d[grow:grow + 1, :].partition_broadcast(128), writes=[b_pg], key="ln_g")
            P.dma("sp", bb[:], lnp_d[brow:brow + 1, :].partition_broadcast(128), writes=[b_pb], key="ln_b")
            ys = [sb("ln_y%d" % i, [128, D], F32, st) for i in range(NB)]
            b_ys = [Buf() for i in range(NB)]
            hbs = [sb("ln_hb%d" % i, [128, D], BF16, st) for i in range(2)]
            b_hbs = [Buf() for i in range(2)]
            stats = [sb("ln_st%d" % i, [128, 4, 6], F32, st) for i in range(NB)]
            mvs = [sb("ln_mv%d" % i, [128, 4], F32, st) for i in range(NB)]
            b_sts = [Buf() for i in range(NB)]
            stg = [sb("ln_stg%d" % i, [128, 16, 512], BF16, st) for i in range(1)] if write_hT else []
            b_stg = [Buf() for i in range(1)]
            if src_kind == "proj":
                bo = sb("ln_bo", [1, D], BF16, st)
                P.dma("pool", bo[:], lnp_d[4 + 3 * layer_idx:5 + 3 * layer_idx, :], writes=[b_p], key="ln_bo")
                wo = sb_top("wo", [128, 16, 2048], BF16)
                b_wop = [Buf(), Buf()] + list(wo_hi_bufs)
                for i4 in (1, 0):
                    P.dma("pool", wo[:, i4 * 4:(i4 + 1) * 4, :].rearrange("p a b -> p (a b)"),
                          wo_d[layer_idx * 128:(layer_idx + 1) * 128, i4 * 8192:(i4 + 1) * 8192],
                          reads=([b_wop[1]] if i4 == 0 else [b_p]), writes=[b_wop[i4]], key="wo%d" % i4)
                mts = [sb("mt%d" % i, [128, 16, 256], BF16, st) for i in range(2)]
                b_mts = [Buf() for i in range(2)]
                NR = 2
                rts = [sb("rt%d" % i, [128, D], F32, st) for i in range(NR)]
                b_rts = [Buf() for i in range(NR)]
                ea = [sb("ea%d" % i, [128, 8, 256], BF16, st) for i in range(2)]
                eb = [sb("eb%d" % i, [128, 8, 256], BF16, st) for i in range(2)]
                b_ea = [Buf() for i in range(2)]
                b_eb = [Buf() for i in range(2)]
                et = sb("et", [128, 8, 256], BF16, st)
                b_et = Buf()

            def s_pair(tk):
                tt = tk // 4
                t2 = tk // 2
                i2 = t2 % 2
                mt, b_mt = mts[i2], b_mts[i2]
                P.dma("sp", mt[:], mix_v[:, :, t2 * 256:(t2 + 1) * 256], reads=[b_mix[r][tt] for r in range(16)],
                      writes=[b_mt], key="mt%d" % i2)
                P.dma("sp", ea[i2][:].rearrange("p (h r) t -> p h r t", r=2), e2g_v[:, :, :, t2 * 256:(t2 + 1) * 256],
                      reads=b_e2g, writes=[b_ea[i2]], key="ea%d" % i2)
                P.dma("sp", eb[i2][:].rearrange("p (h r) t -> p h r t", r=2), e2g_v[:, :, :, SO + t2 * 256:SO + (t2 + 1) * 256],
                      reads=b_e2g, writes=[b_eb[i2]], key="eb%d" % i2)

            def s_blend(tk):
                i2 = (tk // 2) % 2
                mt, b_mt = mts[i2], b_mts[i2]
                P.op("dve", lambda e: e.tensor_scalar(out=et[:], in0=ea[i2][:], scalar1=coef[:, 0:1], scalar2=None, op0=ALU.mult),
                     reads=[b_ea[i2], b_const], writes=[b_et])
                P.op("dve", lambda e: e.scalar_tensor_tensor(out=et[:], in0=eb[i2][:], scalar=coef[:, 1:2], in1=et[:], op0=ALU.mult, op1=ALU.add),
                     reads=[b_eb[i2], b_et, b_const], writes=[b_et])
                mt8 = mt[:, 0:8, :].rearrange("p (r h) t -> p h r t", r=2)
                etv = et[:].rearrange("p (h r) t -> p h r t", r=2)
                P.op("pool", lambda e: e.tensor_tensor(out=mt8, in0=mt8, in1=etv, op=ALU.mult),
                     reads=[b_et, b_mt], writes=[b_mt])

            def s_load(tk):
                y, b_y = ys[tk % NB], b_ys[tk % NB]
                tsl = slice(tk * 128, (tk + 1) * 128)
                if src_kind == "x":
                    P.dma("sp", y[:], x_d[tsl, :], writes=[b_y], key="ln_y%d" % (tk % NB))
                else:
                    rt, b_rt = rts[tk % NR], b_rts[tk % NR]
                    P.dma("sp", rt[:], resid_d[tsl, :], reads=[b_resid[tk]], writes=[b_rt], key="rt%d" % (tk % NR))

            def s1(tk):
                y, b_y = ys[tk % NB], b_ys[tk % NB]
                stt, mv, b_st = stats[tk % NB], mvs[tk % NB], b_sts[tk % NB]
                if src_kind != "x":
                    t2 = tk // 2
                    mt, b_mt = mts[t2 % 2], b_mts[t2 % 2]
                    rt, b_rt = rts[tk % NR], b_rts[tk % NR]
                    pts = [psum() for cg in range(4)]
                    for cg in range(4):
                        pt, b_pt = pts[cg]
                        P.op("pe", lambda e, pt=pt, cg=cg: e.matmul(pt, ones[0:1, :], bo[0:1, cg * 512:(cg + 1) * 512], start=True, stop=False),
                             reads=[b_p, b_const], writes=[b_pt])
                    for kc in reversed(range(16)):
                        for cg in range(4):
                            pt, b_pt = pts[cg]
                            P.op("pe", lambda e, pt=pt, kc=kc, cg=cg: e.matmul(
                                pt, mt[:, kc, (tk % 2) * 128:(tk % 2 + 1) * 128], wo[:, kc, cg * 512:(cg + 1) * 512],
                                start=False, stop=(kc == 0)), reads=[b_mt, b_wop[kc // 4]], writes=[b_pt])
                    for cg in range(4):
                        pt, b_pt = pts[cg]
                        P.op("dve", lambda e, pt=pt, cg=cg: e.scalar_tensor_tensor(
                            out=y[:, cg * 512:(cg + 1) * 512], in0=rt[:, cg * 512:(cg + 1) * 512], scalar=ALPHA, in1=pt,
                            op0=ALU.mult, op1=ALU.add), reads=[b_pt, b_rt], writes=[b_y])
                for c in range(4):
                    P.op("dve", lambda e, c=c: e.bn_stats(out=stt[:, c, :], in_=y[:, c * 512:(c + 1) * 512]),
                         reads=[b_y], writes=[b_st])
                P.op("dve", lambda e: e.bn_aggr(out=mv[:, 0:2], in_=stt[:].rearrange("p a b -> p (a b)")),
                     reads=[b_st], writes=[b_st])
                P.op("act", lambda e: e.activation(out=mv[:, 2:3], in_=mv[:, 1:2], func=AF.Sqrt, bias=LN_EPS_AP[:, 0:1], scale=1.0),
                     reads=[b_st, b_const], writes=[b_st])
                P.op("dve", lambda e: e.reciprocal(out=mv[:, 2:3], in_=mv[:, 2:3]), reads=[b_st], writes=[b_st])
                P.op("dve", lambda e: e.scalar_tensor_tensor(out=mv[:, 3:4], in0=mv[:, 0:1], scalar=-1.0, in1=mv[:, 2:3],
                                                              op0=ALU.mult, op1=ALU.mult), reads=[b_st], writes=[b_st])

            def s2a(tk):
                y, b_y = ys[tk % NB], b_ys[tk % NB]
                mv, b_st = mvs[tk % NB], b_sts[tk % NB]
                P.op("act", lambda e: e.activation(out=y[:], in_=y[:], func=AF.Identity, bias=mv[:, 3:4], scale=mv[:, 2:3]),
                     reads=[b_y, b_st], writes=[b_y])

            def s2b(tk):
                y, b_y = ys[tk % NB], b_ys[tk % NB]
                P.op("dve", lambda e: e.tensor_tensor(out=y[:], in0=y[:], in1=gb[:], op=ALU.mult), reads=[b_y, b_pg], writes=[b_y])
                P.op("pool", lambda e: e.tensor_tensor(out=y[:], in0=y[:], in1=bb[:], op=ALU.add), reads=[b_y, b_pb], writes=[b_y])

            def s3(tk):
                y, b_y = ys[tk % NB], b_ys[tk % NB]
                hb, b_hb = hbs[tk % 2], b_hbs[tk % 2]
                tsl = slice(tk * 128, (tk + 1) * 128)
                if final:
                    P.dma("sp", out_d[tsl, :], y[:], reads=[b_y], writes=[b_out[tk]], key="ln_o%d" % (tk % NB))
                else:
                    P.dma("sp", resid_d[tsl, :], y[:], reads=[b_y], writes=[b_resid[tk]], key="ln_o%d" % (tk % NB))
                if write_hT:
                    P.op("act", lambda e: e.copy(out=hb[:], in_=y[:]), reads=[b_y], writes=[b_hb])
                    sg_, b_sg_ = stg[0], b_stg[0]
                    for q4 in range(4):
                        pt, b_pt = psum()
                        ptb = pt.bitcast(BF16)
                        for j in range(4):
                            kc = q4 * 4 + j
                            P.op("pe", lambda e, ptb=ptb, kc=kc, j=j: e.transpose(
                                ptb[:, j * 128:(j + 1) * 128], hb[:, kc * 128:(kc + 1) * 128], ident[:]),
                                reads=[b_hb, b_const], writes=[b_pt])
                        if q4 % 2 == 0:
                            P.op("dve", lambda e, ptb=ptb, q4=q4: e.tensor_copy(
                                out=sg_[:, q4 * 4:(q4 + 1) * 4, (tk % 4) * 128:(tk % 4 + 1) * 128],
                                in_=ptb[:, 0:512].rearrange("p (a b) -> p a b", a=4)), reads=[b_pt], writes=[b_sg_])
                        else:
                            P.op("act", lambda e, ptb=ptb, q4=q4: e.copy(
                                out=sg_[:, q4 * 4:(q4 + 1) * 4, (tk % 4) * 128:(tk % 4 + 1) * 128],
                                in_=ptb[:, 0:512].rearrange("p (a b) -> p a b", a=4)), reads=[b_pt], writes=[b_sg_])
                    if tk % 4 == 3:
                        tt = tk // 4
                        P.dma("sp", hT_v[:, :, tt * 512:(tt + 1) * 512], sg_[:], reads=[b_sg_], writes=[b_hT[tt]], key="ln_stg")
                        if tk == NTK - 1:
                            P.dma("sp", tls_v, sg_[:, :, 496:512], reads=[b_sg_], writes=[b_tlsrc], key="ln_tl")
                            P.collective(tl_src, tl_g, reads=[b_tlsrc], writes=[b_tlg], key="cc_e3")

            NTK = SO // 128
            is_proj = (src_kind != "x")
            if is_proj:
                s_pair(0)
                s_blend(0)
            s_load(0)
            for i in range(NTK + 2):
                if is_proj and i % 2 == 0 and i + 2 < NTK:
                    s_pair(i + 2)
                if is_proj and i % 2 == 1 and i + 1 < NTK:
                    s_blend(i + 1)
                if i + 1 < NTK:
                    s_load(i + 1)
                if 0 <= i - 1 < NTK:
                    s2a(i - 1)
                if i < NTK:
                    s1(i)
                if 0 <= i - 1 < NTK:
                    s2b(i - 1)
                if 0 <= i - 2 < NTK:
                    s3(i - 2)

        P.op("dve", lambda e: e.memset(LN_EPS_AP[:, 0:1], LN_EPS), writes=[b_const])
        P.op("dve", lambda e: e.memset(LN_EPS_AP[:, 1:2], RMS_EPS), writes=[b_const])

        with contextlib.ExitStack() as st:
            ln_phase(st, 0, "x", True, False)
            P.barrier(junk)

        def phase_A(L):
            with contextlib.ExitStack() as st:
                areset()
                hTs = sb("hTs", [128, 16, 2048], BF16, st)
                b_hTs = Buf()
                wr = [sb("wr%d" % i, [128, 16, 512], BF16, st) for i in range(2)]
                b_wr = [Buf() for i in range(2)]
                sm = sb("sm", [128, 32], F32, st)
                wp = sb("wp", [128, 4, 128], BF16, st)
                b_sm = Buf()
                P.dma("sp", sm[:], sm_d[L * 128:(L + 1) * 128, :], writes=[b_sm], key="sm")
                P.dma("pool", wp[:].rearrange("p a b -> p (a b)"), wp_d[L * 128:(L + 1) * 128, :], writes=[b_sm], key="wp")
                hp = sb("hp", [128, 4, 16], F32, st)
                hc = sb("hc", [128, 4, 2], F32, st)
                b_hp = [Buf() for i in range(4)]
                b_hc = [Buf() for i in range(4)]
                P.op("pool", lambda e: e.memset(hp[:], 0.0), writes=b_hp)
                P.op("pool", lambda e: e.memset(hc[:], 0.0), writes=b_hc)
                stgs = [sb("stg%d" % i, [128, 4, 512], BF16, st) for i in range(2)]
                b_stgs = [Buf() for i in range(2)]
                stg_ctr = [0]
                NTMP = 6
                tmps = [sb("tmp%d" % i, [128, 528], F32, st) for i in range(NTMP)]
                b_tmps = [Buf() for i in range(NTMP)]
                tmp_ctr = [0]
                sqs = [sb("sq%d" % i, [128, 512], BF16, st) for i in range(2)]
                b_sqs = [Buf() for i in range(2)]
                sq_ctr = [0]
                pls = [sb("pl%d" % i, [128, 512], BF16, st) for i in range(4)]
                b_pls = [Buf() for i in range(4)]
                sgps = [sb("sgp%d" % i, [128, 512], F32, st) for i in range(4)]
                b_sgps = [Buf() for i in range(4)]
                pl_ctr = [0]
                pending = []
                cosT = sb("cosO", [64, SO], BF16, st)
                sinT = sb("sinO", [64, SO], BF16, st)
                b_tab = Buf()
                P.dma("sp", cosT[:], cosO_d, reads=[b_tabd], writes=[b_tab], key="cosO")
                P.dma("sp", sinT[:], sinO_d, reads=[b_tabd], writes=[b_tab], key="sinO")
                hTh = sb("hTh", [128, 16, 16], BF16, st)
                b_hTh = Buf()
                P.dma("sp", hTh[:], tlg_v[:, 0:16, :], reads=[b_tlg], writes=[b_hTh], key="hTh")

                def proj_halo(w, c0):
                    pt, b_pt = psum()
                    for kc in range(16):
                        P.op("pe", lambda e, pt=pt, w=w, kc=kc: e.matmul(
                            pt[:, 0:16], w[0][:, kc, c0:c0 + 128], hTh[:, kc, :],
                            start=(kc == 0), stop=(kc == 15)), reads=[w[1], b_hTh], writes=[b_pt])
                    return pt, b_pt

                def tmp():
                    i = tmp_ctr[0] % NTMP
                    tmp_ctr[0] += 1
                    return tmps[i], b_tmps[i]

                def proj(w, c0, ncols, tt):
                    pt, b_pt = psum()
                    for kc in range(16):
                        P.op("pe", lambda e, pt=pt, w=w, kc=kc: e.matmul(
                            pt[0:ncols, :], w[0][:, kc, c0:c0 + ncols], hTs[:, kc, tt * 512:(tt + 1) * 512],
                            start=(kc == 0), stop=(kc == 15)), reads=[w[1], b_hTs], writes=[b_pt])
                    return pt, b_pt

                def rmsnorm_group(w, c0, nch, tt, dim, dst_v, dst_bufs, gtt, key):
                    pts = [proj(w, c0 + c * 128, 128, tt) for c in range(nch)]
                    ss, b_ss = psum()
                    for c, (pt, b_pt) in enumerate(pts):
                        i = sq_ctr[0] % 2
                        sq_ctr[0] += 1
                        sq, b_sq = sqs[i], b_sqs[i]
                        P.op("act", lambda e, pt=pt, sq=sq: e.activation(out=sq[:], in_=pt, func=AF.Square), reads=[b_pt], writes=[b_sq])
                        P.op("pe", lambda e, ss=ss, sq=sq, c=c: e.matmul(ss, ones[:], sq[:], start=(c == 0), stop=(c == nch - 1)),
                             reads=[b_sq, b_const], writes=[b_ss])
                    rs, b_rs = tmp()
                    P.op("act", lambda e, rs=rs, ss=ss: e.activation(out=rs[:, 0:512], in_=ss, func=AF.Sqrt, bias=LN_EPS_AP[:, 1:2],
                                                                      scale=1.0 / dim), reads=[b_ss, b_const], writes=[b_rs])
                    P.op("dve", lambda e, rs=rs: e.reciprocal(out=rs[:, 0:512], in_=rs[:, 0:512]), reads=[b_rs], writes=[b_rs])
                    i = stg_ctr[0] % 2
                    stg_ctr[0] += 1
                    sg_, b_sg_ = stgs[i], b_stgs[i]
                    for c, (pt, b_pt) in enumerate(pts):
                        P.op("dve", lambda e, pt=pt, rs=rs, sg_=sg_, c=c: e.tensor_tensor(out=sg_[:, c, :], in0=pt, in1=rs[:, 0:512], op=ALU.mult),
                             reads=[b_pt, b_rs], writes=[b_sg_])
                    P.dma("sp", dst_v[:, :, gtt * 512:(gtt + 1) * 512], sg_[:, 0:nch, :], reads=[b_sg_], writes=[dst_bufs[gtt]], key="stg%d" % i)

                for hf in range(1):
                    P.dma("sp", hTs[:], hT_v[:, :, hf * 2048:(hf + 1) * 2048], reads=b_hT[hf * 4:(hf + 1) * 4], writes=[b_hTs], key="hTs")
                    for g in range(NG):
                        wi = (hf * NG + g) % 2
                        w = (wr[wi], b_wr[wi])
                        row0 = (L * NG + g) * 128
                        P.dma("pool", wr[wi][:].rearrange("p a b -> p (a b)"), win_d[row0:row0 + 128, :], writes=[b_wr[wi]], key="wr%d" % wi)
                        if g == 2:
                            if debug:
                                P.dma("sp", dbg_e1a, e1a_src, reads=b_qn, writes=[Buf()], key="dbg_e1a")
                                P.dma("sp", dbg_e1b, e1b_src, reads=b_kvn + b_kr, writes=[Buf()], key="dbg_e1b")
                            P.collective(e1b_src, e1b_g, reads=b_kvn + b_kr, writes=[b_e1g], key="cc_e1b")
                            P.collective(e1a_src, e1a_g, reads=b_qn + [b_e1g], writes=[b_e1g], key="cc_e1a")
                        for tt in range(4):
                            gtt = hf * 4 + tt
                            tok = slice(gtt * 512, (gtt + 1) * 512)
                            if g == 0:
                                rmsnorm_group(w, 0, 2, tt, 256.0, kvn_v, b_kvn, gtt, "kvn")
                                pa, b_pa = proj(w, 256, 64, tt)
                                pb, b_pb = proj(w, 320, 64, tt)
                                t1, b_t1 = tmp()
                                t2, b_t2 = tmp()
                                P.op("dve", lambda e, pa=pa, t1=t1, tok=tok: e.tensor_tensor(out=t1[0:64, 0:512], in0=pa[0:64, :], in1=cosT[:, tok], op=ALU.mult),
                                     reads=[b_pa, b_tab], writes=[b_t1])
                                P.op("dve", lambda e, pb=pb, t2=t2, tok=tok: e.tensor_tensor(out=t2[0:64, 0:512], in0=pb[0:64, :], in1=sinT[:, tok], op=ALU.mult),
                                     reads=[b_pb, b_tab], writes=[b_t2])
                                i = stg_ctr[0] % 2
                                stg_ctr[0] += 1
                                sg_, b_sg_ = stgs[i], b_stgs[i]
                                P.op("pool", lambda e, t1=t1, t2=t2, sg_=sg_: e.tensor_tensor(out=sg_[0:64, 0, :], in0=t1[0:64, 0:512], in1=t2[0:64, 0:512], op=ALU.add),
                                     reads=[b_t1, b_t2], writes=[b_sg_])
                                P.dma("sp", kr_d[:, tok], sg_[0:64, 0, :], reads=[b_sg_], writes=[b_kr[gtt]], key="stg%d" % i)
                            elif g == 1:
                                rmsnorm_group(w, 0, 4, tt, 512.0, qn_v, b_qn, gtt, "qn")
                            elif g in (2, 3):
                                i = stg_ctr[0] % 2
                                stg_ctr[0] += 1
                                sg_, b_sg_ = stgs[i], b_stgs[i]
                                for c in range(4):
                                    pt, b_pt = proj(w, c * 128, 128, tt)
                                    P.op("act", lambda e, pt=pt, sg_=sg_, c=c: e.activation(out=sg_[:, c, :], in_=pt, func=AF.Silu),
                                         reads=[b_pt], writes=[b_sg_])
                                r0 = (g - 2) * 4
                                P.dma("sp", mix_v[:, r0:r0 + 4, tok], sg_[:], reads=[b_sg_], writes=[b_mix[r0 + c][gtt] for c in range(4)], key="stg%d" % i)
                            elif g in (4, 5):
                                tails = []
                                for s_ in range(2):
                                    pg = (g - 4) * 2 + s_
                                    wlen = POOL_W[pg]
                                    px, b_px = proj(w, s_ * 256, 128, tt)
                                    pgt, b_pgt = proj(w, s_ * 256 + 128, 128, tt)
                                    xb, b_xb = tmp()
                                    sa, b_sa = tmp()
                                    sb_, b_sb = tmp()
                                    P.op("act", lambda e, px=px, xb=xb: e.copy(out=xb[:, 16:528], in_=px), reads=[b_px], writes=[b_xb])
                                    if tt == 0:
                                        ph, b_ph = proj_halo(w, s_ * 256)
                                        P.op("dve", lambda e, xb=xb, ph=ph: e.tensor_scalar(out=xb[:, 0:16], in0=ph[:, 0:16], scalar1=coef[:, 1:2], scalar2=None, op0=ALU.mult),
                                             reads=[b_ph, b_xb, b_const], writes=[b_xb])
                                    else:
                                        P.op("pool", lambda e, xb=xb, pg=pg: e.tensor_copy(out=xb[:, 0:16], in_=hp[:, pg, :]), reads=[b_hp[pg], b_xb], writes=[b_xb])
                                    P.op("pool", lambda e, xb=xb, pg=pg: e.tensor_copy(out=hp[:, pg, :], in_=xb[:, 512:528]), reads=[b_xb], writes=[b_hp[pg]])
                                    P.op("pool", lambda e, xb=xb, sa=sa: e.tensor_tensor(out=sa[:, 1:528], in0=xb[:, 1:528], in1=xb[:, 0:527], op=ALU.add),
                                         reads=[b_xb], writes=[b_sa])
                                    fin, b_fin = sa, b_sa
                                    if wlen >= 4:
                                        P.op("pool", lambda e, sa=sa, sb_=sb_: e.tensor_tensor(out=sb_[:, 3:528], in0=sa[:, 3:528], in1=sa[:, 1:526], op=ALU.add),
                                             reads=[b_sa], writes=[b_sb])
                                        fin, b_fin = sb_, b_sb
                                    if wlen >= 8:
                                        P.op("pool", lambda e, sa=sa, sb_=sb_: e.tensor_tensor(out=sa[:, 7:528], in0=sb_[:, 7:528], in1=sb_[:, 3:524], op=ALU.add),
                                             reads=[b_sb, b_sa], writes=[b_sa])
                                        fin, b_fin = sa, b_sa
                                    if wlen >= 16:
                                        P.op("pool", lambda e, sa=sa, sb_=sb_: e.tensor_tensor(out=sb_[:, 15:528], in0=sa[:, 15:528], in1=sa[:, 7:520], op=ALU.add),
                                             reads=[b_sa, b_sb], writes=[b_sb])
                                        fin, b_fin = sb_, b_sb
                                    ip = pl_ctr[0] % 4
                                    pl_ctr[0] += 1
                                    pl, b_pl = pls[ip], b_pls[ip]
                                    sgp, b_sgp = sgps[ip], b_sgps[ip]
                                    P.op("dve", lambda e, fin=fin, xb=xb, pl=pl, wlen=wlen: e.scalar_tensor_tensor(
                                        out=pl[:], in0=fin[:, 16:528], scalar=1.0 / wlen, in1=xb[:, 16:528], op0=ALU.mult, op1=ALU.subtract),
                                        reads=[b_fin, b_xb], writes=[b_pl])
                                    if gtt == 0:
                                        t16, b_t16 = tmp()
                                        P.op("dve", lambda e, fin=fin, t16=t16, pg=pg: e.tensor_tensor(out=t16[:, 0:16], in0=fin[:, 16:32], in1=invdiv[:, pg, :], op=ALU.mult),
                                             reads=[b_fin, b_const], writes=[b_t16])
                                        P.op("dve", lambda e, t16=t16, xb=xb, pl=pl: e.tensor_tensor(out=pl[:, 0:16], in0=t16[:, 0:16], in1=xb[:, 16:32], op=ALU.subtract),
                                             reads=[b_t16, b_xb, b_pl], writes=[b_pl])
                                    P.op("act", lambda e, pgt=pgt, sgp=sgp: e.activation(out=sgp[:], in_=pgt, func=AF.Silu), reads=[b_pgt], writes=[b_sgp])
                                    tails.append((s_, pg, pl, b_pl, sgp, b_sgp))

                                def pool_tail(tails=tails, g=g, gtt=gtt, tok=tok):
                                    i = stg_ctr[0] % 2
                                    stg_ctr[0] += 1
                                    sg_, b_sg_ = stgs[i], b_stgs[i]
                                    for (s_, pg, pl, b_pl, sgp, b_sgp) in tails:
                                        py, b_py = psum()
                                        P.op("pe", lambda e, py=py, pl=pl, pg=pg: e.matmul(py, wp[:, pg, :], pl[:], start=True, stop=True),
                                             reads=[b_pl, b_sm], writes=[b_py])
                                        P.op("dve", lambda e, py=py, sgp=sgp, pg=pg, s_=s_: e.scalar_tensor_tensor(
                                            out=sg_[:, s_, :], in0=py, scalar=sm[:, 6 + pg:7 + pg], in1=sgp[:], op0=ALU.mult, op1=ALU.mult),
                                            reads=[b_py, b_sgp, b_sm], writes=[b_sg_])
                                    r0 = 8 + (g - 4) * 2
                                    P.dma("sp", mix_v[:, r0:r0 + 2, tok], sg_[:, 0:2, :], reads=[b_sg_], writes=[b_mix[r0 + c][gtt] for c in range(2)], key="stg%d" % i)

                                for fn_ in pending:
                                    fn_()
                                pending.clear()
                                pending.append(pool_tail)
                                if tt == 3:
                                    for fn_ in pending:
                                        fn_()
                                    pending.clear()
                            else:
                                j = g - 6
                                i = stg_ctr[0] % 2
                                stg_ctr[0] += 1
                                sg_, b_sg_ = stgs[i], b_stgs[i]
                                pch, b_pch = proj(w, 0, 128, tt)
                                pcc, b_pcc = proj(w, 128, 128, tt)
                                pcb, b_pcb = proj(w, 256, 128, tt)
                                pgc, b_pgc = proj(w, 384, 128, tt)
                                chs, b_chs = tmp()
                                ub, b_ub = tmp()
                                t1, b_t1 = tmp()
                                t2, b_t2 = tmp()
                                P.op("act", lambda e, pch=pch, chs=chs: e.copy(out=chs[:, 0:512], in_=pch), reads=[b_pch], writes=[b_chs])
                                P.op("dve", lambda e, pcc=pcc, chs=chs, ub=ub: e.tensor_tensor(out=ub[:, 2:514], in0=pcc, in1=chs[:, 0:512], op=ALU.mult),
                                     reads=[b_pcc, b_chs], writes=[b_ub])
                                if tt == 0:
                                    ph1, b_ph1 = proj_halo(w, 0)
                                    ph2, b_ph2 = proj_halo(w, 128)
                                    hh, b_hh = tmp()
                                    P.op("act", lambda e, ph1=ph1, hh=hh: e.copy(out=hh[:, 0:16], in_=ph1[:, 0:16]), reads=[b_ph1], writes=[b_hh])
                                    P.op("dve", lambda e, ph2=ph2, hh=hh: e.tensor_tensor(out=hh[:, 16:32], in0=ph2[:, 0:16], in1=hh[:, 0:16], op=ALU.mult),
                                         reads=[b_ph2, b_hh], writes=[b_hh])
                                    P.op("dve", lambda e, ub=ub, hh=hh: e.tensor_scalar(out=ub[:, 0:2], in0=hh[:, 30:32], scalar1=coef[:, 1:2], scalar2=None, op0=ALU.mult),
                                         reads=[b_hh, b_ub, b_const], writes=[b_ub])
                                else:
                                    P.op("pool", lambda e, ub=ub, j=j: e.tensor_copy(out=ub[:, 0:2], in_=hc[:, j, :]), reads=[b_hc[j], b_ub], writes=[b_ub])
                                P.op("pool", lambda e, ub=ub, j=j: e.tensor_copy(out=hc[:, j, :], in_=ub[:, 512:514]), reads=[b_ub], writes=[b_hc[j]])
                                cw0 = 10 + j * 3
                                P.op("dve", lambda e, ub=ub, t1=t1, cw0=cw0: e.tensor_scalar(out=t1[:, 0:512], in0=ub[:, 0:512], scalar1=sm[:, cw0:cw0 + 1], scalar2=None, op0=ALU.mult),
                                     reads=[b_ub, b_sm], writes=[b_t1])
                                P.op("dve", lambda e, ub=ub, t1=t1, t2=t2, cw0=cw0: e.scalar_tensor_tensor(
                                    out=t2[:, 0:512], in0=ub[:, 1:513], scalar=sm[:, cw0 + 1:cw0 + 2], in1=t1[:, 0:512], op0=ALU.mult, op1=ALU.add),
                                    reads=[b_ub, b_t1, b_sm], writes=[b_t2])
                                P.op("dve", lambda e, ub=ub, t1=t1, t2=t2, cw0=cw0: e.scalar_tensor_tensor(
                                    out=t1[:, 0:512], in0=ub[:, 2:514], scalar=sm[:, cw0 + 2:cw0 + 3], in1=t2[:, 0:512], op0=ALU.mult, op1=ALU.add),
                                    reads=[b_ub, b_t2, b_t1, b_sm], writes=[b_t1])
                                P.op("dve", lambda e, pcb=pcb, t1=t1, t2=t2: e.tensor_tensor(out=t2[:, 0:512], in0=pcb, in1=t1[:, 0:512], op=ALU.mult),
                                     reads=[b_pcb, b_t1, b_t2], writes=[b_t2])
                                P.op("act", lambda e, pgc=pgc, chs=chs: e.activation(out=chs[:, 0:512], in_=pgc, func=AF.Silu), reads=[b_pgc, b_chs], writes=[b_chs])
                                P.op("pool", lambda e, t2=t2, chs=chs, sg_=sg_: e.tensor_tensor(out=sg_[:, 0, :], in0=t2[:, 0:512], in1=chs[:, 0:512], op=ALU.mult),
                                     reads=[b_t2, b_chs], writes=[b_sg_])
                                r0 = 12 + j
                                P.dma("sp", mix_v[:, r0, tok], sg_[:, 0, :], reads=[b_sg_], writes=[b_mix[r0][gtt]], key="stg%d" % i)
                P.barrier(junk)

        def phase_B(L, wo_hi_bufs):
            with contextlib.ExitStack() as st:
                areset()
                wo_hi = sb_top("wo_hi", [128, 8, 2048], BF16)
                SB_ = [4, 5, 6, 7]
                sctr = [0]
                octr = [0]
                lctr = [0]
                qns = sb("qns", [128, 4, S], BF16, st)
                kvns = sb("kvns", [128, 2, S], BF16, st)
                krs = sb("krs", [64, S], BF16, st)
                b_kvl, b_krl, b_qnl = [Buf(), Buf()], [Buf(), Buf()], [Buf(), Buf()]
                b_lc = [Buf(), Buf()]
                for rho in range(2):
                    csl = slice(rho * SO, (rho + 1) * SO)
                    P.dma("sp", kvns[:, :, csl], e1b_g[rho * 320:rho * 320 + 256, :].rearrange("(kc p) t -> p kc t", p=128), reads=[b_e1g], writes=[b_kvl[rho], b_lc[rho]], key="kvns%d" % rho)
                    P.dma("sp", krs[:, csl], e1b_g[rho * 320 + 256:rho * 320 + 320, :], reads=[b_e1g], writes=[b_krl[rho], b_lc[rho]], key="krs%d" % rho)
                    P.dma("sp", qns[:, :, csl], e1a_g[rho * 512:(rho + 1) * 512, :].rearrange("(kc p) t -> p kc t", p=128), reads=[b_e1g], writes=[b_qnl[rho], b_lc[rho]], key="qns%d" % rho)
                cosT = sb("cosA", [64, S], BF16, st)
                sinT = sb("sinA", [64, S], BF16, st)
                b_tab = Buf()
                P.dma("sp", cosT[:], cosA_d, reads=[b_tabd], writes=[b_tab], key="cosA")
                P.dma("sp", sinT[:], sinA_d, reads=[b_tabd], writes=[b_tab], key="sinA")
                wq = sb("wq", [128, 4, 1024], BF16, st)
                wk = sb("wk", [128, 2, 512], BF16, st)
                wv = sb("wv", [128, 2, 512], BF16, st)
                sm = sb("smB", [128, 32], F32, st)
                b_w = Buf()
                P.dma("sp", sm[:], sm_d[L * 128:(L + 1) * 128, :], writes=[b_w], key="smB")
                b_w1, b_w2, b_w3 = Buf(), Buf(), Buf()
                P.dma("pool", wq[:].rearrange("p a b -> p (a b)"), wq_d[L * 128:(L + 1) * 128, :], writes=[b_w1], key="wq")
                P.dma("pool", wk[:].rearrange("p a b -> p (a b)"), wk_d[L * 128:(L + 1) * 128, :], reads=[b_w1], writes=[b_w2], key="wk")
                P.dma("pool", wv[:].rearrange("p a b -> p (a b)"), wv_d[L * 128:(L + 1) * 128, :], reads=[b_w1, b_w2], writes=[b_w, b_w3], key="wv")
                P.dma("pool", wo_hi[:, 4:8, :].rearrange("p a b -> p (a b)"), wo_d[L * 128:(L + 1) * 128, 3 * 8192:4 * 8192],
                      reads=[b_w3], writes=[wo_hi_bufs[1]], key="wo3")
                P.dma("pool", wo_hi[:, 0:4, :].rearrange("p a b -> p (a b)"), wo_d[L * 128:(L + 1) * 128, 2 * 8192:3 * 8192],
                      reads=[wo_hi_bufs[1]], writes=[wo_hi_bufs[0]], key="wo2")
                for kc in range(4):
                    P.op("dve", lambda e, kc=kc: e.tensor_scalar(out=wq[:, kc, :], in0=wq[:, kc, :], scalar1=sm[:, kc:kc + 1], scalar2=None, op0=ALU.mult),
                         reads=[b_w], writes=[b_w])
                for kc in range(2):
                    P.op("dve", lambda e, kc=kc: e.tensor_scalar(out=wk[:, kc, :], in0=wk[:, kc, :], scalar1=sm[:, 4 + kc:5 + kc], scalar2=None, op0=ALU.mult),
                         reads=[b_w], writes=[b_w])
                    P.op("dve", lambda e, kc=kc: e.tensor_scalar(out=wv[:, kc, :], in0=wv[:, kc, :], scalar1=sm[:, 4 + kc:5 + kc], scalar2=None, op0=ALU.mult),
                         reads=[b_w], writes=[b_w])
                Vq = sb("Vq", [128, 32, 512], BF16, st)
                b_V = [Buf() for i in range(32)]
                kTh = sb("kTh", [128, S], BF16, st)
                qTh = sb("qTh", [128, S], BF16, st)
                qrh = sb("qrh", [64, S], BF16, st)
                b_kT = [Buf() for i in range(8)]
                b_qT = [Buf() for i in range(8)]
                b_qr = [Buf() for i in range(8)]
                pTs = [sb("pT%d" % i, [128, 512], BF16, st) for i in range(4)]
                b_pTs = [Buf() for i in range(4)]
                pT_ctr = [0]
                rls = [sb("rl%d" % i, [128, 512], F32, st) for i in range(2)]
                b_rls = [Buf() for i in range(2)]
                outs = [sb("ob%d" % i, [128, 512], BF16, st) for i in range(2)]
                b_outs = [Buf() for i in range(2)]
                rt1 = [sb("rta%d" % i, [64, 512], F32, st) for i in range(2)]
                rt2 = [sb("rtb%d" % i, [64, 512], F32, st) for i in range(2)]
                b_rt1 = [Buf() for i in range(2)]
                b_rt2 = [Buf() for i in range(2)]
                ev = [0]

                def evac(out_ap, in_ap, reads, writes):
                    ev[0] += 1
                    if ev[0] % 2 == 0:
                        P.op("dve", lambda e: e.tensor_copy(out=out_ap, in_=in_ap), reads=reads, writes=writes)
                    else:
                        P.op("act", lambda e: e.copy(out=out_ap, in_=in_ap), reads=reads, writes=writes)

                uc = [0]
                for h in range(HL):
                    if h % 4 == 0:
                        hq = h // 4
                        for tk in range(32):
                            pt, b_pt = psum(SB_, sctr)
                            for kc in range(2):
                                P.op("pe", lambda e, pt=pt, kc=kc, tk=tk, hq=hq: e.matmul(
                                    pt, kvns[:, kc, tk * 128:(tk + 1) * 128], wv[:, kc, hq * 512:(hq + 1) * 512], start=(kc == 0), stop=(kc == 1)),
                                    reads=[b_kvl[tk // 16], b_w], writes=[b_pt])
                            evac(Vq[:, tk, :], pt, [b_pt], [b_V[tk]])
                    for tt in range(8):
                        tok = slice(tt * 512, (tt + 1) * 512)
                        pt, b_pt = psum(SB_, sctr)
                        for kc in range(2):
                            P.op("pe", lambda e, pt=pt, kc=kc, tok=tok, h=h: e.matmul(
                                pt, wk[:, kc, h * 128:(h + 1) * 128], kvns[:, kc, tok], start=(kc == 0), stop=(kc == 1)),
                                reads=[b_kvl[tt // 4], b_w], writes=[b_pt])
                        evac(kTh[:, tok], pt, [b_pt], [b_kT[tt]])
                        pt, b_pt = psum(SB_, sctr)
                        for kc in range(4):
                            P.op("pe", lambda e, pt=pt, kc=kc, tok=tok, h=h: e.matmul(
                                pt, wq[:, kc, h * 256:h * 256 + 128], qns[:, kc, tok], start=(kc == 0), stop=(kc == 3)),
                                reads=[b_qnl[tt // 4], b_w], writes=[b_pt])
                        evac(qTh[:, tok], pt, [b_pt], [b_qT[tt]])
                        pa, b_pa = psum(SB_, sctr)
                        for kc in range(4):
                            P.op("pe", lambda e, pa=pa, kc=kc, tok=tok, h=h: e.matmul(
                                pa[0:64, :], wq[:, kc, h * 256 + 128:h * 256 + 192], qns[:, kc, tok], start=(kc == 0), stop=(kc == 3)),
                                reads=[b_qnl[tt // 4], b_w], writes=[b_pa])
                        pb, b_pb = psum(SB_, sctr)
                        for kc in range(4):
                            P.op("pe", lambda e, pb=pb, kc=kc, tok=tok, h=h: e.matmul(
                                pb[0:64, :], wq[:, kc, h * 256 + 192:h * 256 + 256], qns[:, kc, tok], start=(kc == 0), stop=(kc == 3)),
                                reads=[b_qnl[tt // 4], b_w], writes=[b_pb])
                        i = tt % 2
                        P.op("dve", lambda e, pa=pa, i=i, tok=tok: e.tensor_tensor(out=rt1[i][:], in0=pa[0:64, :], in1=cosT[:, tok], op=ALU.mult),
                             reads=[b_pa, b_tab], writes=[b_rt1[i]])
                        P.op("dve", lambda e, pb=pb, i=i, tok=tok: e.tensor_tensor(out=rt2[i][:], in0=pb[0:64, :], in1=sinT[:, tok], op=ALU.mult),
                             reads=[b_pb, b_tab], writes=[b_rt2[i]])
                        P.op("pool", lambda e, i=i, tok=tok: e.tensor_tensor(out=qrh[:, tok], in0=rt1[i][:], in1=rt2[i][:], op=ALU.add),
                             reads=[b_rt1[i], b_rt2[i]], writes=[b_qr[tt]])
                    units = [(qb, kb) for qb in range(8) for kb in range(4 * qb + 4)]
                    LOOK = 2
                    acc = {}
                    sc = {}

                    def emit_scores(u):
                        qb, kb = units[u]
                        qtok = slice(qb * 512, (qb + 1) * 512)
                        ktok = slice(kb * 128, (kb + 1) * 128)
                        ps_, b_ps_ = psum(SB_, sctr)
                        diag = kb >= 4 * qb
                        P.op("pe", lambda e: e.matmul(ps_, kTh[:, ktok], qTh[:, qtok], start=True, stop=False),
                             reads=[b_kT[kb // 4], b_qT[qb]], writes=[b_ps_])
                        P.op("pe", lambda e: e.matmul(ps_, krs[:, ktok], qrh[:, qtok], start=False, stop=(not diag)),
                             reads=[b_krl[kb // 16], b_qr[qb]], writes=[b_ps_])
                        if diag:
                            jm = kb - 4 * qb
                            P.op("pe", lambda e: e.matmul(ps_, ident[:], masks[:, jm, :], start=False, stop=True),
                                 reads=[b_const], writes=[b_ps_])
                        sc[u] = (ps_, b_ps_)

                    def emit_rest(u, h=h):
                        qb, kb = units[u]
                        nkb = 4 * qb + 4
                        qtok = slice(qb * 512, (qb + 1) * 512)
                        if kb == 0:
                            acc[qb] = (psum([0, 1], octr), psum([2, 3], lctr))
                        (po, b_po), (pl_, b_pl_) = acc[qb]
                        ps_, b_ps_ = sc.pop(u)
                        ip = pT_ctr[0] % 4
                        pT_ctr[0] += 1
                        pT, b_pT = pTs[ip], b_pTs[ip]
                        P.op("act", lambda e: e.activation(out=pT[:], in_=ps_, func=AF.Exp, scale=SCALE), reads=[b_ps_], writes=[b_pT])
                        P.op("pe", lambda e: e.matmul(po, Vq[:, kb, (h % 4) * 128:(h % 4 + 1) * 128], pT[:], start=(kb == 0), stop=(kb == nkb - 1)),
                             reads=[b_V[kb], b_pT], writes=[b_po])
                        P.op("pe", lambda e: e.matmul(pl_, ones[:], pT[:], start=(kb == 0), stop=(kb == nkb - 1)),
                             reads=[b_const, b_pT], writes=[b_pl_])
                        if kb == nkb - 1:
                            i = uc[0] % 2
                            uc[0] += 1
                            P.op("dve", lambda e: e.reciprocal(out=rls[i][:], in_=pl_), reads=[b_pl_], writes=[b_rls[i]])
                            P.op("dve", lambda e: e.tensor_tensor(out=outs[i][:], in0=po, in1=rls[i][:], op=ALU.mult),
                                 reads=[b_po, b_rls[i]], writes=[b_outs[i]])
                            P.dma("sp", e2_src[h][:, qtok], outs[i][:], reads=[b_outs[i]], writes=[b_e2s[h][qb]], key="ob%d" % i)
                            if qb == 7:
                                if debug:
                                    P.dma("sp", dbg_e2[h], e2_src[h], reads=b_e2s[h], writes=[Buf()], key="dbg_e2")
                                P.collective(e2_src[h], e2_g[h], reads=b_e2s[h], writes=[b_e2g[h]], key="cc_e2_%d" % h)

                    for u in range(min(LOOK, len(units))):
                        emit_scores(u)
                    for u in range(len(units)):
                        if u + LOOK < len(units):
                            emit_scores(u + LOOK)
                        emit_rest(u)
                P.barrier(junk)

        for L in range(n_layers):
            if stop_phase == "ln0":
                break
            phase_A(L)
            if stop_phase == "A":
                break
            wo_hi_bufs = [Buf(), Buf()]
            phase_B(L, wo_hi_bufs)
            if stop_phase == "B":
                break
            with contextlib.ExitStack() as st:
                last = (L == DEPTH - 1)
                ln_phase(st, L, "proj", not last, last, wo_hi_bufs)
                P.barrier(junk)
        P.finish()
    return nc


def _tile_k(w, ncols_pad=None):
    K, C = w.shape
    return np.ascontiguousarray(w.reshape(K // 128, 128, C).transpose(1, 0, 2))


def prep_inputs(x, positions, emb_ln_g, emb_ln_b, w_in, q_norm_g, kv_norm_g, w_uq, w_ukv, w_pool,
                pool_scale, conv_w, w_out, b_out, ln_g, ln_b):
    f32 = np.float32
    w_in = np.asarray(w_in, f32)
    offs = np.cumsum([0, 512, 256, 64, 1024, 512, 512, 512, 512, 512, 512])
    o_q, o_kv, o_kr, o_gm, o_pi, o_gp, o_ch, o_cb, o_cc, o_gc = offs[:10]
    groups = []
    zero128 = None
    for L in range(DEPTH):
        W = w_in[L]
        kr = W[:, o_kr:o_kr + 64]
        ksw = np.concatenate([kr[:, 32:64], kr[:, 0:32]], axis=1)
        g0 = np.concatenate([W[:, o_kv:o_kv + 256], kr, ksw, np.zeros((D, 128), f32)], axis=1)
        gl = [g0, W[:, o_q:o_q + 512], W[:, o_gm:o_gm + 512], W[:, o_gm + 512:o_gm + 1024]]
        for a in range(2):
            gl.append(np.concatenate([W[:, o_pi + (2 * a) * 128:o_pi + (2 * a + 1) * 128], W[:, o_gp + (2 * a) * 128:o_gp + (2 * a + 1) * 128],
                                      W[:, o_pi + (2 * a + 1) * 128:o_pi + (2 * a + 2) * 128], W[:, o_gp + (2 * a + 1) * 128:o_gp + (2 * a + 2) * 128]], axis=1))
        for j in range(4):
            sl = slice(j * 128, (j + 1) * 128)
            gl.append(np.concatenate([W[:, o_ch:o_ch + 512][:, sl], W[:, o_cc:o_cc + 512][:, sl], W[:, o_cb:o_cb + 512][:, sl], W[:, o_gc:o_gc + 512][:, sl]], axis=1))
        for gmat in gl:
            groups.append(_tile_k(gmat).reshape(128, 16 * 512))
    w_in_g = np.ascontiguousarray(np.concatenate(groups, axis=0))

    wq_l, wk_l, wv_l = [[], []], [[], []], [[], []]
    wo_l, wp_l, sm_l = [], [], []
    for L in range(DEPTH):
        wq = np.asarray(w_uq[L], f32).reshape(512, NH, 192)
        rope = wq[:, :, 128:192]
        sw = np.concatenate([rope[:, :, 32:64], rope[:, :, 0:32]], axis=2)
        wq2 = np.concatenate([wq, sw], axis=2)
        wkv = np.asarray(w_ukv[L], f32).reshape(256, NH, 256)
        for r in range(2):
            hs = slice(4 * r, 4 * r + 4)
            wq_l[r].append(_tile_k(np.ascontiguousarray(wq2[:, hs]).reshape(512, 1024)).reshape(128, 4 * 1024))
            wk_l[r].append(_tile_k(np.ascontiguousarray(wkv[:, hs, 0:128]).reshape(256, 512)).reshape(128, 2 * 512))
            wv_l[r].append(_tile_k(np.ascontiguousarray(wkv[:, hs, 128:256]).reshape(256, 512)).reshape(128, 2 * 512))
        wo_l.append(_tile_k(np.asarray(w_out[L], f32)).reshape(128, 16 * 2048))
        wp_l.append(np.ascontiguousarray(np.asarray(w_pool[L], f32).transpose(1, 0, 2)).reshape(128, 4 * 128))
        sm = np.zeros((128, 32), f32)
        sm[:, 0:4] = np.asarray(q_norm_g[L], f32).reshape(4, 128).T
        sm[:, 4:6] = np.asarray(kv_norm_g[L], f32).reshape(2, 128).T
        sm[:, 6:10] = np.asarray(pool_scale[L], f32).reshape(4, 128).T
        cw = np.asarray(conv_w[L], f32).reshape(3, 4, 128)
        sm[:, 10:22] = cw.transpose(2, 1, 0).reshape(128, 12)
        sm_l.append(sm)
    lnp = np.stack([np.asarray(emb_ln_g, f32), np.asarray(emb_ln_b, f32)] +
                   sum([[np.asarray(ln_g[L], f32), np.asarray(ln_b[L], f32), np.asarray(b_out[L], f32)] for L in range(DEPTH)], []), axis=0)
    half = 32
    inv_freq = (10000.0 ** (-np.arange(half, dtype=np.float32) / half)).astype(f32)
    ropec = np.zeros((128, 2), f32)
    ropec[:, 0] = np.concatenate([inv_freq] * 4)
    ropec[:, 1] = np.concatenate([-np.ones(32, f32), np.ones(32, f32)] * 2)
    invdiv = np.zeros((2, 128, 4, 16), f32)
    for g, w in enumerate(POOL_W):
        invdiv[0, :, g, :] = 1.0 / np.minimum(np.arange(1, 17, dtype=f32), float(w))
        invdiv[1, :, g, :] = 1.0 / float(w)
    ident = np.eye(128, dtype=f32).astype(ml_dtypes.bfloat16)
    kk = np.arange(128)[:, None]
    qq = np.arange(512)[None, :]
    masks = np.stack([np.where(j * 128 + kk <= qq, 0.0, NEG) for j in range(4)], axis=1).astype(f32)
    masks = masks.reshape(128, 4 * 512).astype(ml_dtypes.bfloat16)
    shared = {
        "ropec": ropec, "lnp": np.ascontiguousarray(lnp), "w_in_g": w_in_g,
        "wo": np.ascontiguousarray(np.concatenate(wo_l, 0)),
        "wp": np.ascontiguousarray(np.concatenate(wp_l, 0)), "small": np.ascontiguousarray(np.concatenate(sm_l, 0)),
        "ident": ident, "masks": masks,
    }
    per_rank = []
    for r in range(2):
        coef = np.zeros((128, 2), f32)
        coef[:, r] = 1.0
        per_rank.append({
            "wq": np.ascontiguousarray(np.concatenate(wq_l[r], 0)), "wk": np.ascontiguousarray(np.concatenate(wk_l[r], 0)),
            "wv": np.ascontiguousarray(np.concatenate(wv_l[r], 0)), "invdiv": np.ascontiguousarray(invdiv[r].reshape(128, 64)),
            "coef": coef,
        })
    x = np.asarray(x, f32)
    positions = np.asarray(positions, np.int32)
    in_maps = []
    for c in range(8):
        b, r = c // 2, c % 2
        m = dict(shared)
        m.update(per_rank[r])
        m["x"] = np.ascontiguousarray(x[b, r * SO:(r + 1) * SO])
        m["pos"] = np.ascontiguousarray(positions[b][None, :])
        m["pos_own"] = np.ascontiguousarray(positions[b, r * SO:(r + 1) * SO][None, :])
        in_maps.append(m)
    return in_maps


def kernel(**inputs):
    in_maps = prep_inputs(**inputs)
    nc = build()
    res = run_bass_kernel_spmd(nc, in_maps, core_ids=list(range(8)))
    out = np.empty((4, S, D), np.float32)
    for c in range(8):
        b, r = c // 2, c % 2
        out[b, r * SO:(r + 1) * SO] = np.asarray(res.results[c]["out"], dtype=np.float32)
    return out
```

```python
import math
import contextlib
import numpy as np
import ml_dtypes
import concourse.bass as bass
import concourse.mybir as mybir
from concourse.bass_utils import run_bass_kernel_spmd

F32 = mybir.dt.float32
BF16 = mybir.dt.bfloat16
I32 = mybir.dt.int32
AF = mybir.ActivationFunctionType
ALU = mybir.AluOpType

S = 4096
SO = 2048
HL = 4
PAIRS = [[0, 1], [2, 3], [4, 5], [6, 7]]
D = 2048
DEPTH = 2
NH = 8
LN_EPS = 1e-5
RMS_EPS = 1e-6
ALPHA = (2 * DEPTH) ** 0.25
SCALE = 192 ** -0.5
NEG = -30000.0
POOL_W = (2, 4, 8, 16)
NG = 10

ENGS = ("pe", "act", "dve", "pool", "sp")


class Buf:
    __slots__ = ("name", "w", "r")

    def __init__(self, name=""):
        self.name = name
        self.w = None
        self.r = []


class Op:
    __slots__ = ("eng", "fn", "waits", "flag", "dma_key", "dma_val", "seq")

    def __init__(self, eng, fn):
        self.eng = eng
        self.fn = fn
        self.waits = []
        self.flag = False
        self.dma_key = None
        self.dma_val = 0
        self.seq = -1


class Prog:
    def __init__(self, nc):
        self.nc = nc
        self.ops = {e: [] for e in ENGS}
        self.seen = {e: {} for e in ENGS}
        self.seen_dma = {e: {} for e in ENGS}
        self.dma_counts = {}
        self.cc_keys = set()
        self.jb = [Buf(), Buf(), Buf()]

    def _add(self, eng, fn, reads, writes, dma_key=None):
        op = Op(eng, fn)
        op.seq = len(self.ops[eng])
        deps = []
        for b in reads:
            if b.w is not None:
                deps.append(b.w)
        for b in writes:
            if b.w is not None:
                deps.append(b.w)
            deps.extend(b.r)
        best = {}
        dma_deps = {}
        for d in deps:
            if d.dma_key is not None:
                if dma_deps.get(d.dma_key, 0) < d.dma_val:
                    dma_deps[d.dma_key] = d.dma_val
            else:
                if d.eng == eng and eng == "pe":
                    continue
                if best.get(d.eng, -1) < d.seq:
                    best[d.eng] = d.seq
        for f, s in best.items():
            if self.seen[eng].get(f, -1) >= s:
                continue
            self.seen[eng][f] = s
            dop = self.ops[f][s]
            dop.flag = True
            op.waits.append(("eng", f, dop))
        for k, v in dma_deps.items():
            if self.seen_dma[eng].get(k, 0) >= v:
                continue
            self.seen_dma[eng][k] = v
            op.waits.append(("dma", k, v))
        if dma_key is not None:
            op.dma_key = dma_key
            self.dma_counts[dma_key] = self.dma_counts.get(dma_key, 0) + 16
            op.dma_val = self.dma_counts[dma_key]
        for b in reads:
            b.r.append(op)
        for b in writes:
            b.w = op
            b.r = []
        self.ops[eng].append(op)
        return op

    def op(self, eng, fn, reads=(), writes=()):
        return self._add(eng, fn, reads, writes, None)

    def dma(self, eng, out, in_, reads=(), writes=(), key=None):
        def fn(e):
            return e.dma_start(out=out, in_=in_)
        return self._add(eng, fn, reads, writes, key)

    def collective(self, src, dst, reads, writes, key):
        def fn(e):
            return e.collective_compute("AllGather", ALU.bypass, replica_groups=PAIRS, ins=[src.opt()], outs=[dst.opt()])
        o = self._add("pool", fn, reads, writes, key)
        self.dma_counts[key] -= 15
        o.dma_val = self.dma_counts[key]
        self.cc_keys.add(key)
        return o

    def barrier(self, junk):
        marks = []
        jb = self.jb
        b = Buf()
        self.op("act", lambda e: e.activation(out=junk[:, 0:1], in_=junk[:, 4:5], func=AF.Copy), writes=[b, jb[0]])
        marks.append(b)
        b = Buf()
        self.op("dve", lambda e: e.memset(junk[:, 1:2], 0.0), writes=[b, jb[1]])
        marks.append(b)
        b = Buf()
        self.op("pool", lambda e: e.memset(junk[:, 2:3], 0.0), writes=[b, jb[2]])
        marks.append(b)
        fence = Buf()
        o = self.op("sp", lambda e: e.nop(), reads=marks, writes=[fence])
        for k, v in self.dma_counts.items():
            if self.seen_dma["sp"].get(k, 0) < v:
                self.seen_dma["sp"][k] = v
                o.waits.append(("dma", k, v))
        self.op("act", lambda e: e.activation(out=junk[:, 0:1], in_=junk[:, 4:5], func=AF.Copy), reads=[fence], writes=[jb[0]])
        self.op("dve", lambda e: e.memset(junk[:, 1:2], 0.0), reads=[fence], writes=[jb[1]])
        self.op("pool", lambda e: e.memset(junk[:, 2:3], 0.0), reads=[fence], writes=[jb[2]])
        self.op("pe", lambda e: e.nop(), reads=[fence])
        for e in ENGS:
            for k, v in self.dma_counts.items():
                if self.seen_dma[e].get(k, 0) < v:
                    self.seen_dma[e][k] = v

    def finish(self):
        nc = self.nc
        with contextlib.ExitStack() as st:
            esem = {e: st.enter_context(nc.semaphore("s_" + e)) for e in ENGS}
            dsem = {k: st.enter_context(nc.semaphore("d_%s" % (k,))) for k in self.dma_counts}
            block = st.enter_context(nc.Block())
            for e in ENGS:
                c = 0
                for o in self.ops[e]:
                    if o.flag:
                        c += 1
                        o.dma_val = c

            def emit(e, eng):
                for o in self.ops[e]:
                    for kind, k, v in o.waits:
                        if kind == "eng":
                            eng.wait_ge(esem[k], v.dma_val)
                        else:
                            eng.wait_ge(dsem[k], v)
                    inst = o.fn(eng)
                    if o.dma_key is not None and o.dma_key in self.cc_keys:
                        inst.then_inc(dsem[o.dma_key])
                    elif o.dma_key is not None:
                        inst.then_inc(dsem[o.dma_key], 16)
                    elif o.flag:
                        inst.then_inc(esem[e], 1)
                if e == "sp":
                    for k, v in self.dma_counts.items():
                        eng.wait_ge(dsem[k], v)

            @block.tensor
            def _(eng):
                emit("pe", eng)

            @block.scalar
            def _(eng):
                emit("act", eng)

            @block.vector
            def _(eng):
                emit("dve", eng)

            @block.gpsimd
            def _(eng):
                emit("pool", eng)

            @block.sync
            def _(eng):
                emit("sp", eng)


def build(debug=False, n_layers=DEPTH, stop_phase=None):
    nc = bass.Bass("TRN2", target_bir_lowering=False)
    P = Prog(nc)

    def din(name, shape, dt):
        return nc.dram_tensor(name, shape, dt, kind="ExternalInput").ap()

    x_d = din("x", [SO, D], F32)
    pos_d = din("pos", [1, S], I32)
    poso_d = din("pos_own", [1, SO], I32)
    coef_d = din("coef", [128, 2], F32)
    rc_d = din("ropec", [128, 2], F32)
    lnp_d = din("lnp", [2 + 3 * DEPTH, D], F32)
    win_d = din("w_in_g", [DEPTH * NG * 128, 16 * 512], F32)
    wq_d = din("wq", [DEPTH * 128, 4 * 1024], F32)
    wk_d = din("wk", [DEPTH * 128, 2 * 512], F32)
    wv_d = din("wv", [DEPTH * 128, 2 * 512], F32)
    wo_d = din("wo", [DEPTH * 128, 16 * 2048], F32)
    wp_d = din("wp", [DEPTH * 128, 4 * 128], F32)
    sm_d = din("small", [DEPTH * 128, 32], F32)
    idv_d = din("invdiv", [128, 64], F32)
    ident_d = din("ident", [128, 128], BF16)
    mask_d = din("masks", [128, 4 * 512], BF16)
    out_d = nc.dram_tensor("out", [SO, D], F32, kind="ExternalOutput").ap()

    skind = "ExternalOutput" if debug else "Internal"
    resid_d = nc.dram_tensor("resid", [SO, D], F32, kind=skind).ap()
    hT_d = nc.dram_tensor("hT", [D, SO], BF16, kind=skind).ap()
    mix_d = nc.dram_tensor("mixT", [D, SO], BF16, kind=skind).ap()
    cosA_d = nc.dram_tensor("cosA", [64, S], BF16).ap()
    sinA_d = nc.dram_tensor("sinA", [64, S], BF16).ap()
    cosO_d = nc.dram_tensor("cosO", [64, SO], BF16).ap()
    sinO_d = nc.dram_tensor("sinO", [64, SO], BF16).ap()
    e1a_src = nc.dram_tensor("e1a_src", [512, SO], BF16).ap()
    e1a_g = nc.dram_tensor("e1a_g", [1024, SO], BF16).ap()
    e1b_src = nc.dram_tensor("e1b_src", [320, SO], BF16).ap()
    e1b_g = nc.dram_tensor("e1b_g", [640, SO], BF16).ap()
    e2_src = [nc.dram_tensor("e2_src%d" % i, [128, S], BF16).ap() for i in range(4)]
    e2_gall = nc.dram_tensor("e2_gall", [4 * 256, S], BF16).ap()
    e2_g = [e2_gall[i * 256:(i + 1) * 256, :] for i in range(4)]
    e2g_v = e2_gall.rearrange("(h r p) t -> p h r t", h=4, r=2)
    tl_src = nc.dram_tensor("tl_src", [D, 16], BF16).ap()
    tl_g = nc.dram_tensor("tl_g", [2 * D, 16], BF16).ap()
    if debug:
        dbg_e1a = nc.dram_tensor("dbg_e1a", [512, SO], BF16, kind="ExternalOutput").ap()
        dbg_e1b = nc.dram_tensor("dbg_e1b", [320, SO], BF16, kind="ExternalOutput").ap()
        dbg_e2 = [nc.dram_tensor("dbg_e2_%d" % i, [128, S], BF16, kind="ExternalOutput").ap() for i in range(4)]

    b_resid = [Buf("resid%d" % i) for i in range(16)]
    b_hT = [Buf("hT%d" % i) for i in range(4)]
    b_qn = [Buf() for i in range(4)]
    b_kvn = [Buf() for i in range(4)]
    b_kr = [Buf() for i in range(4)]
    b_mix = [[Buf() for t in range(4)] for r in range(16)]
    b_out = [Buf() for i in range(16)]
    b_e1g, b_tlsrc, b_tlg = Buf(), Buf(), Buf()
    b_e2g = [Buf() for i in range(4)]
    b_e2s = [[Buf() for t in range(8)] for r in range(4)]
    b_tabd = Buf()

    hT_v = hT_d.rearrange("(kc p) t -> p kc t", p=128)
    mix_v = mix_d.rearrange("(kc p) t -> p kc t", p=128)
    qn_v = e1a_src.rearrange("(kc p) t -> p kc t", p=128)
    kvn_v = e1b_src[0:256, :].rearrange("(kc p) t -> p kc t", p=128)
    kr_d = e1b_src[256:320, :]
    tls_v = tl_src.rearrange("(kc p) t -> p kc t", p=128)
    tlg_v = tl_g.rearrange("(kc p) t -> p kc t", p=128)

    with contextlib.ExitStack() as gst:
        ARENA_WORDS = 52224
        a_hi = [ARENA_WORDS]
        arena = gst.enter_context(nc.sbuf_tensor("arena", [128, ARENA_WORDS], F32))
        a_top = [0]
        a_mark = [0]

        def sb(name, shape, dt, st=None):
            n = 1
            for d_ in shape[1:]:
                n *= d_
            esz = 4 if dt in (F32, I32) else 2
            words = (n * esz + 3) // 4
            words = (words + 7) // 8 * 8
            off = a_top[0]
            assert off + words <= a_hi[0], ("SBUF arena overflow", name, off, words)
            a_top[0] = off + words
            v = arena[0:shape[0], off:off + words]
            if dt != F32:
                v = v.bitcast(dt)
            v = v[:, 0:n]
            if len(shape) == 3:
                v = v.rearrange("p (a b) -> p a b", a=shape[1])
            return v

        def sb_top(name, shape, dt):
            n = 1
            for d_ in shape[1:]:
                n *= d_
            esz = 4 if dt in (F32, I32) else 2
            words = (n * esz + 3) // 4
            words = (words + 7) // 8 * 8
            off = a_hi[0] - words
            assert off >= a_top[0], ("SBUF arena overflow (top)", name)
            a_hi[0] = off
            v = arena[0:shape[0], off:off + words]
            if dt != F32:
                v = v.bitcast(dt)
            v = v[:, 0:n]
            if len(shape) == 3:
                v = v.rearrange("p (a b) -> p a b", a=shape[1])
            return v

        def areset():
            a_top[0] = a_mark[0]
            a_hi[0] = ARENA_WORDS

        ps_all = gst.enter_context(nc.psum_tensor("ps", [128, 8 * 512], F32))
        ps_bufs = [Buf("ps%d" % i) for i in range(8)]
        ps_ctr = [0]

        def psum(banks=None, ctr=None):
            if banks is None:
                i = ps_ctr[0] % 8
                ps_ctr[0] += 1
            else:
                i = banks[ctr[0] % len(banks)]
                ctr[0] += 1
            return ps_all[:, i * 512:(i + 1) * 512], ps_bufs[i]

        junk = sb("junk", [128, 8], F32)
        ident = sb("ident", [128, 128], BF16)
        ones = sb("ones", [128, 128], BF16)
        masks = sb("masks", [128, 4, 512], BF16)
        rc = sb("rc", [128, 2], F32)
        coef = sb("coef", [128, 2], F32)
        invdiv = sb("invdiv", [128, 4, 16], F32)
        b_const = Buf("const")
        b_tab = Buf("tab")

        P.op("dve", lambda e: e.memset(junk[:], 0.0), writes=[b_const] + P.jb)
        P.op("dve", lambda e: e.memset(ones[:], 1.0), writes=[b_const])
        P.dma("sp", ident[:], ident_d, writes=[b_const], key="c_ident")
        P.dma("sp", masks[:].rearrange("p a b -> p (a b)"), mask_d, writes=[b_const], key="c_mask")
        P.dma("sp", rc[:], rc_d, writes=[b_const], key="c_rc")
        P.dma("sp", coef[:], coef_d, writes=[b_const], key="c_coef")
        P.dma("sp", invdiv[:].rearrange("p a b -> p (a b)"), idv_d, writes=[b_const], key="c_idv")

        LN_EPS_AP = sb("lneps", [128, 4], F32)
        a_mark[0] = a_top[0]

        def rope_tables(pos_ap, N, cos_dst, sin_dst, tag):
            H = N // 2
            posi = sb("posi" + tag, [128, H], I32)
            ang = sb("ang" + tag, [128, H], F32)
            ta = sb("ta" + tag, [128, H], F32)
            tb = sb("tb" + tag, [128, H], F32)
            obs = [sb("ob%d" % i + tag, [128, H], BF16) for i in range(2)]
            b_posi, b_ang, b_ta, b_tb = Buf(), Buf(), Buf(), Buf()
            b_obs = [Buf(), Buf()]
            P.dma("sp", posi[0:64, :], pos_ap[:, 0:H].partition_broadcast(64), writes=[b_posi], key="t_pos0" + tag)
            P.dma("sp", posi[64:128, :], pos_ap[:, H:N].partition_broadcast(64), writes=[b_posi], key="t_pos1" + tag)
            P.op("dve", lambda e: e.tensor_copy(out=ang[:], in_=posi[:]), reads=[b_posi], writes=[b_ang])
            P.op("dve", lambda e: e.tensor_scalar(out=ang[:], in0=ang[:], scalar1=rc[:, 0:1], scalar2=None, op0=ALU.mult),
                 reads=[b_ang, b_const], writes=[b_ang])
            TWO_PI = 2.0 * math.pi
            for wi_, (which, phase, dst) in enumerate((("sin", 0.0, sin_dst), ("cos", math.pi / 2, cos_dst))):
                ob, b_ob = obs[wi_], b_obs[wi_]
                P.op("dve", lambda e, phase=phase: e.tensor_scalar(out=ta[:], in0=ang[:], scalar1=phase, scalar2=1.0 / TWO_PI,
                                                                    op0=ALU.add, op1=ALU.mult), reads=[b_ang], writes=[b_ta])
                P.op("dve", lambda e: e.tensor_copy(out=posi[:], in_=ta[:]), reads=[b_ta], writes=[b_posi])
                P.op("dve", lambda e: e.tensor_copy(out=ta[:], in_=posi[:]), reads=[b_posi], writes=[b_ta])
                P.op("dve", lambda e: e.scalar_tensor_tensor(out=tb[:], in0=ta[:], scalar=-TWO_PI, in1=ang[:], op0=ALU.mult, op1=ALU.add),
                     reads=[b_ta, b_ang], writes=[b_tb])
                P.op("dve", lambda e, phase=phase: e.tensor_scalar(out=tb[:], in0=tb[:], scalar1=phase, scalar2=None, op0=ALU.add),
                     reads=[b_tb], writes=[b_tb])
                P.op("dve", lambda e: e.tensor_scalar(out=ta[:], in0=tb[:], scalar1=math.pi, scalar2=TWO_PI, op0=ALU.is_gt, op1=ALU.mult),
                     reads=[b_tb], writes=[b_ta])
                P.op("dve", lambda e: e.tensor_tensor(out=tb[:], in0=tb[:], in1=ta[:], op=ALU.subtract), reads=[b_tb, b_ta], writes=[b_tb])
                P.op("dve", lambda e: e.tensor_scalar(out=tb[:], in0=tb[:], scalar1=math.pi, scalar2=-math.pi, op0=ALU.min, op1=ALU.max),
                     reads=[b_tb], writes=[b_tb])
                P.op("act", lambda e: e.activation(out=ta[:], in_=tb[:], func=AF.Sin), reads=[b_tb], writes=[b_ta])
                if which == "sin":
                    P.op("dve", lambda e, ob=ob: e.tensor_scalar(out=ob[:], in0=ta[:], scalar1=rc[:, 1:2], scalar2=None, op0=ALU.mult),
                         reads=[b_ta, b_const], writes=[b_ob])
                else:
                    P.op("dve", lambda e, ob=ob: e.tensor_copy(out=ob[:], in_=ta[:]), reads=[b_ta], writes=[b_ob])
                P.dma("sp", dst[:, 0:H], ob[0:64, :], reads=[b_ob], writes=[b_tabd], key="t_ob0%d" % wi_ + tag)
                P.dma("sp", dst[:, H:N], ob[64:128, :], reads=[b_ob], writes=[b_tabd], key="t_ob1%d" % wi_ + tag)

        areset()
        rope_tables(pos_d, S, cosA_d, sinA_d, "a")
        rope_tables(poso_d, SO, cosO_d, sinO_d, "o")

        def ln_phase(st, layer_idx, src_kind, write_hT, final, wo_hi_bufs=None):
            if src_kind != "x":
                areset()
            NB = 4 if src_kind == "x" else 3
            gb = sb("ln_g", [128, D], F32, st)
            bb = sb("ln_b", [128, D], F32, st)
            b_p = Buf()
            if src_kind == "x":
                grow, brow = 0, 1
            else:
                grow, brow = 2 + 3 * layer_idx, 3 + 3 * layer_idx
            b_pg, b_pb = Buf(), Buf()
            P.dma("sp", gb[:], lnp_d[grow:grow + 1, :].partition_broadcast(128), writes=[b_pg], key="ln_g")
            P.dma("sp", bb[:], lnp_d[brow:brow + 1, :].partition_broadcast(128), writes=[b_pb], key="ln_b")
            ys = [sb("ln_y%d" % i, [128, D], F32, st) for i in range(NB)]
            b_ys = [Buf() for i in range(NB)]
            hbs = [sb("ln_hb%d" % i, [128, D], BF16, st) for i in range(2)]
            b_hbs = [Buf() for i in range(2)]
            stats = [sb("ln_st%d" % i, [128, 4, 6], F32, st) for i in range(NB)]
            mvs = [sb("ln_mv%d" % i, [128, 4], F32, st) for i in range(NB)]
            b_sts = [Buf() for i in range(NB)]
            stg = [sb("ln_stg%d" % i, [128, 16, 512], BF16, st) for i in range(1)] if write_hT else []
            b_stg = [Buf() for i in range(1)]
            if src_kind == "proj":
                bo = sb("ln_bo", [1, D], BF16, st)
                P.dma("pool", bo[:], lnp_d[4 + 3 * layer_idx:5 + 3 * layer_idx, :], writes=[b_p], key="ln_bo")
                wo = sb_top("wo", [128, 16, 2048], BF16)
                b_wop = [Buf(), Buf()] + list(wo_hi_bufs)
                for i4 in (1, 0):
                    P.dma("pool", wo[:, i4 * 4:(i4 + 1) * 4, :].rearrange("p a b -> p (a b)"),
                          wo_d[layer_idx * 128:(layer_idx + 1) * 128, i4 * 8192:(i4 + 1) * 8192],
                          reads=([b_wop[1]] if i4 == 0 else [b_p]), writes=[b_wop[i4]], key="wo%d" % i4)
                mts = [sb("mt%d" % i, [128, 16, 256], BF16, st) for i in range(2)]
                b_mts = [Buf() for i in range(2)]
                NR = 2
                rts = [sb("rt%d" % i, [128, D], F32, st) for i in range(NR)]
                b_rts = [Buf() for i in range(NR)]
                ea = [sb("ea%d" % i, [128, 8, 256], BF16, st) for i in range(2)]
                eb = [sb("eb%d" % i, [128, 8, 256], BF16, st) for i in range(2)]
                b_ea = [Buf() for i in range(2)]
                b_eb = [Buf() for i in range(2)]
                et = sb("et", [128, 8, 256], BF16, st)
                b_et = Buf()

            def s_pair(tk):
                tt = tk // 4
                t2 = tk // 2
                i2 = t2 % 2
                mt, b_mt = mts[i2], b_mts[i2]
                P.dma("sp", mt[:], mix_v[:, :, t2 * 256:(t2 + 1) * 256], reads=[b_mix[r][tt] for r in range(16)],
                      writes=[b_mt], key="mt%d" % i2)
                P.dma("sp", ea[i2][:].rearrange("p (h r) t -> p h r t", r=2), e2g_v[:, :, :, t2 * 256:(t2 + 1) * 256],
                      reads=b_e2g, writes=[b_ea[i2]], key="ea%d" % i2)
                P.dma("sp", eb[i2][:].rearrange("p (h r) t -> p h r t", r=2), e2g_v[:, :, :, SO + t2 * 256:SO + (t2 + 1) * 256],
                      reads=b_e2g, writes=[b_eb[i2]], key="eb%d" % i2)

            def s_blend(tk):
                i2 = (tk // 2) % 2
                mt, b_mt = mts[i2], b_mts[i2]
                P.op("dve", lambda e: e.tensor_scalar(out=et[:], in0=ea[i2][:], scalar1=coef[:, 0:1], scalar2=None, op0=ALU.mult),
                     reads=[b_ea[i2], b_const], writes=[b_et])
                P.op("dve", lambda e: e.scalar_tensor_tensor(out=et[:], in0=eb[i2][:], scalar=coef[:, 1:2], in1=et[:], op0=ALU.mult, op1=ALU.add),
                     reads=[b_eb[i2], b_et, b_const], writes=[b_et])
                mt8 = mt[:, 0:8, :].rearrange("p (r h) t -> p h r t", r=2)
                etv = et[:].rearrange("p (h r) t -> p h r t", r=2)
                P.op("pool", lambda e: e.tensor_tensor(out=mt8, in0=mt8, in1=etv, op=ALU.mult),
                     reads=[b_et, b_mt], writes=[b_mt])

            def s_load(tk):
                y, b_y = ys[tk % NB], b_ys[tk % NB]
                tsl = slice(tk * 128, (tk + 1) * 128)
                if src_kind == "x":
                    P.dma("sp", y[:], x_d[tsl, :], writes=[b_y], key="ln_y%d" % (tk % NB))
                else:
                    rt, b_rt = rts[tk % NR], b_rts[tk % NR]
                    P.dma("sp", rt[:], resid_d[tsl, :], reads=[b_resid[tk]], writes=[b_rt], key="rt%d" % (tk % NR))

            def s1(tk):
                y, b_y = ys[tk % NB], b_ys[tk % NB]
                stt, mv, b_st = stats[tk % NB], mvs[tk % NB], b_sts[tk % NB]
                if src_kind != "x":
                    t2 = tk // 2
                    mt, b_mt = mts[t2 % 2], b_mts[t2 % 2]
                    rt, b_rt = rts[tk % NR], b_rts[tk % NR]
                    pts = [psum() for cg in range(4)]
                    for cg in range(4):
                        pt, b_pt = pts[cg]
                        P.op("pe", lambda e, pt=pt, cg=cg: e.matmul(pt, ones[0:1, :], bo[0:1, cg * 512:(cg + 1) * 512], start=True, stop=False),
                             reads=[b_p, b_const], writes=[b_pt])
                    for kc in reversed(range(16)):
                        for cg in range(4):
                            pt, b_pt = pts[cg]
                            P.op("pe", lambda e, pt=pt, kc=kc, cg=cg: e.matmul(
                                pt, mt[:, kc, (tk % 2) * 128:(tk % 2 + 1) * 128], wo[:, kc, cg * 512:(cg + 1) * 512],
                                start=False, stop=(kc == 0)), reads=[b_mt, b_wop[kc // 4]], writes=[b_pt])
                    for cg in range(4):
                        pt, b_pt = pts[cg]
                        P.op("dve", lambda e, pt=pt, cg=cg: e.scalar_tensor_tensor(
                            out=y[:, cg * 512:(cg + 1) * 512], in0=rt[:, cg * 512:(cg + 1) * 512], scalar=ALPHA, in1=pt,
                            op0=ALU.mult, op1=ALU.add), reads=[b_pt, b_rt], writes=[b_y])
                for c in range(4):
                    P.op("dve", lambda e, c=c: e.bn_stats(out=stt[:, c, :], in_=y[:, c * 512:(c + 1) * 512]),
                         reads=[b_y], writes=[b_st])
                P.op("dve", lambda e: e.bn_aggr(out=mv[:, 0:2], in_=stt[:].rearrange("p a b -> p (a b)")),
                     reads=[b_st], writes=[b_st])
                P.op("act", lambda e: e.activation(out=mv[:, 2:3], in_=mv[:, 1:2], func=AF.Sqrt, bias=LN_EPS_AP[:, 0:1], scale=1.0),
                     reads=[b_st, b_const], writes=[b_st])
                P.op("dve", lambda e: e.reciprocal(out=mv[:, 2:3], in_=mv[:, 2:3]), reads=[b_st], writes=[b_st])
                P.op("dve", lambda e: e.scalar_tensor_tensor(out=mv[:, 3:4], in0=mv[:, 0:1], scalar=-1.0, in1=mv[:, 2:3],
                                                              op0=ALU.mult, op1=ALU.mult), reads=[b_st], writes=[b_st])

            def s2a(tk):
                y, b_y = ys[tk % NB], b_ys[tk % NB]
                mv, b_st = mvs[tk % NB], b_sts[tk % NB]
                P.op("act", lambda e: e.activation(out=y[:], in_=y[:], func=AF.Identity, bias=mv[:, 3:4], scale=mv[:, 2:3]),
                     reads=[b_y, b_st], writes=[b_y])

            def s2b(tk):
                y, b_y = ys[tk % NB], b_ys[tk % NB]
                P.op("dve", lambda e: e.tensor_tensor(out=y[:], in0=y[:], in1=gb[:], op=ALU.mult), reads=[b_y, b_pg], writes=[b_y])
                P.op("pool", lambda e: e.tensor_tensor(out=y[:], in0=y[:], in1=bb[:], op=ALU.add), reads=[b_y, b_pb], writes=[b_y])

            def s3(tk):
                y, b_y = ys[tk % NB], b_ys[tk % NB]
                hb, b_hb = hbs[tk % 2], b_hbs[tk % 2]
                tsl = slice(tk * 128, (tk + 1) * 128)
                if final:
                    P.dma("sp", out_d[tsl, :], y[:], reads=[b_y], writes=[b_out[tk]], key="ln_o%d" % (tk % NB))
                else:
                    P.dma("sp", resid_d[tsl, :], y[:], reads=[b_y], writes=[b_resid[tk]], key="ln_o%d" % (tk % NB))
                if write_hT:
                    P.op("act", lambda e: e.copy(out=hb[:], in_=y[:]), reads=[b_y], writes=[b_hb])
                    sg_, b_sg_ = stg[0], b_stg[0]
                    for q4 in range(4):
                        pt, b_pt = psum()
                        ptb = pt.bitcast(BF16)
                        for j in range(4):
                            kc = q4 * 4 + j
                            P.op("pe", lambda e, ptb=ptb, kc=kc, j=j: e.transpose(
                                ptb[:, j * 128:(j + 1) * 128], hb[:, kc * 128:(kc + 1) * 128], ident[:]),
                                reads=[b_hb, b_const], writes=[b_pt])
                        if q4 % 2 == 0:
                            P.op("dve", lambda e, ptb=ptb, q4=q4: e.tensor_copy(
                                out=sg_[:, q4 * 4:(q4 + 1) * 4, (tk % 4) * 128:(tk % 4 + 1) * 128],
                                in_=ptb[:, 0:512].rearrange("p (a b) -> p a b", a=4)), reads=[b_pt], writes=[b_sg_])
                        else:
                            P.op("act", lambda e, ptb=ptb, q4=q4: e.copy(
                                out=sg_[:, q4 * 4:(q4 + 1) * 4, (tk % 4) * 128:(tk % 4 + 1) * 128],
                                in_=ptb[:, 0:512].rearrange("p (a b) -> p a b", a=4)), reads=[b_pt], writes=[b_sg_])
                    if tk % 4 == 3:
                        tt = tk // 4
                        P.dma("sp", hT_v[:, :, tt * 512:(tt + 1) * 512], sg_[:], reads=[b_sg_], writes=[b_hT[tt]], key="ln_stg")
                        if tk == NTK - 1:
                            P.dma("sp", tls_v, sg_[:, :, 496:512], reads=[b_sg_], writes=[b_tlsrc], key="ln_tl")
                            P.collective(tl_src, tl_g, reads=[b_tlsrc], writes=[b_tlg], key="cc_e3")

            NTK = SO // 128
            is_proj = (src_kind != "x")
            if is_proj:
                s_pair(0)
                s_blend(0)
            s_load(0)
            for i in range(NTK + 2):
                if is_proj and i % 2 == 0 and i + 2 < NTK:
                    s_pair(i + 2)
                if is_proj and i % 2 == 1 and i + 1 < NTK:
                    s_blend(i + 1)
                if i + 1 < NTK:
                    s_load(i + 1)
                if 0 <= i - 1 < NTK:
                    s2a(i - 1)
                if i < NTK:
                    s1(i)
                if 0 <= i - 1 < NTK:
                    s2b(i - 1)
                if 0 <= i - 2 < NTK:
                    s3(i - 2)

        P.op("dve", lambda e: e.memset(LN_EPS_AP[:, 0:1], LN_EPS), writes=[b_const])
        P.op("dve", lambda e: e.memset(LN_EPS_AP[:, 1:2], RMS_EPS), writes=[b_const])

        with contextlib.ExitStack() as st:
            ln_phase(st, 0, "x", True, False)
            P.barrier(junk)

        def phase_A(L):
            with contextlib.ExitStack() as st:
                areset()
                hTs = sb("hTs", [128, 16, 2048], BF16, st)
                b_hTs_t = [Buf() for i in range(4)]
                wr = [sb("wr%d" % i, [128, 16, 512], BF16, st) for i in range(2)]
                b_wr = [Buf() for i in range(2)]
                sm = sb("sm", [128, 32], F32, st)
                wp = sb("wp", [128, 4, 128], BF16, st)
                b_sm = Buf()
                P.dma("sp", sm[:], sm_d[L * 128:(L + 1) * 128, :], writes=[b_sm], key="sm")
                P.dma("pool", wp[:].rearrange("p a b -> p (a b)"), wp_d[L * 128:(L + 1) * 128, :], writes=[b_sm], key="wp")
                hp = sb("hp", [128, 4, 16], F32, st)
                hc = sb("hc", [128, 4, 2], F32, st)
                b_hp = [Buf() for i in range(4)]
                b_hc = [Buf() for i in range(4)]
                P.op("pool", lambda e: e.memset(hp[:], 0.0), writes=b_hp)
                P.op("pool", lambda e: e.memset(hc[:], 0.0), writes=b_hc)
                stgs = [sb("stg%d" % i, [128, 4, 512], BF16, st) for i in range(2)]
                b_stgs = [Buf() for i in range(2)]
                stg_ctr = [0]
                NTMP = 6
                tmps = [sb("tmp%d" % i, [128, 528], F32, st) for i in range(NTMP)]
                b_tmps = [Buf() for i in range(NTMP)]
                tmp_ctr = [0]
                sqs = [sb("sq%d" % i, [128, 512], BF16, st) for i in range(2)]
                b_sqs = [Buf() for i in range(2)]
                sq_ctr = [0]
                pls = [sb("pl%d" % i, [128, 512], BF16, st) for i in range(4)]
                b_pls = [Buf() for i in range(4)]
                sgps = [sb("sgp%d" % i, [128, 512], F32, st) for i in range(4)]
                b_sgps = [Buf() for i in range(4)]
                pl_ctr = [0]
                pending = []
                cosT = sb("cosO", [64, SO], BF16, st)
                sinT = sb("sinO", [64, SO], BF16, st)
                b_tab = Buf()
                P.dma("sp", cosT[:], cosO_d, reads=[b_tabd], writes=[b_tab], key="cosO")
                P.dma("sp", sinT[:], sinO_d, reads=[b_tabd], writes=[b_tab], key="sinO")
                hTh = sb("hTh", [128, 16, 16], BF16, st)
                b_hTh = Buf()
                P.dma("sp", hTh[:], tlg_v[:, 0:16, :], reads=[b_tlg], writes=[b_hTh], key="hTh")

                def proj_halo(w, c0):
                    pt, b_pt = psum()
                    for kc in range(16):
                        P.op("pe", lambda e, pt=pt, w=w, kc=kc: e.matmul(
                            pt[:, 0:16], w[0][:, kc, c0:c0 + 128], hTh[:, kc, :],
                            start=(kc == 0), stop=(kc == 15)), reads=[w[1], b_hTh], writes=[b_pt])
                    return pt, b_pt

                def tmp():
                    i = tmp_ctr[0] % NTMP
                    tmp_ctr[0] += 1
                    return tmps[i], b_tmps[i]

                def proj(w, c0, ncols, tt):
                    pt, b_pt = psum()
                    for kc in range(16):
                        P.op("pe", lambda e, pt=pt, w=w, kc=kc: e.matmul(
                            pt[0:ncols, :], w[0][:, kc, c0:c0 + ncols], hTs[:, kc, tt * 512:(tt + 1) * 512],
                            start=(kc == 0), stop=(kc == 15)), reads=[w[1], b_hTs_t[tt]], writes=[b_pt])
                    return pt, b_pt

                def rmsnorm_group(w, c0, nch, tt, dim, dst_v, dst_bufs, gtt, key):
                    pts = [proj(w, c0 + c * 128, 128, tt) for c in range(nch)]
                    ss, b_ss = psum()
                    for c, (pt, b_pt) in enumerate(pts):
                        i = sq_ctr[0] % 2
                        sq_ctr[0] += 1
                        sq, b_sq = sqs[i], b_sqs[i]
                        P.op("act", lambda e, pt=pt, sq=sq: e.activation(out=sq[:], in_=pt, func=AF.Square), reads=[b_pt], writes=[b_sq])
                        P.op("pe", lambda e, ss=ss, sq=sq, c=c: e.matmul(ss, ones[:], sq[:], start=(c == 0), stop=(c == nch - 1)),
                             reads=[b_sq, b_const], writes=[b_ss])
                    rs, b_rs = tmp()
                    P.op("act", lambda e, rs=rs, ss=ss: e.activation(out=rs[:, 0:512], in_=ss, func=AF.Sqrt, bias=LN_EPS_AP[:, 1:2],
                                                                      scale=1.0 / dim), reads=[b_ss, b_const], writes=[b_rs])
                    P.op("dve", lambda e, rs=rs: e.reciprocal(out=rs[:, 0:512], in_=rs[:, 0:512]), reads=[b_rs], writes=[b_rs])
                    i = stg_ctr[0] % 2
                    stg_ctr[0] += 1
                    sg_, b_sg_ = stgs[i], b_stgs[i]
                    for c, (pt, b_pt) in enumerate(pts):
                        P.op("dve", lambda e, pt=pt, rs=rs, sg_=sg_, c=c: e.tensor_tensor(out=sg_[:, c, :], in0=pt, in1=rs[:, 0:512], op=ALU.mult),
                             reads=[b_pt, b_rs], writes=[b_sg_])
                    P.dma("sp", dst_v[:, :, gtt * 512:(gtt + 1) * 512], sg_[:, 0:nch, :], reads=[b_sg_], writes=[dst_bufs[gtt]], key="stg%d" % i)

                for hf in range(1):
                    b_hch = Buf()
                    for t4 in range(4):
                        P.dma("sp", hTs[:, :, t4 * 512:(t4 + 1) * 512], hT_v[:, :, t4 * 512:(t4 + 1) * 512], reads=[b_hT[t4]],
                              writes=[b_hTs_t[t4], b_hch], key="hTs%d" % t4)
                    for g in range(NG):
                        wi = (hf * NG + g) % 2
                        w = (wr[wi], b_wr[wi])
                        row0 = (L * NG + g) * 128
                        P.dma("pool", wr[wi][:].rearrange("p a b -> p (a b)"), win_d[row0:row0 + 128, :], writes=[b_wr[wi]], key="wr%d" % wi)
                        if g == 2:
                            if debug:
                                P.dma("sp", dbg_e1a, e1a_src, reads=b_qn, writes=[Buf()], key="dbg_e1a")
                                P.dma("sp", dbg_e1b, e1b_src, reads=b_kvn + b_kr, writes=[Buf()], key="dbg_e1b")
                            P.collective(e1b_src, e1b_g, reads=b_kvn + b_kr, writes=[b_e1g], key="cc_e1b")
                            P.collective(e1a_src, e1a_g, reads=b_qn + [b_e1g], writes=[b_e1g], key="cc_e1a")
                        for tt in range(4):
                            gtt = hf * 4 + tt
                            tok = slice(gtt * 512, (gtt + 1) * 512)
                            if g == 0:
                                rmsnorm_group(w, 0, 2, tt, 256.0, kvn_v, b_kvn, gtt, "kvn")
                                pa, b_pa = proj(w, 256, 64, tt)
                                pb, b_pb = proj(w, 320, 64, tt)
                                t1, b_t1 = tmp()
                                t2, b_t2 = tmp()
                                P.op("dve", lambda e, pa=pa, t1=t1, tok=tok: e.tensor_tensor(out=t1[0:64, 0:512], in0=pa[0:64, :], in1=cosT[:, tok], op=ALU.mult),
                                     reads=[b_pa, b_tab], writes=[b_t1])
                                P.op("dve", lambda e, pb=pb, t2=t2, tok=tok: e.tensor_tensor(out=t2[0:64, 0:512], in0=pb[0:64, :], in1=sinT[:, tok], op=ALU.mult),
                                     reads=[b_pb, b_tab], writes=[b_t2])
                                i = stg_ctr[0] % 2
                                stg_ctr[0] += 1
                                sg_, b_sg_ = stgs[i], b_stgs[i]
                                P.op("pool", lambda e, t1=t1, t2=t2, sg_=sg_: e.tensor_tensor(out=sg_[0:64, 0, :], in0=t1[0:64, 0:512], in1=t2[0:64, 0:512], op=ALU.add),
                                     reads=[b_t1, b_t2], writes=[b_sg_])
                                P.dma("sp", kr_d[:, tok], sg_[0:64, 0, :], reads=[b_sg_], writes=[b_kr[gtt]], key="stg%d" % i)
                            elif g == 1:
                                rmsnorm_group(w, 0, 4, tt, 512.0, qn_v, b_qn, gtt, "qn")
                            elif g in (2, 3):
                                i = stg_ctr[0] % 2
                                stg_ctr[0] += 1
                                sg_, b_sg_ = stgs[i], b_stgs[i]
                                for c in range(4):
                                    pt, b_pt = proj(w, c * 128, 128, tt)
                                    P.op("act", lambda e, pt=pt, sg_=sg_, c=c: e.activation(out=sg_[:, c, :], in_=pt, func=AF.Silu),
                                         reads=[b_pt], writes=[b_sg_])
                                r0 = (g - 2) * 4
                                P.dma("sp", mix_v[:, r0:r0 + 4, tok], sg_[:], reads=[b_sg_], writes=[b_mix[r0 + c][gtt] for c in range(4)], key="stg%d" % i)
                            elif g in (4, 5):
                                tails = []
                                for s_ in range(2):
                                    pg = (g - 4) * 2 + s_
                                    wlen = POOL_W[pg]
                                    px, b_px = proj(w, s_ * 256, 128, tt)
                                    pgt, b_pgt = proj(w, s_ * 256 + 128, 128, tt)
                                    xb, b_xb = tmp()
                                    sa, b_sa = tmp()
                                    sb_, b_sb = tmp()
                                    P.op("act", lambda e, px=px, xb=xb: e.copy(out=xb[:, 16:528], in_=px), reads=[b_px], writes=[b_xb])
                                    if tt == 0:
                                        ph, b_ph = proj_halo(w, s_ * 256)
                                        P.op("dve", lambda e, xb=xb, ph=ph: e.tensor_scalar(out=xb[:, 0:16], in0=ph[:, 0:16], scalar1=coef[:, 1:2], scalar2=None, op0=ALU.mult),
                                             reads=[b_ph, b_xb, b_const], writes=[b_xb])
                                    else:
                                        P.op("pool", lambda e, xb=xb, pg=pg: e.tensor_copy(out=xb[:, 0:16], in_=hp[:, pg, :]), reads=[b_hp[pg], b_xb], writes=[b_xb])
                                    P.op("pool", lambda e, xb=xb, pg=pg: e.tensor_copy(out=hp[:, pg, :], in_=xb[:, 512:528]), reads=[b_xb], writes=[b_hp[pg]])
                                    P.op("pool", lambda e, xb=xb, sa=sa: e.tensor_tensor(out=sa[:, 1:528], in0=xb[:, 1:528], in1=xb[:, 0:527], op=ALU.add),
                                         reads=[b_xb], writes=[b_sa])
                                    fin, b_fin = sa, b_sa
                                    if wlen >= 4:
                                        P.op("pool", lambda e, sa=sa, sb_=sb_: e.tensor_tensor(out=sb_[:, 3:528], in0=sa[:, 3:528], in1=sa[:, 1:526], op=ALU.add),
                                             reads=[b_sa], writes=[b_sb])
                                        fin, b_fin = sb_, b_sb
                                    if wlen >= 8:
                                        P.op("pool", lambda e, sa=sa, sb_=sb_: e.tensor_tensor(out=sa[:, 7:528], in0=sb_[:, 7:528], in1=sb_[:, 3:524], op=ALU.add),
                                             reads=[b_sb, b_sa], writes=[b_sa])
                                        fin, b_fin = sa, b_sa
                                    if wlen >= 16:
                                        P.op("pool", lambda e, sa=sa, sb_=sb_: e.tensor_tensor(out=sb_[:, 15:528], in0=sa[:, 15:528], in1=sa[:, 7:520], op=ALU.add),
                                             reads=[b_sa, b_sb], writes=[b_sb])
                                        fin, b_fin = sb_, b_sb
                                    ip = pl_ctr[0] % 4
                                    pl_ctr[0] += 1
                                    pl, b_pl = pls[ip], b_pls[ip]
                                    sgp, b_sgp = sgps[ip], b_sgps[ip]
                                    P.op("dve", lambda e, fin=fin, xb=xb, pl=pl, wlen=wlen: e.scalar_tensor_tensor(
                                        out=pl[:], in0=fin[:, 16:528], scalar=1.0 / wlen, in1=xb[:, 16:528], op0=ALU.mult, op1=ALU.subtract),
                                        reads=[b_fin, b_xb], writes=[b_pl])
                                    if gtt == 0:
                                        t16, b_t16 = tmp()
                                        P.op("dve", lambda e, fin=fin, t16=t16, pg=pg: e.tensor_tensor(out=t16[:, 0:16], in0=fin[:, 16:32], in1=invdiv[:, pg, :], op=ALU.mult),
                                             reads=[b_fin, b_const], writes=[b_t16])
                                        P.op("dve", lambda e, t16=t16, xb=xb, pl=pl: e.tensor_tensor(out=pl[:, 0:16], in0=t16[:, 0:16], in1=xb[:, 16:32], op=ALU.subtract),
                                             reads=[b_t16, b_xb, b_pl], writes=[b_pl])
                                    P.op("act", lambda e, pgt=pgt, sgp=sgp: e.activation(out=sgp[:], in_=pgt, func=AF.Silu), reads=[b_pgt], writes=[b_sgp])
                                    tails.append((s_, pg, pl, b_pl, sgp, b_sgp))

                                def pool_tail(tails=tails, g=g, gtt=gtt, tok=tok):
                                    i = stg_ctr[0] % 2
                                    stg_ctr[0] += 1
                                    sg_, b_sg_ = stgs[i], b_stgs[i]
                                    for (s_, pg, pl, b_pl, sgp, b_sgp) in tails:
                                        py, b_py = psum()
                                        P.op("pe", lambda e, py=py, pl=pl, pg=pg: e.matmul(py, wp[:, pg, :], pl[:], start=True, stop=True),
                                             reads=[b_pl, b_sm], writes=[b_py])
                                        P.op("dve", lambda e, py=py, sgp=sgp, pg=pg, s_=s_: e.scalar_tensor_tensor(
                                            out=sg_[:, s_, :], in0=py, scalar=sm[:, 6 + pg:7 + pg], in1=sgp[:], op0=ALU.mult, op1=ALU.mult),
                                            reads=[b_py, b_sgp, b_sm], writes=[b_sg_])
                                    r0 = 8 + (g - 4) * 2
                                    P.dma("sp", mix_v[:, r0:r0 + 2, tok], sg_[:, 0:2, :], reads=[b_sg_], writes=[b_mix[r0 + c][gtt] for c in range(2)], key="stg%d" % i)

                                for fn_ in pending:
                                    fn_()
                                pending.clear()
                                pending.append(pool_tail)
                                if tt == 3:
                                    for fn_ in pending:
                                        fn_()
                                    pending.clear()
                            else:
                                j = g - 6
                                i = stg_ctr[0] % 2
                                stg_ctr[0] += 1
                                sg_, b_sg_ = stgs[i], b_stgs[i]
                                pch, b_pch = proj(w, 0, 128, tt)
                                pcc, b_pcc = proj(w, 128, 128, tt)
                                pcb, b_pcb = proj(w, 256, 128, tt)
                                pgc, b_pgc = proj(w, 384, 128, tt)
                                chs, b_chs = tmp()
                                ub, b_ub = tmp()
                                t1, b_t1 = tmp()
                                t2, b_t2 = tmp()
                                P.op("act", lambda e, pch=pch, chs=chs: e.copy(out=chs[:, 0:512], in_=pch), reads=[b_pch], writes=[b_chs])
                                P.op("dve", lambda e, pcc=pcc, chs=chs, ub=ub: e.tensor_tensor(out=ub[:, 2:514], in0=pcc, in1=chs[:, 0:512], op=ALU.mult),
                                     reads=[b_pcc, b_chs], writes=[b_ub])
                                if tt == 0:
                                    ph1, b_ph1 = proj_halo(w, 0)
                                    ph2, b_ph2 = proj_halo(w, 128)
                                    hh, b_hh = tmp()
                                    P.op("act", lambda e, ph1=ph1, hh=hh: e.copy(out=hh[:, 0:16], in_=ph1[:, 0:16]), reads=[b_ph1], writes=[b_hh])
                                    P.op("dve", lambda e, ph2=ph2, hh=hh: e.tensor_tensor(out=hh[:, 16:32], in0=ph2[:, 0:16], in1=hh[:, 0:16], op=ALU.mult),
                                         reads=[b_ph2, b_hh], writes=[b_hh])
                                    P.op("dve", lambda e, ub=ub, hh=hh: e.tensor_scalar(out=ub[:, 0:2], in0=hh[:, 30:32], scalar1=coef[:, 1:2], scalar2=None, op0=ALU.mult),
                                         reads=[b_hh, b_ub, b_const], writes=[b_ub])
                                else:
                                    P.op("pool", lambda e, ub=ub, j=j: e.tensor_copy(out=ub[:, 0:2], in_=hc[:, j, :]), reads=[b_hc[j], b_ub], writes=[b_ub])
                                P.op("pool", lambda e, ub=ub, j=j: e.tensor_copy(out=hc[:, j, :], in_=ub[:, 512:514]), reads=[b_ub], writes=[b_hc[j]])
                                cw0 = 10 + j * 3
                                P.op("dve", lambda e, ub=ub, t1=t1, cw0=cw0: e.tensor_scalar(out=t1[:, 0:512], in0=ub[:, 0:512], scalar1=sm[:, cw0:cw0 + 1], scalar2=None, op0=ALU.mult),
                                     reads=[b_ub, b_sm], writes=[b_t1])
                                P.op("dve", lambda e, ub=ub, t1=t1, t2=t2, cw0=cw0: e.scalar_tensor_tensor(
                                    out=t2[:, 0:512], in0=ub[:, 1:513], scalar=sm[:, cw0 + 1:cw0 + 2], in1=t1[:, 0:512], op0=ALU.mult, op1=ALU.add),
                                    reads=[b_ub, b_t1, b_sm], writes=[b_t2])
                                P.op("dve", lambda e, ub=ub, t1=t1, t2=t2, cw0=cw0: e.scalar_tensor_tensor(
                                    out=t1[:, 0:512], in0=ub[:, 2:514], scalar=sm[:, cw0 + 2:cw0 + 3], in1=t2[:, 0:512], op0=ALU.mult, op1=ALU.add),
                                    reads=[b_ub, b_t2, b_t1, b_sm], writes=[b_t1])
                                P.op("dve", lambda e, pcb=pcb, t1=t1, t2=t2: e.tensor_tensor(out=t2[:, 0:512], in0=pcb, in1=t1[:, 0:512], op=ALU.mult),
                                     reads=[b_pcb, b_t1, b_t2], writes=[b_t2])
                                P.op("act", lambda e, pgc=pgc, chs=chs: e.activation(out=chs[:, 0:512], in_=pgc, func=AF.Silu), reads=[b_pgc, b_chs], writes=[b_chs])
                                P.op("pool", lambda e, t2=t2, chs=chs, sg_=sg_: e.tensor_tensor(out=sg_[:, 0, :], in0=t2[:, 0:512], in1=chs[:, 0:512], op=ALU.mult),
                                     reads=[b_t2, b_chs], writes=[b_sg_])
                                r0 = 12 + j
                                P.dma("sp", mix_v[:, r0, tok], sg_[:, 0, :], reads=[b_sg_], writes=[b_mix[r0][gtt]], key="stg%d" % i)
                P.barrier(junk)

        def phase_B(L, wo_hi_bufs):
            with contextlib.ExitStack() as st:
                areset()
                wo_hi = sb_top("wo_hi", [128, 8, 2048], BF16)
                SB_ = [4, 5, 6, 7]
                sctr = [0]
                octr = [0]
                lctr = [0]
                qns = sb("qns", [128, 4, S], BF16, st)
                kvns = sb("kvns", [128, 2, S], BF16, st)
                krs = sb("krs", [64, S], BF16, st)
                b_kvl, b_krl, b_qnl = [Buf(), Buf()], [Buf(), Buf()], [Buf(), Buf()]
                b_lc = [Buf(), Buf()]
                for rho in range(2):
                    csl = slice(rho * SO, (rho + 1) * SO)
                    P.dma("sp", kvns[:, :, csl], e1b_g[rho * 320:rho * 320 + 256, :].rearrange("(kc p) t -> p kc t", p=128), reads=[b_e1g], writes=[b_kvl[rho], b_lc[rho]], key="kvns%d" % rho)
                    P.dma("sp", krs[:, csl], e1b_g[rho * 320 + 256:rho * 320 + 320, :], reads=[b_e1g], writes=[b_krl[rho], b_lc[rho]], key="krs%d" % rho)
                    P.dma("sp", qns[:, :, csl], e1a_g[rho * 512:(rho + 1) * 512, :].rearrange("(kc p) t -> p kc t", p=128), reads=[b_e1g], writes=[b_qnl[rho], b_lc[rho]], key="qns%d" % rho)
                cosT = sb("cosA", [64, S], BF16, st)
                sinT = sb("sinA", [64, S], BF16, st)
                b_tab = Buf()
                P.dma("sp", cosT[:], cosA_d, reads=[b_tabd], writes=[b_tab], key="cosA")
                P.dma("sp", sinT[:], sinA_d, reads=[b_tabd], writes=[b_tab], key="sinA")
                wq = sb("wq", [128, 4, 1024], BF16, st)
                wk = sb("wk", [128, 2, 512], BF16, st)
                wv = sb("wv", [128, 2, 512], BF16, st)
                sm = sb("smB", [128, 32], F32, st)
                b_w = Buf()
                P.dma("sp", sm[:], sm_d[L * 128:(L + 1) * 128, :], writes=[b_w], key="smB")
                b_w1, b_w2, b_w3 = Buf(), Buf(), Buf()
                P.dma("pool", wq[:].rearrange("p a b -> p (a b)"), wq_d[L * 128:(L + 1) * 128, :], writes=[b_w1], key="wq")
                P.dma("pool", wk[:].rearrange("p a b -> p (a b)"), wk_d[L * 128:(L + 1) * 128, :], reads=[b_w1], writes=[b_w2], key="wk")
                P.dma("pool", wv[:].rearrange("p a b -> p (a b)"), wv_d[L * 128:(L + 1) * 128, :], reads=[b_w1, b_w2], writes=[b_w, b_w3], key="wv")
                P.dma("pool", wo_hi[:, 4:8, :].rearrange("p a b -> p (a b)"), wo_d[L * 128:(L + 1) * 128, 3 * 8192:4 * 8192],
                      reads=[b_w3], writes=[wo_hi_bufs[1]], key="wo3")
                P.dma("pool", wo_hi[:, 0:4, :].rearrange("p a b -> p (a b)"), wo_d[L * 128:(L + 1) * 128, 2 * 8192:3 * 8192],
                      reads=[wo_hi_bufs[1]], writes=[wo_hi_bufs[0]], key="wo2")
                for kc in range(4):
                    P.op("dve", lambda e, kc=kc: e.tensor_scalar(out=wq[:, kc, :], in0=wq[:, kc, :], scalar1=sm[:, kc:kc + 1], scalar2=None, op0=ALU.mult),
                         reads=[b_w], writes=[b_w])
                for kc in range(2):
                    P.op("dve", lambda e, kc=kc: e.tensor_scalar(out=wk[:, kc, :], in0=wk[:, kc, :], scalar1=sm[:, 4 + kc:5 + kc], scalar2=None, op0=ALU.mult),
                         reads=[b_w], writes=[b_w])
                    P.op("dve", lambda e, kc=kc: e.tensor_scalar(out=wv[:, kc, :], in0=wv[:, kc, :], scalar1=sm[:, 4 + kc:5 + kc], scalar2=None, op0=ALU.mult),
                         reads=[b_w], writes=[b_w])
                Vq = sb("Vq", [128, 32, 512], BF16, st)
                b_V = [Buf() for i in range(32)]
                kTh = sb("kTh", [128, S], BF16, st)
                qTh = sb("qTh", [128, S], BF16, st)
                qrh = sb("qrh", [64, S], BF16, st)
                b_kT = [Buf() for i in range(8)]
                b_qT = [Buf() for i in range(8)]
                b_qr = [Buf() for i in range(8)]
                pTs = [sb("pT%d" % i, [128, 512], BF16, st) for i in range(4)]
                b_pTs = [Buf() for i in range(4)]
                pT_ctr = [0]
                rls = [sb("rl%d" % i, [128, 512], F32, st) for i in range(2)]
                b_rls = [Buf() for i in range(2)]
                outs = [sb("ob%d" % i, [128, 512], BF16, st) for i in range(2)]
                b_outs = [Buf() for i in range(2)]
                rt1 = [sb("rta%d" % i, [64, 512], F32, st) for i in range(2)]
                rt2 = [sb("rtb%d" % i, [64, 512], F32, st) for i in range(2)]
                b_rt1 = [Buf() for i in range(2)]
                b_rt2 = [Buf() for i in range(2)]
                ev = [0]

                def evac(out_ap, in_ap, reads, writes):
                    ev[0] += 1
                    if ev[0] % 2 == 0:
                        P.op("dve", lambda e: e.tensor_copy(out=out_ap, in_=in_ap), reads=reads, writes=writes)
                    else:
                        P.op("act", lambda e: e.copy(out=out_ap, in_=in_ap), reads=reads, writes=writes)

                uc = [0]
                for h in range(HL):
                    if h % 4 == 0:
                        hq = h // 4
                        for tk in range(32):
                            pt, b_pt = psum(SB_, sctr)
                            for kc in range(2):
                                P.op("pe", lambda e, pt=pt, kc=kc, tk=tk, hq=hq: e.matmul(
                                    pt, kvns[:, kc, tk * 128:(tk + 1) * 128], wv[:, kc, hq * 512:(hq + 1) * 512], start=(kc == 0), stop=(kc == 1)),
                                    reads=[b_kvl[tk // 16], b_w], writes=[b_pt])
                            evac(Vq[:, tk, :], pt, [b_pt], [b_V[tk]])
                    for tt in range(8):
                        tok = slice(tt * 512, (tt + 1) * 512)
                        pt, b_pt = psum(SB_, sctr)
                        for kc in range(2):
                            P.op("pe", lambda e, pt=pt, kc=kc, tok=tok, h=h: e.matmul(
                                pt, wk[:, kc, h * 128:(h + 1) * 128], kvns[:, kc, tok], start=(kc == 0), stop=(kc == 1)),
                                reads=[b_kvl[tt // 4], b_w], writes=[b_pt])
                        evac(kTh[:, tok], pt, [b_pt], [b_kT[tt]])
                        pt, b_pt = psum(SB_, sctr)
                        for kc in range(4):
                            P.op("pe", lambda e, pt=pt, kc=kc, tok=tok, h=h: e.matmul(
                                pt, wq[:, kc, h * 256:h * 256 + 128], qns[:, kc, tok], start=(kc == 0), stop=(kc == 3)),
                                reads=[b_qnl[tt // 4], b_w], writes=[b_pt])
                        evac(qTh[:, tok], pt, [b_pt], [b_qT[tt]])
                        pa, b_pa = psum(SB_, sctr)
                        for kc in range(4):
                            P.op("pe", lambda e, pa=pa, kc=kc, tok=tok, h=h: e.matmul(
                                pa[0:64, :], wq[:, kc, h * 256 + 128:h * 256 + 192], qns[:, kc, tok], start=(kc == 0), stop=(kc == 3)),
                                reads=[b_qnl[tt // 4], b_w], writes=[b_pa])
                        pb, b_pb = psum(SB_, sctr)
                        for kc in range(4):
                            P.op("pe", lambda e, pb=pb, kc=kc, tok=tok, h=h: e.matmul(
                                pb[0:64, :], wq[:, kc, h * 256 + 192:h * 256 + 256], qns[:, kc, tok], start=(kc == 0), stop=(kc == 3)),
                                reads=[b_qnl[tt // 4], b_w], writes=[b_pb])
                        i = tt % 2
                        P.op("dve", lambda e, pa=pa, i=i, tok=tok: e.tensor_tensor(out=rt1[i][:], in0=pa[0:64, :], in1=cosT[:, tok], op=ALU.mult),
                             reads=[b_pa, b_tab], writes=[b_rt1[i]])
                        P.op("dve", lambda e, pb=pb, i=i, tok=tok: e.tensor_tensor(out=rt2[i][:], in0=pb[0:64, :], in1=sinT[:, tok], op=ALU.mult),
                             reads=[b_pb, b_tab], writes=[b_rt2[i]])
                        P.op("pool", lambda e, i=i, tok=tok: e.tensor_tensor(out=qrh[:, tok], in0=rt1[i][:], in1=rt2[i][:], op=ALU.add),
                             reads=[b_rt1[i], b_rt2[i]], writes=[b_qr[tt]])
                    units = [(qb, kb) for qb in range(8) for kb in range(4 * qb + 4)]
                    LOOK = 2
                    acc = {}
                    sc = {}

                    def emit_scores(u):
                        qb, kb = units[u]
                        qtok = slice(qb * 512, (qb + 1) * 512)
                        ktok = slice(kb * 128, (kb + 1) * 128)
                        ps_, b_ps_ = psum(SB_, sctr)
                        diag = kb >= 4 * qb
                        P.op("pe", lambda e: e.matmul(ps_, kTh[:, ktok], qTh[:, qtok], start=True, stop=False),
                             reads=[b_kT[kb // 4], b_qT[qb]], writes=[b_ps_])
                        P.op("pe", lambda e: e.matmul(ps_, krs[:, ktok], qrh[:, qtok], start=False, stop=(not diag)),
                             reads=[b_krl[kb // 16], b_qr[qb]], writes=[b_ps_])
                        if diag:
                            jm = kb - 4 * qb
                            P.op("pe", lambda e: e.matmul(ps_, ident[:], masks[:, jm, :], start=False, stop=True),
                                 reads=[b_const], writes=[b_ps_])
                        sc[u] = (ps_, b_ps_)

                    def emit_rest(u, h=h):
                        qb, kb = units[u]
                        nkb = 4 * qb + 4
                        qtok = slice(qb * 512, (qb + 1) * 512)
                        if kb == 0:
                            acc[qb] = (psum([0, 1], octr), psum([2, 3], lctr))
                        (po, b_po), (pl_, b_pl_) = acc[qb]
                        ps_, b_ps_ = sc.pop(u)
                        ip = pT_ctr[0] % 4
                        pT_ctr[0] += 1
                        pT, b_pT = pTs[ip], b_pTs[ip]
                        P.op("act", lambda e: e.activation(out=pT[:], in_=ps_, func=AF.Exp, scale=SCALE), reads=[b_ps_], writes=[b_pT])
                        P.op("pe", lambda e: e.matmul(po, Vq[:, kb, (h % 4) * 128:(h % 4 + 1) * 128], pT[:], start=(kb == 0), stop=(kb == nkb - 1)),
                             reads=[b_V[kb], b_pT], writes=[b_po])
                        P.op("pe", lambda e: e.matmul(pl_, ones[:], pT[:], start=(kb == 0), stop=(kb == nkb - 1)),
                             reads=[b_const, b_pT], writes=[b_pl_])
                        if kb == nkb - 1:
                            i = uc[0] % 2
                            uc[0] += 1
                            P.op("dve", lambda e: e.reciprocal(out=rls[i][:], in_=pl_), reads=[b_pl_], writes=[b_rls[i]])
                            P.op("dve", lambda e: e.tensor_tensor(out=outs[i][:], in0=po, in1=rls[i][:], op=ALU.mult),
                                 reads=[b_po, b_rls[i]], writes=[b_outs[i]])
                            P.dma("sp", e2_src[h][:, qtok], outs[i][:], reads=[b_outs[i]], writes=[b_e2s[h][qb]], key="ob%d" % i)
                            if qb == 7:
                                if debug:
                                    P.dma("sp", dbg_e2[h], e2_src[h], reads=b_e2s[h], writes=[Buf()], key="dbg_e2")
                                P.collective(e2_src[h], e2_g[h], reads=b_e2s[h], writes=[b_e2g[h]], key="cc_e2_%d" % h)

                    for u in range(min(LOOK, len(units))):
                        emit_scores(u)
                    for u in range(len(units)):
                        if u + LOOK < len(units):
                            emit_scores(u + LOOK)
                        emit_rest(u)
                P.barrier(junk)

        for L in range(n_layers):
            if stop_phase == "ln0":
                break
            phase_A(L)
            if stop_phase == "A":
                break
            wo_hi_bufs = [Buf(), Buf()]
            phase_B(L, wo_hi_bufs)
            if stop_phase == "B":
                break
            with contextlib.ExitStack() as st:
                last = (L == DEPTH - 1)
                ln_phase(st, L, "proj", not last, last, wo_hi_bufs)
                P.barrier(junk)
        P.finish()
    return nc


def _tile_k(w, ncols_pad=None):
    K, C = w.shape
    return np.ascontiguousarray(w.reshape(K // 128, 128, C).transpose(1, 0, 2))


def prep_inputs(x, positions, emb_ln_g, emb_ln_b, w_in, q_norm_g, kv_norm_g, w_uq, w_ukv, w_pool,
                pool_scale, conv_w, w_out, b_out, ln_g, ln_b):
    f32 = np.float32
    w_in = np.asarray(w_in, f32)
    offs = np.cumsum([0, 512, 256, 64, 1024, 512, 512, 512, 512, 512, 512])
    o_q, o_kv, o_kr, o_gm, o_pi, o_gp, o_ch, o_cb, o_cc, o_gc = offs[:10]
    groups = []
    zero128 = None
    for L in range(DEPTH):
        W = w_in[L]
        kr = W[:, o_kr:o_kr + 64]
        ksw = np.concatenate([kr[:, 32:64], kr[:, 0:32]], axis=1)
        g0 = np.concatenate([W[:, o_kv:o_kv + 256], kr, ksw, np.zeros((D, 128), f32)], axis=1)
        gl = [g0, W[:, o_q:o_q + 512], W[:, o_gm:o_gm + 512], W[:, o_gm + 512:o_gm + 1024]]
        for a in range(2):
            gl.append(np.concatenate([W[:, o_pi + (2 * a) * 128:o_pi + (2 * a + 1) * 128], W[:, o_gp + (2 * a) * 128:o_gp + (2 * a + 1) * 128],
                                      W[:, o_pi + (2 * a + 1) * 128:o_pi + (2 * a + 2) * 128], W[:, o_gp + (2 * a + 1) * 128:o_gp + (2 * a + 2) * 128]], axis=1))
        for j in range(4):
            sl = slice(j * 128, (j + 1) * 128)
            gl.append(np.concatenate([W[:, o_ch:o_ch + 512][:, sl], W[:, o_cc:o_cc + 512][:, sl], W[:, o_cb:o_cb + 512][:, sl], W[:, o_gc:o_gc + 512][:, sl]], axis=1))
        for gmat in gl:
            groups.append(_tile_k(gmat).reshape(128, 16 * 512))
    w_in_g = np.ascontiguousarray(np.concatenate(groups, axis=0))

    wq_l, wk_l, wv_l = [[], []], [[], []], [[], []]
    wo_l, wp_l, sm_l = [], [], []
    for L in range(DEPTH):
        wq = np.asarray(w_uq[L], f32).reshape(512, NH, 192)
        rope = wq[:, :, 128:192]
        sw = np.concatenate([rope[:, :, 32:64], rope[:, :, 0:32]], axis=2)
        wq2 = np.concatenate([wq, sw], axis=2)
        wkv = np.asarray(w_ukv[L], f32).reshape(256, NH, 256)
        for r in range(2):
            hs = slice(4 * r, 4 * r + 4)
            wq_l[r].append(_tile_k(np.ascontiguousarray(wq2[:, hs]).reshape(512, 1024)).reshape(128, 4 * 1024))
            wk_l[r].append(_tile_k(np.ascontiguousarray(wkv[:, hs, 0:128]).reshape(256, 512)).reshape(128, 2 * 512))
            wv_l[r].append(_tile_k(np.ascontiguousarray(wkv[:, hs, 128:256]).reshape(256, 512)).reshape(128, 2 * 512))
        wo_l.append(_tile_k(np.asarray(w_out[L], f32)).reshape(128, 16 * 2048))
        wp_l.append(np.ascontiguousarray(np.asarray(w_pool[L], f32).transpose(1, 0, 2)).reshape(128, 4 * 128))
        sm = np.zeros((128, 32), f32)
        sm[:, 0:4] = np.asarray(q_norm_g[L], f32).reshape(4, 128).T
        sm[:, 4:6] = np.asarray(kv_norm_g[L], f32).reshape(2, 128).T
        sm[:, 6:10] = np.asarray(pool_scale[L], f32).reshape(4, 128).T
        cw = np.asarray(conv_w[L], f32).reshape(3, 4, 128)
        sm[:, 10:22] = cw.transpose(2, 1, 0).reshape(128, 12)
        sm_l.append(sm)
    lnp = np.stack([np.asarray(emb_ln_g, f32), np.asarray(emb_ln_b, f32)] +
                   sum([[np.asarray(ln_g[L], f32), np.asarray(ln_b[L], f32), np.asarray(b_out[L], f32)] for L in range(DEPTH)], []), axis=0)
    half = 32
    inv_freq = (10000.0 ** (-np.arange(half, dtype=np.float32) / half)).astype(f32)
    ropec = np.zeros((128, 2), f32)
    ropec[:, 0] = np.concatenate([inv_freq] * 4)
    ropec[:, 1] = np.concatenate([-np.ones(32, f32), np.ones(32, f32)] * 2)
    invdiv = np.zeros((2, 128, 4, 16), f32)
    for g, w in enumerate(POOL_W):
        invdiv[0, :, g, :] = 1.0 / np.minimum(np.arange(1, 17, dtype=f32), float(w))
        invdiv[1, :, g, :] = 1.0 / float(w)
    ident = np.eye(128, dtype=f32).astype(ml_dtypes.bfloat16)
    kk = np.arange(128)[:, None]
    qq = np.arange(512)[None, :]
    masks = np.stack([np.where(j * 128 + kk <= qq, 0.0, NEG) for j in range(4)], axis=1).astype(f32)
    masks = masks.reshape(128, 4 * 512).astype(ml_dtypes.bfloat16)
    shared = {
        "ropec": ropec, "lnp": np.ascontiguousarray(lnp), "w_in_g": w_in_g,
        "wo": np.ascontiguousarray(np.concatenate(wo_l, 0)),
        "wp": np.ascontiguousarray(np.concatenate(wp_l, 0)), "small": np.ascontiguousarray(np.concatenate(sm_l, 0)),
        "ident": ident, "masks": masks,
    }
    per_rank = []
    for r in range(2):
        coef = np.zeros((128, 2), f32)
        coef[:, r] = 1.0
        per_rank.append({
            "wq": np.ascontiguousarray(np.concatenate(wq_l[r], 0)), "wk": np.ascontiguousarray(np.concatenate(wk_l[r], 0)),
            "wv": np.ascontiguousarray(np.concatenate(wv_l[r], 0)), "invdiv": np.ascontiguousarray(invdiv[r].reshape(128, 64)),
            "coef": coef,
        })
    x = np.asarray(x, f32)
    positions = np.asarray(positions, np.int32)
    in_maps = []
    for c in range(8):
        b, r = c // 2, c % 2
        m = dict(shared)
        m.update(per_rank[r])
        m["x"] = np.ascontiguousarray(x[b, r * SO:(r + 1) * SO])
        m["pos"] = np.ascontiguousarray(positions[b][None, :])
        m["pos_own"] = np.ascontiguousarray(positions[b, r * SO:(r + 1) * SO][None, :])
        in_maps.append(m)
    return in_maps


def kernel(**inputs):
    in_maps = prep_inputs(**inputs)
    nc = build()
    res = run_bass_kernel_spmd(nc, in_maps, core_ids=list(range(8)))
    out = np.empty((4, S, D), np.float32)
    for c in range(8):
        b, r = c // 2, c % 2
        out[b, r * SO:(r + 1) * SO] = np.asarray(res.results[c]["out"], dtype=np.float32)
    return out
```

```python
import math
import contextlib
import numpy as np
import ml_dtypes
import concourse.bass as bass
import concourse.mybir as mybir
from concourse.bass_utils import run_bass_kernel_spmd

F32 = mybir.dt.float32
BF16 = mybir.dt.bfloat16
I32 = mybir.dt.int32
AF = mybir.ActivationFunctionType
ALU = mybir.AluOpType

S = 4096
SO = 2048
HL = 4
PAIRS = [[0, 1], [2, 3], [4, 5], [6, 7]]
D = 2048
DEPTH = 2
NH = 8
LN_EPS = 1e-5
RMS_EPS = 1e-6
ALPHA = (2 * DEPTH) ** 0.25
SCALE = 192 ** -0.5
NEG = -30000.0
POOL_W = (2, 4, 8, 16)
NG = 10

ENGS = ("pe", "act", "dve", "pool", "sp")


class Buf:
    __slots__ = ("name", "w", "r")

    def __init__(self, name=""):
        self.name = name
        self.w = None
        self.r = []


class Op:
    __slots__ = ("eng", "fn", "waits", "flag", "dma_key", "dma_val", "seq")

    def __init__(self, eng, fn):
        self.eng = eng
        self.fn = fn
        self.waits = []
        self.flag = False
        self.dma_key = None
        self.dma_val = 0
        self.seq = -1


class Prog:
    def __init__(self, nc):
        self.nc = nc
        self.ops = {e: [] for e in ENGS}
        self.seen = {e: {} for e in ENGS}
        self.seen_dma = {e: {} for e in ENGS}
        self.dma_counts = {}
        self.cc_keys = set()
        self.jb = [Buf(), Buf(), Buf()]

    def _add(self, eng, fn, reads, writes, dma_key=None):
        op = Op(eng, fn)
        op.seq = len(self.ops[eng])
        deps = []
        for b in reads:
            if b.w is not None:
                deps.append(b.w)
        for b in writes:
            if b.w is not None:
                deps.append(b.w)
            deps.extend(b.r)
        best = {}
        dma_deps = {}
        for d in deps:
            if d.dma_key is not None:
                if dma_deps.get(d.dma_key, 0) < d.dma_val:
                    dma_deps[d.dma_key] = d.dma_val
            else:
                if d.eng == eng and eng == "pe":
                    continue
                if best.get(d.eng, -1) < d.seq:
                    best[d.eng] = d.seq
        for f, s in best.items():
            if self.seen[eng].get(f, -1) >= s:
                continue
            self.seen[eng][f] = s
            dop = self.ops[f][s]
            dop.flag = True
            op.waits.append(("eng", f, dop))
        for k, v in dma_deps.items():
            if self.seen_dma[eng].get(k, 0) >= v:
                continue
            self.seen_dma[eng][k] = v
            op.waits.append(("dma", k, v))
        if dma_key is not None:
            op.dma_key = dma_key
            self.dma_counts[dma_key] = self.dma_counts.get(dma_key, 0) + 16
            op.dma_val = self.dma_counts[dma_key]
        for b in reads:
            b.r.append(op)
        for b in writes:
            b.w = op
            b.r = []
        self.ops[eng].append(op)
        return op

    def op(self, eng, fn, reads=(), writes=()):
        return self._add(eng, fn, reads, writes, None)

    def dma(self, eng, out, in_, reads=(), writes=(), key=None):
        def fn(e):
            return e.dma_start(out=out, in_=in_)
        return self._add(eng, fn, reads, writes, key)

    def collective(self, src, dst, reads, writes, key):
        def fn(e):
            return e.collective_compute("AllGather", ALU.bypass, replica_groups=PAIRS, ins=[src.opt()], outs=[dst.opt()])
        o = self._add("pool", fn, reads, writes, key)
        self.dma_counts[key] -= 15
        o.dma_val = self.dma_counts[key]
        self.cc_keys.add(key)
        return o

    def barrier(self, junk):
        marks = []
        jb = self.jb
        b = Buf()
        self.op("act", lambda e: e.activation(out=junk[:, 0:1], in_=junk[:, 4:5], func=AF.Copy), writes=[b, jb[0]])
        marks.append(b)
        b = Buf()
        self.op("dve", lambda e: e.memset(junk[:, 1:2], 0.0), writes=[b, jb[1]])
        marks.append(b)
        b = Buf()
        self.op("pool", lambda e: e.memset(junk[:, 2:3], 0.0), writes=[b, jb[2]])
        marks.append(b)
        fence = Buf()
        o = self.op("sp", lambda e: e.nop(), reads=marks, writes=[fence])
        for k, v in self.dma_counts.items():
            if self.seen_dma["sp"].get(k, 0) < v:
                self.seen_dma["sp"][k] = v
                o.waits.append(("dma", k, v))
        self.op("act", lambda e: e.activation(out=junk[:, 0:1], in_=junk[:, 4:5], func=AF.Copy), reads=[fence], writes=[jb[0]])
        self.op("dve", lambda e: e.memset(junk[:, 1:2], 0.0), reads=[fence], writes=[jb[1]])
        self.op("pool", lambda e: e.memset(junk[:, 2:3], 0.0), reads=[fence], writes=[jb[2]])
        self.op("pe", lambda e: e.nop(), reads=[fence])
        for e in ENGS:
            for k, v in self.dma_counts.items():
                if self.seen_dma[e].get(k, 0) < v:
                    self.seen_dma[e][k] = v

    def finish(self):
        nc = self.nc
        with contextlib.ExitStack() as st:
            esem = {e: st.enter_context(nc.semaphore("s_" + e)) for e in ENGS}
            dsem = {k: st.enter_context(nc.semaphore("d_%s" % (k,))) for k in self.dma_counts}
            block = st.enter_context(nc.Block())
            for e in ENGS:
                c = 0
                for o in self.ops[e]:
                    if o.flag:
                        c += 1
                        o.dma_val = c

            def emit(e, eng):
                for o in self.ops[e]:
                    for kind, k, v in o.waits:
                        if kind == "eng":
                            eng.wait_ge(esem[k], v.dma_val)
                        else:
                            eng.wait_ge(dsem[k], v)
                    inst = o.fn(eng)
                    if o.dma_key is not None and o.dma_key in self.cc_keys:
                        inst.then_inc(dsem[o.dma_key])
                    elif o.dma_key is not None:
                        inst.then_inc(dsem[o.dma_key], 16)
                    elif o.flag:
                        inst.then_inc(esem[e], 1)
                if e == "sp":
                    for k, v in self.dma_counts.items():
                        eng.wait_ge(dsem[k], v)

            @block.tensor
            def _(eng):
                emit("pe", eng)

            @block.scalar
            def _(eng):
                emit("act", eng)

            @block.vector
            def _(eng):
                emit("dve", eng)

            @block.gpsimd
            def _(eng):
                emit("pool", eng)

            @block.sync
            def _(eng):
                emit("sp", eng)


def build(debug=False, n_layers=DEPTH, stop_phase=None):
    nc = bass.Bass("TRN2", target_bir_lowering=False)
    P = Prog(nc)

    def din(name, shape, dt):
        return nc.dram_tensor(name, shape, dt, kind="ExternalInput").ap()

    x_d = din("x", [SO, D], F32)
    pos_d = din("pos", [1, S], I32)
    poso_d = din("pos_own", [1, SO], I32)
    coef_d = din("coef", [128, 2], F32)
    rc_d = din("ropec", [128, 2], F32)
    lnp_d = din("lnp", [2 + 3 * DEPTH, D], F32)
    win_d = din("w_in_g", [DEPTH * NG * 128, 16 * 512], F32)
    wq_d = din("wq", [DEPTH * 128, 4 * 1024], F32)
    wk_d = din("wk", [DEPTH * 128, 2 * 512], F32)
    wv_d = din("wv", [DEPTH * 128, 2 * 512], F32)
    wo_d = din("wo", [DEPTH * 128, 16 * 2048], F32)
    wp_d = din("wp", [DEPTH * 128, 4 * 128], F32)
    sm_d = din("small", [DEPTH * 128, 32], F32)
    idv_d = din("invdiv", [128, 64], F32)
    ident_d = din("ident", [128, 128], BF16)
    mask_d = din("masks", [128, 4 * 512], BF16)
    out_d = nc.dram_tensor("out", [SO, D], F32, kind="ExternalOutput").ap()

    skind = "ExternalOutput" if debug else "Internal"
    resid_d = nc.dram_tensor("resid", [SO, D], F32, kind=skind).ap()
    hT_d = nc.dram_tensor("hT", [D, SO], BF16, kind=skind).ap()
    mix_d = nc.dram_tensor("mixT", [D, SO], BF16, kind=skind).ap()
    cosA_d = nc.dram_tensor("cosA", [64, S], BF16).ap()
    sinA_d = nc.dram_tensor("sinA", [64, S], BF16).ap()
    cosO_d = nc.dram_tensor("cosO", [64, SO], BF16).ap()
    sinO_d = nc.dram_tensor("sinO", [64, SO], BF16).ap()
    e1a_src = nc.dram_tensor("e1a_src", [512, SO], BF16).ap()
    e1a_g = nc.dram_tensor("e1a_g", [1024, SO], BF16).ap()
    e1b_src = nc.dram_tensor("e1b_src", [320, SO], BF16).ap()
    e1b_g = nc.dram_tensor("e1b_g", [640, SO], BF16).ap()
    e2_src = [nc.dram_tensor("e2_src%d" % i, [128, S], BF16).ap() for i in range(4)]
    e2_gall = nc.dram_tensor("e2_gall", [4 * 256, S], BF16).ap()
    e2_g = [e2_gall[i * 256:(i + 1) * 256, :] for i in range(4)]
    e2g_v = e2_gall.rearrange("(h r p) t -> p h r t", h=4, r=2)
    tl_src = nc.dram_tensor("tl_src", [D, 16], BF16).ap()
    tl_g = nc.dram_tensor("tl_g", [2 * D, 16], BF16).ap()
    if debug:
        dbg_e1a = nc.dram_tensor("dbg_e1a", [512, SO], BF16, kind="ExternalOutput").ap()
        dbg_e1b = nc.dram_tensor("dbg_e1b", [320, SO], BF16, kind="ExternalOutput").ap()
        dbg_e2 = [nc.dram_tensor("dbg_e2_%d" % i, [128, S], BF16, kind="ExternalOutput").ap() for i in range(4)]

    b_resid = [Buf("resid%d" % i) for i in range(16)]
    b_hT = [Buf("hT%d" % i) for i in range(4)]
    b_qn = [Buf() for i in range(4)]
    b_kvn = [Buf() for i in range(4)]
    b_kr = [Buf() for i in range(4)]
    b_mix = [[Buf() for t in range(4)] for r in range(16)]
    b_out = [Buf() for i in range(16)]
    b_e1g, b_tlsrc, b_tlg = Buf(), Buf(), Buf()
    b_e2g = [Buf() for i in range(4)]
    b_e2s = [[Buf() for t in range(8)] for r in range(4)]
    b_tabd = Buf()

    hT_v = hT_d.rearrange("(kc p) t -> p kc t", p=128)
    mix_v = mix_d.rearrange("(kc p) t -> p kc t", p=128)
    qn_v = e1a_src.rearrange("(kc p) t -> p kc t", p=128)
    kvn_v = e1b_src[0:256, :].rearrange("(kc p) t -> p kc t", p=128)
    kr_d = e1b_src[256:320, :]
    tls_v = tl_src.rearrange("(kc p) t -> p kc t", p=128)
    tlg_v = tl_g.rearrange("(kc p) t -> p kc t", p=128)

    with contextlib.ExitStack() as gst:
        ARENA_WORDS = 52224
        a_hi = [ARENA_WORDS]
        arena = gst.enter_context(nc.sbuf_tensor("arena", [128, ARENA_WORDS], F32))
        a_top = [0]
        a_mark = [0]

        def sb(name, shape, dt, st=None):
            n = 1
            for d_ in shape[1:]:
                n *= d_
            esz = 4 if dt in (F32, I32) else 2
            words = (n * esz + 3) // 4
            words = (words + 7) // 8 * 8
            off = a_top[0]
            assert off + words <= a_hi[0], ("SBUF arena overflow", name, off, words)
            a_top[0] = off + words
            v = arena[0:shape[0], off:off + words]
            if dt != F32:
                v = v.bitcast(dt)
            v = v[:, 0:n]
            if len(shape) == 3:
                v = v.rearrange("p (a b) -> p a b", a=shape[1])
            return v

        def sb_top(name, shape, dt):
            n = 1
            for d_ in shape[1:]:
                n *= d_
            esz = 4 if dt in (F32, I32) else 2
            words = (n * esz + 3) // 4
            words = (words + 7) // 8 * 8
            off = a_hi[0] - words
            assert off >= a_top[0], ("SBUF arena overflow (top)", name)
            a_hi[0] = off
            v = arena[0:shape[0], off:off + words]
            if dt != F32:
                v = v.bitcast(dt)
            v = v[:, 0:n]
            if len(shape) == 3:
                v = v.rearrange("p (a b) -> p a b", a=shape[1])
            return v

        def areset():
            a_top[0] = a_mark[0]
            a_hi[0] = ARENA_WORDS

        ps_all = gst.enter_context(nc.psum_tensor("ps", [128, 8 * 512], F32))
        ps_bufs = [Buf("ps%d" % i) for i in range(8)]
        ps_ctr = [0]

        def psum(banks=None, ctr=None):
            if banks is None:
                i = ps_ctr[0] % 8
                ps_ctr[0] += 1
            else:
                i = banks[ctr[0] % len(banks)]
                ctr[0] += 1
            return ps_all[:, i * 512:(i + 1) * 512], ps_bufs[i]

        junk = sb("junk", [128, 8], F32)
        ident = sb("ident", [128, 128], BF16)
        ones = sb("ones", [128, 128], BF16)
        masks = sb("masks", [128, 4, 512], BF16)
        rc = sb("rc", [128, 2], F32)
        coef = sb("coef", [128, 2], F32)
        invdiv = sb("invdiv", [128, 4, 16], F32)
        b_const = Buf("const")
        b_tab = Buf("tab")

        P.op("dve", lambda e: e.memset(junk[:], 0.0), writes=[b_const] + P.jb)
        P.op("dve", lambda e: e.memset(ones[:], 1.0), writes=[b_const])
        P.dma("sp", ident[:], ident_d, writes=[b_const], key="c_ident")
        P.dma("sp", masks[:].rearrange("p a b -> p (a b)"), mask_d, writes=[b_const], key="c_mask")
        P.dma("sp", rc[:], rc_d, writes=[b_const], key="c_rc")
        P.dma("sp", coef[:], coef_d, writes=[b_const], key="c_coef")
        P.dma("sp", invdiv[:].rearrange("p a b -> p (a b)"), idv_d, writes=[b_const], key="c_idv")

        LN_EPS_AP = sb("lneps", [128, 4], F32)
        a_mark[0] = a_top[0]

        def rope_tables(pos_ap, N, cos_dst, sin_dst, tag):
            H = N // 2
            posi = sb("posi" + tag, [128, H], I32)
            ang = sb("ang" + tag, [128, H], F32)
            ta = sb("ta" + tag, [128, H], F32)
            tb = sb("tb" + tag, [128, H], F32)
            obs = [sb("ob%d" % i + tag, [128, H], BF16) for i in range(2)]
            b_posi, b_ang, b_ta, b_tb = Buf(), Buf(), Buf(), Buf()
            b_obs = [Buf(), Buf()]
            P.dma("sp", posi[0:64, :], pos_ap[:, 0:H].partition_broadcast(64), writes=[b_posi], key="t_pos0" + tag)
            P.dma("sp", posi[64:128, :], pos_ap[:, H:N].partition_broadcast(64), writes=[b_posi], key="t_pos1" + tag)
            P.op("dve", lambda e: e.tensor_copy(out=ang[:], in_=posi[:]), reads=[b_posi], writes=[b_ang])
            P.op("dve", lambda e: e.tensor_scalar(out=ang[:], in0=ang[:], scalar1=rc[:, 0:1], scalar2=None, op0=ALU.mult),
                 reads=[b_ang, b_const], writes=[b_ang])
            TWO_PI = 2.0 * math.pi
            for wi_, (which, phase, dst) in enumerate((("sin", 0.0, sin_dst), ("cos", math.pi / 2, cos_dst))):
                ob, b_ob = obs[wi_], b_obs[wi_]
                P.op("dve", lambda e, phase=phase: e.tensor_scalar(out=ta[:], in0=ang[:], scalar1=phase, scalar2=1.0 / TWO_PI,
                                                                    op0=ALU.add, op1=ALU.mult), reads=[b_ang], writes=[b_ta])
                P.op("dve", lambda e: e.tensor_copy(out=posi[:], in_=ta[:]), reads=[b_ta], writes=[b_posi])
                P.op("dve", lambda e: e.tensor_copy(out=ta[:], in_=posi[:]), reads=[b_posi], writes=[b_ta])
                P.op("dve", lambda e: e.scalar_tensor_tensor(out=tb[:], in0=ta[:], scalar=-TWO_PI, in1=ang[:], op0=ALU.mult, op1=ALU.add),
                     reads=[b_ta, b_ang], writes=[b_tb])
                P.op("dve", lambda e, phase=phase: e.tensor_scalar(out=tb[:], in0=tb[:], scalar1=phase, scalar2=None, op0=ALU.add),
                     reads=[b_tb], writes=[b_tb])
                P.op("dve", lambda e: e.tensor_scalar(out=ta[:], in0=tb[:], scalar1=math.pi, scalar2=TWO_PI, op0=ALU.is_gt, op1=ALU.mult),
                     reads=[b_tb], writes=[b_ta])
                P.op("dve", lambda e: e.tensor_tensor(out=tb[:], in0=tb[:], in1=ta[:], op=ALU.subtract), reads=[b_tb, b_ta], writes=[b_tb])
                P.op("dve", lambda e: e.tensor_scalar(out=tb[:], in0=tb[:], scalar1=math.pi, scalar2=-math.pi, op0=ALU.min, op1=ALU.max),
                     reads=[b_tb], writes=[b_tb])
                P.op("act", lambda e: e.activation(out=ta[:], in_=tb[:], func=AF.Sin), reads=[b_tb], writes=[b_ta])
                if which == "sin":
                    P.op("dve", lambda e, ob=ob: e.tensor_scalar(out=ob[:], in0=ta[:], scalar1=rc[:, 1:2], scalar2=None, op0=ALU.mult),
                         reads=[b_ta, b_const], writes=[b_ob])
                else:
                    P.op("dve", lambda e, ob=ob: e.tensor_copy(out=ob[:], in_=ta[:]), reads=[b_ta], writes=[b_ob])
                P.dma("sp", dst[:, 0:H], ob[0:64, :], reads=[b_ob], writes=[b_tabd], key="t_ob0%d" % wi_ + tag)
                P.dma("sp", dst[:, H:N], ob[64:128, :], reads=[b_ob], writes=[b_tabd], key="t_ob1%d" % wi_ + tag)

        areset()
        rope_tables(pos_d, S, cosA_d, sinA_d, "a")
        rope_tables(poso_d, SO, cosO_d, sinO_d, "o")

        def ln_phase(st, layer_idx, src_kind, write_hT, final, wo_hi_bufs=None):
            if src_kind != "x":
                areset()
            NB = 4 if src_kind == "x" else 3
            gb = sb("ln_g", [128, D], F32, st)
            bb = sb("ln_b", [128, D], F32, st)
            b_p = Buf()
            if src_kind == "x":
                grow, brow = 0, 1
            else:
                grow, brow = 2 + 3 * layer_idx, 3 + 3 * layer_idx
            b_pg, b_pb = Buf(), Buf()
            P.dma("sp", gb[:], lnp_d[grow:grow + 1, :].partition_broadcast(128), writes=[b_pg], key="ln_g")
            P.dma("sp", bb[:], lnp_d[brow:brow + 1, :].partition_broadcast(128), writes=[b_pb], key="ln_b")
            ys = [sb("ln_y%d" % i, [128, D], F32, st) for i in range(NB)]
            b_ys = [Buf() for i in range(NB)]
            hbs = [sb("ln_hb%d" % i, [128, D], BF16, st) for i in range(2)]
            b_hbs = [Buf() for i in range(2)]
            stats = [sb("ln_st%d" % i, [128, 4, 6], F32, st) for i in range(NB)]
            mvs = [sb("ln_mv%d" % i, [128, 4], F32, st) for i in range(NB)]
            b_sts = [Buf() for i in range(NB)]
            stg = [sb("ln_stg%d" % i, [128, 16, 512], BF16, st) for i in range(1)] if write_hT else []
            b_stg = [Buf() for i in range(1)]
            if src_kind == "proj":
                bo = sb("ln_bo", [1, D], BF16, st)
                P.dma("pool", bo[:], lnp_d[4 + 3 * layer_idx:5 + 3 * layer_idx, :], writes=[b_p], key="ln_bo")
                wo = sb_top("wo", [128, 16, 2048], BF16)
                b_wop = [Buf(), Buf()] + list(wo_hi_bufs)
                for i4 in (1, 0):
                    P.dma("pool", wo[:, i4 * 4:(i4 + 1) * 4, :].rearrange("p a b -> p (a b)"),
                          wo_d[layer_idx * 128:(layer_idx + 1) * 128, i4 * 8192:(i4 + 1) * 8192],
                          reads=([b_wop[1]] if i4 == 0 else [b_p]), writes=[b_wop[i4]], key="wo%d" % i4)
                mts = [sb("mt%d" % i, [128, 16, 256], BF16, st) for i in range(2)]
                b_mts = [Buf() for i in range(2)]
                NR = 2
                rts = [sb("rt%d" % i, [128, D], F32, st) for i in range(NR)]
                b_rts = [Buf() for i in range(NR)]
                ea = [sb("ea%d" % i, [128, 8, 256], BF16, st) for i in range(2)]
                eb = [sb("eb%d" % i, [128, 8, 256], BF16, st) for i in range(2)]
                b_ea = [Buf() for i in range(2)]
                b_eb = [Buf() for i in range(2)]
                et = sb("et", [128, 8, 256], BF16, st)
                b_et = Buf()

            def s_pair(tk):
                tt = tk // 4
                t2 = tk // 2
                i2 = t2 % 2
                mt, b_mt = mts[i2], b_mts[i2]
                P.dma("sp", mt[:], mix_v[:, :, t2 * 256:(t2 + 1) * 256], reads=[b_mix[r][tt] for r in range(16)],
                      writes=[b_mt], key="mt%d" % i2)
                P.dma("sp", ea[i2][:].rearrange("p (h r) t -> p h r t", r=2), e2g_v[:, :, :, t2 * 256:(t2 + 1) * 256],
                      reads=b_e2g, writes=[b_ea[i2]], key="ea%d" % i2)
                P.dma("sp", eb[i2][:].rearrange("p (h r) t -> p h r t", r=2), e2g_v[:, :, :, SO + t2 * 256:SO + (t2 + 1) * 256],
                      reads=b_e2g, writes=[b_eb[i2]], key="eb%d" % i2)

            def s_blend(tk):
                i2 = (tk // 2) % 2
                mt, b_mt = mts[i2], b_mts[i2]
                P.op("dve", lambda e: e.tensor_scalar(out=et[:], in0=ea[i2][:], scalar1=coef[:, 0:1], scalar2=None, op0=ALU.mult),
                     reads=[b_ea[i2], b_const], writes=[b_et])
                P.op("dve", lambda e: e.scalar_tensor_tensor(out=et[:], in0=eb[i2][:], scalar=coef[:, 1:2], in1=et[:], op0=ALU.mult, op1=ALU.add),
                     reads=[b_eb[i2], b_et, b_const], writes=[b_et])
                mt8 = mt[:, 0:8, :].rearrange("p (r h) t -> p h r t", r=2)
                etv = et[:].rearrange("p (h r) t -> p h r t", r=2)
                P.op("pool", lambda e: e.tensor_tensor(out=mt8, in0=mt8, in1=etv, op=ALU.mult),
                     reads=[b_et, b_mt], writes=[b_mt])

            def s_load(tk):
                y, b_y = ys[tk % NB], b_ys[tk % NB]
                tsl = slice(tk * 128, (tk + 1) * 128)
                if src_kind == "x":
                    P.dma("sp", y[:], x_d[tsl, :], writes=[b_y], key="ln_y%d" % (tk % NB))
                else:
                    rt, b_rt = rts[tk % NR], b_rts[tk % NR]
                    P.dma("sp", rt[:], resid_d[tsl, :], reads=[b_resid[tk]], writes=[b_rt], key="rt%d" % (tk % NR))

            def s1(tk):
                y, b_y = ys[tk % NB], b_ys[tk % NB]
                stt, mv, b_st = stats[tk % NB], mvs[tk % NB], b_sts[tk % NB]
                if src_kind != "x":
                    t2 = tk // 2
                    mt, b_mt = mts[t2 % 2], b_mts[t2 % 2]
                    rt, b_rt = rts[tk % NR], b_rts[tk % NR]
                    pts = [psum() for cg in range(4)]
                    for cg in range(4):
                        pt, b_pt = pts[cg]
                        P.op("pe", lambda e, pt=pt, cg=cg: e.matmul(pt, ones[0:1, :], bo[0:1, cg * 512:(cg + 1) * 512], start=True, stop=False),
                             reads=[b_p, b_const], writes=[b_pt])
                    for kc in reversed(range(16)):
                        for cg in range(4):
                            pt, b_pt = pts[cg]
                            P.op("pe", lambda e, pt=pt, kc=kc, cg=cg: e.matmul(
                                pt, mt[:, kc, (tk % 2) * 128:(tk % 2 + 1) * 128], wo[:, kc, cg * 512:(cg + 1) * 512],
                                start=False, stop=(kc == 0)), reads=[b_mt, b_wop[kc // 4]], writes=[b_pt])
                    for cg in range(4):
                        pt, b_pt = pts[cg]
                        P.op("dve", lambda e, pt=pt, cg=cg: e.scalar_tensor_tensor(
                            out=y[:, cg * 512:(cg + 1) * 512], in0=rt[:, cg * 512:(cg + 1) * 512], scalar=ALPHA, in1=pt,
                            op0=ALU.mult, op1=ALU.add), reads=[b_pt, b_rt], writes=[b_y])
                for c in range(4):
                    P.op("dve", lambda e, c=c: e.bn_stats(out=stt[:, c, :], in_=y[:, c * 512:(c + 1) * 512]),
                         reads=[b_y], writes=[b_st])
                P.op("dve", lambda e: e.bn_aggr(out=mv[:, 0:2], in_=stt[:].rearrange("p a b -> p (a b)")),
                     reads=[b_st], writes=[b_st])
                P.op("act", lambda e: e.activation(out=mv[:, 2:3], in_=mv[:, 1:2], func=AF.Sqrt, bias=LN_EPS_AP[:, 0:1], scale=1.0),
                     reads=[b_st, b_const], writes=[b_st])
                P.op("dve", lambda e: e.reciprocal(out=mv[:, 2:3], in_=mv[:, 2:3]), reads=[b_st], writes=[b_st])
                P.op("dve", lambda e: e.scalar_tensor_tensor(out=mv[:, 3:4], in0=mv[:, 0:1], scalar=-1.0, in1=mv[:, 2:3],
                                                              op0=ALU.mult, op1=ALU.mult), reads=[b_st], writes=[b_st])

            def s2a(tk):
                y, b_y = ys[tk % NB], b_ys[tk % NB]
                mv, b_st = mvs[tk % NB], b_sts[tk % NB]
                P.op("act", lambda e: e.activation(out=y[:], in_=y[:], func=AF.Identity, bias=mv[:, 3:4], scale=mv[:, 2:3]),
                     reads=[b_y, b_st], writes=[b_y])

            def s2b(tk):
                y, b_y = ys[tk % NB], b_ys[tk % NB]
                P.op("dve", lambda e: e.tensor_tensor(out=y[:], in0=y[:], in1=gb[:], op=ALU.mult), reads=[b_y, b_pg], writes=[b_y])
                P.op("pool", lambda e: e.tensor_tensor(out=y[:], in0=y[:], in1=bb[:], op=ALU.add), reads=[b_y, b_pb], writes=[b_y])

            def s3(tk):
                y, b_y = ys[tk % NB], b_ys[tk % NB]
                hb, b_hb = hbs[tk % 2], b_hbs[tk % 2]
                tsl = slice(tk * 128, (tk + 1) * 128)
                if final:
                    P.dma("sp", out_d[tsl, :], y[:], reads=[b_y], writes=[b_out[tk]], key="ln_o%d" % (tk % NB))
                else:
                    P.dma("sp", resid_d[tsl, :], y[:], reads=[b_y], writes=[b_resid[tk]], key="ln_o%d" % (tk % NB))
                if write_hT:
                    P.op("act", lambda e: e.copy(out=hb[:], in_=y[:]), reads=[b_y], writes=[b_hb])
                    sg_, b_sg_ = stg[0], b_stg[0]
                    for q4 in range(4):
                        pt, b_pt = psum()
                        ptb = pt.bitcast(BF16)
                        for j in range(4):
                            kc = q4 * 4 + j
                            P.op("pe", lambda e, ptb=ptb, kc=kc, j=j: e.transpose(
                                ptb[:, j * 128:(j + 1) * 128], hb[:, kc * 128:(kc + 1) * 128], ident[:]),
                                reads=[b_hb, b_const], writes=[b_pt])
                        if q4 % 2 == 0:
                            P.op("dve", lambda e, ptb=ptb, q4=q4: e.tensor_copy(
                                out=sg_[:, q4 * 4:(q4 + 1) * 4, (tk % 4) * 128:(tk % 4 + 1) * 128],
                                in_=ptb[:, 0:512].rearrange("p (a b) -> p a b", a=4)), reads=[b_pt], writes=[b_sg_])
                        else:
                            P.op("act", lambda e, ptb=ptb, q4=q4: e.copy(
                                out=sg_[:, q4 * 4:(q4 + 1) * 4, (tk % 4) * 128:(tk % 4 + 1) * 128],
                                in_=ptb[:, 0:512].rearrange("p (a b) -> p a b", a=4)), reads=[b_pt], writes=[b_sg_])
                    if tk % 4 == 3:
                        tt = tk // 4
                        P.dma("sp", hT_v[:, :, tt * 512:(tt + 1) * 512], sg_[:], reads=[b_sg_], writes=[b_hT[tt]], key="ln_stg")
                        if tk == NTK - 1:
                            P.dma("sp", tls_v, sg_[:, :, 496:512], reads=[b_sg_], writes=[b_tlsrc], key="ln_tl")
                            P.collective(tl_src, tl_g, reads=[b_tlsrc], writes=[b_tlg], key="cc_e3")

            NTK = SO // 128
            is_proj = (src_kind != "x")
            if is_proj:
                s_pair(0)
                s_blend(0)
            s_load(0)
            for i in range(NTK + 2):
                if is_proj and i % 2 == 0 and i + 2 < NTK:
                    s_pair(i + 2)
                if is_proj and i % 2 == 1 and i + 1 < NTK:
                    s_blend(i + 1)
                if i + 1 < NTK:
                    s_load(i + 1)
                if 0 <= i - 1 < NTK:
                    s2a(i - 1)
                if i < NTK:
                    s1(i)
                if 0 <= i - 1 < NTK:
                    s2b(i - 1)
                if 0 <= i - 2 < NTK:
                    s3(i - 2)

        P.op("dve", lambda e: e.memset(LN_EPS_AP[:, 0:1], LN_EPS), writes=[b_const])
        P.op("dve", lambda e: e.memset(LN_EPS_AP[:, 1:2], RMS_EPS), writes=[b_const])

        with contextlib.ExitStack() as st:
            ln_phase(st, 0, "x", True, False)
            P.barrier(junk)

        def phase_A(L):
            with contextlib.ExitStack() as st:
                areset()
                hTs = sb("hTs", [128, 16, 2048], BF16, st)
                b_hTs_t = [Buf() for i in range(4)]
                wr = [sb("wr%d" % i, [128, 16, 512], BF16, st) for i in range(2)]
                b_wr = [Buf() for i in range(2)]
                sm = sb("sm", [128, 32], F32, st)
                wp = sb("wp", [128, 4, 128], BF16, st)
                b_sm = Buf()
                P.dma("sp", sm[:], sm_d[L * 128:(L + 1) * 128, :], writes=[b_sm], key="sm")
                P.dma("pool", wp[:].rearrange("p a b -> p (a b)"), wp_d[L * 128:(L + 1) * 128, :], writes=[b_sm], key="wp")
                hp = sb("hp", [128, 4, 16], F32, st)
                hc = sb("hc", [128, 4, 2], F32, st)
                b_hp = [Buf() for i in range(4)]
                b_hc = [Buf() for i in range(4)]
                P.op("pool", lambda e: e.memset(hp[:], 0.0), writes=b_hp)
                P.op("pool", lambda e: e.memset(hc[:], 0.0), writes=b_hc)
                stgs = [sb("stg%d" % i, [128, 4, 512], BF16, st) for i in range(2)]
                b_stgs = [Buf() for i in range(2)]
                stg_ctr = [0]
                NTMP = 6
                tmps = [sb("tmp%d" % i, [128, 528], F32, st) for i in range(NTMP)]
                b_tmps = [Buf() for i in range(NTMP)]
                tmp_ctr = [0]
                sqs = [sb("sq%d" % i, [128, 512], BF16, st) for i in range(2)]
                b_sqs = [Buf() for i in range(2)]
                sq_ctr = [0]
                pls = [sb("pl%d" % i, [128, 512], BF16, st) for i in range(4)]
                b_pls = [Buf() for i in range(4)]
                sgps = [sb("sgp%d" % i, [128, 512], F32, st) for i in range(4)]
                b_sgps = [Buf() for i in range(4)]
                pl_ctr = [0]
                pending = []
                cosT = sb("cosO", [64, SO], BF16, st)
                sinT = sb("sinO", [64, SO], BF16, st)
                b_tab = Buf()
                P.dma("sp", cosT[:], cosO_d, reads=[b_tabd], writes=[b_tab], key="cosO")
                P.dma("sp", sinT[:], sinO_d, reads=[b_tabd], writes=[b_tab], key="sinO")
                hTh = sb("hTh", [128, 16, 16], BF16, st)
                b_hTh = Buf()
                P.dma("sp", hTh[:], tlg_v[:, 0:16, :], reads=[b_tlg], writes=[b_hTh], key="hTh")

                def proj_halo(w, c0):
                    pt, b_pt = psum()
                    for kc in range(16):
                        P.op("pe", lambda e, pt=pt, w=w, kc=kc: e.matmul(
                            pt[:, 0:16], w[0][:, kc, c0:c0 + 128], hTh[:, kc, :],
                            start=(kc == 0), stop=(kc == 15)), reads=[w[1], b_hTh], writes=[b_pt])
                    return pt, b_pt

                def tmp():
                    i = tmp_ctr[0] % NTMP
                    tmp_ctr[0] += 1
                    return tmps[i], b_tmps[i]

                def proj(w, c0, ncols, tt):
                    pt, b_pt = psum()
                    for kc in range(16):
                        P.op("pe", lambda e, pt=pt, w=w, kc=kc: e.matmul(
                            pt[0:ncols, :], w[0][:, kc, c0:c0 + ncols], hTs[:, kc, tt * 512:(tt + 1) * 512],
                            start=(kc == 0), stop=(kc == 15)), reads=[w[1], b_hTs_t[tt]], writes=[b_pt])
                    return pt, b_pt

                def rmsnorm_group(w, c0, nch, tt, dim, dst_v, dst_bufs, gtt, key):
                    pts = [proj(w, c0 + c * 128, 128, tt) for c in range(nch)]
                    ss, b_ss = psum()
                    for c, (pt, b_pt) in enumerate(pts):
                        i = sq_ctr[0] % 2
                        sq_ctr[0] += 1
                        sq, b_sq = sqs[i], b_sqs[i]
                        P.op("act", lambda e, pt=pt, sq=sq: e.activation(out=sq[:], in_=pt, func=AF.Square), reads=[b_pt], writes=[b_sq])
                        P.op("pe", lambda e, ss=ss, sq=sq, c=c: e.matmul(ss, ones[:], sq[:], start=(c == 0), stop=(c == nch - 1)),
                             reads=[b_sq, b_const], writes=[b_ss])
                    rs, b_rs = tmp()
                    P.op("act", lambda e, rs=rs, ss=ss: e.activation(out=rs[:, 0:512], in_=ss, func=AF.Sqrt, bias=LN_EPS_AP[:, 1:2],
                                                                      scale=1.0 / dim), reads=[b_ss, b_const], writes=[b_rs])
                    P.op("dve", lambda e, rs=rs: e.reciprocal(out=rs[:, 0:512], in_=rs[:, 0:512]), reads=[b_rs], writes=[b_rs])
                    i = stg_ctr[0] % 2
                    stg_ctr[0] += 1
                    sg_, b_sg_ = stgs[i], b_stgs[i]
                    for c, (pt, b_pt) in enumerate(pts):
                        P.op("dve", lambda e, pt=pt, rs=rs, sg_=sg_, c=c: e.tensor_tensor(out=sg_[:, c, :], in0=pt, in1=rs[:, 0:512], op=ALU.mult),
                             reads=[b_pt, b_rs], writes=[b_sg_])
                    P.dma("sp", dst_v[:, :, gtt * 512:(gtt + 1) * 512], sg_[:, 0:nch, :], reads=[b_sg_], writes=[dst_bufs[gtt]], key="stg%d" % i)

                for hf in range(1):
                    b_hch = Buf()
                    for t4 in range(4):
                        P.dma("sp", hTs[:, :, t4 * 512:(t4 + 1) * 512], hT_v[:, :, t4 * 512:(t4 + 1) * 512], reads=[b_hT[t4]],
                              writes=[b_hTs_t[t4], b_hch], key="hTs%d" % t4)
                    for g in range(NG):
                        wi = (hf * NG + g) % 2
                        w = (wr[wi], b_wr[wi])
                        row0 = (L * NG + g) * 128
                        P.dma("pool", wr[wi][:].rearrange("p a b -> p (a b)"), win_d[row0:row0 + 128, :], writes=[b_wr[wi]], key="wr%d" % wi)
                        if g == 2:
                            if debug:
                                P.dma("sp", dbg_e1a, e1a_src, reads=b_qn, writes=[Buf()], key="dbg_e1a")
                                P.dma("sp", dbg_e1b, e1b_src, reads=b_kvn + b_kr, writes=[Buf()], key="dbg_e1b")
                            P.collective(e1b_src, e1b_g, reads=b_kvn + b_kr, writes=[b_e1g], key="cc_e1b")
                            P.collective(e1a_src, e1a_g, reads=b_qn + [b_e1g], writes=[b_e1g], key="cc_e1a")
                        for tt in range(4):
                            gtt = hf * 4 + tt
                            tok = slice(gtt * 512, (gtt + 1) * 512)
                            if g == 0:
                                rmsnorm_group(w, 0, 2, tt, 256.0, kvn_v, b_kvn, gtt, "kvn")
                                pa, b_pa = proj(w, 256, 64, tt)
                                pb, b_pb = proj(w, 320, 64, tt)
                                t1, b_t1 = tmp()
                                t2, b_t2 = tmp()
                                P.op("dve", lambda e, pa=pa, t1=t1, tok=tok: e.tensor_tensor(out=t1[0:64, 0:512], in0=pa[0:64, :], in1=cosT[:, tok], op=ALU.mult),
                                     reads=[b_pa, b_tab], writes=[b_t1])
                                P.op("dve", lambda e, pb=pb, t2=t2, tok=tok: e.tensor_tensor(out=t2[0:64, 0:512], in0=pb[0:64, :], in1=sinT[:, tok], op=ALU.mult),
                                     reads=[b_pb, b_tab], writes=[b_t2])
                                i = stg_ctr[0] % 2
                                stg_ctr[0] += 1
                                sg_, b_sg_ = stgs[i], b_stgs[i]
                                P.op("pool", lambda e, t1=t1, t2=t2, sg_=sg_: e.tensor_tensor(out=sg_[0:64, 0, :], in0=t1[0:64, 0:512], in1=t2[0:64, 0:512], op=ALU.add),
                                     reads=[b_t1, b_t2], writes=[b_sg_])
                                P.dma("sp", kr_d[:, tok], sg_[0:64, 0, :], reads=[b_sg_], writes=[b_kr[gtt]], key="stg%d" % i)
                            elif g == 1:
                                rmsnorm_group(w, 0, 4, tt, 512.0, qn_v, b_qn, gtt, "qn")
                            elif g in (2, 3):
                                i = stg_ctr[0] % 2
                                stg_ctr[0] += 1
                                sg_, b_sg_ = stgs[i], b_stgs[i]
                                for c in range(4):
                                    pt, b_pt = proj(w, c * 128, 128, tt)
                                    P.op("act", lambda e, pt=pt, sg_=sg_, c=c: e.activation(out=sg_[:, c, :], in_=pt, func=AF.Silu),
                                         reads=[b_pt], writes=[b_sg_])
                                r0 = (g - 2) * 4
                                P.dma("sp", mix_v[:, r0:r0 + 4, tok], sg_[:], reads=[b_sg_], writes=[b_mix[r0 + c][gtt] for c in range(4)], key="stg%d" % i)
                            elif g in (4, 5):
                                tails = []
                                for s_ in range(2):
                                    pg = (g - 4) * 2 + s_
                                    wlen = POOL_W[pg]
                                    px, b_px = proj(w, s_ * 256, 128, tt)
                                    pgt, b_pgt = proj(w, s_ * 256 + 128, 128, tt)
                                    xb, b_xb = tmp()
                                    sa, b_sa = tmp()
                                    sb_, b_sb = tmp()
                                    P.op("act", lambda e, px=px, xb=xb: e.copy(out=xb[:, 16:528], in_=px), reads=[b_px], writes=[b_xb])
                                    if tt == 0:
                                        ph, b_ph = proj_halo(w, s_ * 256)
                                        P.op("dve", lambda e, xb=xb, ph=ph: e.tensor_scalar(out=xb[:, 0:16], in0=ph[:, 0:16], scalar1=coef[:, 1:2], scalar2=None, op0=ALU.mult),
                                             reads=[b_ph, b_xb, b_const], writes=[b_xb])
                                    else:
                                        P.op("pool", lambda e, xb=xb, pg=pg: e.tensor_copy(out=xb[:, 0:16], in_=hp[:, pg, :]), reads=[b_hp[pg], b_xb], writes=[b_xb])
                                    P.op("pool", lambda e, xb=xb, pg=pg: e.tensor_copy(out=hp[:, pg, :], in_=xb[:, 512:528]), reads=[b_xb], writes=[b_hp[pg]])
                                    P.op("pool", lambda e, xb=xb, sa=sa: e.tensor_tensor(out=sa[:, 1:528], in0=xb[:, 1:528], in1=xb[:, 0:527], op=ALU.add),
                                         reads=[b_xb], writes=[b_sa])
                                    fin, b_fin = sa, b_sa
                                    if wlen >= 4:
                                        P.op("pool", lambda e, sa=sa, sb_=sb_: e.tensor_tensor(out=sb_[:, 3:528], in0=sa[:, 3:528], in1=sa[:, 1:526], op=ALU.add),
                                             reads=[b_sa], writes=[b_sb])
                                        fin, b_fin = sb_, b_sb
                                    if wlen >= 8:
                                        P.op("pool", lambda e, sa=sa, sb_=sb_: e.tensor_tensor(out=sa[:, 7:528], in0=sb_[:, 7:528], in1=sb_[:, 3:524], op=ALU.add),
                                             reads=[b_sb, b_sa], writes=[b_sa])
                                        fin, b_fin = sa, b_sa
                                    if wlen >= 16:
                                        P.op("pool", lambda e, sa=sa, sb_=sb_: e.tensor_tensor(out=sb_[:, 15:528], in0=sa[:, 15:528], in1=sa[:, 7:520], op=ALU.add),
                                             reads=[b_sa, b_sb], writes=[b_sb])
                                        fin, b_fin = sb_, b_sb
                                    ip = pl_ctr[0] % 4
                                    pl_ctr[0] += 1
                                    pl, b_pl = pls[ip], b_pls[ip]
                                    sgp, b_sgp = sgps[ip], b_sgps[ip]
                                    P.op("dve", lambda e, fin=fin, xb=xb, pl=pl, wlen=wlen: e.scalar_tensor_tensor(
                                        out=pl[:], in0=fin[:, 16:528], scalar=1.0 / wlen, in1=xb[:, 16:528], op0=ALU.mult, op1=ALU.subtract),
                                        reads=[b_fin, b_xb], writes=[b_pl])
                                    if gtt == 0:
                                        t16, b_t16 = tmp()
                                        P.op("dve", lambda e, fin=fin, t16=t16, pg=pg: e.tensor_tensor(out=t16[:, 0:16], in0=fin[:, 16:32], in1=invdiv[:, pg, :], op=ALU.mult),
                                             reads=[b_fin, b_const], writes=[b_t16])
                                        P.op("dve", lambda e, t16=t16, xb=xb, pl=pl: e.tensor_tensor(out=pl[:, 0:16], in0=t16[:, 0:16], in1=xb[:, 16:32], op=ALU.subtract),
                                             reads=[b_t16, b_xb, b_pl], writes=[b_pl])
                                    P.op("act", lambda e, pgt=pgt, sgp=sgp: e.activation(out=sgp[:], in_=pgt, func=AF.Silu), reads=[b_pgt], writes=[b_sgp])
                                    tails.append((s_, pg, pl, b_pl, sgp, b_sgp))

                                def pool_tail(tails=tails, g=g, gtt=gtt, tok=tok):
                                    i = stg_ctr[0] % 2
                                    stg_ctr[0] += 1
                                    sg_, b_sg_ = stgs[i], b_stgs[i]
                                    for (s_, pg, pl, b_pl, sgp, b_sgp) in tails:
                                        py, b_py = psum()
                                        P.op("pe", lambda e, py=py, pl=pl, pg=pg: e.matmul(py, wp[:, pg, :], pl[:], start=True, stop=True),
                                             reads=[b_pl, b_sm], writes=[b_py])
                                        P.op("dve", lambda e, py=py, sgp=sgp, pg=pg, s_=s_: e.scalar_tensor_tensor(
                                            out=sg_[:, s_, :], in0=py, scalar=sm[:, 6 + pg:7 + pg], in1=sgp[:], op0=ALU.mult, op1=ALU.mult),
                                            reads=[b_py, b_sgp, b_sm], writes=[b_sg_])
                                    r0 = 8 + (g - 4) * 2
                                    P.dma("sp", mix_v[:, r0:r0 + 2, tok], sg_[:, 0:2, :], reads=[b_sg_], writes=[b_mix[r0 + c][gtt] for c in range(2)], key="stg%d" % i)

                                for fn_ in pending:
                                    fn_()
                                pending.clear()
                                pending.append(pool_tail)
                                if tt == 3:
                                    for fn_ in pending:
                                        fn_()
                                    pending.clear()
                            else:
                                j = g - 6
                                i = stg_ctr[0] % 2
                                stg_ctr[0] += 1
                                sg_, b_sg_ = stgs[i], b_stgs[i]
                                pch, b_pch = proj(w, 0, 128, tt)
                                pcc, b_pcc = proj(w, 128, 128, tt)
                                pcb, b_pcb = proj(w, 256, 128, tt)
                                pgc, b_pgc = proj(w, 384, 128, tt)
                                chs, b_chs = tmp()
                                ub, b_ub = tmp()
                                t1, b_t1 = tmp()
                                t2, b_t2 = tmp()
                                P.op("act", lambda e, pch=pch, chs=chs: e.copy(out=chs[:, 0:512], in_=pch), reads=[b_pch], writes=[b_chs])
                                P.op("dve", lambda e, pcc=pcc, chs=chs, ub=ub: e.tensor_tensor(out=ub[:, 2:514], in0=pcc, in1=chs[:, 0:512], op=ALU.mult),
                                     reads=[b_pcc, b_chs], writes=[b_ub])
                                if tt == 0:
                                    ph1, b_ph1 = proj_halo(w, 0)
                                    ph2, b_ph2 = proj_halo(w, 128)
                                    hh, b_hh = tmp()
                                    P.op("act", lambda e, ph1=ph1, hh=hh: e.copy(out=hh[:, 0:16], in_=ph1[:, 0:16]), reads=[b_ph1], writes=[b_hh])
                                    P.op("dve", lambda e, ph2=ph2, hh=hh: e.tensor_tensor(out=hh[:, 16:32], in0=ph2[:, 0:16], in1=hh[:, 0:16], op=ALU.mult),
                                         reads=[b_ph2, b_hh], writes=[b_hh])
                                    P.op("dve", lambda e, ub=ub, hh=hh: e.tensor_scalar(out=ub[:, 0:2], in0=hh[:, 30:32], scalar1=coef[:, 1:2], scalar2=None, op0=ALU.mult),
                                         reads=[b_hh, b_ub, b_const], writes=[b_ub])
                                else:
                                    P.op("pool", lambda e, ub=ub, j=j: e.tensor_copy(out=ub[:, 0:2], in_=hc[:, j, :]), reads=[b_hc[j], b_ub], writes=[b_ub])
                                P.op("pool", lambda e, ub=ub, j=j: e.tensor_copy(out=hc[:, j, :], in_=ub[:, 512:514]), reads=[b_ub], writes=[b_hc[j]])
                                cw0 = 10 + j * 3
                                P.op("dve", lambda e, ub=ub, t1=t1, cw0=cw0: e.tensor_scalar(out=t1[:, 0:512], in0=ub[:, 0:512], scalar1=sm[:, cw0:cw0 + 1], scalar2=None, op0=ALU.mult),
                                     reads=[b_ub, b_sm], writes=[b_t1])
                                P.op("dve", lambda e, ub=ub, t1=t1, t2=t2, cw0=cw0: e.scalar_tensor_tensor(
                                    out=t2[:, 0:512], in0=ub[:, 1:513], scalar=sm[:, cw0 + 1:cw0 + 2], in1=t1[:, 0:512], op0=ALU.mult, op1=ALU.add),
                                    reads=[b_ub, b_t1, b_sm], writes=[b_t2])
                                P.op("dve", lambda e, ub=ub, t1=t1, t2=t2, cw0=cw0: e.scalar_tensor_tensor(
                                    out=t1[:, 0:512], in0=ub[:, 2:514], scalar=sm[:, cw0 + 2:cw0 + 3], in1=t2[:, 0:512], op0=ALU.mult, op1=ALU.add),
                                    reads=[b_ub, b_t2, b_t1, b_sm], writes=[b_t1])
                                P.op("dve", lambda e, pcb=pcb, t1=t1, t2=t2: e.tensor_tensor(out=t2[:, 0:512], in0=pcb, in1=t1[:, 0:512], op=ALU.mult),
                                     reads=[b_pcb, b_t1, b_t2], writes=[b_t2])
                                P.op("act", lambda e, pgc=pgc, chs=chs: e.activation(out=chs[:, 0:512], in_=pgc, func=AF.Silu), reads=[b_pgc, b_chs], writes=[b_chs])
                                P.op("pool", lambda e, t2=t2, chs=chs, sg_=sg_: e.tensor_tensor(out=sg_[:, 0, :], in0=t2[:, 0:512], in1=chs[:, 0:512], op=ALU.mult),
                                     reads=[b_t2, b_chs], writes=[b_sg_])
                                r0 = 12 + j
                                P.dma("sp", mix_v[:, r0, tok], sg_[:, 0, :], reads=[b_sg_], writes=[b_mix[r0][gtt]], key="stg%d" % i)
                P.barrier(junk)

        def phase_B(L, wo_hi_bufs):
            with contextlib.ExitStack() as st:
                areset()
                wo_hi = sb_top("wo_hi", [128, 8, 2048], BF16)
                SB_ = [4, 5, 6, 7]
                sctr = [0]
                octr = [0]
                lctr = [0]
                qns = sb("qns", [128, 4, S], BF16, st)
                kvns = sb("kvns", [128, 2, S], BF16, st)
                krs = sb("krs", [64, S], BF16, st)
                b_kvl, b_krl, b_qnl = [Buf(), Buf()], [Buf(), Buf()], [Buf(), Buf()]
                b_lc = [Buf(), Buf()]
                for rho in range(2):
                    csl = slice(rho * SO, (rho + 1) * SO)
                    P.dma("sp", kvns[:, :, csl], e1b_g[rho * 320:rho * 320 + 256, :].rearrange("(kc p) t -> p kc t", p=128), reads=[b_e1g], writes=[b_kvl[rho], b_lc[rho]], key="kvns%d" % rho)
                    P.dma("sp", krs[:, csl], e1b_g[rho * 320 + 256:rho * 320 + 320, :], reads=[b_e1g], writes=[b_krl[rho], b_lc[rho]], key="krs%d" % rho)
                    P.dma("sp", qns[:, :, csl], e1a_g[rho * 512:(rho + 1) * 512, :].rearrange("(kc p) t -> p kc t", p=128), reads=[b_e1g], writes=[b_qnl[rho], b_lc[rho]], key="qns%d" % rho)
                cosT = sb("cosA", [64, S], BF16, st)
                sinT = sb("sinA", [64, S], BF16, st)
                b_tab = Buf()
                P.dma("sp", cosT[:], cosA_d, reads=[b_tabd], writes=[b_tab], key="cosA")
                P.dma("sp", sinT[:], sinA_d, reads=[b_tabd], writes=[b_tab], key="sinA")
                wq = sb("wq", [128, 4, 1024], BF16, st)
                wk = sb("wk", [128, 2, 512], BF16, st)
                wv = sb("wv", [128, 2, 512], BF16, st)
                sm = sb("smB", [128, 32], F32, st)
                b_w = Buf()
                P.dma("sp", sm[:], sm_d[L * 128:(L + 1) * 128, :], writes=[b_w], key="smB")
                b_w1, b_w2, b_w3 = Buf(), Buf(), Buf()
                P.dma("pool", wq[:].rearrange("p a b -> p (a b)"), wq_d[L * 128:(L + 1) * 128, :], writes=[b_w1], key="wq")
                P.dma("pool", wk[:].rearrange("p a b -> p (a b)"), wk_d[L * 128:(L + 1) * 128, :], reads=[b_w1], writes=[b_w2], key="wk")
                P.dma("pool", wv[:].rearrange("p a b -> p (a b)"), wv_d[L * 128:(L + 1) * 128, :], reads=[b_w1, b_w2], writes=[b_w, b_w3], key="wv")
                P.dma("pool", wo_hi[:, 4:8, :].rearrange("p a b -> p (a b)"), wo_d[L * 128:(L + 1) * 128, 3 * 8192:4 * 8192],
                      reads=[b_w3], writes=[wo_hi_bufs[1]], key="wo3")
                P.dma("pool", wo_hi[:, 0:4, :].rearrange("p a b -> p (a b)"), wo_d[L * 128:(L + 1) * 128, 2 * 8192:3 * 8192],
                      reads=[wo_hi_bufs[1]], writes=[wo_hi_bufs[0]], key="wo2")
                for kc in range(4):
                    P.op("dve", lambda e, kc=kc: e.tensor_scalar(out=wq[:, kc, :], in0=wq[:, kc, :], scalar1=sm[:, kc:kc + 1], scalar2=None, op0=ALU.mult),
                         reads=[b_w], writes=[b_w])
                for kc in range(2):
                    P.op("dve", lambda e, kc=kc: e.tensor_scalar(out=wk[:, kc, :], in0=wk[:, kc, :], scalar1=sm[:, 4 + kc:5 + kc], scalar2=None, op0=ALU.mult),
                         reads=[b_w], writes=[b_w])
                    P.op("dve", lambda e, kc=kc: e.tensor_scalar(out=wv[:, kc, :], in0=wv[:, kc, :], scalar1=sm[:, 4 + kc:5 + kc], scalar2=None, op0=ALU.mult),
                         reads=[b_w], writes=[b_w])
                Vq = sb("Vq", [128, 32, 512], BF16, st)
                b_V = [Buf() for i in range(32)]
                kTh = sb("kTh", [128, S], BF16, st)
                qTh = sb("qTh", [128, S], BF16, st)
                qrh = sb("qrh", [64, S], BF16, st)
                b_kT = [Buf() for i in range(8)]
                b_qT = [Buf() for i in range(8)]
                b_qr = [Buf() for i in range(8)]
                pTs = [sb("pT%d" % i, [128, 512], BF16, st) for i in range(4)]
                b_pTs = [Buf() for i in range(4)]
                pT_ctr = [0]
                rls = [sb("rl%d" % i, [128, 512], F32, st) for i in range(2)]
                b_rls = [Buf() for i in range(2)]
                outs = [sb("ob%d" % i, [128, 512], BF16, st) for i in range(2)]
                b_outs = [Buf() for i in range(2)]
                rt1 = [sb("rta%d" % i, [64, 512], F32, st) for i in range(2)]
                rt2 = [sb("rtb%d" % i, [64, 512], F32, st) for i in range(2)]
                b_rt1 = [Buf() for i in range(2)]
                b_rt2 = [Buf() for i in range(2)]
                ev = [0]

                def evac(out_ap, in_ap, reads, writes):
                    ev[0] += 1
                    if ev[0] % 2 == 0:
                        P.op("dve", lambda e: e.tensor_copy(out=out_ap, in_=in_ap), reads=reads, writes=writes)
                    else:
                        P.op("act", lambda e: e.copy(out=out_ap, in_=in_ap), reads=reads, writes=writes)

                uc = [0]
                for h in range(HL):
                    if h % 4 == 0:
                        hq = h // 4
                        for tk in range(32):
                            pt, b_pt = psum(SB_, sctr)
                            for kc in range(2):
                                P.op("pe", lambda e, pt=pt, kc=kc, tk=tk, hq=hq: e.matmul(
                                    pt, kvns[:, kc, tk * 128:(tk + 1) * 128], wv[:, kc, hq * 512:(hq + 1) * 512], start=(kc == 0), stop=(kc == 1)),
                                    reads=[b_kvl[tk // 16], b_w], writes=[b_pt])
                            evac(Vq[:, tk, :], pt, [b_pt], [b_V[tk]])
                    for tt in range(8):
                        tok = slice(tt * 512, (tt + 1) * 512)
                        pt, b_pt = psum(SB_, sctr)
                        for kc in range(2):
                            P.op("pe", lambda e, pt=pt, kc=kc, tok=tok, h=h: e.matmul(
                                pt, wk[:, kc, h * 128:(h + 1) * 128], kvns[:, kc, tok], start=(kc == 0), stop=(kc == 1)),
                                reads=[b_kvl[tt // 4], b_w], writes=[b_pt])
                        evac(kTh[:, tok], pt, [b_pt], [b_kT[tt]])
                        pt, b_pt = psum(SB_, sctr)
                        for kc in range(4):
                            P.op("pe", lambda e, pt=pt, kc=kc, tok=tok, h=h: e.matmul(
                                pt, wq[:, kc, h * 256:h * 256 + 128], qns[:, kc, tok], start=(kc == 0), stop=(kc == 3)),
                                reads=[b_qnl[tt // 4], b_w], writes=[b_pt])
                        evac(qTh[:, tok], pt, [b_pt], [b_qT[tt]])
                        pa, b_pa = psum(SB_, sctr)
                        for kc in range(4):
                            P.op("pe", lambda e, pa=pa, kc=kc, tok=tok, h=h: e.matmul(
                                pa[0:64, :], wq[:, kc, h * 256 + 128:h * 256 + 192], qns[:, kc, tok], start=(kc == 0), stop=(kc == 3)),
                                reads=[b_qnl[tt // 4], b_w], writes=[b_pa])
                        pb, b_pb = psum(SB_, sctr)
                        for kc in range(4):
                            P.op("pe", lambda e, pb=pb, kc=kc, tok=tok, h=h: e.matmul(
                                pb[0:64, :], wq[:, kc, h * 256 + 192:h * 256 + 256], qns[:, kc, tok], start=(kc == 0), stop=(kc == 3)),
                                reads=[b_qnl[tt // 4], b_w], writes=[b_pb])
                        i = tt % 2
                        P.op("dve", lambda e, pa=pa, i=i, tok=tok: e.tensor_tensor(out=rt1[i][:], in0=pa[0:64, :], in1=cosT[:, tok], op=ALU.mult),
                             reads=[b_pa, b_tab], writes=[b_rt1[i]])
                        P.op("dve", lambda e, pb=pb, i=i, tok=tok: e.tensor_tensor(out=rt2[i][:], in0=pb[0:64, :], in1=sinT[:, tok], op=ALU.mult),
                             reads=[b_pb, b_tab], writes=[b_rt2[i]])
                        P.op("pool", lambda e, i=i, tok=tok: e.tensor_tensor(out=qrh[:, tok], in0=rt1[i][:], in1=rt2[i][:], op=ALU.add),
                             reads=[b_rt1[i], b_rt2[i]], writes=[b_qr[tt]])
                    units = [(qb, kb) for qb in range(8) for kb in range(4 * qb + 4)]
                    LOOK = 2
                    acc = {}
                    sc = {}

                    def emit_scores(u):
                        qb, kb = units[u]
                        qtok = slice(qb * 512, (qb + 1) * 512)
                        ktok = slice(kb * 128, (kb + 1) * 128)
                        ps_, b_ps_ = psum(SB_, sctr)
                        diag = kb >= 4 * qb
                        c0 = (kb - 4 * qb) * 128 if diag else 0
                        qsl = slice(qb * 512 + c0, (qb + 1) * 512)
                        P.op("pe", lambda e: e.matmul(ps_[:, c0:512], kTh[:, ktok], qTh[:, qsl], start=True, stop=False),
                             reads=[b_kT[kb // 4], b_qT[qb]], writes=[b_ps_])
                        P.op("pe", lambda e: e.matmul(ps_[:, c0:512], krs[:, ktok], qrh[:, qsl], start=False, stop=(not diag)),
                             reads=[b_krl[kb // 16], b_qr[qb]], writes=[b_ps_])
                        if diag:
                            jm = kb - 4 * qb
                            P.op("pe", lambda e: e.matmul(ps_[:, c0:512], ident[:], masks[:, jm, c0:512], start=False, stop=True),
                                 reads=[b_const], writes=[b_ps_])
                        sc[u] = (ps_, b_ps_, c0)

                    def emit_rest(u, h=h):
                        qb, kb = units[u]
                        nkb = 4 * qb + 4
                        qtok = slice(qb * 512, (qb + 1) * 512)
                        if kb == 0:
                            acc[qb] = (psum([0, 1], octr), psum([2, 3], lctr))
                        (po, b_po), (pl_, b_pl_) = acc[qb]
                        ps_, b_ps_, c0 = sc.pop(u)
                        ip = pT_ctr[0] % 4
                        pT_ctr[0] += 1
                        pT, b_pT = pTs[ip], b_pTs[ip]
                        P.op("act", lambda e: e.activation(out=pT[:, c0:512], in_=ps_[:, c0:512], func=AF.Exp, scale=SCALE), reads=[b_ps_], writes=[b_pT])
                        P.op("pe", lambda e: e.matmul(po[:, c0:512], Vq[:, kb, (h % 4) * 128:(h % 4 + 1) * 128], pT[:, c0:512], start=(kb == 0), stop=(kb == nkb - 1)),
                             reads=[b_V[kb], b_pT], writes=[b_po])
                        P.op("pe", lambda e: e.matmul(pl_[:, c0:512], ones[:], pT[:, c0:512], start=(kb == 0), stop=(kb == nkb - 1)),
                             reads=[b_const, b_pT], writes=[b_pl_])
                        if kb == nkb - 1:
                            i = uc[0] % 2
                            uc[0] += 1
                            P.op("dve", lambda e: e.reciprocal(out=rls[i][:], in_=pl_), reads=[b_pl_], writes=[b_rls[i]])
                            P.op("dve", lambda e: e.tensor_tensor(out=outs[i][:], in0=po, in1=rls[i][:], op=ALU.mult),
                                 reads=[b_po, b_rls[i]], writes=[b_outs[i]])
                            P.dma("sp", e2_src[h][:, qtok], outs[i][:], reads=[b_outs[i]], writes=[b_e2s[h][qb]], key="ob%d" % i)
                            if qb == 7:
                                if debug:
                                    P.dma("sp", dbg_e2[h], e2_src[h], reads=b_e2s[h], writes=[Buf()], key="dbg_e2")
                                P.collective(e2_src[h], e2_g[h], reads=b_e2s[h], writes=[b_e2g[h]], key="cc_e2_%d" % h)

                    for u in range(min(LOOK, len(units))):
                        emit_scores(u)
                    for u in range(len(units)):
                        if u + LOOK < len(units):
                            emit_scores(u + LOOK)
                        emit_rest(u)
                P.barrier(junk)

        for L in range(n_layers):
            if stop_phase == "ln0":
                break
            phase_A(L)
            if stop_phase == "A":
                break
            wo_hi_bufs = [Buf(), Buf()]
            phase_B(L, wo_hi_bufs)
            if stop_phase == "B":
                break
            with contextlib.ExitStack() as st:
                last = (L == DEPTH - 1)
                ln_phase(st, L, "proj", not last, last, wo_hi_bufs)
                P.barrier(junk)
        P.finish()
    return nc


def _tile_k(w, ncols_pad=None):
    K, C = w.shape
    return np.ascontiguousarray(w.reshape(K // 128, 128, C).transpose(1, 0, 2))


def prep_inputs(x, positions, emb_ln_g, emb_ln_b, w_in, q_norm_g, kv_norm_g, w_uq, w_ukv, w_pool,
                pool_scale, conv_w, w_out, b_out, ln_g, ln_b):
    f32 = np.float32
    w_in = np.asarray(w_in, f32)
    offs = np.cumsum([0, 512, 256, 64, 1024, 512, 512, 512, 512, 512, 512])
    o_q, o_kv, o_kr, o_gm, o_pi, o_gp, o_ch, o_cb, o_cc, o_gc = offs[:10]
    groups = []
    zero128 = None
    for L in range(DEPTH):
        W = w_in[L]
        kr = W[:, o_kr:o_kr + 64]
        ksw = np.concatenate([kr[:, 32:64], kr[:, 0:32]], axis=1)
        g0 = np.concatenate([W[:, o_kv:o_kv + 256], kr, ksw, np.zeros((D, 128), f32)], axis=1)
        gl = [g0, W[:, o_q:o_q + 512], W[:, o_gm:o_gm + 512], W[:, o_gm + 512:o_gm + 1024]]
        for a in range(2):
            gl.append(np.concatenate([W[:, o_pi + (2 * a) * 128:o_pi + (2 * a + 1) * 128], W[:, o_gp + (2 * a) * 128:o_gp + (2 * a + 1) * 128],
                                      W[:, o_pi + (2 * a + 1) * 128:o_pi + (2 * a + 2) * 128], W[:, o_gp + (2 * a + 1) * 128:o_gp + (2 * a + 2) * 128]], axis=1))
        for j in range(4):
            sl = slice(j * 128, (j + 1) * 128)
            gl.append(np.concatenate([W[:, o_ch:o_ch + 512][:, sl], W[:, o_cc:o_cc + 512][:, sl], W[:, o_cb:o_cb + 512][:, sl], W[:, o_gc:o_gc + 512][:, sl]], axis=1))
        for gmat in gl:
            groups.append(_tile_k(gmat).reshape(128, 16 * 512))
    w_in_g = np.ascontiguousarray(np.concatenate(groups, axis=0))

    wq_l, wk_l, wv_l = [[], []], [[], []], [[], []]
    wo_l, wp_l, sm_l = [], [], []
    for L in range(DEPTH):
        wq = np.asarray(w_uq[L], f32).reshape(512, NH, 192)
        rope = wq[:, :, 128:192]
        sw = np.concatenate([rope[:, :, 32:64], rope[:, :, 0:32]], axis=2)
        wq2 = np.concatenate([wq, sw], axis=2)
        wkv = np.asarray(w_ukv[L], f32).reshape(256, NH, 256)
        for r in range(2):
            hs = slice(4 * r, 4 * r + 4)
            wq_l[r].append(_tile_k(np.ascontiguousarray(wq2[:, hs]).reshape(512, 1024)).reshape(128, 4 * 1024))
            wk_l[r].append(_tile_k(np.ascontiguousarray(wkv[:, hs, 0:128]).reshape(256, 512)).reshape(128, 2 * 512))
            wv_l[r].append(_tile_k(np.ascontiguousarray(wkv[:, hs, 128:256]).reshape(256, 512)).reshape(128, 2 * 512))
        wo_l.append(_tile_k(np.asarray(w_out[L], f32)).reshape(128, 16 * 2048))
        wp_l.append(np.ascontiguousarray(np.asarray(w_pool[L], f32).transpose(1, 0, 2)).reshape(128, 4 * 128))
        sm = np.zeros((128, 32), f32)
        sm[:, 0:4] = np.asarray(q_norm_g[L], f32).reshape(4, 128).T
        sm[:, 4:6] = np.asarray(kv_norm_g[L], f32).reshape(2, 128).T
        sm[:, 6:10] = np.asarray(pool_scale[L], f32).reshape(4, 128).T
        cw = np.asarray(conv_w[L], f32).reshape(3, 4, 128)
        sm[:, 10:22] = cw.transpose(2, 1, 0).reshape(128, 12)
        sm_l.append(sm)
    lnp = np.stack([np.asarray(emb_ln_g, f32), np.asarray(emb_ln_b, f32)] +
                   sum([[np.asarray(ln_g[L], f32), np.asarray(ln_b[L], f32), np.asarray(b_out[L], f32)] for L in range(DEPTH)], []), axis=0)
    half = 32
    inv_freq = (10000.0 ** (-np.arange(half, dtype=np.float32) / half)).astype(f32)
    ropec = np.zeros((128, 2), f32)
    ropec[:, 0] = np.concatenate([inv_freq] * 4)
    ropec[:, 1] = np.concatenate([-np.ones(32, f32), np.ones(32, f32)] * 2)
    invdiv = np.zeros((2, 128, 4, 16), f32)
    for g, w in enumerate(POOL_W):
        invdiv[0, :, g, :] = 1.0 / np.minimum(np.arange(1, 17, dtype=f32), float(w))
        invdiv[1, :, g, :] = 1.0 / float(w)
    ident = np.eye(128, dtype=f32).astype(ml_dtypes.bfloat16)
    kk = np.arange(128)[:, None]
    qq = np.arange(512)[None, :]
    masks = np.stack([np.where(j * 128 + kk <= qq, 0.0, NEG) for j in range(4)], axis=1).astype(f32)
    masks = masks.reshape(128, 4 * 512).astype(ml_dtypes.bfloat16)
    shared = {
        "ropec": ropec, "lnp": np.ascontiguousarray(lnp), "w_in_g": w_in_g,
        "wo": np.ascontiguousarray(np.concatenate(wo_l, 0)),
        "wp": np.ascontiguousarray(np.concatenate(wp_l, 0)), "small": np.ascontiguousarray(np.concatenate(sm_l, 0)),
        "ident": ident, "masks": masks,
    }
    per_rank = []
    for r in range(2):
        coef = np.zeros((128, 2), f32)
        coef[:, r] = 1.0
        per_rank.append({
            "wq": np.ascontiguousarray(np.concatenate(wq_l[r], 0)), "wk": np.ascontiguousarray(np.concatenate(wk_l[r], 0)),
            "wv": np.ascontiguousarray(np.concatenate(wv_l[r], 0)), "invdiv": np.ascontiguousarray(invdiv[r].reshape(128, 64)),
            "coef": coef,
        })
    x = np.asarray(x, f32)
    positions = np.asarray(positions, np.int32)
    in_maps = []
    for c in range(8):
        b, r = c // 2, c % 2
        m = dict(shared)
        m.update(per_rank[r])
        m["x"] = np.ascontiguousarray(x[b, r * SO:(r + 1) * SO])
        m["pos"] = np.ascontiguousarray(positions[b][None, :])
        m["pos_own"] = np.ascontiguousarray(positions[b, r * SO:(r + 1) * SO][None, :])
        in_maps.append(m)
    return in_maps


def kernel(**inputs):
    in_maps = prep_inputs(**inputs)
    nc = build()
    res = run_bass_kernel_spmd(nc, in_maps, core_ids=list(range(8)))
    out = np.empty((4, S, D), np.float32)
    for c in range(8):
        b, r = c // 2, c % 2
        out[b, r * SO:(r + 1) * SO] = np.asarray(res.results[c]["out"], dtype=np.float32)
    return out
```
